# Optimizing a Trainium2 kernel written in Bass

```python
import math
import jax, jax.numpy as jnp
from jax import lax
import numpy as np

D_MODEL = 2048
BATCH = 4
SEQ = 4096
DEPTH = 1

MEM_LEN = 256
LN_EPS = 1e-5
GLA_HEADS = 4
GLA_DV = (D_MODEL // 2) // GLA_HEADS
GLA_DK = GLA_DV // 2
GLA_GATE_RANK = 16
GLA_TAU = 16.0
GLA_CHUNK = 64
DIL_HD = 128
DIL_HEADS = (D_MODEL // 2) // DIL_HD
DIL_PATTERNS = ((128, 1), (512, 4), (2048, 16))
ROPE_THETA = 500000.0
ROPE_DIMS = DIL_HD // 4
CA_HEADS = 4
CA_HD = D_MODEL // CA_HEADS
D_FF = 5504
CONV_W = 3
DEEPNORM_ALPHA = (2.0 * DEPTH) ** 0.25
DEEPNORM_BETA = (8.0 * DEPTH) ** -0.25
IN_WIDTHS = (GLA_HEADS * GLA_DK, GLA_HEADS * GLA_DK, GLA_HEADS * GLA_DV, GLA_HEADS * GLA_DV,
             GLA_GATE_RANK, DIL_HEADS * DIL_HD, DIL_HEADS * DIL_HD, DIL_HEADS * DIL_HD)
IN_COLS = sum(IN_WIDTHS)
MIX_WIDTH = GLA_HEADS * GLA_DV + DIL_HEADS * DIL_HD

kernel_name = "hybrid_gla_dilated_attn_deepnorm_layer"


def split_cols(h, widths):
    outs, start = [], 0
    for w in widths:
        outs.append(h[..., start:start + w])
        start += w
    return outs


def layer_norm(x, g, b):
    xf = x.astype(jnp.float32)
    mu = jnp.mean(xf, axis=-1, keepdims=True)
    var = jnp.mean(jnp.square(xf - mu), axis=-1, keepdims=True)
    y = (xf - mu) * lax.rsqrt(var + LN_EPS)
    return (y * g.astype(jnp.float32) + b.astype(jnp.float32)).astype(x.dtype)


def partial_rotary(t, cos, sin):
    half = ROPE_DIMS // 2
    t1 = t[..., :half].astype(jnp.float32)
    t2 = t[..., half:ROPE_DIMS].astype(jnp.float32)
    rot = jnp.concatenate([t1 * cos - t2 * sin, t2 * cos + t1 * sin], axis=-1)
    return jnp.concatenate([rot.astype(t.dtype), t[..., ROPE_DIMS:]], axis=-1)


def gla_chunked(q, k, v, log_g):
    B, S, H, dk = q.shape
    dv = v.shape[-1]
    C = GLA_CHUNK
    N = S // C

    def chunk(t):
        return t.astype(jnp.float32).reshape(B, N, C, H, -1).transpose(0, 3, 1, 2, 4)

    qc = chunk(q) * (dk ** -0.5)
    kc, vc, gc = chunk(k), chunk(v), chunk(log_g)
    b = lax.cumsum(gc, axis=3)
    b_last = b[:, :, :, -1:, :]
    q_in = qc * jnp.exp(b)
    k_in = kc * jnp.exp(-b)
    k_end = kc * jnp.exp(b_last - b)
    causal = jnp.tril(jnp.ones((C, C), dtype=bool))
    A = jnp.where(causal, jnp.einsum('bhnik,bhnjk->bhnij', q_in, k_in), 0.0)
    o = jnp.einsum('bhnij,bhnjv->bhniv', A, vc)
    dS = jnp.einsum('bhnjk,bhnjv->bhnkv', k_end, vc)
    decay = jnp.exp(b_last[:, :, :, 0, :])

    def step(state, inp):
        dec, ds = inp
        return dec[..., None] * state + ds, state

    _, s_before = lax.scan(step, jnp.zeros((B, H, dk, dv), jnp.float32),
                           (jnp.moveaxis(decay, 2, 0), jnp.moveaxis(dS, 2, 0)))
    s_before = jnp.moveaxis(s_before, 0, 2)
    o = o + jnp.einsum('bhnik,bhnkv->bhniv', q_in, s_before)
    return o.transpose(0, 2, 3, 1, 4).reshape(B, S, H, dv)


def dilated_branch(q, k, v, window, dilation):
    B, S, H, hd = q.shape
    L = S // dilation
    band = window // dilation
    nb = -(-L // band)
    Lp = nb * band

    def to_sub(t):
        t = t.reshape(B, L, dilation, H, hd).transpose(0, 2, 3, 1, 4)
        t = jnp.pad(t, ((0, 0), (0, 0), (0, 0), (0, Lp - L), (0, 0)))
        return t.reshape(B, dilation, H, nb, band, hd)

    def with_prev(t):
        prev = jnp.pad(t[:, :, :, :-1], ((0, 0), (0, 0), (0, 0), (1, 0), (0, 0), (0, 0)))
        return jnp.concatenate([prev, t], axis=4)

    qb = to_sub(q)
    kk = with_prev(to_sub(k))
    vv = with_prev(to_sub(v))
    s = jnp.einsum('bdhnqc,bdhnkc->bdhnqk', qb, kk).astype(jnp.float32)
    qi = jnp.arange(band)[:, None] + band
    kj = jnp.arange(2 * band)[None, :]
    diff = qi - kj
    blk = jnp.arange(nb)[:, None, None]
    mask = (diff >= 0) & (diff <= band) & ((blk > 0) | (kj >= band))
    s = jnp.where(mask, s, -jnp.inf)
    m = jnp.max(s, axis=-1, keepdims=True)
    p = jnp.exp(s - m)
    den = jnp.sum(p, axis=-1, keepdims=True)
    o = jnp.einsum('bdhnqk,bdhnkc->bdhnqc', p.astype(v.dtype), vv).astype(jnp.float32) / den
    lse = (m + jnp.log(den))[..., 0]
    o = o.reshape(B, dilation, H, Lp, hd)[:, :, :, :L].transpose(0, 3, 1, 2, 4).reshape(B, S, H, hd)
    lse = lse.reshape(B, dilation, H, Lp)[:, :, :, :L].transpose(0, 3, 1, 2).reshape(B, S, H)
    return o, lse


def dilated_attention(q, k, v):
    outs, lses = [], []
    for window, dilation in DIL_PATTERNS:
        o, lse = dilated_branch(q, k, v, window, dilation)
        outs.append(o)
        lses.append(lse)
    w = jax.nn.softmax(jnp.stack(lses, axis=0), axis=0)
    return jnp.sum(w[..., None] * jnp.stack(outs, axis=0), axis=0)


def causal_dwconv(u, w, b):
    S = u.shape[1]
    up = jnp.pad(u, ((0, 0), (CONV_W - 1, 0), (0, 0)))
    y = b + w[0] * up[:, 0:S]
    for i in range(1, CONV_W):
        y = y + w[i] * up[:, i:i + S]
    return y


def setup_inputs(seed: int = 0) -> dict:
    key = jax.random.key(seed)
    ks = jax.random.split(key, 24)
    f32 = jnp.float32
    nrm = lambda k, shape, std: jax.random.normal(k, shape, f32) * std
    beta = DEEPNORM_BETA
    x = jax.random.normal(ks[0], (BATCH, SEQ, D_MODEL), f32)
    mem = jax.random.normal(ks[1], (BATCH, MEM_LEN, D_MODEL), f32)
    offs = jax.random.randint(ks[2], (BATCH, 1), 0, 4096, dtype=jnp.int32)
    positions = offs + jnp.arange(SEQ, dtype=jnp.int32)[None, :]
    col_scale = jnp.concatenate([jnp.full((w,), beta if i in (2, 7) else 1.0, f32)
                                 for i, w in enumerate(IN_WIDTHS)])
    w_in = nrm(ks[3], (DEPTH, D_MODEL, IN_COLS), D_MODEL ** -0.5) * col_scale
    gla_gate_w2 = nrm(ks[4], (DEPTH, GLA_GATE_RANK, GLA_HEADS * GLA_DK), GLA_GATE_RANK ** -0.5)
    gla_gate_b = nrm(ks[5], (DEPTH, GLA_HEADS * GLA_DK), 0.01)
    gla_norm_g = 1.0 + nrm(ks[6], (DEPTH, GLA_HEADS * GLA_DV), 0.02)
    w_out = nrm(ks[7], (DEPTH, MIX_WIDTH, D_MODEL), MIX_WIDTH ** -0.5) * beta
    ln1_g = 1.0 + nrm(ks[8], (DEPTH, D_MODEL), 0.02)
    ln1_b = nrm(ks[9], (DEPTH, D_MODEL), 0.02)
    ca_wq = nrm(ks[10], (DEPTH, D_MODEL, D_MODEL), D_MODEL ** -0.5)
    kv_scale = jnp.concatenate([jnp.ones((D_MODEL,), f32), jnp.full((D_MODEL,), beta, f32)])
    ca_wkv = nrm(ks[11], (DEPTH, D_MODEL, 2 * D_MODEL), D_MODEL ** -0.5) * kv_scale
    ca_wo = nrm(ks[12], (DEPTH, D_MODEL, D_MODEL), D_MODEL ** -0.5) * beta
    ln2_g = 1.0 + nrm(ks[13], (DEPTH, D_MODEL), 0.02)
    ln2_b = nrm(ks[14], (DEPTH, D_MODEL), 0.02)
    ffn_w_in = nrm(ks[15], (DEPTH, D_MODEL, 2 * D_FF), D_MODEL ** -0.5) * beta
    ffn_conv_w = nrm(ks[16], (DEPTH, CONV_W, 2 * D_FF), CONV_W ** -0.5)
    ffn_conv_b = nrm(ks[17], (DEPTH, 2 * D_FF), 0.02)
    ffn_w_out = nrm(ks[18], (DEPTH, D_FF, D_MODEL), D_FF ** -0.5) * beta
    ln3_g = 1.0 + nrm(ks[19], (DEPTH, D_MODEL), 0.02)
    ln3_b = nrm(ks[20], (DEPTH, D_MODEL), 0.02)
    return {"x": x, "mem": mem, "positions": positions, "w_in": w_in,
            "gla_gate_w2": gla_gate_w2, "gla_gate_b": gla_gate_b, "gla_norm_g": gla_norm_g,
            "w_out": w_out, "ln1_g": ln1_g, "ln1_b": ln1_b,
            "ca_wq": ca_wq, "ca_wkv": ca_wkv, "ca_wo": ca_wo, "ln2_g": ln2_g, "ln2_b": ln2_b,
            "ffn_w_in": ffn_w_in, "ffn_conv_w": ffn_conv_w, "ffn_conv_b": ffn_conv_b,
            "ffn_w_out": ffn_w_out, "ln3_g": ln3_g, "ln3_b": ln3_b}


def reference(x, mem, positions, w_in, gla_gate_w2, gla_gate_b, gla_norm_g, w_out, ln1_g, ln1_b,
              ca_wq, ca_wkv, ca_wo, ln2_g, ln2_b, ffn_w_in, ffn_conv_w, ffn_conv_b, ffn_w_out,
              ln3_g, ln3_b):
    B, S, D = x.shape
    M = mem.shape[1]
    inv_freq = ROPE_THETA ** (-jnp.arange(0, ROPE_DIMS, 2, dtype=jnp.float32) / ROPE_DIMS)
    ang = positions.astype(jnp.float32)[..., None] * inv_freq
    cos = jnp.cos(ang)[:, :, None, :]
    sin = jnp.sin(ang)[:, :, None, :]

    for l in range(DEPTH):
        h = x @ w_in[l]
        qg, kg, vg, rg, glr, qd, kd, vd = split_cols(h, IN_WIDTHS)
        log_g = jax.nn.log_sigmoid((glr @ gla_gate_w2[l] + gla_gate_b[l]).astype(jnp.float32)) / GLA_TAU
        o_g = gla_chunked(qg.reshape(B, S, GLA_HEADS, GLA_DK), kg.reshape(B, S, GLA_HEADS, GLA_DK),
                          vg.reshape(B, S, GLA_HEADS, GLA_DV), log_g.reshape(B, S, GLA_HEADS, GLA_DK))
        mu = jnp.mean(o_g, axis=-1, keepdims=True)
        var = jnp.mean(jnp.square(o_g - mu), axis=-1, keepdims=True)
        o_g = ((o_g - mu) * lax.rsqrt(var + LN_EPS)).reshape(B, S, GLA_HEADS * GLA_DV)
        o_g = (o_g * gla_norm_g[l].astype(jnp.float32) * jax.nn.silu(rg.astype(jnp.float32))).astype(x.dtype)
        qd = partial_rotary(qd.reshape(B, S, DIL_HEADS, DIL_HD) * (DIL_HD ** -0.5), cos, sin)
        kd = partial_rotary(kd.reshape(B, S, DIL_HEADS, DIL_HD), cos, sin)
        o_d = dilated_attention(qd, kd, vd.reshape(B, S, DIL_HEADS, DIL_HD))
        o_d = o_d.reshape(B, S, DIL_HEADS * DIL_HD).astype(x.dtype)
        mix = jnp.concatenate([o_g, o_d], axis=-1) @ w_out[l]
        x = layer_norm(DEEPNORM_ALPHA * x + mix, ln1_g[l], ln1_b[l])

        q = (x @ ca_wq[l]).reshape(B, S, CA_HEADS, CA_HD)
        mk, mv = split_cols(mem @ ca_wkv[l], (D_MODEL, D_MODEL))
        mk = mk.reshape(B, M, CA_HEADS, CA_HD)
        mv = mv.reshape(B, M, CA_HEADS, CA_HD)
        s = jnp.einsum('bshc,bmhc->bhsm', q, mk).astype(jnp.float32) * (CA_HD ** -0.5)
        p = jax.nn.softmax(s, axis=-1).astype(mv.dtype)
        o_c = jnp.einsum('bhsm,bmhc->bshc', p, mv).reshape(B, S, D_MODEL)
        x = layer_norm(DEEPNORM_ALPHA * x + o_c @ ca_wo[l], ln2_g[l], ln2_b[l])

        u = causal_dwconv(x @ ffn_w_in[l], ffn_conv_w[l], ffn_conv_b[l])
        gate, up = split_cols(u, (D_FF, D_FF))
        f = (jax.nn.silu(gate) * up) @ ffn_w_out[l]
        x = layer_norm(DEEPNORM_ALPHA * x + f, ln3_g[l], ln3_b[l])
    return x
```

```python
import math
from contextlib import ExitStack, contextmanager
import numpy as np
import concourse.bass as bass
import concourse.mybir as mybir
from concourse.bass_utils import run_bass_kernel_spmd

F32 = mybir.dt.float32
BF16 = mybir.dt.bfloat16
I32 = mybir.dt.int32
AF = mybir.ActivationFunctionType
ALU = mybir.AluOpType

D = 2048
KC = 16
NL = 4096
TQ = 2050
QN = 2176
DFF = 5504
NJ = 43
ALPHA = 2.0 ** 0.25
LN_EPS = 1e-5
TCH = [(0, 2), (2, 514), (514, 1026), (1026, 1538), (1538, 2050)]
ENGS = ["tensor", "vector", "scalar", "gpsimd", "sync"]
PI = math.pi


class Buf:
    __slots__ = ("w", "r", "excl")

    def __init__(self, excl=False):
        self.w = None
        self.r = {}
        self.excl = excl


class Prog:
    def __init__(self, nc, es):
        self.nc = nc
        self.q = {e: [] for e in ENGS}
        self.cnt = {e: 0 for e in ENGS}
        self.sem = {e: es.enter_context(nc.semaphore("s_" + e)) for e in ENGS}
        self.grp = {e: None for e in ENGS}
        self.es = es
        self.dsems = []
        self.bar = {e: [] for e in ENGS}
        self.waited = {e: {} for e in ENGS}

    def _deps(self, reads, writes, extra):
        d = [x for x in extra if x is not None]
        if any(b.excl for b in reads):
            writes = list(writes) + [b for b in reads if b.excl]
            reads = [b for b in reads if not b.excl]
        for b in reads:
            if b.w is not None:
                d.append(b.w)
        for b in writes:
            if b.w is not None:
                d.append(b.w)
            d.extend(b.r.values())
        return d

    def _mark(self, tok, reads, writes):
        if any(b.excl for b in reads):
            writes = list(writes) + [b for b in reads if b.excl]
            reads = [b for b in reads if not b.excl]
        for b in reads:
            b.r[id(tok[0])] = tok
        for b in writes:
            b.w = tok
            b.r = {}

    def op(self, eng, fn, reads=(), writes=(), deps=()):
        d = self._deps(reads, writes, deps)
        if self.bar[eng]:
            d.extend(self.bar[eng])
            self.bar[eng] = []
        if eng == "tensor":
            d = [t for t in d if t[2] != "tensor"]
        if self.grp[eng] is not None:
            tok = self.grp[eng]
            d = [t for t in d if not (t[0] is tok[0] and t[1] == tok[1])]
            self.q[eng].append(["op", fn, d, False])
        else:
            self.cnt[eng] += 1
            tok = (self.sem[eng], self.cnt[eng], eng)
            self.q[eng].append(["op", fn, d, True])
        self._mark(tok, reads, writes)
        return tok

    @contextmanager
    def group(self, eng):
        tok = (self.sem[eng], self.cnt[eng] + 1, eng)
        self.grp[eng] = tok
        n0 = len(self.q[eng])
        yield tok
        self.grp[eng] = None
        assert len(self.q[eng]) > n0
        self.q[eng][-1][3] = True
        self.cnt[eng] += 1

    def _raw_dsem(self):
        s = [self.es.enter_context(self.nc.semaphore("d%d" % len(self.dsems))), 0]
        self.dsems.append(s)
        return s

    def new_dsem(self):
        return {}

    def dma(self, eng, out, in_, sem, reads=(), writes=(), deps=()):
        d = self._deps(reads, writes, deps)
        if self.bar[eng]:
            d.extend(self.bar[eng])
            self.bar[eng] = []
        if eng not in sem:
            sem[eng] = self._raw_dsem()
        sem = sem[eng]
        sem[1] += 16
        tok = (sem[0], sem[1], "dma")
        self.q[eng].append(["dma", (out, in_, sem[0]), d, True])
        self._mark(tok, reads, writes)
        return tok

    def barrier(self):
        toks = [(self.sem[e], self.cnt[e], "bar") for e in ENGS if self.cnt[e] > 0]
        toks += [(s[0], s[1], "dma") for s in self.dsems if s[1] > 0]
        for e in ENGS:
            self.bar[e] = list(toks)

    def flush(self, final=()):
        with self.nc.Block() as block:
            def mk(ename):
                def body(eng):
                    waited = self.waited[ename]
                    for kind, payload, deps, sig in self.q[ename]:
                        for (s, v, src) in deps:
                            key = id(s)
                            if waited.get(key, 0) >= v:
                                continue
                            eng.wait_ge(s, v)
                            waited[key] = v
                        if kind == "op":
                            ins = payload(eng)
                            if sig:
                                ins.then_inc(self.sem[ename], 1)
                        else:
                            out, in_, s = payload
                            eng.dma_start(out=out, in_=in_).then_inc(s, 16)
                    if ename == "sync":
                        for (s, v, src) in final:
                            eng.wait_ge(s, v)
                    self.q[ename] = []
                return body
            block.tensor(mk("tensor"))
            block.vector(mk("vector"))
            block.scalar(mk("scalar"))
            block.gpsimd(mk("gpsimd"))
            block.sync(mk("sync"))


def pipeline(n, stages):
    S = len(stages)
    ctx = [dict() for _ in range(n)]
    for t in range(n + S - 1):
        for s_ in reversed(range(S)):
            i = t - s_
            if 0 <= i < n:
                stages[s_](i, ctx[i])


def pipeline_gen(n, stages):
    S = len(stages)
    ctx = [dict() for _ in range(n)]
    for t in range(n + S - 1):
        for s_ in reversed(range(S)):
            i = t - s_
            if 0 <= i < n:
                stages[s_](i, ctx[i])
        yield t


class Ring:
    def __init__(self, P, tensors):
        self.t = tensors
        self.b = [Buf() for _ in tensors]
        self.s = [P.new_dsem() for _ in tensors]
        self.i = -1

    def next(self):
        self.i = (self.i + 1) % len(self.t)
        return self.t[self.i], self.b[self.i], self.s[self.i]


def build(debug=False, stop=None, ngla=4, ndil=8):
    nc = bass.Bass("TRN2", target_bir_lowering=False)
    din = lambda name, shape, dt=F32: nc.dram_tensor(name, list(shape), dt, kind="ExternalInput").ap()
    okind = "ExternalOutput" if debug else "Internal"
    dscr = lambda name, shape, dt: nc.dram_tensor(name, list(shape), dt, kind=okind).ap()
    xT_nat = din("xT_nat", [D, NL])
    xT_perm = din("xT_perm", [D, NL])
    memT = din("memT", [D, 256])
    pos_in = din("pos_perm", [1, NL], I32)
    flag_in = din("flag", [128, 1])
    masks_in = din("masks", [128, 6 * 128])
    uneg_in = din("uneg", [128, 128])
    rotc_in = din("rotc", [32, 2])
    wg_in = din("wg", [4, D, 768])
    wglr_in = din("wglr", [D, 16])
    w2aug_in = din("w2aug", [17, 512])
    glag_in = din("glag", [128, 8])
    wd_in = din("wd", [8, D, 320])
    wvd_in = din("wvd", [D, 1024])
    wout_in = din("w_out", [D, D])
    lnp_in = din("lnp", [128, 6 * 16])
    wq_in = din("ca_wq", [D, D])
    wkv_in = din("ca_wkv", [D, 2 * D])
    wo_in = din("ca_wo", [D, D])
    fwin_in = din("ffn_w_in", [D, 2 * DFF])
    cw_in = din("convw", [128, 86 * 3])
    cb_in = din("convb", [128, 86])
    fwout_in = din("ffn_w_out", [DFF, D])
    outT = nc.dram_tensor("outT", [D, 2048], F32, kind="ExternalOutput").ap()

    xb_nat = dscr("xb_nat", [D, NL], BF16)
    xb_perm = dscr("xb_perm", [D, NL], BF16)
    vd_scr = dscr("vd_scr", [NL, 1024], BF16)
    vd4_scr = dscr("vd4_scr", [NL, 1024], BF16)
    vd1_scr = dscr("vd1_scr", [NL, 1024], BF16)
    mixT = dscr("mixT", [D, TQ], BF16)
    x1T = dscr("x1T", [D, TQ], F32)
    ocT = dscr("ocT", [D, TQ], BF16)
    x2T = dscr("x2T", [D, TQ], F32)
    hT = dscr("hT", [DFF, 2048], BF16)
    w2b = dscr("w2b", [DFF, D], BF16)

    with ExitStack() as ges:
        P = Prog(nc, ges)
        gsb = lambda name, shape, dt: ges.enter_context(nc.sbuf_tensor(name, list(shape), dt))
        banks = [ges.enter_context(nc.psum_tensor("pb%d" % i, [128, 512], F32)) for i in range(7)]
        bankb = [Buf(True) for _ in range(7)]
        ptb = ges.enter_context(nc.psum_tensor("ptb", [128, 1024], BF16))
        ptb_b = Buf(True)
        bi = [0]

        def finish():
            final = [(s_[0], s_[1], "dma") for s_ in P.dsems if s_[1] > 0]
            final += [(P.sem[e_], P.cnt[e_], "bar") for e_ in ENGS if P.cnt[e_] > 0 and e_ != "sync"]
            P.flush(final)

        busy = set()

        def nb(reserve=False):
            for _ in range(8):
                bi[0] = (bi[0] + 1) % 7
                if bi[0] not in busy:
                    break
            else:
                raise RuntimeError("no free PSUM bank")
            if reserve:
                busy.add(bi[0])
            return banks[bi[0]], bankb[bi[0]]

        def rel(bb):
            busy.discard(bankb.index(bb))

        def mm(out, lhsT, rhs, start, stop, reads, writes):
            return P.op("tensor", lambda e: e.matmul(out, lhsT=lhsT, rhs=rhs, start=start, stop=stop), reads, writes)

        def act(out, in_, func, reads, writes, bias=None, scale=None):
            kw = {}
            if bias is not None:
                kw["bias"] = bias
            if scale is not None:
                kw["scale"] = scale
            return P.op("scalar", lambda e: e.activation(out=out, in_=in_, func=func, **kw), reads, writes)

        def tt(eng, out, in0, in1, op, reads, writes):
            return P.op(eng, lambda e: e.tensor_tensor(out=out, in0=in0, in1=in1, op=op), reads, writes)

        def ts(eng, out, in0, s1, s2, op0, op1, reads, writes):
            if op1 is None:
                return P.op(eng, lambda e: e.tensor_scalar(out=out, in0=in0, scalar1=s1, scalar2=None, op0=op0), reads, writes)
            return P.op(eng, lambda e: e.tensor_scalar(out=out, in0=in0, scalar1=s1, scalar2=s2, op0=op0, op1=op1), reads, writes)

        def stt(eng, out, in0, scalar, in1, op0, op1, reads, writes):
            return P.op(eng, lambda e: e.scalar_tensor_tensor(out=out, in0=in0, scalar=scalar, in1=in1, op0=op0, op1=op1), reads, writes)

        def cp(eng, out, in_, reads, writes):
            if eng == "scalar":
                return P.op(eng, lambda e: e.activation(out=out, in_=in_, func=AF.Copy), reads, writes)
            return P.op(eng, lambda e: e.tensor_copy(out=out, in_=in_), reads, writes)

        def recip(eng, out, in_, reads, writes):
            return P.op(eng, lambda e: e.reciprocal(out=out, in_=in_), reads, writes)

        def mset(eng, ap, val, writes):
            return P.op(eng, lambda e: e.memset(ap, val), (), writes)

        ident = gsb("ident", [128, 128], BF16)
        ones_bf = gsb("ones_bf", [128, 128], BF16)
        ones256 = gsb("ones256", [128, 128], F32)
        ones2048 = gsb("ones2048", [128, 128], F32)
        masks = gsb("masks_sb", [128, 6, 128], BF16)
        masksc = gsb("masksc_sb", [128, 6, 128], BF16)
        uneg = gsb("uneg_sb", [128, 128], F32)
        flag = gsb("flag_sb", [128, 1], F32)
        lnp = gsb("lnp_sb", [128, 6, 16], F32)
        glag = gsb("glag_sb", [128, 8], F32)
        cw = gsb("cw_sb", [128, 86, 3], F32)
        cb = gsb("cb_sb", [128, 86], F32)
        mtmp = gsb("mtmp", [128, 6, 128], F32)
        CB = Buf()
        csem = P.new_dsem()
        P.dma("sync", mtmp[:, :, :], masks_in.rearrange("p (a b) -> p a b", a=6), csem, (), [CB])
        P.dma("sync", uneg[:, :], uneg_in, csem, (), [CB])
        P.dma("sync", flag[:, :], flag_in, csem, (), [CB])
        P.dma("sync", lnp[:, :, :], lnp_in.rearrange("p (a b) -> p a b", a=6), csem, (), [CB])
        P.dma("sync", glag[:, :], glag_in, csem, (), [CB])
        P.dma("sync", cw[:, :, :], cw_in.rearrange("p (a b) -> p a b", b=3), csem, (), [CB])
        P.dma("sync", cb[:, :], cb_in, csem, (), [CB])
        mset("gpsimd", ident[:, :], 0.0, [CB])
        P.op("gpsimd", lambda e: e.affine_select(out=ident[:, :], in_=ident[:, :], pattern=[[-1, 128]],
                                                 compare_op=ALU.not_equal, fill=1.0, base=0, channel_multiplier=1), [CB], [CB])
        mset("gpsimd", ones_bf[:, :], 1.0, [CB])
        mset("gpsimd", ones256[:, :], 1.0 / 256.0, [CB])
        mset("gpsimd", ones2048[:, :], 1.0 / 2048.0, [CB])
        cp("vector", masks[:, :, :], mtmp[:, :, :], [CB], [CB])
        ts("vector", masksc[:, :, :], mtmp[:, :, :], flag[:, 0:1], None, ALU.mult, None, [CB], [CB])

        XN = [Buf() for _ in range(8)]
        XP = [Buf() for _ in range(8)]
        P.flush()
        if stop == "pre":
            finish()
            return nc

        MIXB = Buf()
        VDB = Buf()

        with ExitStack() as es:
            sb = lambda name, shape, dt: es.enter_context(nc.sbuf_tensor(name, list(shape), dt))
            xring = Ring(P, [sb("xblk%d" % i, [128, KC, 512], BF16) for i in range(2)])

            def load_x(src, srcbufs, blk):
                t, b, s = xring.next()
                P.dma("sync", t[:, :, :], src[:, blk * 512:(blk + 1) * 512].rearrange("(k p) n -> p k n", p=128), s, [srcbufs[blk]], [b])
                return t, b

            glrT = sb("glrT", [32, NL], F32)
            GLR = Buf()
            wglr = sb("wglr_sb", [128, KC, 16], BF16)
            w2aug = sb("w2aug_sb", [17, 512], F32)
            WS = Buf()
            wsem = P.new_dsem()
            P.dma("gpsimd", wglr[:, :, :], wglr_in.rearrange("(k p) n -> p k n", p=128), wsem, (), [WS])
            P.dma("sync", w2aug[:, :], w2aug_in, wsem, (), [WS])
            def load_first(ring, src32, dst16, dbufs, blk):
                t, b, s_ = ring.next()
                cols = slice(blk * 512, (blk + 1) * 512)
                P.dma("gpsimd", t[:, :, :], src32[:, cols].rearrange("(k p) n -> p k n", p=128), s_, (), [b])
                P.dma("sync", dst16[:, cols].rearrange("(k p) n -> p k n", p=128), t[:, :, :], s_, [b], [dbufs[blk]])
                return t, b
            mset("vector", glrT[:, :], 1.0, [GLR])
            for blk in range(8):
                xb, xbb = load_first(xring, xT_nat, xb_nat, XN, blk)
                bk, bkb = nb()
                with P.group("tensor"):
                    for kc in range(KC):
                        mm(bk[0:16, :], wglr[:, kc, :], xb[:, kc, :], kc == 0, kc == KC - 1, [xbb, WS], [bkb])
                cp("scalar", glrT[0:16, blk * 512:(blk + 1) * 512], bk[0:16, :], [bkb], [GLR])

            if stop == "glr":
                finish()
                return nc
            wgring = Ring(P, [sb("wg%d" % i, [128, KC, 768], BF16) for i in range(1)])
            qT = sb("g_qT", [128, QN], BF16)
            kT = sb("g_kT", [128, NL], BF16)
            rT = sb("g_rT", [128, 2, QN], BF16)
            vsb = sb("g_v", [128, 32, 256], BF16)
            enb = sb("g_enb", [128, NL], BF16)
            kef = sb("g_kef", [128, NL], BF16)
            ebq = sb("g_ebq", [128, QN], BF16)
            decay = sb("g_decay", [128, 32], F32)
            blast = sb("g_blast", [128, 32], F32)
            Sst = sb("g_S", [128, 256], F32)
            Sbf = [sb("g_Sbf%d" % i, [128, 256], BF16) for i in range(3)]
            og = sb("g_og", [128, 2, TQ], BF16)
            tA = [sb("g_tA%d" % i, [128, 128], F32) for i in range(4)]
            spb = [sb("g_sp%d" % i, [128, 128], F32) for i in range(4)]
            kin = [sb("g_kin%d" % i, [128, 128], BF16) for i in range(4)]
            kend = [sb("g_kend%d" % i, [128, 128], BF16) for i in range(4)]
            kendT = [sb("g_kendT%d" % i, [128, 128], BF16) for i in range(4)]
            qin = [sb("g_qin%d" % i, [128, 128], BF16) for i in range(4)]
            Am = [sb("g_Am%d" % i, [128, 128], BF16) for i in range(4)]
            osb = [sb("g_osb%d" % i, [128, 2, 128], F32) for i in range(4)]
            osq = [sb("g_osq%d" % i, [128, 2, 128], F32) for i in range(4)]
            mst = [sb("g_mst%d" % i, [128, 256], F32) for i in range(4)]
            t1 = [sb("g_t1%d" % i, [128, 128], F32) for i in range(4)]
            t2 = [sb("g_t2%d" % i, [128, 2, 128], F32) for i in range(4)]
            sr = [sb("g_sr%d" % i, [128, 2, 128], F32) for i in range(4)]
            on = [sb("g_on%d" % i, [128, 128], F32) for i in range(4)]
            on2 = [[sb("g_on2_%d_%d" % (i, e_), [128, 128], F32) for e_ in range(2)] for i in range(4)]
            TB2 = {"on": [[Buf(), Buf()] for _ in range(4)]}
            HB = {k: Buf() for k in ["q", "k", "r", "v", "gate", "S", "og", "bl"]}
            Sbfb = [Buf(), Buf(), Buf()]
            TB = {k: [Buf() for _ in range(4)] for k in ["tA", "sp", "kin", "kend", "kendT", "qin", "Am", "osb", "osq", "mst", "t1", "t2", "sr", "on"]}
            ogsem = P.new_dsem()
            LNS = math.log(128.0 ** -0.5)

            wg, wgb, wgs = wgring.next()
            P.dma("gpsimd", wg[:, :, :], wg_in[0].rearrange("(k p) n -> p k n", p=128), wgs, (), [wgb])
            for h in range(ngla):
                def g0(n, c):
                    c["bz"], c["bzb"] = nb(True)
                    mm(c["bz"][:, 0:128], glrT[0:17, n * 128:(n + 1) * 128], w2aug[0:17, h * 128:(h + 1) * 128], True, True, [GLR, WS], [c["bzb"]])

                def g1(n, c):
                    i4 = n % 4
                    act(tA[i4][:, :], c["bz"][:, 0:128], AF.Exp, [c["bzb"]], [TB["tA"][i4]], scale=-1.0)
                    act(spb[i4][:, :], tA[i4][:, :], AF.Ln, [TB["tA"][i4]], [TB["sp"][i4]], bias=1.0)
                    rel(c["bzb"])

                def g2(n, c):
                    i4 = n % 4
                    c["bt"], c["btb"] = nb(True)
                    mm(c["bt"][:, 0:128], spb[i4][:, :], uneg[:, :], True, True, [TB["sp"][i4], CB], [c["btb"]])

                def g3(n, c):
                    bt, btb = c["bt"], c["btb"]
                    cp("vector", blast[:, n:n + 1], bt[:, 127:128], [btb], [HB["bl"]])
                    act(enb[:, n * 128:(n + 1) * 128], bt[:, 0:128], AF.Exp, [btb], [HB["gate"]], scale=-1.0)
                    act(kef[:, n * 128:(n + 1) * 128], bt[:, 0:128], AF.Exp, [btb, HB["bl"]], [HB["gate"]], scale=-1.0, bias=blast[:, n:n + 1])
                    if n >= 15:
                        act(ebq[:, (n - 15) * 128:(n - 14) * 128], bt[:, 0:128], AF.Exp, [btb], [HB["gate"]], bias=LNS)
                    act(decay[:, n:n + 1], blast[:, n:n + 1], AF.Exp, [HB["bl"]], [HB["gate"]])
                    rel(btb)
                gsteps = pipeline_gen(32, [g0, g1, g2, g3])
                for blk in range(8):
                    for _ in range(5):
                        next(gsteps, None)
                    xb, xbb = load_x(xb_nat, XN, blk)
                    bk, bkb = nb()
                    with P.group("tensor"):
                        for kc in range(KC):
                            mm(bk[:, :], wg[:, kc, 128:256], xb[:, kc, :], kc == 0, kc == KC - 1, [xbb, wgb], [bkb])
                    cp("scalar", kT[:, blk * 512:(blk + 1) * 512], bk[:, :], [bkb], [HB["k"]])
                    if blk >= 3:
                        x0, nn_, q0 = (384, 128, 0) if blk == 3 else (0, 512, 128 + (blk - 4) * 512)
                        for (c0, dst, hb) in [(0, qT[:, q0:q0 + nn_], "q"), (256, rT[:, 0, q0:q0 + nn_], "r"), (384, rT[:, 1, q0:q0 + nn_], "r")]:
                            bq, bqb = nb()
                            with P.group("tensor"):
                                for kc in range(KC):
                                    mm(bq[:, 0:nn_], wg[:, kc, c0:c0 + 128], xb[:, kc, x0:x0 + nn_], kc == 0, kc == KC - 1, [xbb, wgb], [bqb])
                            cp("scalar" if hb == "q" else "vector", dst, bq[:, 0:nn_], [bqb], [HB[hb]])
                    for sub in range(4):
                        bv, bvb = nb()
                        with P.group("tensor"):
                            for kc in range(KC):
                                mm(bv[:, 0:256], xb[:, kc, sub * 128:(sub + 1) * 128], wg[:, kc, 512:768], kc == 0, kc == KC - 1, [xbb, wgb], [bvb])
                        cp("vector", vsb[:, blk * 4 + sub, :], bv[:, 0:256], [bvb], [HB["v"]])
                for _ in gsteps:
                    pass
                if h + 1 < ngla:
                    P.dma("gpsimd", wg[:, :, :], wg_in[h + 1].rearrange("(k p) n -> p k n", p=128), wgs, (), [wgb])
                act(rT[:, :, :], rT[:, :, :], AF.Silu, [], [HB["r"]])
                if stop == "proj":
                    finish()
                    return nc
                mset("vector", Sst[:, :], 0.0, [HB["S"]])
                mset("gpsimd", Sbf[2][:, :], 0.0, [Sbfb[2]])

                def c0_(n, c):
                    i4 = n % 4
                    tok = slice(n * 128, (n + 1) * 128)
                    tt("vector", kin[i4][:, :], kT[:, tok], enb[:, tok], ALU.mult, [HB["k"], HB["gate"]], [TB["kin"][i4]])
                    tt("gpsimd", kend[i4][:, :], kT[:, tok], kef[:, tok], ALU.mult, [HB["k"], HB["gate"]], [TB["kend"][i4]])
                    if n >= 15:
                        q0 = (n - 15) * 128
                        tt("vector", qin[i4][:, :], qT[:, q0:q0 + 128], ebq[:, q0:q0 + 128], ALU.mult, [HB["q"], HB["gate"]], [TB["qin"][i4]])

                def c1_(n, c):
                    i4 = n % 4
                    P.op("tensor", lambda e, i4=i4: e.transpose(out=ptb[:, i4 * 128:(i4 + 1) * 128], in_=kend[i4][:, :], identity=ident[:, :]),
                         [TB["kend"][i4], CB], [ptb_b])
                    if n >= 15:
                        c["ba"], c["bab"] = nb(True)
                        mm(c["ba"][:, 0:128], kin[i4][:, :], qin[i4][:, :], True, True, [TB["kin"][i4], TB["qin"][i4]], [c["bab"]])

                def c2_(n, c):
                    i4 = n % 4
                    cp("scalar", kendT[i4][:, :], ptb[:, i4 * 128:(i4 + 1) * 128], [ptb_b], [TB["kendT"][i4]])
                    if n >= 15:
                        q0 = (n - 15) * 128
                        tt("vector", Am[i4][:, :], c["ba"][:, 0:128], masks[:, 1, :], ALU.mult, [c["bab"], CB], [TB["Am"][i4]])
                        rel(c["bab"])

                def c3_(n, c):
                    i4 = n % 4
                    Sprev, Sprevb = Sbf[(n + 2) % 3], Sbfb[(n + 2) % 3]
                    if n >= 15:
                        c["bo"], c["bob"] = nb(True)
                        for e_ in range(2):
                            with P.group("tensor"):
                                mm(c["bo"][:, e_ * 128:(e_ + 1) * 128], vsb[:, n, e_ * 128:(e_ + 1) * 128], Am[i4][:, :], True, False, [HB["v"], TB["Am"][i4]], [c["bob"]])
                                mm(c["bo"][:, e_ * 128:(e_ + 1) * 128], Sprev[:, e_ * 128:(e_ + 1) * 128], qin[i4][:, :], False, True, [Sprevb, TB["qin"][i4]], [c["bob"]])
                    c["bs"], c["bsb"] = nb(True)
                    mm(c["bs"][:, 0:256], kendT[i4][:, :], vsb[:, n, :], True, True, [TB["kendT"][i4], HB["v"]], [c["bsb"]])

                def c4_(n, c):
                    i4 = n % 4
                    stt("vector", Sst[:, :], Sst[:, :], decay[:, n:n + 1], c["bs"][:, 0:256], ALU.mult, ALU.add, [c["bsb"], HB["gate"]], [HB["S"]])
                    cp("gpsimd", Sbf[n % 3][:, :], Sst[:, :], [HB["S"]], [Sbfb[n % 3]])
                    rel(c["bsb"])
                    if n >= 15:
                        cp("scalar", osb[i4][:, :, :], c["bo"][:, 0:256].rearrange("p (a b) -> p a b", a=2), [c["bob"]], [TB["osb"][i4]])
                        rel(c["bob"])

                def c5_(n, c):
                    i4 = n % 4
                    if n < 15:
                        return
                    tt("gpsimd", osq[i4][:, :, :], osb[i4][:, :, :], osb[i4][:, :, :], ALU.mult, [TB["osb"][i4]], [TB["osq"][i4]])
                    c["bm"], c["bmb"] = nb(True)
                    bm, bmb = c["bm"], c["bmb"]
                    with P.group("tensor"):
                        mm(bm[:, 0:128], ones256[:, :], osb[i4][:, 0, :], True, False, [CB, TB["osb"][i4]], [bmb])
                        mm(bm[:, 0:128], ones256[:, :], osb[i4][:, 1, :], False, True, [CB, TB["osb"][i4]], [bmb])
                    with P.group("tensor"):
                        mm(bm[:, 128:256], ones256[:, :], osq[i4][:, 0, :], True, False, [CB, TB["osq"][i4]], [bmb])
                        mm(bm[:, 128:256], ones256[:, :], osq[i4][:, 1, :], False, True, [CB, TB["osq"][i4]], [bmb])

                def c6_(n, c):
                    i4 = n % 4
                    if n < 15:
                        return
                    q0 = (n - 15) * 128
                    cp("scalar", mst[i4][:, :], c["bm"][:, 0:256], [c["bmb"]], [TB["mst"][i4]])
                    rel(c["bmb"])
                    tt("vector", t1[i4][:, :], mst[i4][:, 0:128], mst[i4][:, 0:128], ALU.mult, [TB["mst"][i4]], [TB["t1"][i4]])
                    tt("vector", t1[i4][:, :], mst[i4][:, 128:256], t1[i4][:, :], ALU.subtract, [TB["mst"][i4]], [TB["t1"][i4]])
                    act(t1[i4][:, :], t1[i4][:, :], AF.Ln, [], [TB["t1"][i4]], bias=LN_EPS)
                    act(t1[i4][:, :], t1[i4][:, :], AF.Exp, [], [TB["t1"][i4]], scale=-0.5)

                def c7_(n, c):
                    i4 = n % 4
                    if n < 15:
                        return
                    q0 = (n - 15) * 128
                    for e_ in range(2):
                        onb, ONB = on2[i4][e_], TB2["on"][i4][e_]
                        tt("vector", onb[:, :], osb[i4][:, e_, :], mst[i4][:, 0:128], ALU.subtract, [TB["osb"][i4], TB["mst"][i4]], [ONB])
                        tt("gpsimd", onb[:, :], onb[:, :], t1[i4][:, :], ALU.mult, [TB["t1"][i4]], [ONB])
                        if n == 15:
                            stt("vector", og[:, e_, 0:2], onb[:, 126:128], glag[:, 2 * h + e_:2 * h + e_ + 1], rT[:, e_, q0 + 126:q0 + 128],
                                ALU.mult, ALU.mult, [ONB, HB["r"], CB], [HB["og"]])
                        else:
                            o0 = 2 + (n - 16) * 128
                            stt("vector", og[:, e_, o0:o0 + 128], onb[:, :], glag[:, 2 * h + e_:2 * h + e_ + 1], rT[:, e_, q0:q0 + 128],
                                ALU.mult, ALU.mult, [ONB, HB["r"], CB], [HB["og"]])
                pipeline(32, [c0_, c1_, c2_, c3_, c4_, c5_, c6_, c7_])
                for e_ in range(2):
                    P.dma("sync", mixT[(2 * h + e_) * 128:(2 * h + e_ + 1) * 128, :], og[:, e_, :], ogsem, [HB["og"]], [MIXB])
            P.flush()

        P.barrier()
        if stop == "A":
            finish()
            return nc
        with ExitStack() as es:
            sb = lambda name, shape, dt: es.enter_context(nc.sbuf_tensor(name, list(shape), dt))
            xring = Ring(P, [sb("xblkb%d" % i, [128, KC, 512], BF16) for i in range(2)])

            def load_xp(blk):
                t, b, s = xring.next()
                P.dma("sync", t[:, :, :], xb_perm[:, blk * 512:(blk + 1) * 512].rearrange("(k p) n -> p k n", p=128), s, [XP[blk]], [b])
                return t, b

            cosT = sb("cosT", [32, NL], F32)
            sinT = sb("sinT", [32, NL], F32)
            ROT = Buf()
            with ExitStack() as es2:
                sb2 = lambda name, shape, dt: es2.enter_context(nc.sbuf_tensor(name, list(shape), dt))
                posi = sb2("posi", [32, NL], I32)
                ang = sb2("ang", [32, NL], F32)
                tf = sb2("tf", [32, NL], F32)
                rr = sb2("rr", [32, NL], F32)
                mk_ = sb2("mk_", [32, NL], F32)
                rotc = sb2("rotc_sb", [32, 2], F32)
                RB = Buf()
                rsem = P.new_dsem()
                P.dma("sync", posi[:, :], pos_in.partition_broadcast(32), rsem, (), [RB])
                P.dma("sync", rotc[:, :], rotc_in, rsem, (), [RB])
                defer = []

                def DF(fn, *a_, **k_):
                    defer.append(lambda: fn(*a_, **k_))
                DF(cp, "vector", ang[:, :], posi[:, :], [RB], [RB])
                DF(ts, "vector", ang[:, :], ang[:, :], rotc[:, 0:1], None, ALU.mult, None, [RB], [RB])
                DF(ts, "vector", tf[:, :], ang[:, :], 1.0 / (2 * PI), 0.5, ALU.mult, ALU.add, [RB], [RB])
                DF(cp, "vector", posi[:, :], tf[:, :], [RB], [RB])
                DF(cp, "vector", tf[:, :], posi[:, :], [RB], [RB])
                C1 = 6.28125
                C2 = 2 * PI - C1
                DF(stt, "vector", rr[:, :], tf[:, :], -C1, ang[:, :], ALU.mult, ALU.add, [RB], [RB])
                DF(stt, "vector", rr[:, :], tf[:, :], -C2, rr[:, :], ALU.mult, ALU.add, [RB], [RB])

                def wrap_clamp(r):
                    DF(ts, "vector", mk_[:, :], r[:, :], -PI, None, ALU.is_lt, None, [RB], [RB])
                    DF(stt, "vector", r[:, :], mk_[:, :], 2 * PI, r[:, :], ALU.mult, ALU.add, [RB], [RB])
                    DF(ts, "vector", mk_[:, :], r[:, :], PI, None, ALU.is_gt, None, [RB], [RB])
                    DF(stt, "vector", r[:, :], mk_[:, :], -2 * PI, r[:, :], ALU.mult, ALU.add, [RB], [RB])
                    DF(ts, "vector", r[:, :], r[:, :], -3.141592, 3.141592, ALU.max, ALU.min, [RB], [RB])
                wrap_clamp(rr)
                DF(act, sinT[:, :], rr[:, :], AF.Sin, [RB], [ROT], scale=rotc[:, 1:2])
                DF(ts, "vector", rr[:, :], rr[:, :], PI / 2, None, ALU.add, None, [RB], [RB])
                wrap_clamp(rr)
                DF(act, cosT[:, :], rr[:, :], AF.Sin, [RB], [ROT])

                wvd = sb2("wvd_sb", [128, KC, 1024], BF16)
                WV = Buf()
                wvs = P.new_dsem()
                for g in range(2):
                    P.dma("gpsimd", wvd[:, :, g * 512:(g + 1) * 512], wvd_in[:, g * 512:(g + 1) * 512].rearrange("(k p) n -> p k n", p=128), wvs, (), [WV])
                vst = Ring(P, [sb2("vst%d" % i, [128, 1024], BF16) for i in range(2)])
                for blk in range(8):
                    t_, b_, s__ = xring.next()
                    cols_ = slice(blk * 512, (blk + 1) * 512)
                    P.dma("gpsimd", t_[:, :, :], xT_perm[:, cols_].rearrange("(k p) n -> p k n", p=128), s__, (), [b_])
                    P.dma("sync", xb_perm[:, cols_].rearrange("(k p) n -> p k n", p=128), t_[:, :, :], s__, [b_], [XP[blk]])
                    xb, xbb = t_, b_
                    for sub in range(4):
                        if defer:
                            defer.pop(0)()
                        vt, vtb, vts = vst.next()
                        for g in range(2):
                            bv, bvb = nb()
                            with P.group("tensor"):
                                for kc in range(KC):
                                    mm(bv[:, :], xb[:, kc, sub * 128:(sub + 1) * 128], wvd[:, kc, g * 512:(g + 1) * 512], kc == 0, kc == KC - 1, [xbb, WV], [bvb])
                            cp("scalar" if g == 0 else "vector", vt[:, g * 512:(g + 1) * 512], bv[:, :], [bvb], [vtb])
                        r0 = (blk * 4 + sub) * 128
                        P.dma("scalar", vd_scr[r0:r0 + 128, :], vt[:, :], vts, [vtb], [VDB])
                        T_ = blk * 4 + sub
                        s_, r16 = T_ // 16, T_ % 16
                        a_, r4 = r16 // 4, r16 % 4
                        d4 = vd4_scr.rearrange("(t i) c -> t i c", i=128)[r4 * 8 + 4 * s_:r4 * 8 + 4 * s_ + 4, 32 * a_:32 * a_ + 32, :]
                        P.dma("scalar", d4, vt[:, :], vts, [vtb], [VDB])
                        d1 = vd1_scr.rearrange("(t i) c -> t i c", i=128)[16 * s_:16 * s_ + 16, 8 * r16:8 * r16 + 8, :]
                        P.dma("scalar", d1, vt[:, :], vts, [vtb], [VDB])
                while defer:
                    defer.pop(0)()
                P.flush()
            P.barrier()
            if stop == "V":
                finish()
                return nc

            wdring = Ring(P, [sb("wd%d" % i, [128, KC, 320], BF16) for i in range(2)])
            dq2 = [sb("d_qq%d" % i, [128, 2560], BF16) for i in range(2)]
            dk2 = [sb("d_kk%d" % i, [128, NL], BF16) for i in range(2)]
            DBQ, DBK = [Buf(), Buf()], [Buf(), Buf()]
            dk4 = sb("d_k4", [128, NL], BF16)
            dk1 = sb("d_k1", [128, NL], BF16)
            dq4 = sb("d_q4", [128, 2048], BF16)
            dq1 = sb("d_q1", [128, 2048], BF16)
            hq1 = sb("d_hq1", [128, 16], BF16)
            V16 = sb("d_v16", [128, 32, 128], BF16)
            V4 = sb("d_v4", [128, 32, 128], BF16)
            V1 = sb("d_v1", [128, 32, 128], BF16)
            acc = sb("d_acc", [128, 2560], F32)
            dacc = sb("d_dacc", [128, 2560], F32)
            odb = sb("d_od", [128, TQ], BF16)
            rt1 = [sb("d_rt1%d" % i, [32, 512], F32) for i in range(2)]
            rt2 = [sb("d_rt2%d" % i, [32, 512], F32) for i in range(2)]
            Pm = [sb("d_P%d" % i, [128, 2, 128], BF16) for i in range(4)]
            DB = {k: Buf() for k in ["q", "k", "v", "acc", "od", "k4", "k1", "q4", "q1"]}
            RTB = [[Buf(), Buf()], [Buf(), Buf()]]
            PmB = [Buf() for _ in range(4)]
            vsem = P.new_dsem()
            odsem = P.new_dsem()
            SC = 128.0 ** -0.5
            vd5 = vd_scr

            def kcols(t, off, kind, a, b_):
                if kind == 16:
                    p0 = 2048 * a + 128 * b_ - off
                    return t[:, p0:p0 + 128]
                if kind == 4:
                    r4, n = a, b_
                    s_, m = n // 4, n % 4
                    base = 2048 * s_ - off
                    return t[:, base:base + 2048].rearrange("p (a r u) -> p a r u", a=4, r=4, u=128)[:, :, r4, 32 * m:32 * m + 32]
                s_, m = a, b_
                base = 2048 * s_ - off
                return t[:, base:base + 2048].rearrange("p (r m u) -> p r m u", r=16, m=16, u=8)[:, :, m, :]

            def head_setup(h):
                wd, wdb, wds = wdring.next()
                P.dma("gpsimd", wd[:, :, :], wd_in[h].rearrange("(k p) n -> p k n", p=128), wds, (), [wdb])
                return wd, wdb

            def proj_gen(h, wd, wdb):
                dq, dk = dq2[h % 2], dk2[h % 2]
                DBq = {"q": DBQ[h % 2], "k": DBK[h % 2]}
                for blk in range(8):
                    xb, xbb = load_xp(blk)
                    cols = slice(blk * 512, (blk + 1) * 512)
                    todo = [(128, 288, dk[:, cols], "k")]
                    if blk >= 3:
                        todo.append((0, 256, dq[:, (blk - 3) * 512:(blk - 2) * 512], "q"))
                    for ti, (c0, cs, dst, hb) in enumerate(todo):
                        bk, bkb = nb()
                        with P.group("tensor"):
                            for kc in range(KC):
                                mm(bk[:, :], wd[:, kc, c0:c0 + 128], xb[:, kc, :], kc == 0, kc == KC - 1, [xbb, wdb], [bkb])
                        bs_, bsb_ = nb()
                        with P.group("tensor"):
                            for kc in range(KC):
                                mm(bs_[0:32, :], wd[:, kc, cs:cs + 32], xb[:, kc, :], kc == 0, kc == KC - 1, [xbb, wdb], [bsb_])
                        cp("scalar", dst, bk[:, :], [bkb], [DBq[hb]])
                        tt("vector", rt1[ti][:, :], bk[0:32, :], cosT[:, cols], ALU.mult, [bkb, ROT], [RTB[ti][0]])
                        tt("vector", rt2[ti][:, :], bs_[0:32, :], sinT[:, cols], ALU.mult, [bsb_, ROT], [RTB[ti][1]])
                        P.op("vector", lambda e, dst=dst, ti=ti: e.tensor_tensor(out=dst[0:32], in0=rt1[ti][:, :], in1=rt2[ti][:, :], op=ALU.add),
                             [RTB[ti][0], RTB[ti][1]], [DBq[hb]])
                    yield blk

            def post_proj(h):
                dq, dk = dq2[h % 2], dk2[h % 2]
                DBq = {"q": DBQ[h % 2], "k": DBK[h % 2]}
                hc = slice(h * 128, (h + 1) * 128)
                P.dma("gpsimd", V16[:, :, :], vd5[:, hc].rearrange("(t p) c -> p t c", p=128), vsem, [VDB], [DB["v"]])
                P.dma("gpsimd", V4[:, :, :], vd4_scr[:, hc].rearrange("(t p) c -> p t c", p=128), vsem, [VDB], [DB["v"]])
                P.dma("gpsimd", V1[:, :, :], vd1_scr[:, hc].rearrange("(t p) c -> p t c", p=128), vsem, [VDB], [DB["v"]])
                for s_ in range(2):
                    srck = dk[:, 2048 * s_:2048 * s_ + 2048]
                    for r4 in range(4):
                        P.op("vector" if r4 % 2 == 0 else "gpsimd", lambda e, s_=s_, r4=r4, srck=srck: e.tensor_copy(
                            out=dk4[:, (r4 * 8 + s_ * 4) * 128:(r4 * 8 + s_ * 4 + 4) * 128].rearrange("p (m a u) -> p m a u", m=4, a=4, u=32),
                            in_=srck.rearrange("p (a r m u) -> p r m a u", a=4, r=4, m=4, u=32)[:, r4]), [DBq["k"]], [DB["k4"]])
                    P.op("vector", lambda e, s_=s_, srck=srck: e.tensor_copy(
                        out=dk1[:, 2048 * s_:2048 * s_ + 2048].rearrange("p (m r u) -> p m r u", m=16, r=16, u=8),
                        in_=srck.rearrange("p (r m u) -> p m r u", r=16, m=16, u=8)), [DBq["k"]], [DB["k1"]])
                srcq = dq[:, 512:2560]
                for r4 in range(4):
                    P.op("scalar", lambda e, r4=r4: e.activation(
                        out=dq4[:, r4 * 512:(r4 + 1) * 512].rearrange("p (m a u) -> p m a u", m=4, a=4, u=32),
                        in_=srcq.rearrange("p (a r m u) -> p r m a u", a=4, r=4, m=4, u=32)[:, r4], func=AF.Copy), [DBq["q"]], [DB["q4"]])
                P.op("scalar", lambda e: e.activation(
                    out=dq1[:, :].rearrange("p (m r u) -> p m r u", m=16, r=16, u=8),
                    in_=srcq.rearrange("p (r m u) -> p m r u", r=16, m=16, u=8), func=AF.Copy), [DBq["q"]], [DB["q1"]])
                P.op("scalar", lambda e: e.activation(
                    out=hq1[:, :].rearrange("p (r u) -> p r u", r=2),
                    in_=dq[:, 256:512].rearrange("p (r u) -> p r u", r=2)[:, :, 120:128], func=AF.Copy), [DBq["q"]], [DB["q1"]])


            def att_gen(h):
                dq, dk = dq2[h % 2], dk2[h % 2]
                DBq = {"q": DBQ[h % 2], "k": DBK[h % 2]}

                def kap(kb):
                    kind, a_, b_ = kb
                    if kind == 16:
                        p0 = 2048 * a_ + 128 * b_
                        return dk[:, p0:p0 + 128], DBq["k"]
                    if kind == 4:
                        p0 = (a_ * 8 + b_) * 128
                        return dk4[:, p0:p0 + 128], DB["k4"]
                    p0 = (16 * a_ + b_) * 128
                    return dk1[:, p0:p0 + 128], DB["k1"]
                mset("gpsimd", acc[:, :], 0.0, [DB["acc"]])
                mset("gpsimd", dacc[:, :], 0.0, [DB["acc"]])
                blocks = []
                for r in range(16):
                    blocks.append((16, (kcols(dq, 1536, 16, 1, r), DBq["q"]), kcols(acc, 1536, 16, 1, r), kcols(dacc, 1536, 16, 1, r), 128, None,
                                   [((16, 0, r), V16[:, r, :], masksc[:, 0, :]), ((16, 1, r), V16[:, 16 + r, :], masks[:, 1, :])]))
                for r in (14, 15):
                    blocks.append((16, (kcols(dq, 1536, 16, 0, r), DBq["q"]), kcols(acc, 1536, 16, 0, r), kcols(dacc, 1536, 16, 0, r), 128, None,
                                   [((16, 0, r), V16[:, r, :], masksc[:, 1, :])]))
                for r4 in range(4):
                    for n in range(4, 8):
                        pm = masksc[:, 2, :] if n == 4 else masks[:, 2, :]
                        blocks.append((4, (dq4[:, (r4 * 4 + n - 4) * 128:(r4 * 4 + n - 3) * 128], DB["q4"]), kcols(acc, 1536, 4, r4, n), kcols(dacc, 1536, 4, r4, n), 128, [4, 32],
                                       [((4, r4, n - 1), V4[:, r4 * 8 + n - 1, :], pm), ((4, r4, n), V4[:, r4 * 8 + n, :], masks[:, 3, :])]))
                for r4 in (2, 3):
                    p0 = (12 + r4) * 128 + 96 - 1536
                    blocks.append((4, (dq[:, p0:p0 + 32], DBq["q"]), acc[:, p0:p0 + 32], dacc[:, p0:p0 + 32], 32, None,
                                   [((4, r4, 2), V4[:, r4 * 8 + 2, :], masksc[:, 2, 96:128]), ((4, r4, 3), V4[:, r4 * 8 + 3, :], masksc[:, 3, 96:128])]))
                for m in range(16):
                    pk = (1, 0, 15) if m == 0 else (1, 1, m - 1)
                    pm = masksc[:, 4, :] if m == 0 else masks[:, 4, :]
                    blocks.append((1, (dq1[:, m * 128:(m + 1) * 128], DB["q1"]), kcols(acc, 1536, 1, 1, m), kcols(dacc, 1536, 1, 1, m), 128, [16, 8],
                                   [(pk, V1[:, 16 * pk[1] + pk[2], :], pm), ((1, 1, m), V1[:, 16 + m, :], masks[:, 5, :])]))
                hq = lambda t: t[:, 14 * 128 - 1536:16 * 128 - 1536].rearrange("p (r u) -> p r u", r=2)[:, :, 120:128]
                blocks.append((1, (hq1[:, :], DB["q1"]), hq(acc), hq(dacc), 16, [2, 8],
                               [((1, 0, 14), V1[:, 14, :], masksc[:, 4, 112:128]), ((1, 0, 15), V1[:, 15, :], masksc[:, 5, 112:128])]))
                def a0(i, c):
                    kind, (qap, qbuf), accap, daccap, nq, qshape, keys = blocks[i]
                    c["bsc"], c["bscb"] = nb(True)
                    for ki, (kb, vt, mk) in enumerate(keys):
                        ka, kbuf = kap(kb)
                        mm(c["bsc"][:, ki * 128:ki * 128 + nq], ka, qap, True, True, [kbuf, qbuf], [c["bscb"]])

                def a1(i, c):
                    kind, (qap, qbuf), accap, daccap, nq, qshape, keys = blocks[i]
                    pi = i % 4
                    nk = len(keys)
                    act(Pm[pi][:, 0:nk, 0:nq], c["bsc"][:, 0:nk * 128].rearrange("p (a b) -> p a b", a=nk)[:, :, 0:nq], AF.Exp, [c["bscb"]], [PmB[pi]], scale=SC)
                    rel(c["bscb"])
                    for ki, (kb, vt, mk) in enumerate(keys):
                        tt("gpsimd", Pm[pi][:, ki, 0:nq], Pm[pi][:, ki, 0:nq], mk, ALU.mult, [CB], [PmB[pi]])

                def a2(i, c):
                    kind, (qap, qbuf), accap, daccap, nq, qshape, keys = blocks[i]
                    pi = i % 4
                    nk = len(keys)
                    c["bo"], c["bob"] = nb(True)
                    with P.group("tensor"):
                        for ki, (kb, vt, mk) in enumerate(keys):
                            mm(c["bo"][:, 0:nq], vt, Pm[pi][:, ki, 0:nq], ki == 0, ki == nk - 1, [DB["v"], PmB[pi]], [c["bob"]])
                    with P.group("tensor"):
                        for ki, (kb, vt, mk) in enumerate(keys):
                            mm(c["bo"][:, 128:128 + nq], ones_bf[:, :], Pm[pi][:, ki, 0:nq], ki == 0, ki == nk - 1, [CB, PmB[pi]], [c["bob"]])

                def a3(i, c):
                    kind, (qap, qbuf), accap, daccap, nq, qshape, keys = blocks[i]
                    o_in = c["bo"][:, 0:nq]
                    d_in = c["bo"][:, 128:128 + nq]
                    if qshape is not None:
                        o_in = o_in.rearrange("p (a b) -> p a b", a=qshape[0])
                        d_in = d_in.rearrange("p (a b) -> p a b", a=qshape[0])
                    tt("vector", accap, accap, o_in, ALU.add, [c["bob"]], [DB["acc"]])
                    tt("vector", daccap, daccap, d_in, ALU.add, [c["bob"]], [DB["acc"]])
                    rel(c["bob"])
                for t_ in pipeline_gen(len(blocks), [a0, a1, a2, a3]):
                    yield t_
                ts("vector", dacc[:, 256:2560], dacc[:, 256:2560], 1e-30, None, ALU.max, None, [], [DB["acc"]])
                act(dacc[:, 256:2560], dacc[:, 256:2560], AF.Ln, [], [DB["acc"]])
                act(dacc[:, 256:2560], dacc[:, 256:2560], AF.Exp, [], [DB["acc"]], scale=-1.0)
                P.op("vector", lambda e: e.tensor_tensor(out=odb[:, 2:TQ].rearrange("p (u r) -> p r u", r=16),
                                                         in0=acc[:, 512:2560].rearrange("p (r u) -> p r u", r=16),
                                                         in1=dacc[:, 512:2560].rearrange("p (r u) -> p r u", r=16), op=ALU.mult), [DB["acc"]], [DB["od"]])
                tt("vector", odb[:, 0:1], acc[:, 383:384], dacc[:, 383:384], ALU.mult, [DB["acc"]], [DB["od"]])
                tt("vector", odb[:, 1:2], acc[:, 511:512], dacc[:, 511:512], ALU.mult, [DB["acc"]], [DB["od"]])
                P.dma("sync", mixT[(8 + h) * 128:(9 + h) * 128, :], odb[:, :], odsem, [DB["od"]], [MIXB])
                yield -1

            prev_att = None
            for h in range(ndil):
                wd, wdb = head_setup(h)
                pg = proj_gen(h, wd, wdb)
                for _blk in pg:
                    if prev_att is not None:
                        for _ in range(8):
                            if next(prev_att, None) is None:
                                break
                if prev_att is not None:
                    for _ in prev_att:
                        pass
                post_proj(h)
                prev_att = att_gen(h)
            for _ in prev_att:
                pass
            P.flush()
        P.barrier()

        def gemm_ln_phase(tag, w_dram, a_dram, a_cast, resid_dram, lnidx, out_dram, out_col0, chunks, ABUF, RBUF, OBUF):
            with ExitStack() as es:
                sb = lambda name, shape, dt: es.enter_context(nc.sbuf_tensor(tag + name, list(shape), dt))
                W = sb("W", [128, KC, D], BF16)
                WB = [Buf() for _ in range(4)]
                for g in range(4):
                    P.dma("gpsimd", W[:, :, g * 512:(g + 1) * 512], w_dram[:, g * 512:(g + 1) * 512].rearrange("(k p) n -> p k n", p=128), P.new_dsem(), (), [WB[g]])
                aring = Ring(P, [sb("a%d" % i, [128, KC, 512], BF16) for i in range(2)])
                ln_chunks(tag, sb, chunks, KC,
                          lambda dc, kc: W[:, kc, dc * 128:(dc + 1) * 128], WB,
                          a_dram, a_cast, aring, ABUF, resid_dram, RBUF, lnidx, out_dram, out_col0, OBUF)
                P.flush()
            P.barrier()

        def ln_chunks(tag, sb, chunks, nk, wfn, wbufs, a_dram, a_cast, aring, ABUF, resid_dram, RBUF, lnidx, out_dram, out_col0, OBUF, wstream=None):
            ny = 2
            y = [sb("y%d" % i, [128, KC, 512], F32) for i in range(ny)]
            YB = [Buf() for _ in range(ny)]
            s1 = [sb("s1_%d" % i, [128, 512], F32) for i in range(2)]
            s2 = [sb("s2_%d" % i, [128, 512], F32) for i in range(2)]
            SB1, SB2 = [Buf(), Buf()], [Buf(), Buf()]
            ysq = [sb("ysq%d" % i, [128, 512], F32) for i in range(2)]
            YSQ = [Buf(), Buf()]
            rres = Ring(P, [sb("res%d" % i, [128, 512], F32) for i in range(3)])
            mean = sb("mean", [128, 512], F32)
            rstd = sb("rstd", [128, 512], F32)
            MB = Buf()
            tn = [sb("tn%d" % i, [128, 512], F32) for i in range(2)]
            TN = [Buf(), Buf()]
            oring = Ring(P, [sb("o%d" % i, [128, 512], F32) for i in range(3)])
            pend = []
            for ci, (c0, c1) in enumerate(chunks):
                n = c1 - c0
                yi = ci % ny
                yc, ycb, s1c, s2c, S1B, S2B = y[yi], YB[yi], s1[ci % 2], s2[ci % 2], SB1[ci % 2], SB2[ci % 2]
                if ci == 0:
                    nxt = aring.next()
                    P.dma("gpsimd" if a_cast else "sync", nxt[0][:, 0:nk, 0:n], a_dram[:, c0:c1].rearrange("(k p) n -> p k n", p=128), nxt[2], [ABUF], [nxt[1]])
                a, ab, asem = nxt
                if ci + 1 < len(chunks):
                    d0, d1 = chunks[ci + 1]
                    nxt = aring.next()
                    P.dma("gpsimd" if a_cast else "sync", nxt[0][:, 0:nk, 0:d1 - d0], a_dram[:, d0:d1].rearrange("(k p) n -> p k n", p=128), nxt[2], [ABUF], [nxt[1]])

                def epi(dc, bk, bkb):
                    rt, rtb, rsem_ = rres.next()
                    P.dma("sync", rt[:, 0:n], resid_dram(dc, c0, c1), rsem_, [RBUF], [rtb])
                    stt("vector", yc[:, dc, 0:n], rt[:, 0:n], ALPHA, bk[:, 0:n], ALU.mult, ALU.add, [rtb, bkb], [ycb])
                    i2 = dc % 2
                    if dc == 0:
                        cp("vector", s1c[:, 0:n], yc[:, dc, 0:n], [ycb], [S1B])
                        act(s2c[:, 0:n], yc[:, dc, 0:n], AF.Square, [ycb], [S2B])
                    else:
                        tt("vector", s1c[:, 0:n], s1c[:, 0:n], yc[:, dc, 0:n], ALU.add, [ycb], [S1B])
                        act(ysq[i2][:, 0:n], yc[:, dc, 0:n], AF.Square, [ycb], [YSQ[i2]])
                        tt("gpsimd", s2c[:, 0:n], s2c[:, 0:n], ysq[i2][:, 0:n], ALU.add, [YSQ[i2]], [S2B])

                if wstream is None:
                    for dc in range(KC):
                        bk, bkb = nb()
                        with P.group("tensor"):
                            for kc in range(nk):
                                mm(bk[:, 0:n], wfn(dc, kc), a[:, kc, 0:n], kc == 0, kc == nk - 1, [ab, wbufs[dc // 4]], [bkb])
                        epi(dc, bk, bkb)
                else:
                    for qd_ in range(4):
                        acc4 = [nb(True) for _ in range(4)]
                        for j in range(nk):
                            wt, wtb = wstream(j, qd_)
                            with P.group("tensor"):
                                for i_ in range(4):
                                    mm(acc4[i_][0][:, 0:n], wt[:, i_ * 128:(i_ + 1) * 128], a[:, j, 0:n], j == 0, j == nk - 1, [ab, wtb], [acc4[i_][1]])
                        for i_ in range(4):
                            epi(4 * qd_ + i_, acc4[i_][0], acc4[i_][1])
                            rel(acc4[i_][1])
                def part2(n=n, c0=c0, c1=c1, yc=yc, ycb=ycb, s1c=s1c, s2c=s2c, S1B=S1B, S2B=S2B):
                    bm, bmb = nb()
                    mm(bm[:, 0:n], ones2048[:, :], s1c[:, 0:n], True, True, [CB, S1B], [bmb])
                    bm2, bm2b = nb()
                    mm(bm2[:, 0:n], ones2048[:, :], s2c[:, 0:n], True, True, [CB, S2B], [bm2b])
                    cp("scalar", mean[:, 0:n], bm[:, 0:n], [bmb], [MB])
                    tt("vector", rstd[:, 0:n], mean[:, 0:n], mean[:, 0:n], ALU.mult, [MB], [MB])
                    tt("vector", rstd[:, 0:n], bm2[:, 0:n], rstd[:, 0:n], ALU.subtract, [bm2b], [MB])
                    act(rstd[:, 0:n], rstd[:, 0:n], AF.Ln, [], [MB], bias=LN_EPS)
                    act(rstd[:, 0:n], rstd[:, 0:n], AF.Exp, [], [MB], scale=-0.5)
                    for dc in range(KC):
                        i2 = dc % 2
                        tt("vector", tn[i2][:, 0:n], yc[:, dc, 0:n], mean[:, 0:n], ALU.subtract, [ycb, MB], [TN[i2]])
                        tt("gpsimd", tn[i2][:, 0:n], tn[i2][:, 0:n], rstd[:, 0:n], ALU.mult, [MB], [TN[i2]])
                        ot, otb, osem_ = oring.next()
                        act(ot[:, 0:n], tn[i2][:, 0:n], AF.Identity, [TN[i2], CB], [otb], scale=lnp[:, 2 * lnidx, dc:dc + 1], bias=lnp[:, 2 * lnidx + 1, dc:dc + 1])
                        P.dma("scalar", out_dram[dc * 128:(dc + 1) * 128, c0 - out_col0:c1 - out_col0], ot[:, 0:n], osem_, [otb], [OBUF])
                if pend:
                    pend.pop()()
                pend.append(part2)
            pend.pop()()

        if stop == "B":
            finish()
            return nc
        X1B, OCB, X2B, HTB, OUTB = Buf(), Buf(), Buf(), Buf(), Buf()
        W2B = Buf()
        w2sem = P.new_dsem()
        NOB = Buf()
        xres = lambda dc, c0, c1: xT_nat[dc * 128:(dc + 1) * 128, 2046 + c0:2046 + c1]
        gemm_ln_phase("C", wout_in, mixT, False, xres, 0, x1T, 0, TCH, MIXB, NOB, X1B)
        if stop == "C":
            finish()
            return nc

        with ExitStack() as es:
            sb = lambda name, shape, dt: es.enter_context(nc.sbuf_tensor("D1" + name, list(shape), dt))
            mkT = sb("mkT", [128, 16, 256], BF16)
            mv = sb("mv", [128, 2, D], BF16)
            MKB, MVB, MTB = Buf(), Buf(), Buf()
            Wq = sb("Wq", [128, KC, D], BF16)
            WQB = Buf()
            wqs = P.new_dsem()
            es2 = ExitStack()
            sb2 = lambda name, shape, dt: es2.enter_context(nc.sbuf_tensor("D0" + name, list(shape), dt))
            mT = sb2("memT", [128, KC, 256], BF16)
            P.dma("gpsimd", mT[:, :, :], memT.rearrange("(k p) n -> p k n", p=128), P.new_dsem(), (), [MTB])
            wkvr = Ring(P, [sb2("wkv%d" % i, [128, KC, 512], BF16) for i in range(2)])
            for g in range(8):
                wt, wtb, wts = wkvr.next()
                P.dma("gpsimd", wt[:, :, :], wkv_in[:, g * 512:(g + 1) * 512].rearrange("(k p) n -> p k n", p=128), wts, (), [wtb])
                if g < 4:
                    for j in range(4):
                        bk, bkb = nb()
                        with P.group("tensor"):
                            for kc in range(KC):
                                mm(bk[:, 0:256], wt[:, kc, j * 128:(j + 1) * 128], mT[:, kc, :], kc == 0, kc == KC - 1, [wtb, MTB], [bkb])
                        cp("scalar", mkT[:, 4 * g + j, :], bk[:, 0:256], [bkb], [MKB])
                else:
                    for mt in range(2):
                        bk, bkb = nb()
                        with P.group("tensor"):
                            for kc in range(KC):
                                mm(bk[:, :], mT[:, kc, mt * 128:(mt + 1) * 128], wt[:, kc, :], kc == 0, kc == KC - 1, [wtb, MTB], [bkb])
                        cp("vector", mv[:, mt, (g - 4) * 512:(g - 3) * 512], bk[:, :], [bkb], [MVB])
            for g in range(4):
                P.dma("gpsimd", Wq[:, :, g * 512:(g + 1) * 512], wq_in[:, g * 512:(g + 1) * 512].rearrange("(k p) n -> p k n", p=128), wqs, (), [WQB])
            for g in range(8):
                P.dma("gpsimd", w2b[g * 688:(g + 1) * 688, :], fwout_in[g * 688:(g + 1) * 688, :], w2sem, (), [W2B])
            P.flush()
            es2.close()
            P.barrier()
            aring = Ring(P, [sb("a%d" % i, [128, KC, 512], BF16) for i in range(2)])
            qc = sb("qc", [128, KC, 512], BF16)
            QCB = Buf()
            ocr = Ring(P, [sb("oc%d" % i, [128, KC, 512], BF16) for i in range(2)])
            Pc = [sb("Pc%d" % i, [128, 2, 512], BF16) for i in range(2)]
            PCB = [Buf(), Buf()]
            rden = [sb("rden%d" % i, [128, 512], F32) for i in range(2)]
            RDB = [Buf(), Buf()]
            SCC = 512.0 ** -0.5
            for (c0, c1) in TCH:
                n = c1 - c0
                a, ab, asem = aring.next()
                P.dma("gpsimd", a[:, :, 0:n], x1T[:, c0:c1].rearrange("(k p) n -> p k n", p=128), asem, [X1B], [ab])
                for dc in range(KC):
                    bk, bkb = nb()
                    with P.group("tensor"):
                        for kc in range(KC):
                            mm(bk[:, 0:n], Wq[:, kc, dc * 128:(dc + 1) * 128], a[:, kc, 0:n], kc == 0, kc == KC - 1, [ab, WQB], [bkb])
                    cp("scalar" if dc % 2 == 0 else "vector", qc[:, dc, 0:n], bk[:, 0:n], [bkb], [QCB])
                oc, ocb, ocs = ocr.next()
                for hh in range(4):
                    i2 = hh % 2
                    for mt in range(2):
                        bsx, bsxb = nb()
                        with P.group("tensor"):
                            for c in range(4):
                                mm(bsx[:, 0:n], mkT[:, 4 * hh + c, mt * 128:(mt + 1) * 128], qc[:, 4 * hh + c, 0:n], c == 0, c == 3, [MKB, QCB], [bsxb])
                        act(Pc[i2][:, mt, 0:n], bsx[:, 0:n], AF.Exp, [bsxb], [PCB[i2]], scale=SCC)
                    bd, bdb = nb()
                    with P.group("tensor"):
                        for mt in range(2):
                            mm(bd[:, 0:n], ones_bf[:, :], Pc[i2][:, mt, 0:n], mt == 0, mt == 1, [CB, PCB[i2]], [bdb])
                    recip("vector", rden[i2][:, 0:n], bd[:, 0:n], [bdb], [RDB[i2]])
                    for c in range(4):
                        bo, bob = nb()
                        with P.group("tensor"):
                            for mt in range(2):
                                mm(bo[:, 0:n], mv[:, mt, (4 * hh + c) * 128:(4 * hh + c + 1) * 128], Pc[i2][:, mt, 0:n], mt == 0, mt == 1, [MVB, PCB[i2]], [bob])
                        tt("vector", oc[:, 4 * hh + c, 0:n], bo[:, 0:n], rden[i2][:, 0:n], ALU.mult, [bob, RDB[i2]], [ocb])
                P.dma("sync", ocT[:, c0:c1].rearrange("(k p) n -> p k n", p=128), oc[:, :, 0:n], ocs, [ocb], [OCB])
            P.flush()
        P.barrier()

        if stop == "D1":
            finish()
            return nc
        x1res = lambda dc, c0, c1: x1T[dc * 128:(dc + 1) * 128, c0:c1]
        gemm_ln_phase("D2", wo_in, ocT, False, x1res, 1, x2T, 0, TCH, OCB, X1B, X2B)
        if stop == "D2":
            finish()
            return nc

        with ExitStack() as es:
            sb = lambda name, shape, dt: es.enter_context(nc.sbuf_tensor("E" + name, list(shape), dt))
            x2b = sb("x2b", [128, KC, TQ], BF16)
            X2S = Buf()
            xs = P.new_dsem()
            for (c0, c1) in TCH[1:]:
                P.dma("gpsimd", x2b[:, :, c0:c1], x2T[:, c0:c1].rearrange("(k p) n -> p k n", p=128), xs, [X2B], [X2S])
            P.dma("gpsimd", x2b[:, :, 0:2], x2T[:, 0:2].rearrange("(k p) n -> p k n", p=128), xs, [X2B], [X2S])
            ts("vector", x2b[:, :, 0:2], x2b[:, :, 0:2], flag[:, 0:1], None, ALU.mult, None, [CB], [X2S])
            wr = Ring(P, [sb("w%d" % i, [128, KC, 256], BF16) for i in range(3)])
            ug = [sb("ug%d" % i, [128, TQ], F32) for i in range(2)]
            uu = [sb("uu%d" % i, [128, TQ], F32) for i in range(2)]
            yg = [sb("yg%d" % i, [128, 2048], F32) for i in range(2)]
            yu = [sb("yu%d" % i, [128, 2048], F32) for i in range(2)]
            hr = Ring(P, [sb("h%d" % i, [128, 2048], BF16) for i in range(2)])
            UG, UU, YG, YU = [Buf(), Buf()], [Buf(), Buf()], [Buf(), Buf()], [Buf(), Buf()]
            for j in range(NJ):
                i2 = j % 2
                wt, wtb, wts = wr.next()
                P.dma("gpsimd", wt[:, :, 0:128], fwin_in[:, j * 128:(j + 1) * 128].rearrange("(k p) n -> p k n", p=128), wts, (), [wtb])
                P.dma("gpsimd", wt[:, :, 128:256], fwin_in[:, DFF + j * 128:DFF + (j + 1) * 128].rearrange("(k p) n -> p k n", p=128), wts, (), [wtb])
                for part, (ubuf, UB) in enumerate([(ug[i2], UG[i2]), (uu[i2], UU[i2])]):
                    for ci, (c0, c1) in enumerate(TCH):
                        n = c1 - c0
                        bk, bkb = nb()
                        with P.group("tensor"):
                            for kc in range(KC):
                                mm(bk[:, 0:n], wt[:, kc, part * 128:(part + 1) * 128], x2b[:, kc, c0:c1], kc == 0, kc == KC - 1, [wtb, X2S], [bkb])
                        cp("scalar", ubuf[:, c0:c1], bk[:, 0:n], [bkb], [UB])
                for part, (eng, ubuf, UB, ybuf, YB_) in enumerate([("vector", ug[i2], UG[i2], yg[i2], YG[i2]), ("gpsimd", uu[i2], UU[i2], yu[i2], YU[i2])]):
                    cj = part * NJ + j
                    act(ybuf[:, :], ubuf[:, 2:TQ], AF.Identity, [UB, CB], [YB_], scale=cw[:, cj, 2:3], bias=cb[:, cj:cj + 1])
                    stt("vector", ybuf[:, :], ubuf[:, 1:TQ - 1], cw[:, cj, 1:2], ybuf[:, :], ALU.mult, ALU.add, [UB, CB], [YB_])
                    stt("vector", ybuf[:, :], ubuf[:, 0:TQ - 2], cw[:, cj, 0:1], ybuf[:, :], ALU.mult, ALU.add, [UB, CB], [YB_])
                act(yg[i2][:, :], yg[i2][:, :], AF.Silu, [], [YG[i2]])
                ht, htb, hts = hr.next()
                tt("vector", ht[:, :], yg[i2][:, :], yu[i2][:, :], ALU.mult, [YG[i2], YU[i2]], [htb])
                P.dma("sync", hT[j * 128:(j + 1) * 128, :], ht[:, :], hts, [htb], [HTB])
            P.flush()
        P.barrier()

        if stop == "E":
            finish()
            return nc
        with ExitStack() as es:
            sb = lambda name, shape, dt: es.enter_context(nc.sbuf_tensor("F" + name, list(shape), dt))
            aring = Ring(P, [sb("a%d" % i, [128, NJ, 512], BF16) for i in range(2)])
            w2r = Ring(P, [sb("w%d" % i, [128, 512], BF16) for i in range(8)])

            def wstream(j, qd_):
                wt, wtb, wts = w2r.next()
                P.dma("sync", wt[:, :], w2b[j * 128:(j + 1) * 128, qd_ * 512:(qd_ + 1) * 512], wts, [W2B], [wtb])
                return wt, wtb
            x2res = lambda dc, c0, c1: x2T[dc * 128:(dc + 1) * 128, 2 + c0:2 + c1]
            ln_chunks("F", sb, [(i * 512, (i + 1) * 512) for i in range(4)], NJ, None, None,
                      hT, False, aring, HTB, x2res, X2B, 2, outT, 0, OUTB, wstream=wstream)
            final = [(s[0], s[1], "dma") for s in P.dsems if s[1] > 0]
            P.flush(final)
    return nc


def _perm_idx():
    idx = np.empty(NL, np.int64)
    for s in range(2):
        for r in range(16):
            idx[s * 2048 + r * 128:s * 2048 + (r + 1) * 128] = s * 2048 + 16 * np.arange(128) + r
    return idx


def _masks():
    m = np.zeros((128, 6, 128), np.float32)
    j = np.arange(128)[:, None]
    i = np.arange(128)[None, :]
    for pi_, nat in enumerate([lambda x: x, lambda x: 4 * (x % 32) + x // 32, lambda x: 16 * (x % 8) + x // 8]):
        nj, ni = nat(j), nat(i)
        m[:, 2 * pi_, :] = (nj >= ni)
        m[:, 2 * pi_ + 1, :] = (nj <= ni)
    return m.reshape(128, 768)


def _fm(v, nchunk):
    return np.ascontiguousarray(v.reshape(nchunk, 128).T)


_CACHE = {}


def make_in_maps(x, mem, positions, w_in, gla_gate_w2, gla_gate_b, gla_norm_g, w_out, ln1_g, ln1_b,
                 ca_wq, ca_wkv, ca_wo, ln2_g, ln2_b, ffn_w_in, ffn_conv_w, ffn_conv_b, ffn_w_out, ln3_g, ln3_b):
    f32 = np.float32
    x = np.asarray(x, f32)
    mem = np.asarray(mem, f32)
    positions = np.asarray(positions, np.int32)
    w_in = np.asarray(w_in, f32)[0]
    pidx = _perm_idx()
    o = 0
    cols = {}
    for name, w in zip(["qg", "kg", "vg", "rg", "glr", "qd", "kd", "vd"], [512, 512, 1024, 1024, 16, 1024, 1024, 1024]):
        cols[name] = w_in[:, o:o + w]
        o += w
    wg = np.stack([np.concatenate([cols["qg"][:, h * 128:(h + 1) * 128], cols["kg"][:, h * 128:(h + 1) * 128],
                                   cols["rg"][:, h * 256:(h + 1) * 256], cols["vg"][:, h * 256:(h + 1) * 256]], axis=1) for h in range(4)])
    swap = np.concatenate([np.arange(16, 32), np.arange(0, 16)])
    wd = np.stack([np.concatenate([cols["qd"][:, h * 128:(h + 1) * 128], cols["kd"][:, h * 128:(h + 1) * 128],
                                   cols["qd"][:, h * 128 + swap], cols["kd"][:, h * 128 + swap]], axis=1) for h in range(8)])
    w2aug = np.concatenate([np.asarray(gla_gate_w2, f32)[0], np.asarray(gla_gate_b, f32)[0][None, :]], axis=0)
    lnp = np.stack([_fm(np.asarray(v, f32)[0], 16) for v in [ln1_g, ln1_b, ln2_g, ln2_b, ln3_g, ln3_b]], axis=1).reshape(128, 96)
    convw = np.ascontiguousarray(np.asarray(ffn_conv_w, f32)[0].T.reshape(86, 128, 3).transpose(1, 0, 2)).reshape(128, 258)
    convb = _fm(np.asarray(ffn_conv_b, f32)[0], 86)
    jj = np.arange(128)[:, None]
    ii = np.arange(128)[None, :]
    uneg = np.where(jj <= ii, f32(-1.0 / 16.0), f32(0.0)).astype(f32)
    invf = (500000.0 ** (-(np.arange(0, 32, 2, dtype=np.float32)) / 32.0)).astype(f32)
    rotc = np.stack([np.concatenate([invf, invf]), np.concatenate([-np.ones(16, f32), np.ones(16, f32)])], axis=1).astype(f32)
    shared = dict(masks=_masks(), uneg=uneg, rotc=rotc, wg=np.ascontiguousarray(wg), wglr=np.ascontiguousarray(cols["glr"]),
                  w2aug=np.ascontiguousarray(w2aug), glag=_fm(np.asarray(gla_norm_g, f32)[0], 8), wd=np.ascontiguousarray(wd),
                  wvd=np.ascontiguousarray(cols["vd"]), w_out=np.asarray(w_out, f32)[0], lnp=np.ascontiguousarray(lnp),
                  ca_wq=np.asarray(ca_wq, f32)[0], ca_wkv=np.asarray(ca_wkv, f32)[0], ca_wo=np.asarray(ca_wo, f32)[0],
                  ffn_w_in=np.asarray(ffn_w_in, f32)[0], convw=convw, convb=convb, ffn_w_out=np.asarray(ffn_w_out, f32)[0])
    in_maps = []
    for c in range(8):
        b, hf = c // 2, c % 2
        xl = np.zeros((NL, D), f32)
        pl = np.zeros((NL,), np.int32)
        if hf == 1:
            xl[:] = x[b]
            pl[:] = positions[b]
        else:
            xl[2048:] = x[b, :2048]
            pl[2048:] = positions[b, :2048]
        xTn = np.ascontiguousarray(xl.T)
        m = dict(shared)
        m.update(xT_nat=xTn, xT_perm=np.ascontiguousarray(xTn[:, pidx]), memT=np.ascontiguousarray(mem[b].T),
                 pos_perm=np.ascontiguousarray(pl[pidx][None, :]), flag=np.full((128, 1), float(hf), f32))
        in_maps.append(m)
    return in_maps


def kernel(**inputs):
    if "nc" not in _CACHE:
        _CACHE["nc"] = build(False)
    nc = _CACHE["nc"]
    in_maps = make_in_maps(**inputs)
    res = run_bass_kernel_spmd(nc, in_maps, core_ids=list(range(8)))
    out = np.empty((4, 4096, D), np.float32)
    for c in range(8):
        b, hf = c // 2, c % 2
        out[b, hf * 2048:(hf + 1) * 2048, :] = np.asarray(res.results[c]["outT"]).T
    return out
```

```python
import math
from contextlib import ExitStack, contextmanager
import numpy as np
import concourse.bass as bass
import concourse.mybir as mybir
from concourse.bass_utils import run_bass_kernel_spmd

F32 = mybir.dt.float32
BF16 = mybir.dt.bfloat16
I32 = mybir.dt.int32
AF = mybir.ActivationFunctionType
ALU = mybir.AluOpType

D = 2048
KC = 16
NL = 4096
TQ = 2050
QN = 2176
DFF = 5504
NJ = 43
ALPHA = 2.0 ** 0.25
LN_EPS = 1e-5
TCH = [(0, 2), (2, 514), (514, 1026), (1026, 1538), (1538, 2050)]
ENGS = ["tensor", "vector", "scalar", "gpsimd", "sync"]
PI = math.pi


class Buf:
    __slots__ = ("w", "r", "excl")

    def __init__(self, excl=False):
        self.w = None
        self.r = {}
        self.excl = excl


class Prog:
    def __init__(self, nc, es):
        self.nc = nc
        self.q = {e: [] for e in ENGS}
        self.cnt = {e: 0 for e in ENGS}
        self.sem = {e: es.enter_context(nc.semaphore("s_" + e)) for e in ENGS}
        self.grp = {e: None for e in ENGS}
        self.es = es
        self.dsems = []
        self.bar = {e: [] for e in ENGS}
        self.waited = {e: {} for e in ENGS}

    def _deps(self, reads, writes, extra):
        d = [x for x in extra if x is not None]
        if any(b.excl for b in reads):
            writes = list(writes) + [b for b in reads if b.excl]
            reads = [b for b in reads if not b.excl]
        for b in reads:
            if b.w is not None:
                d.append(b.w)
        for b in writes:
            if b.w is not None:
                d.append(b.w)
            d.extend(b.r.values())
        return d

    def _mark(self, tok, reads, writes):
        if any(b.excl for b in reads):
            writes = list(writes) + [b for b in reads if b.excl]
            reads = [b for b in reads if not b.excl]
        for b in reads:
            b.r[id(tok[0])] = tok
        for b in writes:
            b.w = tok
            b.r = {}

    def op(self, eng, fn, reads=(), writes=(), deps=()):
        d = self._deps(reads, writes, deps)
        if self.bar[eng]:
            d.extend(self.bar[eng])
            self.bar[eng] = []
        if eng == "tensor":
            d = [t for t in d if t[2] != "tensor"]
        if self.grp[eng] is not None:
            tok = self.grp[eng]
            d = [t for t in d if not (t[0] is tok[0] and t[1] == tok[1])]
            self.q[eng].append(["op", fn, d, False])
        else:
            self.cnt[eng] += 1
            tok = (self.sem[eng], self.cnt[eng], eng)
            self.q[eng].append(["op", fn, d, True])
        self._mark(tok, reads, writes)
        return tok

    @contextmanager
    def group(self, eng):
        tok = (self.sem[eng], self.cnt[eng] + 1, eng)
        self.grp[eng] = tok
        n0 = len(self.q[eng])
        yield tok
        self.grp[eng] = None
        assert len(self.q[eng]) > n0
        self.q[eng][-1][3] = True
        self.cnt[eng] += 1

    def _raw_dsem(self):
        s = [self.es.enter_context(self.nc.semaphore("d%d" % len(self.dsems))), 0]
        self.dsems.append(s)
        return s

    def new_dsem(self):
        return {}

    def dma(self, eng, out, in_, sem, reads=(), writes=(), deps=()):
        d = self._deps(reads, writes, deps)
        if self.bar[eng]:
            d.extend(self.bar[eng])
            self.bar[eng] = []
        if eng not in sem:
            sem[eng] = self._raw_dsem()
        sem = sem[eng]
        sem[1] += 16
        tok = (sem[0], sem[1], "dma")
        self.q[eng].append(["dma", (out, in_, sem[0]), d, True])
        self._mark(tok, reads, writes)
        return tok

    def barrier(self):
        toks = [(self.sem[e], self.cnt[e], "bar") for e in ENGS if self.cnt[e] > 0]
        toks += [(s[0], s[1], "dma") for s in self.dsems if s[1] > 0]
        for e in ENGS:
            self.bar[e] = list(toks)

    def flush(self, final=()):
        with self.nc.Block() as block:
            def mk(ename):
                def body(eng):
                    waited = self.waited[ename]
                    for kind, payload, deps, sig in self.q[ename]:
                        for (s, v, src) in deps:
                            key = id(s)
                            if waited.get(key, 0) >= v:
                                continue
                            eng.wait_ge(s, v)
                            waited[key] = v
                        if kind == "op":
                            ins = payload(eng)
                            if sig:
                                ins.then_inc(self.sem[ename], 1)
                        else:
                            out, in_, s = payload
                            eng.dma_start(out=out, in_=in_).then_inc(s, 16)
                    if ename == "sync":
                        for (s, v, src) in final:
                            eng.wait_ge(s, v)
                    self.q[ename] = []
                return body
            block.tensor(mk("tensor"))
            block.vector(mk("vector"))
            block.scalar(mk("scalar"))
            block.gpsimd(mk("gpsimd"))
            block.sync(mk("sync"))


def pipeline(n, stages):
    S = len(stages)
    ctx = [dict() for _ in range(n)]
    for t in range(n + S - 1):
        for s_ in reversed(range(S)):
            i = t - s_
            if 0 <= i < n:
                stages[s_](i, ctx[i])


def pipeline_gen(n, stages):
    S = len(stages)
    ctx = [dict() for _ in range(n)]
    for t in range(n + S - 1):
        for s_ in reversed(range(S)):
            i = t - s_
            if 0 <= i < n:
                stages[s_](i, ctx[i])
        yield t


class Ring:
    def __init__(self, P, tensors):
        self.t = tensors
        self.b = [Buf() for _ in tensors]
        self.s = [P.new_dsem() for _ in tensors]
        self.i = -1

    def next(self):
        self.i = (self.i + 1) % len(self.t)
        return self.t[self.i], self.b[self.i], self.s[self.i]


def build(debug=False, stop=None, ngla=4, ndil=8):
    nc = bass.Bass("TRN2", target_bir_lowering=False)
    din = lambda name, shape, dt=F32: nc.dram_tensor(name, list(shape), dt, kind="ExternalInput").ap()
    okind = "ExternalOutput" if debug else "Internal"
    dscr = lambda name, shape, dt: nc.dram_tensor(name, list(shape), dt, kind=okind).ap()
    xT_nat = din("xT_nat", [D, NL])
    xT_perm = din("xT_perm", [D, NL])
    memT = din("memT", [D, 256])
    pos_in = din("pos_perm", [1, NL], I32)
    flag_in = din("flag", [128, 1])
    masks_in = din("masks", [128, 6 * 128])
    uneg_in = din("uneg", [128, 128])
    rotc_in = din("rotc", [32, 2])
    wg_in = din("wg", [4, D, 768])
    wglr_in = din("wglr", [D, 16])
    w2aug_in = din("w2aug", [17, 512])
    glag_in = din("glag", [128, 8])
    wd_in = din("wd", [8, D, 320])
    wvd_in = din("wvd", [D, 1024])
    wout_in = din("w_out", [D, D])
    lnp_in = din("lnp", [128, 6 * 16])
    wq_in = din("ca_wq", [D, D])
    wkv_in = din("ca_wkv", [D, 2 * D])
    wo_in = din("ca_wo", [D, D])
    fwin_in = din("ffn_w_in", [D, 2 * DFF])
    cw_in = din("convw", [128, 86 * 3])
    cb_in = din("convb", [128, 86])
    fwout_in = din("ffn_w_out", [DFF, D])
    outT = nc.dram_tensor("outT", [D, 2048], F32, kind="ExternalOutput").ap()

    xb_nat = dscr("xb_nat", [D, NL], BF16)
    xb_perm = dscr("xb_perm", [D, NL], BF16)
    vd_scr = dscr("vd_scr", [NL, 1024], BF16)
    vd4_scr = dscr("vd4_scr", [NL, 1024], BF16)
    vd1_scr = dscr("vd1_scr", [NL, 1024], BF16)
    mixT = dscr("mixT", [D, TQ], BF16)
    x1T = dscr("x1T", [D, TQ], F32)
    ocT = dscr("ocT", [D, TQ], BF16)
    x2T = dscr("x2T", [D, TQ], F32)
    hT = dscr("hT", [DFF, 2048], BF16)
    w2b = dscr("w2b", [DFF, D], BF16)

    with ExitStack() as ges:
        P = Prog(nc, ges)
        gsb = lambda name, shape, dt: ges.enter_context(nc.sbuf_tensor(name, list(shape), dt))
        banks = [ges.enter_context(nc.psum_tensor("pb%d" % i, [128, 512], F32)) for i in range(7)]
        bankb = [Buf(True) for _ in range(7)]
        ptb = ges.enter_context(nc.psum_tensor("ptb", [128, 1024], BF16))
        ptb_b = Buf(True)
        bi = [0]

        def finish():
            final = [(s_[0], s_[1], "dma") for s_ in P.dsems if s_[1] > 0]
            final += [(P.sem[e_], P.cnt[e_], "bar") for e_ in ENGS if P.cnt[e_] > 0 and e_ != "sync"]
            P.flush(final)

        busy = set()

        def nb(reserve=False):
            for _ in range(8):
                bi[0] = (bi[0] + 1) % 7
                if bi[0] not in busy:
                    break
            else:
                raise RuntimeError("no free PSUM bank")
            if reserve:
                busy.add(bi[0])
            return banks[bi[0]], bankb[bi[0]]

        def rel(bb):
            busy.discard(bankb.index(bb))

        def mm(out, lhsT, rhs, start, stop, reads, writes):
            return P.op("tensor", lambda e: e.matmul(out, lhsT=lhsT, rhs=rhs, start=start, stop=stop), reads, writes)

        def act(out, in_, func, reads, writes, bias=None, scale=None):
            kw = {}
            if bias is not None:
                kw["bias"] = bias
            if scale is not None:
                kw["scale"] = scale
            return P.op("scalar", lambda e: e.activation(out=out, in_=in_, func=func, **kw), reads, writes)

        def tt(eng, out, in0, in1, op, reads, writes):
            return P.op(eng, lambda e: e.tensor_tensor(out=out, in0=in0, in1=in1, op=op), reads, writes)

        def ts(eng, out, in0, s1, s2, op0, op1, reads, writes):
            if op1 is None:
                return P.op(eng, lambda e: e.tensor_scalar(out=out, in0=in0, scalar1=s1, scalar2=None, op0=op0), reads, writes)
            return P.op(eng, lambda e: e.tensor_scalar(out=out, in0=in0, scalar1=s1, scalar2=s2, op0=op0, op1=op1), reads, writes)

        def stt(eng, out, in0, scalar, in1, op0, op1, reads, writes):
            return P.op(eng, lambda e: e.scalar_tensor_tensor(out=out, in0=in0, scalar=scalar, in1=in1, op0=op0, op1=op1), reads, writes)

        def cp(eng, out, in_, reads, writes):
            if eng == "scalar":
                return P.op(eng, lambda e: e.activation(out=out, in_=in_, func=AF.Copy), reads, writes)
            return P.op(eng, lambda e: e.tensor_copy(out=out, in_=in_), reads, writes)

        def recip(eng, out, in_, reads, writes):
            return P.op(eng, lambda e: e.reciprocal(out=out, in_=in_), reads, writes)

        def mset(eng, ap, val, writes):
            return P.op(eng, lambda e: e.memset(ap, val), (), writes)

        ident = gsb("ident", [128, 128], BF16)
        ones_bf = gsb("ones_bf", [128, 128], BF16)
        ones256 = gsb("ones256", [128, 128], F32)
        ones2048 = gsb("ones2048", [128, 128], F32)
        masks = gsb("masks_sb", [128, 6, 128], BF16)
        masksc = gsb("masksc_sb", [128, 6, 128], BF16)
        uneg = gsb("uneg_sb", [128, 128], F32)
        flag = gsb("flag_sb", [128, 1], F32)
        lnp = gsb("lnp_sb", [128, 6, 16], F32)
        glag = gsb("glag_sb", [128, 8], F32)
        cw = gsb("cw_sb", [128, 86, 3], F32)
        cb = gsb("cb_sb", [128, 86], F32)
        mtmp = gsb("mtmp", [128, 6, 128], F32)
        CB = Buf()
        csem = P.new_dsem()
        P.dma("sync", mtmp[:, :, :], masks_in.rearrange("p (a b) -> p a b", a=6), csem, (), [CB])
        P.dma("sync", uneg[:, :], uneg_in, csem, (), [CB])
        P.dma("sync", flag[:, :], flag_in, csem, (), [CB])
        P.dma("sync", lnp[:, :, :], lnp_in.rearrange("p (a b) -> p a b", a=6), csem, (), [CB])
        P.dma("sync", glag[:, :], glag_in, csem, (), [CB])
        P.dma("sync", cw[:, :, :], cw_in.rearrange("p (a b) -> p a b", b=3), csem, (), [CB])
        P.dma("sync", cb[:, :], cb_in, csem, (), [CB])
        mset("gpsimd", ident[:, :], 0.0, [CB])
        P.op("gpsimd", lambda e: e.affine_select(out=ident[:, :], in_=ident[:, :], pattern=[[-1, 128]],
                                                 compare_op=ALU.not_equal, fill=1.0, base=0, channel_multiplier=1), [CB], [CB])
        mset("gpsimd", ones_bf[:, :], 1.0, [CB])
        mset("gpsimd", ones256[:, :], 1.0 / 256.0, [CB])
        mset("gpsimd", ones2048[:, :], 1.0 / 2048.0, [CB])
        cp("vector", masks[:, :, :], mtmp[:, :, :], [CB], [CB])
        ts("vector", masksc[:, :, :], mtmp[:, :, :], flag[:, 0:1], None, ALU.mult, None, [CB], [CB])

        XN = [Buf() for _ in range(8)]
        XP = [Buf() for _ in range(8)]
        P.flush()
        if stop == "pre":
            finish()
            return nc

        MIXB = Buf()
        VDB = Buf()

        with ExitStack() as es:
            sb = lambda name, shape, dt: es.enter_context(nc.sbuf_tensor(name, list(shape), dt))
            xring = Ring(P, [sb("xblk%d" % i, [128, KC, 512], BF16) for i in range(2)])

            def load_x(src, srcbufs, blk):
                t, b, s = xring.next()
                P.dma("sync", t[:, :, :], src[:, blk * 512:(blk + 1) * 512].rearrange("(k p) n -> p k n", p=128), s, [srcbufs[blk]], [b])
                return t, b

            glrT = sb("glrT", [32, NL], F32)
            GLR = Buf()
            wglr = sb("wglr_sb", [128, KC, 16], BF16)
            w2aug = sb("w2aug_sb", [17, 512], F32)
            WS = Buf()
            wsem = P.new_dsem()
            P.dma("gpsimd", wglr[:, :, :], wglr_in.rearrange("(k p) n -> p k n", p=128), wsem, (), [WS])
            P.dma("sync", w2aug[:, :], w2aug_in, wsem, (), [WS])
            def load_first(ring, src32, dst16, dbufs, blk):
                t, b, s_ = ring.next()
                cols = slice(blk * 512, (blk + 1) * 512)
                P.dma("gpsimd", t[:, :, :], src32[:, cols].rearrange("(k p) n -> p k n", p=128), s_, (), [b])
                P.dma("sync", dst16[:, cols].rearrange("(k p) n -> p k n", p=128), t[:, :, :], s_, [b], [dbufs[blk]])
                return t, b
            mset("vector", glrT[:, :], 1.0, [GLR])
            for blk in range(8):
                xb, xbb = load_first(xring, xT_nat, xb_nat, XN, blk)
                bk, bkb = nb()
                with P.group("tensor"):
                    for kc in range(KC):
                        mm(bk[0:16, :], wglr[:, kc, :], xb[:, kc, :], kc == 0, kc == KC - 1, [xbb, WS], [bkb])
                cp("scalar", glrT[0:16, blk * 512:(blk + 1) * 512], bk[0:16, :], [bkb], [GLR])

            if stop == "glr":
                finish()
                return nc
            wgring = Ring(P, [sb("wg%d" % i, [128, KC, 768], BF16) for i in range(1)])
            qT = sb("g_qT", [128, QN], BF16)
            kT = sb("g_kT", [128, NL], BF16)
            rT = sb("g_rT", [128, 2, QN], BF16)
            vsb = sb("g_v", [128, 32, 256], BF16)
            enb = sb("g_enb", [128, NL], BF16)
            kef = sb("g_kef", [128, NL], BF16)
            ebq = sb("g_ebq", [128, QN], BF16)
            decay = sb("g_decay", [128, 32], F32)
            blast = sb("g_blast", [128, 32], F32)
            Sst = sb("g_S", [128, 256], F32)
            Sbf = [sb("g_Sbf%d" % i, [128, 256], BF16) for i in range(3)]
            og = sb("g_og", [128, 2, TQ], BF16)
            tA = [sb("g_tA%d" % i, [128, 128], F32) for i in range(4)]
            spb = [sb("g_sp%d" % i, [128, 128], F32) for i in range(4)]
            kin = [sb("g_kin%d" % i, [128, 128], BF16) for i in range(4)]
            kend = [sb("g_kend%d" % i, [128, 128], BF16) for i in range(4)]
            kendT = [sb("g_kendT%d" % i, [128, 128], BF16) for i in range(4)]
            qin = [sb("g_qin%d" % i, [128, 128], BF16) for i in range(4)]
            Am = [sb("g_Am%d" % i, [128, 128], BF16) for i in range(4)]
            osb = [sb("g_osb%d" % i, [128, 2, 128], F32) for i in range(4)]
            osq = [sb("g_osq%d" % i, [128, 2, 128], F32) for i in range(4)]
            mst = [sb("g_mst%d" % i, [128, 256], F32) for i in range(4)]
            t1 = [sb("g_t1%d" % i, [128, 128], F32) for i in range(4)]
            t2 = [sb("g_t2%d" % i, [128, 2, 128], F32) for i in range(4)]
            sr = [sb("g_sr%d" % i, [128, 2, 128], F32) for i in range(4)]
            on = [sb("g_on%d" % i, [128, 128], F32) for i in range(4)]
            on2 = [[sb("g_on2_%d_%d" % (i, e_), [128, 128], F32) for e_ in range(2)] for i in range(4)]
            TB2 = {"on": [[Buf(), Buf()] for _ in range(4)]}
            HB = {k: Buf() for k in ["q", "k", "r", "v", "gate", "S", "og", "bl"]}
            Sbfb = [Buf(), Buf(), Buf()]
            TB = {k: [Buf() for _ in range(4)] for k in ["tA", "sp", "kin", "kend", "kendT", "qin", "Am", "osb", "osq", "mst", "t1", "t2", "sr", "on"]}
            ogsem = P.new_dsem()
            LNS = math.log(128.0 ** -0.5)

            wg, wgb, wgs = wgring.next()
            P.dma("gpsimd", wg[:, :, :], wg_in[0].rearrange("(k p) n -> p k n", p=128), wgs, (), [wgb])
            for h in range(ngla):
                def g0(n, c):
                    c["bz"], c["bzb"] = nb(True)
                    mm(c["bz"][:, 0:128], glrT[0:17, n * 128:(n + 1) * 128], w2aug[0:17, h * 128:(h + 1) * 128], True, True, [GLR, WS], [c["bzb"]])

                def g1(n, c):
                    i4 = n % 4
                    act(tA[i4][:, :], c["bz"][:, 0:128], AF.Exp, [c["bzb"]], [TB["tA"][i4]], scale=-1.0)
                    act(spb[i4][:, :], tA[i4][:, :], AF.Ln, [TB["tA"][i4]], [TB["sp"][i4]], bias=1.0)
                    rel(c["bzb"])

                def g2(n, c):
                    i4 = n % 4
                    c["bt"], c["btb"] = nb(True)
                    mm(c["bt"][:, 0:128], spb[i4][:, :], uneg[:, :], True, True, [TB["sp"][i4], CB], [c["btb"]])

                def g3(n, c):
                    bt, btb = c["bt"], c["btb"]
                    cp("vector", blast[:, n:n + 1], bt[:, 127:128], [btb], [HB["bl"]])
                    act(enb[:, n * 128:(n + 1) * 128], bt[:, 0:128], AF.Exp, [btb], [HB["gate"]], scale=-1.0)
                    act(kef[:, n * 128:(n + 1) * 128], bt[:, 0:128], AF.Exp, [btb, HB["bl"]], [HB["gate"]], scale=-1.0, bias=blast[:, n:n + 1])
                    if n >= 15:
                        act(ebq[:, (n - 15) * 128:(n - 14) * 128], bt[:, 0:128], AF.Exp, [btb], [HB["gate"]], bias=LNS)
                    act(decay[:, n:n + 1], blast[:, n:n + 1], AF.Exp, [HB["bl"]], [HB["gate"]])
                    rel(btb)
                gsteps = pipeline_gen(32, [g0, g1, g2, g3])
                for blk in range(8):
                    for _ in range(5):
                        next(gsteps, None)
                    xb, xbb = load_x(xb_nat, XN, blk)
                    bk, bkb = nb()
                    with P.group("tensor"):
                        for kc in range(KC):
                            mm(bk[:, :], wg[:, kc, 128:256], xb[:, kc, :], kc == 0, kc == KC - 1, [xbb, wgb], [bkb])
                    cp("scalar", kT[:, blk * 512:(blk + 1) * 512], bk[:, :], [bkb], [HB["k"]])
                    if blk >= 3:
                        x0, nn_, q0 = (384, 128, 0) if blk == 3 else (0, 512, 128 + (blk - 4) * 512)
                        for (c0, dst, hb) in [(0, qT[:, q0:q0 + nn_], "q"), (256, rT[:, 0, q0:q0 + nn_], "r"), (384, rT[:, 1, q0:q0 + nn_], "r")]:
                            bq, bqb = nb()
                            with P.group("tensor"):
                                for kc in range(KC):
                                    mm(bq[:, 0:nn_], wg[:, kc, c0:c0 + 128], xb[:, kc, x0:x0 + nn_], kc == 0, kc == KC - 1, [xbb, wgb], [bqb])
                            cp("scalar" if hb == "q" else "vector", dst, bq[:, 0:nn_], [bqb], [HB[hb]])
                    for sub in range(4):
                        bv, bvb = nb()
                        with P.group("tensor"):
                            for kc in range(KC):
                                mm(bv[:, 0:256], xb[:, kc, sub * 128:(sub + 1) * 128], wg[:, kc, 512:768], kc == 0, kc == KC - 1, [xbb, wgb], [bvb])
                        cp("vector", vsb[:, blk * 4 + sub, :], bv[:, 0:256], [bvb], [HB["v"]])
                for _ in gsteps:
                    pass
                if h + 1 < ngla:
                    P.dma("gpsimd", wg[:, :, :], wg_in[h + 1].rearrange("(k p) n -> p k n", p=128), wgs, (), [wgb])
                act(rT[:, :, :], rT[:, :, :], AF.Silu, [], [HB["r"]])
                if stop == "proj":
                    finish()
                    return nc
                mset("vector", Sst[:, :], 0.0, [HB["S"]])
                mset("gpsimd", Sbf[2][:, :], 0.0, [Sbfb[2]])

                def c0_(n, c):
                    i4 = n % 4
                    tok = slice(n * 128, (n + 1) * 128)
                    tt("vector", kin[i4][:, :], kT[:, tok], enb[:, tok], ALU.mult, [HB["k"], HB["gate"]], [TB["kin"][i4]])
                    tt("gpsimd", kend[i4][:, :], kT[:, tok], kef[:, tok], ALU.mult, [HB["k"], HB["gate"]], [TB["kend"][i4]])
                    if n >= 15:
                        q0 = (n - 15) * 128
                        tt("vector", qin[i4][:, :], qT[:, q0:q0 + 128], ebq[:, q0:q0 + 128], ALU.mult, [HB["q"], HB["gate"]], [TB["qin"][i4]])

                def c1_(n, c):
                    i4 = n % 4
                    P.op("tensor", lambda e, i4=i4: e.transpose(out=ptb[:, i4 * 128:(i4 + 1) * 128], in_=kend[i4][:, :], identity=ident[:, :]),
                         [TB["kend"][i4], CB], [ptb_b])
                    if n >= 15:
                        c["ba"], c["bab"] = nb(True)
                        mm(c["ba"][:, 0:128], kin[i4][:, :], qin[i4][:, :], True, True, [TB["kin"][i4], TB["qin"][i4]], [c["bab"]])

                def c2_(n, c):
                    i4 = n % 4
                    cp("scalar", kendT[i4][:, :], ptb[:, i4 * 128:(i4 + 1) * 128], [ptb_b], [TB["kendT"][i4]])
                    if n >= 15:
                        q0 = (n - 15) * 128
                        tt("vector", Am[i4][:, :], c["ba"][:, 0:128], masks[:, 1, :], ALU.mult, [c["bab"], CB], [TB["Am"][i4]])
                        rel(c["bab"])

                def c3_(n, c):
                    i4 = n % 4
                    Sprev, Sprevb = Sbf[(n + 2) % 3], Sbfb[(n + 2) % 3]
                    if n >= 15:
                        c["bo"], c["bob"] = nb(True)
                        for e_ in range(2):
                            with P.group("tensor"):
                                mm(c["bo"][:, e_ * 128:(e_ + 1) * 128], vsb[:, n, e_ * 128:(e_ + 1) * 128], Am[i4][:, :], True, False, [HB["v"], TB["Am"][i4]], [c["bob"]])
                                mm(c["bo"][:, e_ * 128:(e_ + 1) * 128], Sprev[:, e_ * 128:(e_ + 1) * 128], qin[i4][:, :], False, True, [Sprevb, TB["qin"][i4]], [c["bob"]])
                    c["bs"], c["bsb"] = nb(True)
                    mm(c["bs"][:, 0:256], kendT[i4][:, :], vsb[:, n, :], True, True, [TB["kendT"][i4], HB["v"]], [c["bsb"]])

                def c4_(n, c):
                    i4 = n % 4
                    stt("vector", Sst[:, :], Sst[:, :], decay[:, n:n + 1], c["bs"][:, 0:256], ALU.mult, ALU.add, [c["bsb"], HB["gate"]], [HB["S"]])
                    cp("gpsimd", Sbf[n % 3][:, :], Sst[:, :], [HB["S"]], [Sbfb[n % 3]])
                    rel(c["bsb"])
                    if n >= 15:
                        cp("scalar", osb[i4][:, :, :], c["bo"][:, 0:256].rearrange("p (a b) -> p a b", a=2), [c["bob"]], [TB["osb"][i4]])
                        rel(c["bob"])

                def c5_(n, c):
                    i4 = n % 4
                    if n < 15:
                        return
                    tt("gpsimd", osq[i4][:, :, :], osb[i4][:, :, :], osb[i4][:, :, :], ALU.mult, [TB["osb"][i4]], [TB["osq"][i4]])
                    c["bm"], c["bmb"] = nb(True)
                    bm, bmb = c["bm"], c["bmb"]
                    with P.group("tensor"):
                        mm(bm[:, 0:128], ones256[:, :], osb[i4][:, 0, :], True, False, [CB, TB["osb"][i4]], [bmb])
                        mm(bm[:, 0:128], ones256[:, :], osb[i4][:, 1, :], False, True, [CB, TB["osb"][i4]], [bmb])
                    with P.group("tensor"):
                        mm(bm[:, 128:256], ones256[:, :], osq[i4][:, 0, :], True, False, [CB, TB["osq"][i4]], [bmb])
                        mm(bm[:, 128:256], ones256[:, :], osq[i4][:, 1, :], False, True, [CB, TB["osq"][i4]], [bmb])

                def c6_(n, c):
                    i4 = n % 4
                    if n < 15:
                        return
                    q0 = (n - 15) * 128
                    cp("scalar", mst[i4][:, :], c["bm"][:, 0:256], [c["bmb"]], [TB["mst"][i4]])
                    rel(c["bmb"])
                    tt("vector", t1[i4][:, :], mst[i4][:, 0:128], mst[i4][:, 0:128], ALU.mult, [TB["mst"][i4]], [TB["t1"][i4]])
                    tt("vector", t1[i4][:, :], mst[i4][:, 128:256], t1[i4][:, :], ALU.subtract, [TB["mst"][i4]], [TB["t1"][i4]])
                    act(t1[i4][:, :], t1[i4][:, :], AF.Ln, [], [TB["t1"][i4]], bias=LN_EPS)
                    act(t1[i4][:, :], t1[i4][:, :], AF.Exp, [], [TB["t1"][i4]], scale=-0.5)

                def c7_(n, c):
                    i4 = n % 4
                    if n < 15:
                        return
                    q0 = (n - 15) * 128
                    for e_ in range(2):
                        onb, ONB = on2[i4][e_], TB2["on"][i4][e_]
                        tt("vector", onb[:, :], osb[i4][:, e_, :], mst[i4][:, 0:128], ALU.subtract, [TB["osb"][i4], TB["mst"][i4]], [ONB])
                        tt("gpsimd", onb[:, :], onb[:, :], t1[i4][:, :], ALU.mult, [TB["t1"][i4]], [ONB])
                        if n == 15:
                            stt("vector", og[:, e_, 0:2], onb[:, 126:128], glag[:, 2 * h + e_:2 * h + e_ + 1], rT[:, e_, q0 + 126:q0 + 128],
                                ALU.mult, ALU.mult, [ONB, HB["r"], CB], [HB["og"]])
                        else:
                            o0 = 2 + (n - 16) * 128
                            stt("vector", og[:, e_, o0:o0 + 128], onb[:, :], glag[:, 2 * h + e_:2 * h + e_ + 1], rT[:, e_, q0:q0 + 128],
                                ALU.mult, ALU.mult, [ONB, HB["r"], CB], [HB["og"]])
                pipeline(32, [c0_, c1_, c2_, c3_, c4_, c5_, c6_, c7_])
                for e_ in range(2):
                    P.dma("sync", mixT[(2 * h + e_) * 128:(2 * h + e_ + 1) * 128, :], og[:, e_, :], ogsem, [HB["og"]], [MIXB])
            P.flush()

        P.barrier()
        if stop == "A":
            finish()
            return nc
        with ExitStack() as es:
            sb = lambda name, shape, dt: es.enter_context(nc.sbuf_tensor(name, list(shape), dt))
            xring = Ring(P, [sb("xblkb%d" % i, [128, KC, 512], BF16) for i in range(2)])

            def load_xp(blk):
                t, b, s = xring.next()
                P.dma("sync", t[:, :, :], xb_perm[:, blk * 512:(blk + 1) * 512].rearrange("(k p) n -> p k n", p=128), s, [XP[blk]], [b])
                return t, b

            cosT = sb("cosT", [32, NL], F32)
            sinT = sb("sinT", [32, NL], F32)
            ROT = Buf()
            with ExitStack() as es2:
                sb2 = lambda name, shape, dt: es2.enter_context(nc.sbuf_tensor(name, list(shape), dt))
                posi = sb2("posi", [32, NL], I32)
                ang = sb2("ang", [32, NL], F32)
                tf = sb2("tf", [32, NL], F32)
                rr = sb2("rr", [32, NL], F32)
                mk_ = sb2("mk_", [32, NL], F32)
                rotc = sb2("rotc_sb", [32, 2], F32)
                RB = Buf()
                rsem = P.new_dsem()
                P.dma("sync", posi[:, :], pos_in.partition_broadcast(32), rsem, (), [RB])
                P.dma("sync", rotc[:, :], rotc_in, rsem, (), [RB])
                defer = []

                def DF(fn, *a_, **k_):
                    defer.append(lambda: fn(*a_, **k_))
                DF(cp, "vector", ang[:, :], posi[:, :], [RB], [RB])
                DF(ts, "vector", ang[:, :], ang[:, :], rotc[:, 0:1], None, ALU.mult, None, [RB], [RB])
                DF(ts, "vector", tf[:, :], ang[:, :], 1.0 / (2 * PI), 0.5, ALU.mult, ALU.add, [RB], [RB])
                DF(cp, "vector", posi[:, :], tf[:, :], [RB], [RB])
                DF(cp, "vector", tf[:, :], posi[:, :], [RB], [RB])
                C1 = 6.28125
                C2 = 2 * PI - C1
                DF(stt, "vector", rr[:, :], tf[:, :], -C1, ang[:, :], ALU.mult, ALU.add, [RB], [RB])
                DF(stt, "vector", rr[:, :], tf[:, :], -C2, rr[:, :], ALU.mult, ALU.add, [RB], [RB])

                def wrap_clamp(r):
                    DF(ts, "vector", mk_[:, :], r[:, :], -PI, None, ALU.is_lt, None, [RB], [RB])
                    DF(stt, "vector", r[:, :], mk_[:, :], 2 * PI, r[:, :], ALU.mult, ALU.add, [RB], [RB])
                    DF(ts, "vector", mk_[:, :], r[:, :], PI, None, ALU.is_gt, None, [RB], [RB])
                    DF(stt, "vector", r[:, :], mk_[:, :], -2 * PI, r[:, :], ALU.mult, ALU.add, [RB], [RB])
                    DF(ts, "vector", r[:, :], r[:, :], -3.141592, 3.141592, ALU.max, ALU.min, [RB], [RB])
                wrap_clamp(rr)
                DF(act, sinT[:, :], rr[:, :], AF.Sin, [RB], [ROT], scale=rotc[:, 1:2])
                DF(ts, "vector", rr[:, :], rr[:, :], PI / 2, None, ALU.add, None, [RB], [RB])
                wrap_clamp(rr)
                DF(act, cosT[:, :], rr[:, :], AF.Sin, [RB], [ROT])

                wvd = sb2("wvd_sb", [128, KC, 1024], BF16)
                WV = Buf()
                wvs = P.new_dsem()
                for g in range(2):
                    P.dma("gpsimd", wvd[:, :, g * 512:(g + 1) * 512], wvd_in[:, g * 512:(g + 1) * 512].rearrange("(k p) n -> p k n", p=128), wvs, (), [WV])
                vst = Ring(P, [sb2("vst%d" % i, [128, 1024], BF16) for i in range(2)])
                for blk in range(8):
                    t_, b_, s__ = xring.next()
                    cols_ = slice(blk * 512, (blk + 1) * 512)
                    P.dma("gpsimd", t_[:, :, :], xT_perm[:, cols_].rearrange("(k p) n -> p k n", p=128), s__, (), [b_])
                    P.dma("sync", xb_perm[:, cols_].rearrange("(k p) n -> p k n", p=128), t_[:, :, :], s__, [b_], [XP[blk]])
                    xb, xbb = t_, b_
                    for sub in range(4):
                        if defer:
                            defer.pop(0)()
                        vt, vtb, vts = vst.next()
                        for g in range(2):
                            bv, bvb = nb()
                            with P.group("tensor"):
                                for kc in range(KC):
                                    mm(bv[:, :], xb[:, kc, sub * 128:(sub + 1) * 128], wvd[:, kc, g * 512:(g + 1) * 512], kc == 0, kc == KC - 1, [xbb, WV], [bvb])
                            cp("scalar" if g == 0 else "vector", vt[:, g * 512:(g + 1) * 512], bv[:, :], [bvb], [vtb])
                        r0 = (blk * 4 + sub) * 128
                        P.dma("scalar", vd_scr[r0:r0 + 128, :], vt[:, :], vts, [vtb], [VDB])
                        T_ = blk * 4 + sub
                        s_, r16 = T_ // 16, T_ % 16
                        a_, r4 = r16 // 4, r16 % 4
                        d4 = vd4_scr.rearrange("(t i) c -> t i c", i=128)[r4 * 8 + 4 * s_:r4 * 8 + 4 * s_ + 4, 32 * a_:32 * a_ + 32, :]
                        P.dma("scalar", d4, vt[:, :], vts, [vtb], [VDB])
                        d1 = vd1_scr.rearrange("(t i) c -> t i c", i=128)[16 * s_:16 * s_ + 16, 8 * r16:8 * r16 + 8, :]
                        P.dma("scalar", d1, vt[:, :], vts, [vtb], [VDB])
                while defer:
                    defer.pop(0)()
                P.flush()
            P.barrier()
            if stop == "V":
                finish()
                return nc

            wdring = Ring(P, [sb("wd%d" % i, [128, KC, 320], BF16) for i in range(2)])
            dq2 = [sb("d_qq%d" % i, [128, 2560], BF16) for i in range(2)]
            dk2 = [sb("d_kk%d" % i, [128, NL], BF16) for i in range(2)]
            DBQ, DBK = [Buf(), Buf()], [Buf(), Buf()]
            dk4 = sb("d_k4", [128, NL], BF16)
            dk1 = sb("d_k1", [128, NL], BF16)
            dq4 = sb("d_q4", [128, 2048], BF16)
            dq1 = sb("d_q1", [128, 2048], BF16)
            hq1 = sb("d_hq1", [128, 16], BF16)
            V16 = sb("d_v16", [128, 32, 128], BF16)
            V4 = sb("d_v4", [128, 32, 128], BF16)
            V1 = sb("d_v1", [128, 32, 128], BF16)
            acc = sb("d_acc", [128, 2560], F32)
            dacc = sb("d_dacc", [128, 2560], F32)
            odb = sb("d_od", [128, TQ], BF16)
            rt1 = [sb("d_rt1%d" % i, [32, 512], F32) for i in range(2)]
            rt2 = [sb("d_rt2%d" % i, [32, 512], F32) for i in range(2)]
            Pm = [sb("d_P%d" % i, [128, 2, 128], BF16) for i in range(4)]
            DB = {k: Buf() for k in ["q", "k", "v", "acc", "od", "k4", "k1", "q4", "q1"]}
            DBV = {16: Buf(), 4: Buf(), 1: Buf()}
            RTB = [[Buf(), Buf()], [Buf(), Buf()]]
            PmB = [Buf() for _ in range(4)]
            vsem = P.new_dsem()
            odsem = P.new_dsem()
            SC = 128.0 ** -0.5
            vd5 = vd_scr

            def kcols(t, off, kind, a, b_):
                if kind == 16:
                    p0 = 2048 * a + 128 * b_ - off
                    return t[:, p0:p0 + 128]
                if kind == 4:
                    r4, n = a, b_
                    s_, m = n // 4, n % 4
                    base = 2048 * s_ - off
                    return t[:, base:base + 2048].rearrange("p (a r u) -> p a r u", a=4, r=4, u=128)[:, :, r4, 32 * m:32 * m + 32]
                s_, m = a, b_
                base = 2048 * s_ - off
                return t[:, base:base + 2048].rearrange("p (r m u) -> p r m u", r=16, m=16, u=8)[:, :, m, :]

            def head_setup(h):
                wd, wdb, wds = wdring.next()
                P.dma("gpsimd", wd[:, :, :], wd_in[h].rearrange("(k p) n -> p k n", p=128), wds, (), [wdb])
                return wd, wdb

            def proj_gen(h, wd, wdb):
                dq, dk = dq2[h % 2], dk2[h % 2]
                DBq = {"q": DBQ[h % 2], "k": DBK[h % 2]}
                for blk in range(8):
                    xb, xbb = load_xp(blk)
                    cols = slice(blk * 512, (blk + 1) * 512)
                    todo = [(128, 288, dk[:, cols], "k")]
                    if blk >= 3:
                        todo.append((0, 256, dq[:, (blk - 3) * 512:(blk - 2) * 512], "q"))
                    for ti, (c0, cs, dst, hb) in enumerate(todo):
                        bk, bkb = nb()
                        with P.group("tensor"):
                            for kc in range(KC):
                                mm(bk[:, :], wd[:, kc, c0:c0 + 128], xb[:, kc, :], kc == 0, kc == KC - 1, [xbb, wdb], [bkb])
                        bs_, bsb_ = nb()
                        with P.group("tensor"):
                            for kc in range(KC):
                                mm(bs_[0:32, :], wd[:, kc, cs:cs + 32], xb[:, kc, :], kc == 0, kc == KC - 1, [xbb, wdb], [bsb_])
                        cp("scalar", dst, bk[:, :], [bkb], [DBq[hb]])
                        tt("vector", rt1[ti][:, :], bk[0:32, :], cosT[:, cols], ALU.mult, [bkb, ROT], [RTB[ti][0]])
                        tt("vector", rt2[ti][:, :], bs_[0:32, :], sinT[:, cols], ALU.mult, [bsb_, ROT], [RTB[ti][1]])
                        P.op("vector", lambda e, dst=dst, ti=ti: e.tensor_tensor(out=dst[0:32], in0=rt1[ti][:, :], in1=rt2[ti][:, :], op=ALU.add),
                             [RTB[ti][0], RTB[ti][1]], [DBq[hb]])
                    yield blk

            vdone = set()
            vsems = {16: P.new_dsem(), 4: P.new_dsem(), 1: P.new_dsem()}

            def vload(h, kind_):
                if (h, kind_) in vdone:
                    return
                vdone.add((h, kind_))
                hc = slice(h * 128, (h + 1) * 128)
                dst, src = {16: (V16, vd5), 4: (V4, vd4_scr), 1: (V1, vd1_scr)}[kind_]
                P.dma("gpsimd", dst[:, :, :], src[:, hc].rearrange("(t p) c -> p t c", p=128), vsems[kind_], [VDB], [DBV[kind_]])

            def post_proj(h):
                dq, dk = dq2[h % 2], dk2[h % 2]
                DBq = {"q": DBQ[h % 2], "k": DBK[h % 2]}
                for kind_ in (16, 4, 1):
                    vload(h, kind_)
                for s_ in range(2):
                    srck = dk[:, 2048 * s_:2048 * s_ + 2048]
                    for r4 in range(4):
                        P.op("vector" if r4 % 2 == 0 else "gpsimd", lambda e, s_=s_, r4=r4, srck=srck: e.tensor_copy(
                            out=dk4[:, (r4 * 8 + s_ * 4) * 128:(r4 * 8 + s_ * 4 + 4) * 128].rearrange("p (m a u) -> p m a u", m=4, a=4, u=32),
                            in_=srck.rearrange("p (a r m u) -> p r m a u", a=4, r=4, m=4, u=32)[:, r4]), [DBq["k"]], [DB["k4"]])
                    P.op("vector", lambda e, s_=s_, srck=srck: e.tensor_copy(
                        out=dk1[:, 2048 * s_:2048 * s_ + 2048].rearrange("p (m r u) -> p m r u", m=16, r=16, u=8),
                        in_=srck.rearrange("p (r m u) -> p m r u", r=16, m=16, u=8)), [DBq["k"]], [DB["k1"]])
                srcq = dq[:, 512:2560]
                for r4 in range(4):
                    P.op("scalar", lambda e, r4=r4: e.activation(
                        out=dq4[:, r4 * 512:(r4 + 1) * 512].rearrange("p (m a u) -> p m a u", m=4, a=4, u=32),
                        in_=srcq.rearrange("p (a r m u) -> p r m a u", a=4, r=4, m=4, u=32)[:, r4], func=AF.Copy), [DBq["q"]], [DB["q4"]])
                P.op("scalar", lambda e: e.activation(
                    out=dq1[:, :].rearrange("p (m r u) -> p m r u", m=16, r=16, u=8),
                    in_=srcq.rearrange("p (r m u) -> p m r u", r=16, m=16, u=8), func=AF.Copy), [DBq["q"]], [DB["q1"]])
                P.op("scalar", lambda e: e.activation(
                    out=hq1[:, :].rearrange("p (r u) -> p r u", r=2),
                    in_=dq[:, 256:512].rearrange("p (r u) -> p r u", r=2)[:, :, 120:128], func=AF.Copy), [DBq["q"]], [DB["q1"]])


            def att_gen(h):
                dq, dk = dq2[h % 2], dk2[h % 2]
                DBq = {"q": DBQ[h % 2], "k": DBK[h % 2]}

                def kap(kb):
                    kind, a_, b_ = kb
                    if kind == 16:
                        p0 = 2048 * a_ + 128 * b_
                        return dk[:, p0:p0 + 128], DBq["k"]
                    if kind == 4:
                        p0 = (a_ * 8 + b_) * 128
                        return dk4[:, p0:p0 + 128], DB["k4"]
                    p0 = (16 * a_ + b_) * 128
                    return dk1[:, p0:p0 + 128], DB["k1"]
                mset("gpsimd", acc[:, :], 0.0, [DB["acc"]])
                mset("gpsimd", dacc[:, :], 0.0, [DB["acc"]])
                blocks = []
                for r in range(16):
                    blocks.append((16, (kcols(dq, 1536, 16, 1, r), DBq["q"]), kcols(acc, 1536, 16, 1, r), kcols(dacc, 1536, 16, 1, r), 128, None,
                                   [((16, 0, r), V16[:, r, :], masksc[:, 0, :]), ((16, 1, r), V16[:, 16 + r, :], masks[:, 1, :])]))
                for r in (14, 15):
                    blocks.append((16, (kcols(dq, 1536, 16, 0, r), DBq["q"]), kcols(acc, 1536, 16, 0, r), kcols(dacc, 1536, 16, 0, r), 128, None,
                                   [((16, 0, r), V16[:, r, :], masksc[:, 1, :])]))
                for r4 in range(4):
                    for n in range(4, 8):
                        pm = masksc[:, 2, :] if n == 4 else masks[:, 2, :]
                        blocks.append((4, (dq4[:, (r4 * 4 + n - 4) * 128:(r4 * 4 + n - 3) * 128], DB["q4"]), kcols(acc, 1536, 4, r4, n), kcols(dacc, 1536, 4, r4, n), 128, [4, 32],
                                       [((4, r4, n - 1), V4[:, r4 * 8 + n - 1, :], pm), ((4, r4, n), V4[:, r4 * 8 + n, :], masks[:, 3, :])]))
                for r4 in (2, 3):
                    p0 = (12 + r4) * 128 + 96 - 1536
                    blocks.append((4, (dq[:, p0:p0 + 32], DBq["q"]), acc[:, p0:p0 + 32], dacc[:, p0:p0 + 32], 32, None,
                                   [((4, r4, 2), V4[:, r4 * 8 + 2, :], masksc[:, 2, 96:128]), ((4, r4, 3), V4[:, r4 * 8 + 3, :], masksc[:, 3, 96:128])]))
                for m in range(16):
                    pk = (1, 0, 15) if m == 0 else (1, 1, m - 1)
                    pm = masksc[:, 4, :] if m == 0 else masks[:, 4, :]
                    blocks.append((1, (dq1[:, m * 128:(m + 1) * 128], DB["q1"]), kcols(acc, 1536, 1, 1, m), kcols(dacc, 1536, 1, 1, m), 128, [16, 8],
                                   [(pk, V1[:, 16 * pk[1] + pk[2], :], pm), ((1, 1, m), V1[:, 16 + m, :], masks[:, 5, :])]))
                hq = lambda t: t[:, 14 * 128 - 1536:16 * 128 - 1536].rearrange("p (r u) -> p r u", r=2)[:, :, 120:128]
                blocks.append((1, (hq1[:, :], DB["q1"]), hq(acc), hq(dacc), 16, [2, 8],
                               [((1, 0, 14), V1[:, 14, :], masksc[:, 4, 112:128]), ((1, 0, 15), V1[:, 15, :], masksc[:, 5, 112:128])]))
                def a0(i, c):
                    kind, (qap, qbuf), accap, daccap, nq, qshape, keys = blocks[i]
                    c["bsc"], c["bscb"] = nb(True)
                    for ki, (kb, vt, mk) in enumerate(keys):
                        ka, kbuf = kap(kb)
                        mm(c["bsc"][:, ki * 128:ki * 128 + nq], ka, qap, True, True, [kbuf, qbuf], [c["bscb"]])

                def a1(i, c):
                    kind, (qap, qbuf), accap, daccap, nq, qshape, keys = blocks[i]
                    pi = i % 4
                    nk = len(keys)
                    act(Pm[pi][:, 0:nk, 0:nq], c["bsc"][:, 0:nk * 128].rearrange("p (a b) -> p a b", a=nk)[:, :, 0:nq], AF.Exp, [c["bscb"]], [PmB[pi]], scale=SC)
                    rel(c["bscb"])
                    for ki, (kb, vt, mk) in enumerate(keys):
                        tt("gpsimd", Pm[pi][:, ki, 0:nq], Pm[pi][:, ki, 0:nq], mk, ALU.mult, [CB], [PmB[pi]])

                def a2(i, c):
                    kind, (qap, qbuf), accap, daccap, nq, qshape, keys = blocks[i]
                    pi = i % 4
                    nk = len(keys)
                    c["bo"], c["bob"] = nb(True)
                    with P.group("tensor"):
                        for ki, (kb, vt, mk) in enumerate(keys):
                            mm(c["bo"][:, 0:nq], vt, Pm[pi][:, ki, 0:nq], ki == 0, ki == nk - 1, [DBV[kind], PmB[pi]], [c["bob"]])
                    with P.group("tensor"):
                        for ki, (kb, vt, mk) in enumerate(keys):
                            mm(c["bo"][:, 128:128 + nq], ones_bf[:, :], Pm[pi][:, ki, 0:nq], ki == 0, ki == nk - 1, [CB, PmB[pi]], [c["bob"]])

                def a3(i, c):
                    kind, (qap, qbuf), accap, daccap, nq, qshape, keys = blocks[i]
                    o_in = c["bo"][:, 0:nq]
                    d_in = c["bo"][:, 128:128 + nq]
                    if qshape is not None:
                        o_in = o_in.rearrange("p (a b) -> p a b", a=qshape[0])
                        d_in = d_in.rearrange("p (a b) -> p a b", a=qshape[0])
                    tt("vector", accap, accap, o_in, ALU.add, [c["bob"]], [DB["acc"]])
                    tt("vector", daccap, daccap, d_in, ALU.add, [c["bob"]], [DB["acc"]])
                    rel(c["bob"])
                for t_ in pipeline_gen(len(blocks), [a0, a1, a2, a3]):
                    yield t_
                ts("vector", dacc[:, 256:2560], dacc[:, 256:2560], 1e-30, None, ALU.max, None, [], [DB["acc"]])
                act(dacc[:, 256:2560], dacc[:, 256:2560], AF.Ln, [], [DB["acc"]])
                act(dacc[:, 256:2560], dacc[:, 256:2560], AF.Exp, [], [DB["acc"]], scale=-1.0)
                P.op("vector", lambda e: e.tensor_tensor(out=odb[:, 2:TQ].rearrange("p (u r) -> p r u", r=16),
                                                         in0=acc[:, 512:2560].rearrange("p (r u) -> p r u", r=16),
                                                         in1=dacc[:, 512:2560].rearrange("p (r u) -> p r u", r=16), op=ALU.mult), [DB["acc"]], [DB["od"]])
                tt("vector", odb[:, 0:1], acc[:, 383:384], dacc[:, 383:384], ALU.mult, [DB["acc"]], [DB["od"]])
                tt("vector", odb[:, 1:2], acc[:, 511:512], dacc[:, 511:512], ALU.mult, [DB["acc"]], [DB["od"]])
                P.dma("sync", mixT[(8 + h) * 128:(9 + h) * 128, :], odb[:, :], odsem, [DB["od"]], [MIXB])
                yield -1

            prev_att = None
            for h in range(ndil):
                wd, wdb = head_setup(h)
                pg = proj_gen(h, wd, wdb)
                nst = 0
                for _blk in pg:
                    if prev_att is not None:
                        for _ in range(8):
                            if next(prev_att, None) is None:
                                break
                            nst += 1
                            if nst == 23:
                                vload(h, 16)
                            if nst == 41:
                                vload(h, 4)
                if prev_att is not None:
                    for _ in prev_att:
                        pass
                post_proj(h)
                prev_att = att_gen(h)
            for _ in prev_att:
                pass
            P.flush()
        P.barrier()

        def gemm_ln_phase(tag, w_dram, a_dram, a_cast, resid_dram, lnidx, out_dram, out_col0, chunks, ABUF, RBUF, OBUF):
            with ExitStack() as es:
                sb = lambda name, shape, dt: es.enter_context(nc.sbuf_tensor(tag + name, list(shape), dt))
                W = sb("W", [128, KC, D], BF16)
                WB = [Buf() for _ in range(4)]
                for g in range(4):
                    P.dma("gpsimd", W[:, :, g * 512:(g + 1) * 512], w_dram[:, g * 512:(g + 1) * 512].rearrange("(k p) n -> p k n", p=128), P.new_dsem(), (), [WB[g]])
                aring = Ring(P, [sb("a%d" % i, [128, KC, 512], BF16) for i in range(2)])
                ln_chunks(tag, sb, chunks, KC,
                          lambda dc, kc: W[:, kc, dc * 128:(dc + 1) * 128], WB,
                          a_dram, a_cast, aring, ABUF, resid_dram, RBUF, lnidx, out_dram, out_col0, OBUF)
                P.flush()
            P.barrier()

        def ln_chunks(tag, sb, chunks, nk, wfn, wbufs, a_dram, a_cast, aring, ABUF, resid_dram, RBUF, lnidx, out_dram, out_col0, OBUF, wstream=None):
            ny = 2
            y = [sb("y%d" % i, [128, KC, 512], F32) for i in range(ny)]
            YB = [Buf() for _ in range(ny)]
            s1 = [sb("s1_%d" % i, [128, 512], F32) for i in range(2)]
            s2 = [sb("s2_%d" % i, [128, 512], F32) for i in range(2)]
            SB1, SB2 = [Buf(), Buf()], [Buf(), Buf()]
            ysq = [sb("ysq%d" % i, [128, 512], F32) for i in range(2)]
            YSQ = [Buf(), Buf()]
            rres = Ring(P, [sb("res%d" % i, [128, 512], F32) for i in range(3)])
            mean = sb("mean", [128, 512], F32)
            rstd = sb("rstd", [128, 512], F32)
            MB = Buf()
            tn = [sb("tn%d" % i, [128, 512], F32) for i in range(2)]
            TN = [Buf(), Buf()]
            oring = Ring(P, [sb("o%d" % i, [128, 512], F32) for i in range(3)])
            pend = []
            for ci, (c0, c1) in enumerate(chunks):
                n = c1 - c0
                yi = ci % ny
                yc, ycb, s1c, s2c, S1B, S2B = y[yi], YB[yi], s1[ci % 2], s2[ci % 2], SB1[ci % 2], SB2[ci % 2]
                if ci == 0:
                    nxt = aring.next()
                    P.dma("gpsimd" if a_cast else "sync", nxt[0][:, 0:nk, 0:n], a_dram[:, c0:c1].rearrange("(k p) n -> p k n", p=128), nxt[2], [ABUF], [nxt[1]])
                a, ab, asem = nxt
                if ci + 1 < len(chunks):
                    d0, d1 = chunks[ci + 1]
                    nxt = aring.next()
                    P.dma("gpsimd" if a_cast else "sync", nxt[0][:, 0:nk, 0:d1 - d0], a_dram[:, d0:d1].rearrange("(k p) n -> p k n", p=128), nxt[2], [ABUF], [nxt[1]])

                def epi(dc, bk, bkb):
                    rt, rtb, rsem_ = rres.next()
                    P.dma("sync", rt[:, 0:n], resid_dram(dc, c0, c1), rsem_, [RBUF], [rtb])
                    stt("vector", yc[:, dc, 0:n], rt[:, 0:n], ALPHA, bk[:, 0:n], ALU.mult, ALU.add, [rtb, bkb], [ycb])
                    i2 = dc % 2
                    if dc == 0:
                        cp("vector", s1c[:, 0:n], yc[:, dc, 0:n], [ycb], [S1B])
                        act(s2c[:, 0:n], yc[:, dc, 0:n], AF.Square, [ycb], [S2B])
                    else:
                        tt("vector", s1c[:, 0:n], s1c[:, 0:n], yc[:, dc, 0:n], ALU.add, [ycb], [S1B])
                        act(ysq[i2][:, 0:n], yc[:, dc, 0:n], AF.Square, [ycb], [YSQ[i2]])
                        tt("gpsimd", s2c[:, 0:n], s2c[:, 0:n], ysq[i2][:, 0:n], ALU.add, [YSQ[i2]], [S2B])

                if wstream is None:
                    for dc in range(KC):
                        bk, bkb = nb()
                        with P.group("tensor"):
                            for kc in range(nk):
                                mm(bk[:, 0:n], wfn(dc, kc), a[:, kc, 0:n], kc == 0, kc == nk - 1, [ab, wbufs[dc // 4]], [bkb])
                        epi(dc, bk, bkb)
                else:
                    for qd_ in range(4):
                        acc4 = [nb(True) for _ in range(4)]
                        for j in range(nk):
                            wt, wtb = wstream(j, qd_)
                            with P.group("tensor"):
                                for i_ in range(4):
                                    mm(acc4[i_][0][:, 0:n], wt[:, i_ * 128:(i_ + 1) * 128], a[:, j, 0:n], j == 0, j == nk - 1, [ab, wtb], [acc4[i_][1]])
                        for i_ in range(4):
                            epi(4 * qd_ + i_, acc4[i_][0], acc4[i_][1])
                            rel(acc4[i_][1])
                def part2(n=n, c0=c0, c1=c1, yc=yc, ycb=ycb, s1c=s1c, s2c=s2c, S1B=S1B, S2B=S2B):
                    bm, bmb = nb()
                    mm(bm[:, 0:n], ones2048[:, :], s1c[:, 0:n], True, True, [CB, S1B], [bmb])
                    bm2, bm2b = nb()
                    mm(bm2[:, 0:n], ones2048[:, :], s2c[:, 0:n], True, True, [CB, S2B], [bm2b])
                    cp("scalar", mean[:, 0:n], bm[:, 0:n], [bmb], [MB])
                    tt("vector", rstd[:, 0:n], mean[:, 0:n], mean[:, 0:n], ALU.mult, [MB], [MB])
                    tt("vector", rstd[:, 0:n], bm2[:, 0:n], rstd[:, 0:n], ALU.subtract, [bm2b], [MB])
                    act(rstd[:, 0:n], rstd[:, 0:n], AF.Ln, [], [MB], bias=LN_EPS)
                    act(rstd[:, 0:n], rstd[:, 0:n], AF.Exp, [], [MB], scale=-0.5)
                    for dc in range(KC):
                        i2 = dc % 2
                        tt("vector", tn[i2][:, 0:n], yc[:, dc, 0:n], mean[:, 0:n], ALU.subtract, [ycb, MB], [TN[i2]])
                        tt("gpsimd", tn[i2][:, 0:n], tn[i2][:, 0:n], rstd[:, 0:n], ALU.mult, [MB], [TN[i2]])
                        ot, otb, osem_ = oring.next()
                        act(ot[:, 0:n], tn[i2][:, 0:n], AF.Identity, [TN[i2], CB], [otb], scale=lnp[:, 2 * lnidx, dc:dc + 1], bias=lnp[:, 2 * lnidx + 1, dc:dc + 1])
                        P.dma("scalar", out_dram[dc * 128:(dc + 1) * 128, c0 - out_col0:c1 - out_col0], ot[:, 0:n], osem_, [otb], [OBUF])
                if pend:
                    pend.pop()()
                pend.append(part2)
            pend.pop()()

        if stop == "B":
            finish()
            return nc
        X1B, OCB, X2B, HTB, OUTB = Buf(), Buf(), Buf(), Buf(), Buf()
        W2B = Buf()
        w2sem = P.new_dsem()
        NOB = Buf()
        xres = lambda dc, c0, c1: xT_nat[dc * 128:(dc + 1) * 128, 2046 + c0:2046 + c1]
        gemm_ln_phase("C", wout_in, mixT, False, xres, 0, x1T, 0, TCH, MIXB, NOB, X1B)
        if stop == "C":
            finish()
            return nc

        with ExitStack() as es:
            sb = lambda name, shape, dt: es.enter_context(nc.sbuf_tensor("D1" + name, list(shape), dt))
            mkT = sb("mkT", [128, 16, 256], BF16)
            mv = sb("mv", [128, 2, D], BF16)
            MKB, MVB, MTB = Buf(), Buf(), Buf()
            Wq = sb("Wq", [128, KC, D], BF16)
            WQB = Buf()
            wqs = P.new_dsem()
            es2 = ExitStack()
            sb2 = lambda name, shape, dt: es2.enter_context(nc.sbuf_tensor("D0" + name, list(shape), dt))
            mT = sb2("memT", [128, KC, 256], BF16)
            P.dma("gpsimd", mT[:, :, :], memT.rearrange("(k p) n -> p k n", p=128), P.new_dsem(), (), [MTB])
            wkvr = Ring(P, [sb2("wkv%d" % i, [128, KC, 512], BF16) for i in range(2)])
            for g in range(8):
                wt, wtb, wts = wkvr.next()
                P.dma("gpsimd", wt[:, :, :], wkv_in[:, g * 512:(g + 1) * 512].rearrange("(k p) n -> p k n", p=128), wts, (), [wtb])
                if g < 4:
                    for j in range(4):
                        bk, bkb = nb()
                        with P.group("tensor"):
                            for kc in range(KC):
                                mm(bk[:, 0:256], wt[:, kc, j * 128:(j + 1) * 128], mT[:, kc, :], kc == 0, kc == KC - 1, [wtb, MTB], [bkb])
                        cp("scalar", mkT[:, 4 * g + j, :], bk[:, 0:256], [bkb], [MKB])
                else:
                    for mt in range(2):
                        bk, bkb = nb()
                        with P.group("tensor"):
                            for kc in range(KC):
                                mm(bk[:, :], mT[:, kc, mt * 128:(mt + 1) * 128], wt[:, kc, :], kc == 0, kc == KC - 1, [wtb, MTB], [bkb])
                        cp("vector", mv[:, mt, (g - 4) * 512:(g - 3) * 512], bk[:, :], [bkb], [MVB])
            for g in range(4):
                P.dma("gpsimd", Wq[:, :, g * 512:(g + 1) * 512], wq_in[:, g * 512:(g + 1) * 512].rearrange("(k p) n -> p k n", p=128), wqs, (), [WQB])
            for g in range(8):
                P.dma("gpsimd", w2b[g * 688:(g + 1) * 688, :], fwout_in[g * 688:(g + 1) * 688, :], w2sem, (), [W2B])
            P.flush()
            es2.close()
            P.barrier()
            aring = Ring(P, [sb("a%d" % i, [128, KC, 512], BF16) for i in range(2)])
            qc = sb("qc", [128, KC, 512], BF16)
            QCB = Buf()
            ocr = Ring(P, [sb("oc%d" % i, [128, KC, 512], BF16) for i in range(2)])
            Pc = [sb("Pc%d" % i, [128, 2, 512], BF16) for i in range(2)]
            PCB = [Buf(), Buf()]
            rden = [sb("rden%d" % i, [128, 512], F32) for i in range(2)]
            RDB = [Buf(), Buf()]
            SCC = 512.0 ** -0.5
            for (c0, c1) in TCH:
                n = c1 - c0
                a, ab, asem = aring.next()
                P.dma("gpsimd", a[:, :, 0:n], x1T[:, c0:c1].rearrange("(k p) n -> p k n", p=128), asem, [X1B], [ab])
                for dc in range(KC):
                    bk, bkb = nb()
                    with P.group("tensor"):
                        for kc in range(KC):
                            mm(bk[:, 0:n], Wq[:, kc, dc * 128:(dc + 1) * 128], a[:, kc, 0:n], kc == 0, kc == KC - 1, [ab, WQB], [bkb])
                    cp("scalar" if dc % 2 == 0 else "vector", qc[:, dc, 0:n], bk[:, 0:n], [bkb], [QCB])
                oc, ocb, ocs = ocr.next()
                for hh in range(4):
                    i2 = hh % 2
                    for mt in range(2):
                        bsx, bsxb = nb()
                        with P.group("tensor"):
                            for c in range(4):
                                mm(bsx[:, 0:n], mkT[:, 4 * hh + c, mt * 128:(mt + 1) * 128], qc[:, 4 * hh + c, 0:n], c == 0, c == 3, [MKB, QCB], [bsxb])
                        act(Pc[i2][:, mt, 0:n], bsx[:, 0:n], AF.Exp, [bsxb], [PCB[i2]], scale=SCC)
                    bd, bdb = nb()
                    with P.group("tensor"):
                        for mt in range(2):
                            mm(bd[:, 0:n], ones_bf[:, :], Pc[i2][:, mt, 0:n], mt == 0, mt == 1, [CB, PCB[i2]], [bdb])
                    recip("vector", rden[i2][:, 0:n], bd[:, 0:n], [bdb], [RDB[i2]])
                    for c in range(4):
                        bo, bob = nb()
                        with P.group("tensor"):
                            for mt in range(2):
                                mm(bo[:, 0:n], mv[:, mt, (4 * hh + c) * 128:(4 * hh + c + 1) * 128], Pc[i2][:, mt, 0:n], mt == 0, mt == 1, [MVB, PCB[i2]], [bob])
                        tt("vector", oc[:, 4 * hh + c, 0:n], bo[:, 0:n], rden[i2][:, 0:n], ALU.mult, [bob, RDB[i2]], [ocb])
                P.dma("sync", ocT[:, c0:c1].rearrange("(k p) n -> p k n", p=128), oc[:, :, 0:n], ocs, [ocb], [OCB])
            P.flush()
        P.barrier()

        if stop == "D1":
            finish()
            return nc
        x1res = lambda dc, c0, c1: x1T[dc * 128:(dc + 1) * 128, c0:c1]
        gemm_ln_phase("D2", wo_in, ocT, False, x1res, 1, x2T, 0, TCH, OCB, X1B, X2B)
        if stop == "D2":
            finish()
            return nc

        with ExitStack() as es:
            sb = lambda name, shape, dt: es.enter_context(nc.sbuf_tensor("E" + name, list(shape), dt))
            x2b = sb("x2b", [128, KC, TQ], BF16)
            X2S = Buf()
            xs = P.new_dsem()
            for (c0, c1) in TCH[1:]:
                P.dma("gpsimd", x2b[:, :, c0:c1], x2T[:, c0:c1].rearrange("(k p) n -> p k n", p=128), xs, [X2B], [X2S])
            P.dma("gpsimd", x2b[:, :, 0:2], x2T[:, 0:2].rearrange("(k p) n -> p k n", p=128), xs, [X2B], [X2S])
            ts("vector", x2b[:, :, 0:2], x2b[:, :, 0:2], flag[:, 0:1], None, ALU.mult, None, [CB], [X2S])
            wr = Ring(P, [sb("w%d" % i, [128, KC, 256], BF16) for i in range(3)])
            ug = [sb("ug%d" % i, [128, TQ], F32) for i in range(2)]
            uu = [sb("uu%d" % i, [128, TQ], F32) for i in range(2)]
            yg = [sb("yg%d" % i, [128, 2048], F32) for i in range(2)]
            yu = [sb("yu%d" % i, [128, 2048], F32) for i in range(2)]
            hr = Ring(P, [sb("h%d" % i, [128, 2048], BF16) for i in range(2)])
            UG, UU, YG, YU = [Buf(), Buf()], [Buf(), Buf()], [Buf(), Buf()], [Buf(), Buf()]
            for j in range(NJ):
                i2 = j % 2
                wt, wtb, wts = wr.next()
                P.dma("gpsimd", wt[:, :, 0:128], fwin_in[:, j * 128:(j + 1) * 128].rearrange("(k p) n -> p k n", p=128), wts, (), [wtb])
                P.dma("gpsimd", wt[:, :, 128:256], fwin_in[:, DFF + j * 128:DFF + (j + 1) * 128].rearrange("(k p) n -> p k n", p=128), wts, (), [wtb])
                for part, (ubuf, UB) in enumerate([(ug[i2], UG[i2]), (uu[i2], UU[i2])]):
                    for ci, (c0, c1) in enumerate(TCH):
                        n = c1 - c0
                        bk, bkb = nb()
                        with P.group("tensor"):
                            for kc in range(KC):
                                mm(bk[:, 0:n], wt[:, kc, part * 128:(part + 1) * 128], x2b[:, kc, c0:c1], kc == 0, kc == KC - 1, [wtb, X2S], [bkb])
                        cp("scalar", ubuf[:, c0:c1], bk[:, 0:n], [bkb], [UB])
                for part, (eng, ubuf, UB, ybuf, YB_) in enumerate([("vector", ug[i2], UG[i2], yg[i2], YG[i2]), ("gpsimd", uu[i2], UU[i2], yu[i2], YU[i2])]):
                    cj = part * NJ + j
                    act(ybuf[:, :], ubuf[:, 2:TQ], AF.Identity, [UB, CB], [YB_], scale=cw[:, cj, 2:3], bias=cb[:, cj:cj + 1])
                    stt("vector", ybuf[:, :], ubuf[:, 1:TQ - 1], cw[:, cj, 1:2], ybuf[:, :], ALU.mult, ALU.add, [UB, CB], [YB_])
                    stt("vector", ybuf[:, :], ubuf[:, 0:TQ - 2], cw[:, cj, 0:1], ybuf[:, :], ALU.mult, ALU.add, [UB, CB], [YB_])
                act(yg[i2][:, :], yg[i2][:, :], AF.Silu, [], [YG[i2]])
                ht, htb, hts = hr.next()
                tt("vector", ht[:, :], yg[i2][:, :], yu[i2][:, :], ALU.mult, [YG[i2], YU[i2]], [htb])
                P.dma("sync", hT[j * 128:(j + 1) * 128, :], ht[:, :], hts, [htb], [HTB])
            P.flush()
        P.barrier()

        if stop == "E":
            finish()
            return nc
        with ExitStack() as es:
            sb = lambda name, shape, dt: es.enter_context(nc.sbuf_tensor("F" + name, list(shape), dt))
            aring = Ring(P, [sb("a%d" % i, [128, NJ, 512], BF16) for i in range(2)])
            w2r = Ring(P, [sb("w%d" % i, [128, 512], BF16) for i in range(8)])

            def wstream(j, qd_):
                wt, wtb, wts = w2r.next()
                P.dma("sync", wt[:, :], w2b[j * 128:(j + 1) * 128, qd_ * 512:(qd_ + 1) * 512], wts, [W2B], [wtb])
                return wt, wtb
            x2res = lambda dc, c0, c1: x2T[dc * 128:(dc + 1) * 128, 2 + c0:2 + c1]
            ln_chunks("F", sb, [(i * 512, (i + 1) * 512) for i in range(4)], NJ, None, None,
                      hT, False, aring, HTB, x2res, X2B, 2, outT, 0, OUTB, wstream=wstream)
            final = [(s[0], s[1], "dma") for s in P.dsems if s[1] > 0]
            P.flush(final)
    return nc


def _perm_idx():
    idx = np.empty(NL, np.int64)
    for s in range(2):
        for r in range(16):
            idx[s * 2048 + r * 128:s * 2048 + (r + 1) * 128] = s * 2048 + 16 * np.arange(128) + r
    return idx


def _masks():
    m = np.zeros((128, 6, 128), np.float32)
    j = np.arange(128)[:, None]
    i = np.arange(128)[None, :]
    for pi_, nat in enumerate([lambda x: x, lambda x: 4 * (x % 32) + x // 32, lambda x: 16 * (x % 8) + x // 8]):
        nj, ni = nat(j), nat(i)
        m[:, 2 * pi_, :] = (nj >= ni)
        m[:, 2 * pi_ + 1, :] = (nj <= ni)
    return m.reshape(128, 768)


def _fm(v, nchunk):
    return np.ascontiguousarray(v.reshape(nchunk, 128).T)


_CACHE = {}


def make_in_maps(x, mem, positions, w_in, gla_gate_w2, gla_gate_b, gla_norm_g, w_out, ln1_g, ln1_b,
                 ca_wq, ca_wkv, ca_wo, ln2_g, ln2_b, ffn_w_in, ffn_conv_w, ffn_conv_b, ffn_w_out, ln3_g, ln3_b):
    f32 = np.float32
    x = np.asarray(x, f32)
    mem = np.asarray(mem, f32)
    positions = np.asarray(positions, np.int32)
    w_in = np.asarray(w_in, f32)[0]
    pidx = _perm_idx()
    o = 0
    cols = {}
    for name, w in zip(["qg", "kg", "vg", "rg", "glr", "qd", "kd", "vd"], [512, 512, 1024, 1024, 16, 1024, 1024, 1024]):
        cols[name] = w_in[:, o:o + w]
        o += w
    wg = np.stack([np.concatenate([cols["qg"][:, h * 128:(h + 1) * 128], cols["kg"][:, h * 128:(h + 1) * 128],
                                   cols["rg"][:, h * 256:(h + 1) * 256], cols["vg"][:, h * 256:(h + 1) * 256]], axis=1) for h in range(4)])
    swap = np.concatenate([np.arange(16, 32), np.arange(0, 16)])
    wd = np.stack([np.concatenate([cols["qd"][:, h * 128:(h + 1) * 128], cols["kd"][:, h * 128:(h + 1) * 128],
                                   cols["qd"][:, h * 128 + swap], cols["kd"][:, h * 128 + swap]], axis=1) for h in range(8)])
    w2aug = np.concatenate([np.asarray(gla_gate_w2, f32)[0], np.asarray(gla_gate_b, f32)[0][None, :]], axis=0)
    lnp = np.stack([_fm(np.asarray(v, f32)[0], 16) for v in [ln1_g, ln1_b, ln2_g, ln2_b, ln3_g, ln3_b]], axis=1).reshape(128, 96)
    convw = np.ascontiguousarray(np.asarray(ffn_conv_w, f32)[0].T.reshape(86, 128, 3).transpose(1, 0, 2)).reshape(128, 258)
    convb = _fm(np.asarray(ffn_conv_b, f32)[0], 86)
    jj = np.arange(128)[:, None]
    ii = np.arange(128)[None, :]
    uneg = np.where(jj <= ii, f32(-1.0 / 16.0), f32(0.0)).astype(f32)
    invf = (500000.0 ** (-(np.arange(0, 32, 2, dtype=np.float32)) / 32.0)).astype(f32)
    rotc = np.stack([np.concatenate([invf, invf]), np.concatenate([-np.ones(16, f32), np.ones(16, f32)])], axis=1).astype(f32)
    shared = dict(masks=_masks(), uneg=uneg, rotc=rotc, wg=np.ascontiguousarray(wg), wglr=np.ascontiguousarray(cols["glr"]),
                  w2aug=np.ascontiguousarray(w2aug), glag=_fm(np.asarray(gla_norm_g, f32)[0], 8), wd=np.ascontiguousarray(wd),
                  wvd=np.ascontiguousarray(cols["vd"]), w_out=np.asarray(w_out, f32)[0], lnp=np.ascontiguousarray(lnp),
                  ca_wq=np.asarray(ca_wq, f32)[0], ca_wkv=np.asarray(ca_wkv, f32)[0], ca_wo=np.asarray(ca_wo, f32)[0],
                  ffn_w_in=np.asarray(ffn_w_in, f32)[0], convw=convw, convb=convb, ffn_w_out=np.asarray(ffn_w_out, f32)[0])
    in_maps = []
    for c in range(8):
        b, hf = c // 2, c % 2
        xl = np.zeros((NL, D), f32)
        pl = np.zeros((NL,), np.int32)
        if hf == 1:
            xl[:] = x[b]
            pl[:] = positions[b]
        else:
            xl[2048:] = x[b, :2048]
            pl[2048:] = positions[b, :2048]
        xTn = np.ascontiguousarray(xl.T)
        m = dict(shared)
        m.update(xT_nat=xTn, xT_perm=np.ascontiguousarray(xTn[:, pidx]), memT=np.ascontiguousarray(mem[b].T),
                 pos_perm=np.ascontiguousarray(pl[pidx][None, :]), flag=np.full((128, 1), float(hf), f32))
        in_maps.append(m)
    return in_maps


def kernel(**inputs):
    if "nc" not in _CACHE:
        _CACHE["nc"] = build(False)
    nc = _CACHE["nc"]
    in_maps = make_in_maps(**inputs)
    res = run_bass_kernel_spmd(nc, in_maps, core_ids=list(range(8)))
    out = np.empty((4, 4096, D), np.float32)
    for c in range(8):
        b, hf = c // 2, c % 2
        out[b, hf * 2048:(hf + 1) * 2048, :] = np.asarray(res.results[c]["outT"]).T
    return out
```

```python
import math
from contextlib import ExitStack, contextmanager
import numpy as np
import concourse.bass as bass
import concourse.mybir as mybir
from concourse.bass_utils import run_bass_kernel_spmd

F32 = mybir.dt.float32
BF16 = mybir.dt.bfloat16
I32 = mybir.dt.int32
AF = mybir.ActivationFunctionType
ALU = mybir.AluOpType

D = 2048
KC = 16
NL = 4096
TQ = 2050
QN = 2176
DFF = 5504
NJ = 43
ALPHA = 2.0 ** 0.25
LN_EPS = 1e-5
TCH = [(0, 2), (2, 514), (514, 1026), (1026, 1538), (1538, 2050)]
ENGS = ["tensor", "vector", "scalar", "gpsimd", "sync"]
PI = math.pi


class Buf:
    __slots__ = ("w", "r", "excl")

    def __init__(self, excl=False):
        self.w = None
        self.r = {}
        self.excl = excl


class Prog:
    def __init__(self, nc, es):
        self.nc = nc
        self.q = {e: [] for e in ENGS}
        self.cnt = {e: 0 for e in ENGS}
        self.sem = {e: es.enter_context(nc.semaphore("s_" + e)) for e in ENGS}
        self.grp = {e: None for e in ENGS}
        self.es = es
        self.dsems = []
        self.bar = {e: [] for e in ENGS}
        self.waited = {e: {} for e in ENGS}

    def _deps(self, reads, writes, extra):
        d = [x for x in extra if x is not None]
        if any(b.excl for b in reads):
            writes = list(writes) + [b for b in reads if b.excl]
            reads = [b for b in reads if not b.excl]
        for b in reads:
            if b.w is not None:
                d.append(b.w)
        for b in writes:
            if b.w is not None:
                d.append(b.w)
            d.extend(b.r.values())
        return d

    def _mark(self, tok, reads, writes):
        if any(b.excl for b in reads):
            writes = list(writes) + [b for b in reads if b.excl]
            reads = [b for b in reads if not b.excl]
        for b in reads:
            b.r[id(tok[0])] = tok
        for b in writes:
            b.w = tok
            b.r = {}

    def op(self, eng, fn, reads=(), writes=(), deps=()):
        d = self._deps(reads, writes, deps)
        if self.bar[eng]:
            d.extend(self.bar[eng])
            self.bar[eng] = []
        if eng == "tensor":
            d = [t for t in d if t[2] != "tensor"]
        if self.grp[eng] is not None:
            tok = self.grp[eng]
            d = [t for t in d if not (t[0] is tok[0] and t[1] == tok[1])]
            self.q[eng].append(["op", fn, d, False])
        else:
            self.cnt[eng] += 1
            tok = (self.sem[eng], self.cnt[eng], eng)
            self.q[eng].append(["op", fn, d, True])
        self._mark(tok, reads, writes)
        return tok

    @contextmanager
    def group(self, eng):
        tok = (self.sem[eng], self.cnt[eng] + 1, eng)
        self.grp[eng] = tok
        n0 = len(self.q[eng])
        yield tok
        self.grp[eng] = None
        assert len(self.q[eng]) > n0
        self.q[eng][-1][3] = True
        self.cnt[eng] += 1

    def _raw_dsem(self):
        s = [self.es.enter_context(self.nc.semaphore("d%d" % len(self.dsems))), 0]
        self.dsems.append(s)
        return s

    def new_dsem(self):
        return {}

    def dma(self, eng, out, in_, sem, reads=(), writes=(), deps=()):
        d = self._deps(reads, writes, deps)
        if self.bar[eng]:
            d.extend(self.bar[eng])
            self.bar[eng] = []
        if eng not in sem:
            sem[eng] = self._raw_dsem()
        sem = sem[eng]
        sem[1] += 16
        tok = (sem[0], sem[1], "dma")
        self.q[eng].append(["dma", (out, in_, sem[0]), d, True])
        self._mark(tok, reads, writes)
        return tok

    def barrier(self):
        toks = [(self.sem[e], self.cnt[e], "bar") for e in ENGS if self.cnt[e] > 0]
        toks += [(s[0], s[1], "dma") for s in self.dsems if s[1] > 0]
        for e in ENGS:
            self.bar[e] = list(toks)

    def flush(self, final=()):
        with self.nc.Block() as block:
            def mk(ename):
                def body(eng):
                    waited = self.waited[ename]
                    for kind, payload, deps, sig in self.q[ename]:
                        for (s, v, src) in deps:
                            key = id(s)
                            if waited.get(key, 0) >= v:
                                continue
                            eng.wait_ge(s, v)
                            waited[key] = v
                        if kind == "op":
                            ins = payload(eng)
                            if sig:
                                ins.then_inc(self.sem[ename], 1)
                        else:
                            out, in_, s = payload
                            eng.dma_start(out=out, in_=in_).then_inc(s, 16)
                    if ename == "sync":
                        for (s, v, src) in final:
                            eng.wait_ge(s, v)
                    self.q[ename] = []
                return body
            block.tensor(mk("tensor"))
            block.vector(mk("vector"))
            block.scalar(mk("scalar"))
            block.gpsimd(mk("gpsimd"))
            block.sync(mk("sync"))


def pipeline(n, stages):
    S = len(stages)
    ctx = [dict() for _ in range(n)]
    for t in range(n + S - 1):
        for s_ in reversed(range(S)):
            i = t - s_
            if 0 <= i < n:
                stages[s_](i, ctx[i])


def pipeline_gen(n, stages):
    S = len(stages)
    ctx = [dict() for _ in range(n)]
    for t in range(n + S - 1):
        for s_ in reversed(range(S)):
            i = t - s_
            if 0 <= i < n:
                stages[s_](i, ctx[i])
        yield t


class Ring:
    def __init__(self, P, tensors):
        self.t = tensors
        self.b = [Buf() for _ in tensors]
        self.s = [P.new_dsem() for _ in tensors]
        self.i = -1

    def next(self):
        self.i = (self.i + 1) % len(self.t)
        return self.t[self.i], self.b[self.i], self.s[self.i]


def build(debug=False, stop=None, ngla=4, ndil=8):
    nc = bass.Bass("TRN2", target_bir_lowering=False)
    din = lambda name, shape, dt=F32: nc.dram_tensor(name, list(shape), dt, kind="ExternalInput").ap()
    okind = "ExternalOutput" if debug else "Internal"
    dscr = lambda name, shape, dt: nc.dram_tensor(name, list(shape), dt, kind=okind).ap()
    xT_nat = din("xT_nat", [D, NL])
    xT_perm = din("xT_perm", [D, NL])
    memT = din("memT", [D, 256])
    pos_in = din("pos_perm", [1, NL], I32)
    flag_in = din("flag", [128, 1])
    masks_in = din("masks", [128, 6 * 128])
    uneg_in = din("uneg", [128, 128])
    rotc_in = din("rotc", [32, 2])
    wg_in = din("wg", [4, D, 768])
    wglr_in = din("wglr", [D, 16])
    w2aug_in = din("w2aug", [17, 512])
    glag_in = din("glag", [128, 8])
    wd_in = din("wd", [8, D, 320])
    wvd_in = din("wvd", [D, 1024])
    wout_in = din("w_out", [D, D])
    lnp_in = din("lnp", [128, 6 * 16])
    wq_in = din("ca_wq", [D, D])
    wkv_in = din("ca_wkv", [D, 2 * D])
    wo_in = din("ca_wo", [D, D])
    fwin_in = din("ffn_w_in", [D, 2 * DFF])
    cw_in = din("convw", [128, 86 * 3])
    cb_in = din("convb", [128, 86])
    fwout_in = din("ffn_w_out", [DFF, D])
    outT = nc.dram_tensor("outT", [D, 2048], F32, kind="ExternalOutput").ap()

    xb_nat = dscr("xb_nat", [D, NL], BF16)
    xb_perm = dscr("xb_perm", [D, NL], BF16)
    vd_scr = dscr("vd_scr", [NL, 1024], BF16)
    vd4_scr = dscr("vd4_scr", [NL, 1024], BF16)
    vd1_scr = dscr("vd1_scr", [NL, 1024], BF16)
    mixT = dscr("mixT", [D, TQ], BF16)
    x1T = dscr("x1T", [D, TQ], F32)
    ocT = dscr("ocT", [D, TQ], BF16)
    x2T = dscr("x2T", [D, TQ], F32)
    hT = dscr("hT", [DFF, 2048], BF16)
    w2b = dscr("w2b", [DFF, D], BF16)

    with ExitStack() as ges:
        P = Prog(nc, ges)
        gsb = lambda name, shape, dt: ges.enter_context(nc.sbuf_tensor(name, list(shape), dt))
        banks = [ges.enter_context(nc.psum_tensor("pb%d" % i, [128, 512], F32)) for i in range(7)]
        bankb = [Buf(True) for _ in range(7)]
        ptb = ges.enter_context(nc.psum_tensor("ptb", [128, 1024], BF16))
        ptb_b = Buf(True)
        bi = [0]

        def finish():
            final = [(s_[0], s_[1], "dma") for s_ in P.dsems if s_[1] > 0]
            final += [(P.sem[e_], P.cnt[e_], "bar") for e_ in ENGS if P.cnt[e_] > 0 and e_ != "sync"]
            P.flush(final)

        busy = set()

        def nb(reserve=False):
            for _ in range(8):
                bi[0] = (bi[0] + 1) % 7
                if bi[0] not in busy:
                    break
            else:
                raise RuntimeError("no free PSUM bank")
            if reserve:
                busy.add(bi[0])
            return banks[bi[0]], bankb[bi[0]]

        def rel(bb):
            busy.discard(bankb.index(bb))

        def mm(out, lhsT, rhs, start, stop, reads, writes):
            return P.op("tensor", lambda e: e.matmul(out, lhsT=lhsT, rhs=rhs, start=start, stop=stop), reads, writes)

        def act(out, in_, func, reads, writes, bias=None, scale=None):
            kw = {}
            if bias is not None:
                kw["bias"] = bias
            if scale is not None:
                kw["scale"] = scale
            return P.op("scalar", lambda e: e.activation(out=out, in_=in_, func=func, **kw), reads, writes)

        def tt(eng, out, in0, in1, op, reads, writes):
            return P.op(eng, lambda e: e.tensor_tensor(out=out, in0=in0, in1=in1, op=op), reads, writes)

        def ts(eng, out, in0, s1, s2, op0, op1, reads, writes):
            if op1 is None:
                return P.op(eng, lambda e: e.tensor_scalar(out=out, in0=in0, scalar1=s1, scalar2=None, op0=op0), reads, writes)
            return P.op(eng, lambda e: e.tensor_scalar(out=out, in0=in0, scalar1=s1, scalar2=s2, op0=op0, op1=op1), reads, writes)

        def stt(eng, out, in0, scalar, in1, op0, op1, reads, writes):
            return P.op(eng, lambda e: e.scalar_tensor_tensor(out=out, in0=in0, scalar=scalar, in1=in1, op0=op0, op1=op1), reads, writes)

        def cp(eng, out, in_, reads, writes):
            if eng == "scalar":
                return P.op(eng, lambda e: e.activation(out=out, in_=in_, func=AF.Copy), reads, writes)
            return P.op(eng, lambda e: e.tensor_copy(out=out, in_=in_), reads, writes)

        def recip(eng, out, in_, reads, writes):
            return P.op(eng, lambda e: e.reciprocal(out=out, in_=in_), reads, writes)

        def mset(eng, ap, val, writes):
            return P.op(eng, lambda e: e.memset(ap, val), (), writes)

        ident = gsb("ident", [128, 128], BF16)
        ones_bf = gsb("ones_bf", [128, 128], BF16)
        ones256 = gsb("ones256", [128, 128], F32)
        ones2048 = gsb("ones2048", [128, 128], F32)
        masks = gsb("masks_sb", [128, 6, 128], BF16)
        masksc = gsb("masksc_sb", [128, 6, 128], BF16)
        uneg = gsb("uneg_sb", [128, 128], F32)
        flag = gsb("flag_sb", [128, 1], F32)
        lnp = gsb("lnp_sb", [128, 6, 16], F32)
        glag = gsb("glag_sb", [128, 8], F32)
        cw = gsb("cw_sb", [128, 86, 3], F32)
        cb = gsb("cb_sb", [128, 86], F32)
        mtmp = gsb("mtmp", [128, 6, 128], F32)
        CB = Buf()
        csem = P.new_dsem()
        P.dma("sync", mtmp[:, :, :], masks_in.rearrange("p (a b) -> p a b", a=6), csem, (), [CB])
        P.dma("sync", uneg[:, :], uneg_in, csem, (), [CB])
        P.dma("sync", flag[:, :], flag_in, csem, (), [CB])
        P.dma("sync", lnp[:, :, :], lnp_in.rearrange("p (a b) -> p a b", a=6), csem, (), [CB])
        P.dma("sync", glag[:, :], glag_in, csem, (), [CB])
        P.dma("sync", cw[:, :, :], cw_in.rearrange("p (a b) -> p a b", b=3), csem, (), [CB])
        P.dma("sync", cb[:, :], cb_in, csem, (), [CB])
        mset("gpsimd", ident[:, :], 0.0, [CB])
        P.op("gpsimd", lambda e: e.affine_select(out=ident[:, :], in_=ident[:, :], pattern=[[-1, 128]],
                                                 compare_op=ALU.not_equal, fill=1.0, base=0, channel_multiplier=1), [CB], [CB])
        mset("gpsimd", ones_bf[:, :], 1.0, [CB])
        mset("gpsimd", ones256[:, :], 1.0 / 256.0, [CB])
        mset("gpsimd", ones2048[:, :], 1.0 / 2048.0, [CB])
        cp("vector", masks[:, :, :], mtmp[:, :, :], [CB], [CB])
        ts("vector", masksc[:, :, :], mtmp[:, :, :], flag[:, 0:1], None, ALU.mult, None, [CB], [CB])

        XN = [Buf() for _ in range(8)]
        XP = [Buf() for _ in range(8)]
        P.flush()
        if stop == "pre":
            finish()
            return nc

        MIXB = Buf()
        VDB = Buf()

        with ExitStack() as es:
            sb = lambda name, shape, dt: es.enter_context(nc.sbuf_tensor(name, list(shape), dt))
            xring = Ring(P, [sb("xblk%d" % i, [128, KC, 512], BF16) for i in range(2)])

            def load_x(src, srcbufs, blk):
                t, b, s = xring.next()
                P.dma("sync", t[:, :, :], src[:, blk * 512:(blk + 1) * 512].rearrange("(k p) n -> p k n", p=128), s, [srcbufs[blk]], [b])
                return t, b

            glrT = sb("glrT", [32, NL], F32)
            GLR = Buf()
            wglr = sb("wglr_sb", [128, KC, 16], BF16)
            w2aug = sb("w2aug_sb", [17, 512], F32)
            WS = Buf()
            wsem = P.new_dsem()
            P.dma("gpsimd", wglr[:, :, :], wglr_in.rearrange("(k p) n -> p k n", p=128), wsem, (), [WS])
            P.dma("sync", w2aug[:, :], w2aug_in, wsem, (), [WS])
            def load_first(ring, src32, dst16, dbufs, blk):
                t, b, s_ = ring.next()
                cols = slice(blk * 512, (blk + 1) * 512)
                P.dma("gpsimd", t[:, :, :], src32[:, cols].rearrange("(k p) n -> p k n", p=128), s_, (), [b])
                P.dma("sync", dst16[:, cols].rearrange("(k p) n -> p k n", p=128), t[:, :, :], s_, [b], [dbufs[blk]])
                return t, b
            mset("vector", glrT[:, :], 1.0, [GLR])
            for blk in range(8):
                xb, xbb = load_first(xring, xT_nat, xb_nat, XN, blk)
                bk, bkb = nb()
                with P.group("tensor"):
                    for kc in range(KC):
                        mm(bk[0:16, :], wglr[:, kc, :], xb[:, kc, :], kc == 0, kc == KC - 1, [xbb, WS], [bkb])
                cp("scalar", glrT[0:16, blk * 512:(blk + 1) * 512], bk[0:16, :], [bkb], [GLR])

            if stop == "glr":
                finish()
                return nc
            wgring = Ring(P, [sb("wg%d" % i, [128, KC, 768], BF16) for i in range(1)])
            qT = sb("g_qT", [128, QN], BF16)
            kT = sb("g_kT", [128, NL], BF16)
            rT = sb("g_rT", [128, 2, QN], BF16)
            vsb = sb("g_v", [128, 32, 256], BF16)
            enb = sb("g_enb", [128, NL], BF16)
            kef = sb("g_kef", [128, NL], BF16)
            ebq = sb("g_ebq", [128, QN], BF16)
            decay = sb("g_decay", [128, 32], F32)
            blast = sb("g_blast", [128, 32], F32)
            Sst = sb("g_S", [128, 256], F32)
            Sbf = [sb("g_Sbf%d" % i, [128, 256], BF16) for i in range(3)]
            og = sb("g_og", [128, 2, TQ], BF16)
            tA = [sb("g_tA%d" % i, [128, 128], F32) for i in range(4)]
            spb = [sb("g_sp%d" % i, [128, 128], F32) for i in range(4)]
            kin = [sb("g_kin%d" % i, [128, 128], BF16) for i in range(4)]
            kend = [sb("g_kend%d" % i, [128, 128], BF16) for i in range(4)]
            kendT = [sb("g_kendT%d" % i, [128, 128], BF16) for i in range(4)]
            qin = [sb("g_qin%d" % i, [128, 128], BF16) for i in range(4)]
            Am = [sb("g_Am%d" % i, [128, 128], BF16) for i in range(4)]
            osb = [sb("g_osb%d" % i, [128, 2, 128], F32) for i in range(4)]
            osq = [sb("g_osq%d" % i, [128, 2, 128], F32) for i in range(4)]
            mst = [sb("g_mst%d" % i, [128, 256], F32) for i in range(4)]
            t1 = [sb("g_t1%d" % i, [128, 128], F32) for i in range(4)]
            t2 = [sb("g_t2%d" % i, [128, 2, 128], F32) for i in range(4)]
            sr = [sb("g_sr%d" % i, [128, 2, 128], F32) for i in range(4)]
            on = [sb("g_on%d" % i, [128, 128], F32) for i in range(4)]
            on2 = [[sb("g_on2_%d_%d" % (i, e_), [128, 128], F32) for e_ in range(2)] for i in range(4)]
            TB2 = {"on": [[Buf(), Buf()] for _ in range(4)]}
            HB = {k: Buf() for k in ["q", "k", "r", "v", "gate", "S", "og", "bl"]}
            Sbfb = [Buf(), Buf(), Buf()]
            TB = {k: [Buf() for _ in range(4)] for k in ["tA", "sp", "kin", "kend", "kendT", "qin", "Am", "osb", "osq", "mst", "t1", "t2", "sr", "on"]}
            ogsem = P.new_dsem()
            LNS = math.log(128.0 ** -0.5)

            wg, wgb, wgs = wgring.next()
            P.dma("gpsimd", wg[:, :, :], wg_in[0].rearrange("(k p) n -> p k n", p=128), wgs, (), [wgb])
            for h in range(ngla):
                def g0(n, c):
                    c["bz"], c["bzb"] = nb(True)
                    mm(c["bz"][:, 0:128], glrT[0:17, n * 128:(n + 1) * 128], w2aug[0:17, h * 128:(h + 1) * 128], True, True, [GLR, WS], [c["bzb"]])

                def g1(n, c):
                    i4 = n % 4
                    act(tA[i4][:, :], c["bz"][:, 0:128], AF.Exp, [c["bzb"]], [TB["tA"][i4]], scale=-1.0)
                    act(spb[i4][:, :], tA[i4][:, :], AF.Ln, [TB["tA"][i4]], [TB["sp"][i4]], bias=1.0)
                    rel(c["bzb"])

                def g2(n, c):
                    i4 = n % 4
                    c["bt"], c["btb"] = nb(True)
                    mm(c["bt"][:, 0:128], spb[i4][:, :], uneg[:, :], True, True, [TB["sp"][i4], CB], [c["btb"]])

                def g3(n, c):
                    bt, btb = c["bt"], c["btb"]
                    cp("vector", blast[:, n:n + 1], bt[:, 127:128], [btb], [HB["bl"]])
                    act(enb[:, n * 128:(n + 1) * 128], bt[:, 0:128], AF.Exp, [btb], [HB["gate"]], scale=-1.0)
                    act(kef[:, n * 128:(n + 1) * 128], bt[:, 0:128], AF.Exp, [btb, HB["bl"]], [HB["gate"]], scale=-1.0, bias=blast[:, n:n + 1])
                    if n >= 15:
                        act(ebq[:, (n - 15) * 128:(n - 14) * 128], bt[:, 0:128], AF.Exp, [btb], [HB["gate"]], bias=LNS)
                    act(decay[:, n:n + 1], blast[:, n:n + 1], AF.Exp, [HB["bl"]], [HB["gate"]])
                    rel(btb)
                gsteps = pipeline_gen(32, [g0, g1, g2, g3])
                for blk in range(8):
                    for _ in range(5):
                        next(gsteps, None)
                    xb, xbb = load_x(xb_nat, XN, blk)
                    bk, bkb = nb()
                    with P.group("tensor"):
                        for kc in range(KC):
                            mm(bk[:, :], wg[:, kc, 128:256], xb[:, kc, :], kc == 0, kc == KC - 1, [xbb, wgb], [bkb])
                    cp("scalar", kT[:, blk * 512:(blk + 1) * 512], bk[:, :], [bkb], [HB["k"]])
                    if blk >= 3:
                        x0, nn_, q0 = (384, 128, 0) if blk == 3 else (0, 512, 128 + (blk - 4) * 512)
                        for (c0, dst, hb) in [(0, qT[:, q0:q0 + nn_], "q"), (256, rT[:, 0, q0:q0 + nn_], "r"), (384, rT[:, 1, q0:q0 + nn_], "r")]:
                            bq, bqb = nb()
                            with P.group("tensor"):
                                for kc in range(KC):
                                    mm(bq[:, 0:nn_], wg[:, kc, c0:c0 + 128], xb[:, kc, x0:x0 + nn_], kc == 0, kc == KC - 1, [xbb, wgb], [bqb])
                            cp("scalar" if hb == "q" else "vector", dst, bq[:, 0:nn_], [bqb], [HB[hb]])
                    for sub in range(4):
                        bv, bvb = nb()
                        with P.group("tensor"):
                            for kc in range(KC):
                                mm(bv[:, 0:256], xb[:, kc, sub * 128:(sub + 1) * 128], wg[:, kc, 512:768], kc == 0, kc == KC - 1, [xbb, wgb], [bvb])
                        cp("vector", vsb[:, blk * 4 + sub, :], bv[:, 0:256], [bvb], [HB["v"]])
                for _ in gsteps:
                    pass
                if h + 1 < ngla:
                    P.dma("gpsimd", wg[:, :, :], wg_in[h + 1].rearrange("(k p) n -> p k n", p=128), wgs, (), [wgb])
                act(rT[:, :, :], rT[:, :, :], AF.Silu, [], [HB["r"]])
                if stop == "proj":
                    finish()
                    return nc
                mset("vector", Sst[:, :], 0.0, [HB["S"]])
                mset("gpsimd", Sbf[2][:, :], 0.0, [Sbfb[2]])

                def c0_(n, c):
                    i4 = n % 4
                    tok = slice(n * 128, (n + 1) * 128)
                    tt("vector", kin[i4][:, :], kT[:, tok], enb[:, tok], ALU.mult, [HB["k"], HB["gate"]], [TB["kin"][i4]])
                    tt("gpsimd", kend[i4][:, :], kT[:, tok], kef[:, tok], ALU.mult, [HB["k"], HB["gate"]], [TB["kend"][i4]])
                    if n >= 15:
                        q0 = (n - 15) * 128
                        tt("vector", qin[i4][:, :], qT[:, q0:q0 + 128], ebq[:, q0:q0 + 128], ALU.mult, [HB["q"], HB["gate"]], [TB["qin"][i4]])

                def c1_(n, c):
                    i4 = n % 4
                    P.op("tensor", lambda e, i4=i4: e.transpose(out=ptb[:, i4 * 128:(i4 + 1) * 128], in_=kend[i4][:, :], identity=ident[:, :]),
                         [TB["kend"][i4], CB], [ptb_b])
                    if n >= 15:
                        c["ba"], c["bab"] = nb(True)
                        mm(c["ba"][:, 0:128], kin[i4][:, :], qin[i4][:, :], True, True, [TB["kin"][i4], TB["qin"][i4]], [c["bab"]])

                def c2_(n, c):
                    i4 = n % 4
                    cp("scalar", kendT[i4][:, :], ptb[:, i4 * 128:(i4 + 1) * 128], [ptb_b], [TB["kendT"][i4]])
                    if n >= 15:
                        q0 = (n - 15) * 128
                        tt("vector", Am[i4][:, :], c["ba"][:, 0:128], masks[:, 1, :], ALU.mult, [c["bab"], CB], [TB["Am"][i4]])
                        rel(c["bab"])

                def c3_(n, c):
                    i4 = n % 4
                    Sprev, Sprevb = Sbf[(n + 2) % 3], Sbfb[(n + 2) % 3]
                    if n >= 15:
                        c["bo"], c["bob"] = nb(True)
                        for e_ in range(2):
                            with P.group("tensor"):
                                mm(c["bo"][:, e_ * 128:(e_ + 1) * 128], vsb[:, n, e_ * 128:(e_ + 1) * 128], Am[i4][:, :], True, False, [HB["v"], TB["Am"][i4]], [c["bob"]])
                                mm(c["bo"][:, e_ * 128:(e_ + 1) * 128], Sprev[:, e_ * 128:(e_ + 1) * 128], qin[i4][:, :], False, True, [Sprevb, TB["qin"][i4]], [c["bob"]])
                    c["bs"], c["bsb"] = nb(True)
                    mm(c["bs"][:, 0:256], kendT[i4][:, :], vsb[:, n, :], True, True, [TB["kendT"][i4], HB["v"]], [c["bsb"]])

                def c4_(n, c):
                    i4 = n % 4
                    stt("vector", Sst[:, :], Sst[:, :], decay[:, n:n + 1], c["bs"][:, 0:256], ALU.mult, ALU.add, [c["bsb"], HB["gate"]], [HB["S"]])
                    cp("gpsimd", Sbf[n % 3][:, :], Sst[:, :], [HB["S"]], [Sbfb[n % 3]])
                    rel(c["bsb"])
                    if n >= 15:
                        cp("scalar", osb[i4][:, :, :], c["bo"][:, 0:256].rearrange("p (a b) -> p a b", a=2), [c["bob"]], [TB["osb"][i4]])
                        rel(c["bob"])

                def c5_(n, c):
                    i4 = n % 4
                    if n < 15:
                        return
                    tt("gpsimd", osq[i4][:, :, :], osb[i4][:, :, :], osb[i4][:, :, :], ALU.mult, [TB["osb"][i4]], [TB["osq"][i4]])
                    c["bm"], c["bmb"] = nb(True)
                    bm, bmb = c["bm"], c["bmb"]
                    with P.group("tensor"):
                        mm(bm[:, 0:128], ones256[:, :], osb[i4][:, 0, :], True, False, [CB, TB["osb"][i4]], [bmb])
                        mm(bm[:, 0:128], ones256[:, :], osb[i4][:, 1, :], False, True, [CB, TB["osb"][i4]], [bmb])
                    with P.group("tensor"):
                        mm(bm[:, 128:256], ones256[:, :], osq[i4][:, 0, :], True, False, [CB, TB["osq"][i4]], [bmb])
                        mm(bm[:, 128:256], ones256[:, :], osq[i4][:, 1, :], False, True, [CB, TB["osq"][i4]], [bmb])

                def c6_(n, c):
                    i4 = n % 4
                    if n < 15:
                        return
                    q0 = (n - 15) * 128
                    cp("scalar", mst[i4][:, :], c["bm"][:, 0:256], [c["bmb"]], [TB["mst"][i4]])
                    rel(c["bmb"])
                    tt("vector", t1[i4][:, :], mst[i4][:, 0:128], mst[i4][:, 0:128], ALU.mult, [TB["mst"][i4]], [TB["t1"][i4]])
                    tt("vector", t1[i4][:, :], mst[i4][:, 128:256], t1[i4][:, :], ALU.subtract, [TB["mst"][i4]], [TB["t1"][i4]])
                    act(t1[i4][:, :], t1[i4][:, :], AF.Ln, [], [TB["t1"][i4]], bias=LN_EPS)
                    act(t1[i4][:, :], t1[i4][:, :], AF.Exp, [], [TB["t1"][i4]], scale=-0.5)

                def c7_(n, c):
                    i4 = n % 4
                    if n < 15:
                        return
                    q0 = (n - 15) * 128
                    for e_ in range(2):
                        onb, ONB = on2[i4][e_], TB2["on"][i4][e_]
                        tt("vector", onb[:, :], osb[i4][:, e_, :], mst[i4][:, 0:128], ALU.subtract, [TB["osb"][i4], TB["mst"][i4]], [ONB])
                        tt("gpsimd", onb[:, :], onb[:, :], t1[i4][:, :], ALU.mult, [TB["t1"][i4]], [ONB])
                        if n == 15:
                            stt("vector", og[:, e_, 0:2], onb[:, 126:128], glag[:, 2 * h + e_:2 * h + e_ + 1], rT[:, e_, q0 + 126:q0 + 128],
                                ALU.mult, ALU.mult, [ONB, HB["r"], CB], [HB["og"]])
                        else:
                            o0 = 2 + (n - 16) * 128
                            stt("vector", og[:, e_, o0:o0 + 128], onb[:, :], glag[:, 2 * h + e_:2 * h + e_ + 1], rT[:, e_, q0:q0 + 128],
                                ALU.mult, ALU.mult, [ONB, HB["r"], CB], [HB["og"]])
                pipeline(32, [c0_, c1_, c2_, c3_, c4_, c5_, c6_, c7_])
                for e_ in range(2):
                    P.dma("sync", mixT[(2 * h + e_) * 128:(2 * h + e_ + 1) * 128, :], og[:, e_, :], ogsem, [HB["og"]], [MIXB])
            P.flush()

        P.barrier()
        if stop == "A":
            finish()
            return nc
        with ExitStack() as es:
            sb = lambda name, shape, dt: es.enter_context(nc.sbuf_tensor(name, list(shape), dt))
            xring = Ring(P, [sb("xblkb%d" % i, [128, KC, 512], BF16) for i in range(2)])

            def load_xp(blk):
                t, b, s = xring.next()
                P.dma("sync", t[:, :, :], xb_perm[:, blk * 512:(blk + 1) * 512].rearrange("(k p) n -> p k n", p=128), s, [XP[blk]], [b])
                return t, b

            cosT = sb("cosT", [32, NL], F32)
            sinT = sb("sinT", [32, NL], F32)
            ROT = Buf()
            with ExitStack() as es2:
                sb2 = lambda name, shape, dt: es2.enter_context(nc.sbuf_tensor(name, list(shape), dt))
                posi = sb2("posi", [32, NL], I32)
                ang = sb2("ang", [32, NL], F32)
                tf = sb2("tf", [32, NL], F32)
                rr = sb2("rr", [32, NL], F32)
                mk_ = sb2("mk_", [32, NL], F32)
                rotc = sb2("rotc_sb", [32, 2], F32)
                RB = Buf()
                rsem = P.new_dsem()
                P.dma("sync", posi[:, :], pos_in.partition_broadcast(32), rsem, (), [RB])
                P.dma("sync", rotc[:, :], rotc_in, rsem, (), [RB])
                defer = []

                def DF(fn, *a_, **k_):
                    defer.append(lambda: fn(*a_, **k_))
                DF(cp, "vector", ang[:, :], posi[:, :], [RB], [RB])
                DF(ts, "vector", ang[:, :], ang[:, :], rotc[:, 0:1], None, ALU.mult, None, [RB], [RB])
                DF(ts, "vector", tf[:, :], ang[:, :], 1.0 / (2 * PI), 0.5, ALU.mult, ALU.add, [RB], [RB])
                DF(cp, "vector", posi[:, :], tf[:, :], [RB], [RB])
                DF(cp, "vector", tf[:, :], posi[:, :], [RB], [RB])
                C1 = 6.28125
                C2 = 2 * PI - C1
                DF(stt, "vector", rr[:, :], tf[:, :], -C1, ang[:, :], ALU.mult, ALU.add, [RB], [RB])
                DF(stt, "vector", rr[:, :], tf[:, :], -C2, rr[:, :], ALU.mult, ALU.add, [RB], [RB])

                def wrap_clamp(r):
                    DF(ts, "vector", mk_[:, :], r[:, :], -PI, None, ALU.is_lt, None, [RB], [RB])
                    DF(stt, "vector", r[:, :], mk_[:, :], 2 * PI, r[:, :], ALU.mult, ALU.add, [RB], [RB])
                    DF(ts, "vector", mk_[:, :], r[:, :], PI, None, ALU.is_gt, None, [RB], [RB])
                    DF(stt, "vector", r[:, :], mk_[:, :], -2 * PI, r[:, :], ALU.mult, ALU.add, [RB], [RB])
                    DF(ts, "vector", r[:, :], r[:, :], -3.141592, 3.141592, ALU.max, ALU.min, [RB], [RB])
                wrap_clamp(rr)
                DF(act, sinT[:, :], rr[:, :], AF.Sin, [RB], [ROT], scale=rotc[:, 1:2])
                DF(ts, "vector", rr[:, :], rr[:, :], PI / 2, None, ALU.add, None, [RB], [RB])
                wrap_clamp(rr)
                DF(act, cosT[:, :], rr[:, :], AF.Sin, [RB], [ROT])

                wvd = sb2("wvd_sb", [128, KC, 1024], BF16)
                WV = Buf()
                wvs = P.new_dsem()
                for g in range(2):
                    P.dma("gpsimd", wvd[:, :, g * 512:(g + 1) * 512], wvd_in[:, g * 512:(g + 1) * 512].rearrange("(k p) n -> p k n", p=128), wvs, (), [WV])
                vst = Ring(P, [sb2("vst%d" % i, [128, 1024], BF16) for i in range(2)])
                for blk in range(8):
                    t_, b_, s__ = xring.next()
                    cols_ = slice(blk * 512, (blk + 1) * 512)
                    P.dma("gpsimd", t_[:, :, :], xT_perm[:, cols_].rearrange("(k p) n -> p k n", p=128), s__, (), [b_])
                    P.dma("sync", xb_perm[:, cols_].rearrange("(k p) n -> p k n", p=128), t_[:, :, :], s__, [b_], [XP[blk]])
                    xb, xbb = t_, b_
                    for sub in range(4):
                        if defer:
                            defer.pop(0)()
                        vt, vtb, vts = vst.next()
                        for g in range(2):
                            bv, bvb = nb()
                            with P.group("tensor"):
                                for kc in range(KC):
                                    mm(bv[:, :], xb[:, kc, sub * 128:(sub + 1) * 128], wvd[:, kc, g * 512:(g + 1) * 512], kc == 0, kc == KC - 1, [xbb, WV], [bvb])
                            cp("scalar" if g == 0 else "vector", vt[:, g * 512:(g + 1) * 512], bv[:, :], [bvb], [vtb])
                        r0 = (blk * 4 + sub) * 128
                        P.dma("scalar", vd_scr[r0:r0 + 128, :], vt[:, :], vts, [vtb], [VDB])
                        T_ = blk * 4 + sub
                        s_, r16 = T_ // 16, T_ % 16
                        a_, r4 = r16 // 4, r16 % 4
                        d4 = vd4_scr.rearrange("(t i) c -> t i c", i=128)[r4 * 8 + 4 * s_:r4 * 8 + 4 * s_ + 4, 32 * a_:32 * a_ + 32, :]
                        P.dma("scalar", d4, vt[:, :], vts, [vtb], [VDB])
                        d1 = vd1_scr.rearrange("(t i) c -> t i c", i=128)[16 * s_:16 * s_ + 16, 8 * r16:8 * r16 + 8, :]
                        P.dma("scalar", d1, vt[:, :], vts, [vtb], [VDB])
                while defer:
                    defer.pop(0)()
                P.flush()
            P.barrier()
            if stop == "V":
                finish()
                return nc

            wdring = Ring(P, [sb("wd%d" % i, [128, KC, 320], BF16) for i in range(2)])
            dq2 = [sb("d_qq%d" % i, [128, 2560], BF16) for i in range(2)]
            dk2 = [sb("d_kk%d" % i, [128, NL], BF16) for i in range(2)]
            DBQ, DBK = [Buf(), Buf()], [Buf(), Buf()]
            dk4 = sb("d_k4", [128, NL], BF16)
            dk1 = sb("d_k1", [128, NL], BF16)
            dq4 = sb("d_q4", [128, 2048], BF16)
            dq1 = sb("d_q1", [128, 2048], BF16)
            hq1 = sb("d_hq1", [128, 16], BF16)
            V16 = sb("d_v16", [128, 32, 128], BF16)
            V4 = sb("d_v4", [128, 32, 128], BF16)
            V1 = sb("d_v1", [128, 32, 128], BF16)
            acc = sb("d_acc", [128, 2560], F32)
            dacc = sb("d_dacc", [128, 2560], F32)
            odb = sb("d_od", [128, TQ], BF16)
            rt1 = [sb("d_rt1%d" % i, [32, 512], F32) for i in range(2)]
            rt2 = [sb("d_rt2%d" % i, [32, 512], F32) for i in range(2)]
            Pm = [sb("d_P%d" % i, [128, 2, 128], BF16) for i in range(4)]
            DB = {k: Buf() for k in ["q", "k", "v", "acc", "od", "k4", "k1", "q4", "q1"]}
            DBV = {16: Buf(), 4: Buf(), 1: Buf()}
            RTB = [[Buf(), Buf()], [Buf(), Buf()]]
            PmB = [Buf() for _ in range(4)]
            vsem = P.new_dsem()
            odsem = P.new_dsem()
            SC = 128.0 ** -0.5
            vd5 = vd_scr

            def kcols(t, off, kind, a, b_):
                if kind == 16:
                    p0 = 2048 * a + 128 * b_ - off
                    return t[:, p0:p0 + 128]
                if kind == 4:
                    r4, n = a, b_
                    s_, m = n // 4, n % 4
                    base = 2048 * s_ - off
                    return t[:, base:base + 2048].rearrange("p (a r u) -> p a r u", a=4, r=4, u=128)[:, :, r4, 32 * m:32 * m + 32]
                s_, m = a, b_
                base = 2048 * s_ - off
                return t[:, base:base + 2048].rearrange("p (r m u) -> p r m u", r=16, m=16, u=8)[:, :, m, :]

            def head_setup(h):
                wd, wdb, wds = wdring.next()
                P.dma("gpsimd", wd[:, :, :], wd_in[h].rearrange("(k p) n -> p k n", p=128), wds, (), [wdb])
                return wd, wdb

            def proj_gen(h, wd, wdb):
                dq, dk = dq2[h % 2], dk2[h % 2]
                DBq = {"q": DBQ[h % 2], "k": DBK[h % 2]}
                for blk in range(8):
                    xb, xbb = load_xp(blk)
                    cols = slice(blk * 512, (blk + 1) * 512)
                    todo = [(128, 288, dk[:, cols], "k")]
                    if blk >= 3:
                        todo.append((0, 256, dq[:, (blk - 3) * 512:(blk - 2) * 512], "q"))
                    for ti, (c0, cs, dst, hb) in enumerate(todo):
                        bk, bkb = nb()
                        with P.group("tensor"):
                            for kc in range(KC):
                                mm(bk[:, :], wd[:, kc, c0:c0 + 128], xb[:, kc, :], kc == 0, kc == KC - 1, [xbb, wdb], [bkb])
                        bs_, bsb_ = nb()
                        with P.group("tensor"):
                            for kc in range(KC):
                                mm(bs_[0:32, :], wd[:, kc, cs:cs + 32], xb[:, kc, :], kc == 0, kc == KC - 1, [xbb, wdb], [bsb_])
                        cp("scalar", dst, bk[:, :], [bkb], [DBq[hb]])
                        tt("vector", rt1[ti][:, :], bk[0:32, :], cosT[:, cols], ALU.mult, [bkb, ROT], [RTB[ti][0]])
                        tt("vector", rt2[ti][:, :], bs_[0:32, :], sinT[:, cols], ALU.mult, [bsb_, ROT], [RTB[ti][1]])
                        P.op("vector", lambda e, dst=dst, ti=ti: e.tensor_tensor(out=dst[0:32], in0=rt1[ti][:, :], in1=rt2[ti][:, :], op=ALU.add),
                             [RTB[ti][0], RTB[ti][1]], [DBq[hb]])
                    yield blk

            vdone = set()
            vsems = {16: P.new_dsem(), 4: P.new_dsem(), 1: P.new_dsem()}

            def vload(h, kind_):
                if (h, kind_) in vdone:
                    return
                vdone.add((h, kind_))
                hc = slice(h * 128, (h + 1) * 128)
                dst, src = {16: (V16, vd5), 4: (V4, vd4_scr), 1: (V1, vd1_scr)}[kind_]
                P.dma("gpsimd", dst[:, :, :], src[:, hc].rearrange("(t p) c -> p t c", p=128), vsems[kind_], [VDB], [DBV[kind_]])

            def post_proj(h):
                dq, dk = dq2[h % 2], dk2[h % 2]
                DBq = {"q": DBQ[h % 2], "k": DBK[h % 2]}
                for kind_ in (16, 4, 1):
                    vload(h, kind_)
                for s_ in range(2):
                    srck = dk[:, 2048 * s_:2048 * s_ + 2048]
                    for r4 in range(4):
                        P.op("vector" if r4 % 2 == 0 else "gpsimd", lambda e, s_=s_, r4=r4, srck=srck: e.tensor_copy(
                            out=dk4[:, (r4 * 8 + s_ * 4) * 128:(r4 * 8 + s_ * 4 + 4) * 128].rearrange("p (m a u) -> p m a u", m=4, a=4, u=32),
                            in_=srck.rearrange("p (a r m u) -> p r m a u", a=4, r=4, m=4, u=32)[:, r4]), [DBq["k"]], [DB["k4"]])
                    P.op("vector", lambda e, s_=s_, srck=srck: e.tensor_copy(
                        out=dk1[:, 2048 * s_:2048 * s_ + 2048].rearrange("p (m r u) -> p m r u", m=16, r=16, u=8),
                        in_=srck.rearrange("p (r m u) -> p m r u", r=16, m=16, u=8)), [DBq["k"]], [DB["k1"]])
                srcq = dq[:, 512:2560]
                for r4 in range(4):
                    P.op("scalar", lambda e, r4=r4: e.activation(
                        out=dq4[:, r4 * 512:(r4 + 1) * 512].rearrange("p (m a u) -> p m a u", m=4, a=4, u=32),
                        in_=srcq.rearrange("p (a r m u) -> p r m a u", a=4, r=4, m=4, u=32)[:, r4], func=AF.Copy), [DBq["q"]], [DB["q4"]])
                P.op("scalar", lambda e: e.activation(
                    out=dq1[:, :].rearrange("p (m r u) -> p m r u", m=16, r=16, u=8),
                    in_=srcq.rearrange("p (r m u) -> p m r u", r=16, m=16, u=8), func=AF.Copy), [DBq["q"]], [DB["q1"]])
                P.op("scalar", lambda e: e.activation(
                    out=hq1[:, :].rearrange("p (r u) -> p r u", r=2),
                    in_=dq[:, 256:512].rearrange("p (r u) -> p r u", r=2)[:, :, 120:128], func=AF.Copy), [DBq["q"]], [DB["q1"]])


            def att_gen(h):
                dq, dk = dq2[h % 2], dk2[h % 2]
                DBq = {"q": DBQ[h % 2], "k": DBK[h % 2]}

                def kap(kb):
                    kind, a_, b_ = kb
                    if kind == 16:
                        p0 = 2048 * a_ + 128 * b_
                        return dk[:, p0:p0 + 128], DBq["k"]
                    if kind == 4:
                        p0 = (a_ * 8 + b_) * 128
                        return dk4[:, p0:p0 + 128], DB["k4"]
                    p0 = (16 * a_ + b_) * 128
                    return dk1[:, p0:p0 + 128], DB["k1"]
                mset("gpsimd", acc[:, :], 0.0, [DB["acc"]])
                mset("gpsimd", dacc[:, :], 0.0, [DB["acc"]])
                blocks = []
                for r in range(16):
                    blocks.append((16, (kcols(dq, 1536, 16, 1, r), DBq["q"]), kcols(acc, 1536, 16, 1, r), kcols(dacc, 1536, 16, 1, r), 128, None,
                                   [((16, 0, r), V16[:, r, :], masksc[:, 0, :]), ((16, 1, r), V16[:, 16 + r, :], masks[:, 1, :])]))
                for r in (14, 15):
                    blocks.append((16, (kcols(dq, 1536, 16, 0, r), DBq["q"]), kcols(acc, 1536, 16, 0, r), kcols(dacc, 1536, 16, 0, r), 128, None,
                                   [((16, 0, r), V16[:, r, :], masksc[:, 1, :])]))
                for r4 in range(4):
                    for n in range(4, 8):
                        pm = masksc[:, 2, :] if n == 4 else masks[:, 2, :]
                        blocks.append((4, (dq4[:, (r4 * 4 + n - 4) * 128:(r4 * 4 + n - 3) * 128], DB["q4"]), kcols(acc, 1536, 4, r4, n), kcols(dacc, 1536, 4, r4, n), 128, [4, 32],
                                       [((4, r4, n - 1), V4[:, r4 * 8 + n - 1, :], pm), ((4, r4, n), V4[:, r4 * 8 + n, :], masks[:, 3, :])]))
                for r4 in (2, 3):
                    p0 = (12 + r4) * 128 + 96 - 1536
                    blocks.append((4, (dq[:, p0:p0 + 32], DBq["q"]), acc[:, p0:p0 + 32], dacc[:, p0:p0 + 32], 32, None,
                                   [((4, r4, 2), V4[:, r4 * 8 + 2, :], masksc[:, 2, 96:128]), ((4, r4, 3), V4[:, r4 * 8 + 3, :], masksc[:, 3, 96:128])]))
                for m in range(16):
                    pk = (1, 0, 15) if m == 0 else (1, 1, m - 1)
                    pm = masksc[:, 4, :] if m == 0 else masks[:, 4, :]
                    blocks.append((1, (dq1[:, m * 128:(m + 1) * 128], DB["q1"]), kcols(acc, 1536, 1, 1, m), kcols(dacc, 1536, 1, 1, m), 128, [16, 8],
                                   [(pk, V1[:, 16 * pk[1] + pk[2], :], pm), ((1, 1, m), V1[:, 16 + m, :], masks[:, 5, :])]))
                hq = lambda t: t[:, 14 * 128 - 1536:16 * 128 - 1536].rearrange("p (r u) -> p r u", r=2)[:, :, 120:128]
                blocks.append((1, (hq1[:, :], DB["q1"]), hq(acc), hq(dacc), 16, [2, 8],
                               [((1, 0, 14), V1[:, 14, :], masksc[:, 4, 112:128]), ((1, 0, 15), V1[:, 15, :], masksc[:, 5, 112:128])]))
                def a0(i, c):
                    kind, (qap, qbuf), accap, daccap, nq, qshape, keys = blocks[i]
                    c["bsc"], c["bscb"] = nb(True)
                    for ki, (kb, vt, mk) in enumerate(keys):
                        ka, kbuf = kap(kb)
                        mm(c["bsc"][:, ki * 128:ki * 128 + nq], ka, qap, True, True, [kbuf, qbuf], [c["bscb"]])

                def a1(i, c):
                    kind, (qap, qbuf), accap, daccap, nq, qshape, keys = blocks[i]
                    pi = i % 4
                    nk = len(keys)
                    act(Pm[pi][:, 0:nk, 0:nq], c["bsc"][:, 0:nk * 128].rearrange("p (a b) -> p a b", a=nk)[:, :, 0:nq], AF.Exp, [c["bscb"]], [PmB[pi]], scale=SC)
                    rel(c["bscb"])
                    for ki, (kb, vt, mk) in enumerate(keys):
                        tt("gpsimd", Pm[pi][:, ki, 0:nq], Pm[pi][:, ki, 0:nq], mk, ALU.mult, [CB], [PmB[pi]])

                def a2(i, c):
                    kind, (qap, qbuf), accap, daccap, nq, qshape, keys = blocks[i]
                    pi = i % 4
                    nk = len(keys)
                    c["bo"], c["bob"] = nb(True)
                    with P.group("tensor"):
                        for ki, (kb, vt, mk) in enumerate(keys):
                            mm(c["bo"][:, 0:nq], vt, Pm[pi][:, ki, 0:nq], ki == 0, ki == nk - 1, [DBV[kind], PmB[pi]], [c["bob"]])
                    with P.group("tensor"):
                        for ki, (kb, vt, mk) in enumerate(keys):
                            mm(c["bo"][:, 128:128 + nq], ones_bf[:, :], Pm[pi][:, ki, 0:nq], ki == 0, ki == nk - 1, [CB, PmB[pi]], [c["bob"]])

                def a3(i, c):
                    kind, (qap, qbuf), accap, daccap, nq, qshape, keys = blocks[i]
                    o_in = c["bo"][:, 0:nq]
                    d_in = c["bo"][:, 128:128 + nq]
                    if qshape is not None:
                        o_in = o_in.rearrange("p (a b) -> p a b", a=qshape[0])
                        d_in = d_in.rearrange("p (a b) -> p a b", a=qshape[0])
                    tt("vector", accap, accap, o_in, ALU.add, [c["bob"]], [DB["acc"]])
                    tt("vector", daccap, daccap, d_in, ALU.add, [c["bob"]], [DB["acc"]])
                    rel(c["bob"])
                for t_ in pipeline_gen(len(blocks), [a0, a1, a2, a3]):
                    yield t_
                ts("vector", dacc[:, 256:2560], dacc[:, 256:2560], 1e-30, None, ALU.max, None, [], [DB["acc"]])
                act(dacc[:, 256:2560], dacc[:, 256:2560], AF.Ln, [], [DB["acc"]])
                act(dacc[:, 256:2560], dacc[:, 256:2560], AF.Exp, [], [DB["acc"]], scale=-1.0)
                P.op("vector", lambda e: e.tensor_tensor(out=odb[:, 2:TQ].rearrange("p (u r) -> p r u", r=16),
                                                         in0=acc[:, 512:2560].rearrange("p (r u) -> p r u", r=16),
                                                         in1=dacc[:, 512:2560].rearrange("p (r u) -> p r u", r=16), op=ALU.mult), [DB["acc"]], [DB["od"]])
                tt("vector", odb[:, 0:1], acc[:, 383:384], dacc[:, 383:384], ALU.mult, [DB["acc"]], [DB["od"]])
                tt("vector", odb[:, 1:2], acc[:, 511:512], dacc[:, 511:512], ALU.mult, [DB["acc"]], [DB["od"]])
                P.dma("sync", mixT[(8 + h) * 128:(9 + h) * 128, :], odb[:, :], odsem, [DB["od"]], [MIXB])
                yield -1

            prev_att = None
            wd_next = head_setup(0)
            for h in range(ndil):
                wd, wdb = wd_next
                pg = proj_gen(h, wd, wdb)
                nst = 0
                for _blk in pg:
                    if prev_att is not None:
                        for _ in range(8):
                            if next(prev_att, None) is None:
                                break
                            nst += 1
                            if nst == 23:
                                vload(h, 16)
                            if nst == 41:
                                vload(h, 4)
                if h + 1 < ndil:
                    wd_next = head_setup(h + 1)
                if prev_att is not None:
                    for _ in prev_att:
                        pass
                post_proj(h)
                prev_att = att_gen(h)
            for _ in prev_att:
                pass
            P.flush()
        P.barrier()

        def gemm_ln_phase(tag, w_dram, a_dram, a_cast, resid_dram, lnidx, out_dram, out_col0, chunks, ABUF, RBUF, OBUF):
            with ExitStack() as es:
                sb = lambda name, shape, dt: es.enter_context(nc.sbuf_tensor(tag + name, list(shape), dt))
                W = sb("W", [128, KC, D], BF16)
                WB = [Buf() for _ in range(4)]
                for g in range(4):
                    P.dma("gpsimd", W[:, :, g * 512:(g + 1) * 512], w_dram[:, g * 512:(g + 1) * 512].rearrange("(k p) n -> p k n", p=128), P.new_dsem(), (), [WB[g]])
                aring = Ring(P, [sb("a%d" % i, [128, KC, 512], BF16) for i in range(2)])
                ln_chunks(tag, sb, chunks, KC,
                          lambda dc, kc: W[:, kc, dc * 128:(dc + 1) * 128], WB,
                          a_dram, a_cast, aring, ABUF, resid_dram, RBUF, lnidx, out_dram, out_col0, OBUF)
                P.flush()
            P.barrier()

        def ln_chunks(tag, sb, chunks, nk, wfn, wbufs, a_dram, a_cast, aring, ABUF, resid_dram, RBUF, lnidx, out_dram, out_col0, OBUF, wstream=None):
            ny = 2
            y = [sb("y%d" % i, [128, KC, 512], F32) for i in range(ny)]
            YB = [Buf() for _ in range(ny)]
            s1 = [sb("s1_%d" % i, [128, 512], F32) for i in range(2)]
            s2 = [sb("s2_%d" % i, [128, 512], F32) for i in range(2)]
            SB1, SB2 = [Buf(), Buf()], [Buf(), Buf()]
            ysq = [sb("ysq%d" % i, [128, 512], F32) for i in range(2)]
            YSQ = [Buf(), Buf()]
            rres = Ring(P, [sb("res%d" % i, [128, 512], F32) for i in range(3)])
            mean = sb("mean", [128, 512], F32)
            rstd = sb("rstd", [128, 512], F32)
            MB = Buf()
            tn = [sb("tn%d" % i, [128, 512], F32) for i in range(2)]
            TN = [Buf(), Buf()]
            oring = Ring(P, [sb("o%d" % i, [128, 512], F32) for i in range(3)])
            pend = []
            for ci, (c0, c1) in enumerate(chunks):
                n = c1 - c0
                yi = ci % ny
                yc, ycb, s1c, s2c, S1B, S2B = y[yi], YB[yi], s1[ci % 2], s2[ci % 2], SB1[ci % 2], SB2[ci % 2]
                if ci == 0:
                    nxt = aring.next()
                    P.dma("gpsimd" if a_cast else "sync", nxt[0][:, 0:nk, 0:n], a_dram[:, c0:c1].rearrange("(k p) n -> p k n", p=128), nxt[2], [ABUF], [nxt[1]])
                a, ab, asem = nxt
                if ci + 1 < len(chunks):
                    d0, d1 = chunks[ci + 1]
                    nxt = aring.next()
                    P.dma("gpsimd" if a_cast else "sync", nxt[0][:, 0:nk, 0:d1 - d0], a_dram[:, d0:d1].rearrange("(k p) n -> p k n", p=128), nxt[2], [ABUF], [nxt[1]])

                def epi(dc, bk, bkb):
                    rt, rtb, rsem_ = rres.next()
                    P.dma("sync", rt[:, 0:n], resid_dram(dc, c0, c1), rsem_, [RBUF], [rtb])
                    stt("vector", yc[:, dc, 0:n], rt[:, 0:n], ALPHA, bk[:, 0:n], ALU.mult, ALU.add, [rtb, bkb], [ycb])
                    i2 = dc % 2
                    if dc == 0:
                        cp("vector", s1c[:, 0:n], yc[:, dc, 0:n], [ycb], [S1B])
                        act(s2c[:, 0:n], yc[:, dc, 0:n], AF.Square, [ycb], [S2B])
                    else:
                        tt("vector", s1c[:, 0:n], s1c[:, 0:n], yc[:, dc, 0:n], ALU.add, [ycb], [S1B])
                        act(ysq[i2][:, 0:n], yc[:, dc, 0:n], AF.Square, [ycb], [YSQ[i2]])
                        tt("gpsimd", s2c[:, 0:n], s2c[:, 0:n], ysq[i2][:, 0:n], ALU.add, [YSQ[i2]], [S2B])

                if wstream is None:
                    for dc in range(KC):
                        bk, bkb = nb()
                        with P.group("tensor"):
                            for kc in range(nk):
                                mm(bk[:, 0:n], wfn(dc, kc), a[:, kc, 0:n], kc == 0, kc == nk - 1, [ab, wbufs[dc // 4]], [bkb])
                        epi(dc, bk, bkb)
                else:
                    for qd_ in range(4):
                        acc4 = [nb(True) for _ in range(4)]
                        for j in range(nk):
                            wt, wtb = wstream(j, qd_)
                            with P.group("tensor"):
                                for i_ in range(4):
                                    mm(acc4[i_][0][:, 0:n], wt[:, i_ * 128:(i_ + 1) * 128], a[:, j, 0:n], j == 0, j == nk - 1, [ab, wtb], [acc4[i_][1]])
                        for i_ in range(4):
                            epi(4 * qd_ + i_, acc4[i_][0], acc4[i_][1])
                            rel(acc4[i_][1])
                def part2(n=n, c0=c0, c1=c1, yc=yc, ycb=ycb, s1c=s1c, s2c=s2c, S1B=S1B, S2B=S2B):
                    bm, bmb = nb()
                    mm(bm[:, 0:n], ones2048[:, :], s1c[:, 0:n], True, True, [CB, S1B], [bmb])
                    bm2, bm2b = nb()
                    mm(bm2[:, 0:n], ones2048[:, :], s2c[:, 0:n], True, True, [CB, S2B], [bm2b])
                    cp("scalar", mean[:, 0:n], bm[:, 0:n], [bmb], [MB])
                    tt("vector", rstd[:, 0:n], mean[:, 0:n], mean[:, 0:n], ALU.mult, [MB], [MB])
                    tt("vector", rstd[:, 0:n], bm2[:, 0:n], rstd[:, 0:n], ALU.subtract, [bm2b], [MB])
                    act(rstd[:, 0:n], rstd[:, 0:n], AF.Ln, [], [MB], bias=LN_EPS)
                    act(rstd[:, 0:n], rstd[:, 0:n], AF.Exp, [], [MB], scale=-0.5)
                    for dc in range(KC):
                        i2 = dc % 2
                        tt("vector", tn[i2][:, 0:n], yc[:, dc, 0:n], mean[:, 0:n], ALU.subtract, [ycb, MB], [TN[i2]])
                        tt("gpsimd", tn[i2][:, 0:n], tn[i2][:, 0:n], rstd[:, 0:n], ALU.mult, [MB], [TN[i2]])
                        ot, otb, osem_ = oring.next()
                        act(ot[:, 0:n], tn[i2][:, 0:n], AF.Identity, [TN[i2], CB], [otb], scale=lnp[:, 2 * lnidx, dc:dc + 1], bias=lnp[:, 2 * lnidx + 1, dc:dc + 1])
                        P.dma("scalar", out_dram[dc * 128:(dc + 1) * 128, c0 - out_col0:c1 - out_col0], ot[:, 0:n], osem_, [otb], [OBUF])
                if pend:
                    pend.pop()()
                pend.append(part2)
            pend.pop()()

        if stop == "B":
            finish()
            return nc
        X1B, OCB, X2B, HTB, OUTB = Buf(), Buf(), Buf(), Buf(), Buf()
        W2B = Buf()
        w2sem = P.new_dsem()
        NOB = Buf()
        xres = lambda dc, c0, c1: xT_nat[dc * 128:(dc + 1) * 128, 2046 + c0:2046 + c1]
        gemm_ln_phase("C", wout_in, mixT, False, xres, 0, x1T, 0, TCH, MIXB, NOB, X1B)
        if stop == "C":
            finish()
            return nc

        with ExitStack() as es:
            sb = lambda name, shape, dt: es.enter_context(nc.sbuf_tensor("D1" + name, list(shape), dt))
            mkT = sb("mkT", [128, 16, 256], BF16)
            mv = sb("mv", [128, 2, D], BF16)
            MKB, MVB, MTB = Buf(), Buf(), Buf()
            Wq = sb("Wq", [128, KC, D], BF16)
            WQB = Buf()
            wqs = P.new_dsem()
            es2 = ExitStack()
            sb2 = lambda name, shape, dt: es2.enter_context(nc.sbuf_tensor("D0" + name, list(shape), dt))
            mT = sb2("memT", [128, KC, 256], BF16)
            P.dma("gpsimd", mT[:, :, :], memT.rearrange("(k p) n -> p k n", p=128), P.new_dsem(), (), [MTB])
            wkvr = Ring(P, [sb2("wkv%d" % i, [128, KC, 512], BF16) for i in range(2)])
            for g in range(8):
                wt, wtb, wts = wkvr.next()
                P.dma("gpsimd", wt[:, :, :], wkv_in[:, g * 512:(g + 1) * 512].rearrange("(k p) n -> p k n", p=128), wts, (), [wtb])
                if g < 4:
                    for j in range(4):
                        bk, bkb = nb()
                        with P.group("tensor"):
                            for kc in range(KC):
                                mm(bk[:, 0:256], wt[:, kc, j * 128:(j + 1) * 128], mT[:, kc, :], kc == 0, kc == KC - 1, [wtb, MTB], [bkb])
                        cp("scalar", mkT[:, 4 * g + j, :], bk[:, 0:256], [bkb], [MKB])
                else:
                    for mt in range(2):
                        bk, bkb = nb()
                        with P.group("tensor"):
                            for kc in range(KC):
                                mm(bk[:, :], mT[:, kc, mt * 128:(mt + 1) * 128], wt[:, kc, :], kc == 0, kc == KC - 1, [wtb, MTB], [bkb])
                        cp("vector", mv[:, mt, (g - 4) * 512:(g - 3) * 512], bk[:, :], [bkb], [MVB])
            for g in range(4):
                P.dma("gpsimd", Wq[:, :, g * 512:(g + 1) * 512], wq_in[:, g * 512:(g + 1) * 512].rearrange("(k p) n -> p k n", p=128), wqs, (), [WQB])
            for g in range(8):
                P.dma("gpsimd", w2b[g * 688:(g + 1) * 688, :], fwout_in[g * 688:(g + 1) * 688, :], w2sem, (), [W2B])
            P.flush()
            es2.close()
            P.barrier()
            aring = Ring(P, [sb("a%d" % i, [128, KC, 512], BF16) for i in range(2)])
            qc = sb("qc", [128, KC, 512], BF16)
            QCB = Buf()
            ocr = Ring(P, [sb("oc%d" % i, [128, KC, 512], BF16) for i in range(2)])
            Pc = [sb("Pc%d" % i, [128, 2, 512], BF16) for i in range(2)]
            PCB = [Buf(), Buf()]
            rden = [sb("rden%d" % i, [128, 512], F32) for i in range(2)]
            RDB = [Buf(), Buf()]
            SCC = 512.0 ** -0.5
            for (c0, c1) in TCH:
                n = c1 - c0
                a, ab, asem = aring.next()
                P.dma("gpsimd", a[:, :, 0:n], x1T[:, c0:c1].rearrange("(k p) n -> p k n", p=128), asem, [X1B], [ab])
                for dc in range(KC):
                    bk, bkb = nb()
                    with P.group("tensor"):
                        for kc in range(KC):
                            mm(bk[:, 0:n], Wq[:, kc, dc * 128:(dc + 1) * 128], a[:, kc, 0:n], kc == 0, kc == KC - 1, [ab, WQB], [bkb])
                    cp("scalar" if dc % 2 == 0 else "vector", qc[:, dc, 0:n], bk[:, 0:n], [bkb], [QCB])
                oc, ocb, ocs = ocr.next()
                for hh in range(4):
                    i2 = hh % 2
                    for mt in range(2):
                        bsx, bsxb = nb()
                        with P.group("tensor"):
                            for c in range(4):
                                mm(bsx[:, 0:n], mkT[:, 4 * hh + c, mt * 128:(mt + 1) * 128], qc[:, 4 * hh + c, 0:n], c == 0, c == 3, [MKB, QCB], [bsxb])
                        act(Pc[i2][:, mt, 0:n], bsx[:, 0:n], AF.Exp, [bsxb], [PCB[i2]], scale=SCC)
                    bd, bdb = nb()
                    with P.group("tensor"):
                        for mt in range(2):
                            mm(bd[:, 0:n], ones_bf[:, :], Pc[i2][:, mt, 0:n], mt == 0, mt == 1, [CB, PCB[i2]], [bdb])
                    recip("vector", rden[i2][:, 0:n], bd[:, 0:n], [bdb], [RDB[i2]])
                    for c in range(4):
                        bo, bob = nb()
                        with P.group("tensor"):
                            for mt in range(2):
                                mm(bo[:, 0:n], mv[:, mt, (4 * hh + c) * 128:(4 * hh + c + 1) * 128], Pc[i2][:, mt, 0:n], mt == 0, mt == 1, [MVB, PCB[i2]], [bob])
                        tt("vector", oc[:, 4 * hh + c, 0:n], bo[:, 0:n], rden[i2][:, 0:n], ALU.mult, [bob, RDB[i2]], [ocb])
                P.dma("sync", ocT[:, c0:c1].rearrange("(k p) n -> p k n", p=128), oc[:, :, 0:n], ocs, [ocb], [OCB])
            P.flush()
        P.barrier()

        if stop == "D1":
            finish()
            return nc
        x1res = lambda dc, c0, c1: x1T[dc * 128:(dc + 1) * 128, c0:c1]
        gemm_ln_phase("D2", wo_in, ocT, False, x1res, 1, x2T, 0, TCH, OCB, X1B, X2B)
        if stop == "D2":
            finish()
            return nc

        with ExitStack() as es:
            sb = lambda name, shape, dt: es.enter_context(nc.sbuf_tensor("E" + name, list(shape), dt))
            x2b = sb("x2b", [128, KC, TQ], BF16)
            X2S = Buf()
            xs = P.new_dsem()
            for (c0, c1) in TCH[1:]:
                P.dma("gpsimd", x2b[:, :, c0:c1], x2T[:, c0:c1].rearrange("(k p) n -> p k n", p=128), xs, [X2B], [X2S])
            P.dma("gpsimd", x2b[:, :, 0:2], x2T[:, 0:2].rearrange("(k p) n -> p k n", p=128), xs, [X2B], [X2S])
            ts("vector", x2b[:, :, 0:2], x2b[:, :, 0:2], flag[:, 0:1], None, ALU.mult, None, [CB], [X2S])
            wr = Ring(P, [sb("w%d" % i, [128, KC, 256], BF16) for i in range(3)])
            ug = [sb("ug%d" % i, [128, TQ], F32) for i in range(2)]
            uu = [sb("uu%d" % i, [128, TQ], F32) for i in range(2)]
            yg = [sb("yg%d" % i, [128, 2048], F32) for i in range(2)]
            yu = [sb("yu%d" % i, [128, 2048], F32) for i in range(2)]
            hr = Ring(P, [sb("h%d" % i, [128, 2048], BF16) for i in range(2)])
            UG, UU, YG, YU = [Buf(), Buf()], [Buf(), Buf()], [Buf(), Buf()], [Buf(), Buf()]
            for j in range(NJ):
                i2 = j % 2
                wt, wtb, wts = wr.next()
                P.dma("gpsimd", wt[:, :, 0:128], fwin_in[:, j * 128:(j + 1) * 128].rearrange("(k p) n -> p k n", p=128), wts, (), [wtb])
                P.dma("gpsimd", wt[:, :, 128:256], fwin_in[:, DFF + j * 128:DFF + (j + 1) * 128].rearrange("(k p) n -> p k n", p=128), wts, (), [wtb])
                for part, (ubuf, UB) in enumerate([(ug[i2], UG[i2]), (uu[i2], UU[i2])]):
                    for ci, (c0, c1) in enumerate(TCH):
                        n = c1 - c0
                        bk, bkb = nb()
                        with P.group("tensor"):
                            for kc in range(KC):
                                mm(bk[:, 0:n], wt[:, kc, part * 128:(part + 1) * 128], x2b[:, kc, c0:c1], kc == 0, kc == KC - 1, [wtb, X2S], [bkb])
                        cp("scalar", ubuf[:, c0:c1], bk[:, 0:n], [bkb], [UB])
                for part, (eng, ubuf, UB, ybuf, YB_) in enumerate([("vector", ug[i2], UG[i2], yg[i2], YG[i2]), ("gpsimd", uu[i2], UU[i2], yu[i2], YU[i2])]):
                    cj = part * NJ + j
                    act(ybuf[:, :], ubuf[:, 2:TQ], AF.Identity, [UB, CB], [YB_], scale=cw[:, cj, 2:3], bias=cb[:, cj:cj + 1])
                    stt("vector", ybuf[:, :], ubuf[:, 1:TQ - 1], cw[:, cj, 1:2], ybuf[:, :], ALU.mult, ALU.add, [UB, CB], [YB_])
                    stt("vector", ybuf[:, :], ubuf[:, 0:TQ - 2], cw[:, cj, 0:1], ybuf[:, :], ALU.mult, ALU.add, [UB, CB], [YB_])
                act(yg[i2][:, :], yg[i2][:, :], AF.Silu, [], [YG[i2]])
                ht, htb, hts = hr.next()
                tt("vector", ht[:, :], yg[i2][:, :], yu[i2][:, :], ALU.mult, [YG[i2], YU[i2]], [htb])
                P.dma("sync", hT[j * 128:(j + 1) * 128, :], ht[:, :], hts, [htb], [HTB])
            P.flush()
        P.barrier()

        if stop == "E":
            finish()
            return nc
        with ExitStack() as es:
            sb = lambda name, shape, dt: es.enter_context(nc.sbuf_tensor("F" + name, list(shape), dt))
            aring = Ring(P, [sb("a%d" % i, [128, NJ, 512], BF16) for i in range(2)])
            w2r = Ring(P, [sb("w%d" % i, [128, 512], BF16) for i in range(8)])

            def wstream(j, qd_):
                wt, wtb, wts = w2r.next()
                P.dma("sync", wt[:, :], w2b[j * 128:(j + 1) * 128, qd_ * 512:(qd_ + 1) * 512], wts, [W2B], [wtb])
                return wt, wtb
            x2res = lambda dc, c0, c1: x2T[dc * 128:(dc + 1) * 128, 2 + c0:2 + c1]
            ln_chunks("F", sb, [(i * 512, (i + 1) * 512) for i in range(4)], NJ, None, None,
                      hT, False, aring, HTB, x2res, X2B, 2, outT, 0, OUTB, wstream=wstream)
            final = [(s[0], s[1], "dma") for s in P.dsems if s[1] > 0]
            P.flush(final)
    return nc


def _perm_idx():
    idx = np.empty(NL, np.int64)
    for s in range(2):
        for r in range(16):
            idx[s * 2048 + r * 128:s * 2048 + (r + 1) * 128] = s * 2048 + 16 * np.arange(128) + r
    return idx


def _masks():
    m = np.zeros((128, 6, 128), np.float32)
    j = np.arange(128)[:, None]
    i = np.arange(128)[None, :]
    for pi_, nat in enumerate([lambda x: x, lambda x: 4 * (x % 32) + x // 32, lambda x: 16 * (x % 8) + x // 8]):
        nj, ni = nat(j), nat(i)
        m[:, 2 * pi_, :] = (nj >= ni)
        m[:, 2 * pi_ + 1, :] = (nj <= ni)
    return m.reshape(128, 768)


def _fm(v, nchunk):
    return np.ascontiguousarray(v.reshape(nchunk, 128).T)


_CACHE = {}


def make_in_maps(x, mem, positions, w_in, gla_gate_w2, gla_gate_b, gla_norm_g, w_out, ln1_g, ln1_b,
                 ca_wq, ca_wkv, ca_wo, ln2_g, ln2_b, ffn_w_in, ffn_conv_w, ffn_conv_b, ffn_w_out, ln3_g, ln3_b):
    f32 = np.float32
    x = np.asarray(x, f32)
    mem = np.asarray(mem, f32)
    positions = np.asarray(positions, np.int32)
    w_in = np.asarray(w_in, f32)[0]
    pidx = _perm_idx()
    o = 0
    cols = {}
    for name, w in zip(["qg", "kg", "vg", "rg", "glr", "qd", "kd", "vd"], [512, 512, 1024, 1024, 16, 1024, 1024, 1024]):
        cols[name] = w_in[:, o:o + w]
        o += w
    wg = np.stack([np.concatenate([cols["qg"][:, h * 128:(h + 1) * 128], cols["kg"][:, h * 128:(h + 1) * 128],
                                   cols["rg"][:, h * 256:(h + 1) * 256], cols["vg"][:, h * 256:(h + 1) * 256]], axis=1) for h in range(4)])
    swap = np.concatenate([np.arange(16, 32), np.arange(0, 16)])
    wd = np.stack([np.concatenate([cols["qd"][:, h * 128:(h + 1) * 128], cols["kd"][:, h * 128:(h + 1) * 128],
                                   cols["qd"][:, h * 128 + swap], cols["kd"][:, h * 128 + swap]], axis=1) for h in range(8)])
    w2aug = np.concatenate([np.asarray(gla_gate_w2, f32)[0], np.asarray(gla_gate_b, f32)[0][None, :]], axis=0)
    lnp = np.stack([_fm(np.asarray(v, f32)[0], 16) for v in [ln1_g, ln1_b, ln2_g, ln2_b, ln3_g, ln3_b]], axis=1).reshape(128, 96)
    convw = np.ascontiguousarray(np.asarray(ffn_conv_w, f32)[0].T.reshape(86, 128, 3).transpose(1, 0, 2)).reshape(128, 258)
    convb = _fm(np.asarray(ffn_conv_b, f32)[0], 86)
    jj = np.arange(128)[:, None]
    ii = np.arange(128)[None, :]
    uneg = np.where(jj <= ii, f32(-1.0 / 16.0), f32(0.0)).astype(f32)
    invf = (500000.0 ** (-(np.arange(0, 32, 2, dtype=np.float32)) / 32.0)).astype(f32)
    rotc = np.stack([np.concatenate([invf, invf]), np.concatenate([-np.ones(16, f32), np.ones(16, f32)])], axis=1).astype(f32)
    shared = dict(masks=_masks(), uneg=uneg, rotc=rotc, wg=np.ascontiguousarray(wg), wglr=np.ascontiguousarray(cols["glr"]),
                  w2aug=np.ascontiguousarray(w2aug), glag=_fm(np.asarray(gla_norm_g, f32)[0], 8), wd=np.ascontiguousarray(wd),
                  wvd=np.ascontiguousarray(cols["vd"]), w_out=np.asarray(w_out, f32)[0], lnp=np.ascontiguousarray(lnp),
                  ca_wq=np.asarray(ca_wq, f32)[0], ca_wkv=np.asarray(ca_wkv, f32)[0], ca_wo=np.asarray(ca_wo, f32)[0],
                  ffn_w_in=np.asarray(ffn_w_in, f32)[0], convw=convw, convb=convb, ffn_w_out=np.asarray(ffn_w_out, f32)[0])
    in_maps = []
    for c in range(8):
        b, hf = c // 2, c % 2
        xl = np.zeros((NL, D), f32)
        pl = np.zeros((NL,), np.int32)
        if hf == 1:
            xl[:] = x[b]
            pl[:] = positions[b]
        else:
            xl[2048:] = x[b, :2048]
            pl[2048:] = positions[b, :2048]
        xTn = np.ascontiguousarray(xl.T)
        m = dict(shared)
        m.update(xT_nat=xTn, xT_perm=np.ascontiguousarray(xTn[:, pidx]), memT=np.ascontiguousarray(mem[b].T),
                 pos_perm=np.ascontiguousarray(pl[pidx][None, :]), flag=np.full((128, 1), float(hf), f32))
        in_maps.append(m)
    return in_maps


def kernel(**inputs):
    if "nc" not in _CACHE:
        _CACHE["nc"] = build(False)
    nc = _CACHE["nc"]
    in_maps = make_in_maps(**inputs)
    res = run_bass_kernel_spmd(nc, in_maps, core_ids=list(range(8)))
    out = np.empty((4, 4096, D), np.float32)
    for c in range(8):
        b, hf = c // 2, c % 2
        out[b, hf * 2048:(hf + 1) * 2048, :] = np.asarray(res.results[c]["outT"]).T
    return out
```

```python
import math
from contextlib import ExitStack, contextmanager
import numpy as np
import concourse.bass as bass
import concourse.mybir as mybir
from concourse.bass_utils import run_bass_kernel_spmd

F32 = mybir.dt.float32
BF16 = mybir.dt.bfloat16
I32 = mybir.dt.int32
AF = mybir.ActivationFunctionType
ALU = mybir.AluOpType

D = 2048
KC = 16
NL = 4096
TQ = 2050
QN = 2176
DFF = 5504
NJ = 43
ALPHA = 2.0 ** 0.25
LN_EPS = 1e-5
TCH = [(0, 2), (2, 514), (514, 1026), (1026, 1538), (1538, 2050)]
ENGS = ["tensor", "vector", "scalar", "gpsimd", "sync"]
PI = math.pi


class Buf:
    __slots__ = ("w", "r", "excl")

    def __init__(self, excl=False):
        self.w = None
        self.r = {}
        self.excl = excl


class Prog:
    def __init__(self, nc, es):
        self.nc = nc
        self.q = {e: [] for e in ENGS}
        self.cnt = {e: 0 for e in ENGS}
        self.sem = {e: es.enter_context(nc.semaphore("s_" + e)) for e in ENGS}
        self.grp = {e: None for e in ENGS}
        self.es = es
        self.dsems = []
        self.bar = {e: [] for e in ENGS}
        self.waited = {e: {} for e in ENGS}

    def _deps(self, reads, writes, extra):
        d = [x for x in extra if x is not None]
        if any(b.excl for b in reads):
            writes = list(writes) + [b for b in reads if b.excl]
            reads = [b for b in reads if not b.excl]
        for b in reads:
            if b.w is not None:
                d.append(b.w)
        for b in writes:
            if b.w is not None:
                d.append(b.w)
            d.extend(b.r.values())
        return d

    def _mark(self, tok, reads, writes):
        if any(b.excl for b in reads):
            writes = list(writes) + [b for b in reads if b.excl]
            reads = [b for b in reads if not b.excl]
        for b in reads:
            b.r[id(tok[0])] = tok
        for b in writes:
            b.w = tok
            b.r = {}

    def op(self, eng, fn, reads=(), writes=(), deps=()):
        d = self._deps(reads, writes, deps)
        if self.bar[eng]:
            d.extend(self.bar[eng])
            self.bar[eng] = []
        if eng == "tensor":
            d = [t for t in d if t[2] != "tensor"]
        if self.grp[eng] is not None:
            tok = self.grp[eng]
            d = [t for t in d if not (t[0] is tok[0] and t[1] == tok[1])]
            self.q[eng].append(["op", fn, d, False])
        else:
            self.cnt[eng] += 1
            tok = (self.sem[eng], self.cnt[eng], eng)
            self.q[eng].append(["op", fn, d, True])
        self._mark(tok, reads, writes)
        return tok

    @contextmanager
    def group(self, eng):
        tok = (self.sem[eng], self.cnt[eng] + 1, eng)
        self.grp[eng] = tok
        n0 = len(self.q[eng])
        yield tok
        self.grp[eng] = None
        assert len(self.q[eng]) > n0
        self.q[eng][-1][3] = True
        self.cnt[eng] += 1

    def _raw_dsem(self):
        s = [self.es.enter_context(self.nc.semaphore("d%d" % len(self.dsems))), 0]
        self.dsems.append(s)
        return s

    def new_dsem(self):
        return {}

    def dma(self, eng, out, in_, sem, reads=(), writes=(), deps=()):
        d = self._deps(reads, writes, deps)
        if self.bar[eng]:
            d.extend(self.bar[eng])
            self.bar[eng] = []
        if eng not in sem:
            sem[eng] = self._raw_dsem()
        sem = sem[eng]
        sem[1] += 16
        tok = (sem[0], sem[1], "dma")
        self.q[eng].append(["dma", (out, in_, sem[0]), d, True])
        self._mark(tok, reads, writes)
        return tok

    def barrier(self):
        toks = [(self.sem[e], self.cnt[e], "bar") for e in ENGS if self.cnt[e] > 0]
        toks += [(s[0], s[1], "dma") for s in self.dsems if s[1] > 0 and not (len(s) > 2 and s[2])]
        for e in ENGS:
            self.bar[e] = list(toks)

    def flush(self, final=()):
        with self.nc.Block() as block:
            def mk(ename):
                def body(eng):
                    waited = self.waited[ename]
                    for kind, payload, deps, sig in self.q[ename]:
                        for (s, v, src) in deps:
                            key = id(s)
                            if waited.get(key, 0) >= v:
                                continue
                            eng.wait_ge(s, v)
                            waited[key] = v
                        if kind == "op":
                            ins = payload(eng)
                            if sig:
                                ins.then_inc(self.sem[ename], 1)
                        else:
                            out, in_, s = payload
                            eng.dma_start(out=out, in_=in_).then_inc(s, 16)
                    if ename == "sync":
                        for (s, v, src) in final:
                            eng.wait_ge(s, v)
                    self.q[ename] = []
                return body
            block.tensor(mk("tensor"))
            block.vector(mk("vector"))
            block.scalar(mk("scalar"))
            block.gpsimd(mk("gpsimd"))
            block.sync(mk("sync"))


def pipeline(n, stages):
    S = len(stages)
    ctx = [dict() for _ in range(n)]
    for t in range(n + S - 1):
        for s_ in reversed(range(S)):
            i = t - s_
            if 0 <= i < n:
                stages[s_](i, ctx[i])


def pipeline_gen(n, stages):
    S = len(stages)
    ctx = [dict() for _ in range(n)]
    for t in range(n + S - 1):
        for s_ in reversed(range(S)):
            i = t - s_
            if 0 <= i < n:
                stages[s_](i, ctx[i])
        yield t


class Ring:
    def __init__(self, P, tensors):
        self.t = tensors
        self.b = [Buf() for _ in tensors]
        self.s = [P.new_dsem() for _ in tensors]
        self.i = -1

    def next(self):
        self.i = (self.i + 1) % len(self.t)
        return self.t[self.i], self.b[self.i], self.s[self.i]


def build(debug=False, stop=None, ngla=4, ndil=8):
    nc = bass.Bass("TRN2", target_bir_lowering=False)
    din = lambda name, shape, dt=F32: nc.dram_tensor(name, list(shape), dt, kind="ExternalInput").ap()
    okind = "ExternalOutput" if debug else "Internal"
    dscr = lambda name, shape, dt: nc.dram_tensor(name, list(shape), dt, kind=okind).ap()
    xT_nat = din("xT_nat", [D, NL])
    xT_perm = din("xT_perm", [D, NL])
    memT = din("memT", [D, 256])
    pos_in = din("pos_perm", [1, NL], I32)
    flag_in = din("flag", [128, 1])
    masks_in = din("masks", [128, 6 * 128])
    uneg_in = din("uneg", [128, 128])
    rotc_in = din("rotc", [32, 2])
    wg_in = din("wg", [4, D, 768])
    wglr_in = din("wglr", [D, 16])
    w2aug_in = din("w2aug", [17, 512])
    glag_in = din("glag", [128, 8])
    wd_in = din("wd", [8, D, 320])
    wvd_in = din("wvd", [D, 1024])
    wout_in = din("w_out", [D, D])
    lnp_in = din("lnp", [128, 6 * 16])
    wq_in = din("ca_wq", [D, D])
    wkv_in = din("ca_wkv", [D, 2 * D])
    wo_in = din("ca_wo", [D, D])
    fwin_in = din("ffn_w_in", [D, 2 * DFF])
    cw_in = din("convw", [128, 86 * 3])
    cb_in = din("convb", [128, 86])
    fwout_in = din("ffn_w_out", [DFF, D])
    outT = nc.dram_tensor("outT", [D, 2048], F32, kind="ExternalOutput").ap()

    xb_nat = dscr("xb_nat", [D, NL], BF16)
    xb_perm = dscr("xb_perm", [D, NL], BF16)
    vd_scr = dscr("vd_scr", [NL, 1024], BF16)
    vd4_scr = dscr("vd4_scr", [NL, 1024], BF16)
    vd1_scr = dscr("vd1_scr", [NL, 1024], BF16)
    mixT = dscr("mixT", [D, TQ], BF16)
    x1T = dscr("x1T", [D, TQ], F32)
    ocT = dscr("ocT", [D, TQ], BF16)
    x2T = dscr("x2T", [D, TQ], F32)
    hT = dscr("hT", [DFF, 2048], BF16)
    w2b = dscr("w2b", [DFF, D], BF16)

    with ExitStack() as ges:
        P = Prog(nc, ges)
        gsb = lambda name, shape, dt: ges.enter_context(nc.sbuf_tensor(name, list(shape), dt))
        banks = [ges.enter_context(nc.psum_tensor("pb%d" % i, [128, 512], F32)) for i in range(7)]
        bankb = [Buf(True) for _ in range(7)]
        ptb = ges.enter_context(nc.psum_tensor("ptb", [128, 1024], BF16))
        ptb_b = Buf(True)
        bi = [0]

        def finish():
            final = [(s_[0], s_[1], "dma") for s_ in P.dsems if s_[1] > 0]
            final += [(P.sem[e_], P.cnt[e_], "bar") for e_ in ENGS if P.cnt[e_] > 0 and e_ != "sync"]
            P.flush(final)

        busy = set()

        def nb(reserve=False):
            for _ in range(8):
                bi[0] = (bi[0] + 1) % 7
                if bi[0] not in busy:
                    break
            else:
                raise RuntimeError("no free PSUM bank")
            if reserve:
                busy.add(bi[0])
            return banks[bi[0]], bankb[bi[0]]

        def rel(bb):
            busy.discard(bankb.index(bb))

        def mm(out, lhsT, rhs, start, stop, reads, writes):
            return P.op("tensor", lambda e: e.matmul(out, lhsT=lhsT, rhs=rhs, start=start, stop=stop), reads, writes)

        def act(out, in_, func, reads, writes, bias=None, scale=None):
            kw = {}
            if bias is not None:
                kw["bias"] = bias
            if scale is not None:
                kw["scale"] = scale
            return P.op("scalar", lambda e: e.activation(out=out, in_=in_, func=func, **kw), reads, writes)

        def tt(eng, out, in0, in1, op, reads, writes):
            return P.op(eng, lambda e: e.tensor_tensor(out=out, in0=in0, in1=in1, op=op), reads, writes)

        def ts(eng, out, in0, s1, s2, op0, op1, reads, writes):
            if op1 is None:
                return P.op(eng, lambda e: e.tensor_scalar(out=out, in0=in0, scalar1=s1, scalar2=None, op0=op0), reads, writes)
            return P.op(eng, lambda e: e.tensor_scalar(out=out, in0=in0, scalar1=s1, scalar2=s2, op0=op0, op1=op1), reads, writes)

        def stt(eng, out, in0, scalar, in1, op0, op1, reads, writes):
            return P.op(eng, lambda e: e.scalar_tensor_tensor(out=out, in0=in0, scalar=scalar, in1=in1, op0=op0, op1=op1), reads, writes)

        def cp(eng, out, in_, reads, writes):
            if eng == "scalar":
                return P.op(eng, lambda e: e.activation(out=out, in_=in_, func=AF.Copy), reads, writes)
            return P.op(eng, lambda e: e.tensor_copy(out=out, in_=in_), reads, writes)

        def recip(eng, out, in_, reads, writes):
            return P.op(eng, lambda e: e.reciprocal(out=out, in_=in_), reads, writes)

        def mset(eng, ap, val, writes):
            return P.op(eng, lambda e: e.memset(ap, val), (), writes)

        ident = gsb("ident", [128, 128], BF16)
        ones_bf = gsb("ones_bf", [128, 128], BF16)
        ones256 = gsb("ones256", [128, 128], F32)
        ones2048 = gsb("ones2048", [128, 128], F32)
        masks = gsb("masks_sb", [128, 6, 128], BF16)
        masksc = gsb("masksc_sb", [128, 6, 128], BF16)
        uneg = gsb("uneg_sb", [128, 128], F32)
        flag = gsb("flag_sb", [128, 1], F32)
        lnp = gsb("lnp_sb", [128, 6, 16], F32)
        glag = gsb("glag_sb", [128, 8], F32)
        cw = gsb("cw_sb", [128, 86, 3], F32)
        cb = gsb("cb_sb", [128, 86], F32)
        mtmp = gsb("mtmp", [128, 6, 128], F32)
        CB = Buf()
        csem = P.new_dsem()
        P.dma("sync", mtmp[:, :, :], masks_in.rearrange("p (a b) -> p a b", a=6), csem, (), [CB])
        P.dma("sync", uneg[:, :], uneg_in, csem, (), [CB])
        P.dma("sync", flag[:, :], flag_in, csem, (), [CB])
        P.dma("sync", lnp[:, :, :], lnp_in.rearrange("p (a b) -> p a b", a=6), csem, (), [CB])
        P.dma("sync", glag[:, :], glag_in, csem, (), [CB])
        P.dma("sync", cw[:, :, :], cw_in.rearrange("p (a b) -> p a b", b=3), csem, (), [CB])
        P.dma("sync", cb[:, :], cb_in, csem, (), [CB])
        mset("gpsimd", ident[:, :], 0.0, [CB])
        P.op("gpsimd", lambda e: e.affine_select(out=ident[:, :], in_=ident[:, :], pattern=[[-1, 128]],
                                                 compare_op=ALU.not_equal, fill=1.0, base=0, channel_multiplier=1), [CB], [CB])
        mset("gpsimd", ones_bf[:, :], 1.0, [CB])
        mset("gpsimd", ones256[:, :], 1.0 / 256.0, [CB])
        mset("gpsimd", ones2048[:, :], 1.0 / 2048.0, [CB])
        cp("vector", masks[:, :, :], mtmp[:, :, :], [CB], [CB])
        ts("vector", masksc[:, :, :], mtmp[:, :, :], flag[:, 0:1], None, ALU.mult, None, [CB], [CB])

        XN = [Buf() for _ in range(8)]
        XP = [Buf() for _ in range(8)]
        P.flush()
        if stop == "pre":
            finish()
            return nc

        MIXB = Buf()
        VDB = Buf()

        with ExitStack() as es:
            sb = lambda name, shape, dt: es.enter_context(nc.sbuf_tensor(name, list(shape), dt))
            xring = Ring(P, [sb("xblk%d" % i, [128, KC, 512], BF16) for i in range(2)])

            def load_x(src, srcbufs, blk):
                t, b, s = xring.next()
                P.dma("sync", t[:, :, :], src[:, blk * 512:(blk + 1) * 512].rearrange("(k p) n -> p k n", p=128), s, [srcbufs[blk]], [b])
                return t, b

            glrT = sb("glrT", [32, NL], F32)
            GLR = Buf()
            wglr = sb("wglr_sb", [128, KC, 16], BF16)
            w2aug = sb("w2aug_sb", [17, 512], F32)
            WS = Buf()
            wsem = P.new_dsem()
            P.dma("gpsimd", wglr[:, :, :], wglr_in.rearrange("(k p) n -> p k n", p=128), wsem, (), [WS])
            P.dma("sync", w2aug[:, :], w2aug_in, wsem, (), [WS])
            def load_first(ring, src32, dst16, dbufs, blk):
                t, b, s_ = ring.next()
                cols = slice(blk * 512, (blk + 1) * 512)
                P.dma("gpsimd", t[:, :, :], src32[:, cols].rearrange("(k p) n -> p k n", p=128), s_, (), [b])
                P.dma("sync", dst16[:, cols].rearrange("(k p) n -> p k n", p=128), t[:, :, :], s_, [b], [dbufs[blk]])
                return t, b
            mset("vector", glrT[:, :], 1.0, [GLR])
            for blk in range(8):
                xb, xbb = load_first(xring, xT_nat, xb_nat, XN, blk)
                bk, bkb = nb()
                with P.group("tensor"):
                    for kc in range(KC):
                        mm(bk[0:16, :], wglr[:, kc, :], xb[:, kc, :], kc == 0, kc == KC - 1, [xbb, WS], [bkb])
                cp("scalar", glrT[0:16, blk * 512:(blk + 1) * 512], bk[0:16, :], [bkb], [GLR])

            if stop == "glr":
                finish()
                return nc
            wgring = Ring(P, [sb("wg%d" % i, [128, KC, 768], BF16) for i in range(1)])
            qT = sb("g_qT", [128, QN], BF16)
            kT = sb("g_kT", [128, NL], BF16)
            rT = sb("g_rT", [128, 2, QN], BF16)
            vsb = sb("g_v", [128, 32, 256], BF16)
            enb = sb("g_enb", [128, NL], BF16)
            kef = sb("g_kef", [128, NL], BF16)
            ebq = sb("g_ebq", [128, QN], BF16)
            decay = sb("g_decay", [128, 32], F32)
            blast = sb("g_blast", [128, 32], F32)
            Sst = sb("g_S", [128, 256], F32)
            Sbf = [sb("g_Sbf%d" % i, [128, 256], BF16) for i in range(3)]
            og = sb("g_og", [128, 2, TQ], BF16)
            tA = [sb("g_tA%d" % i, [128, 128], F32) for i in range(4)]
            spb = [sb("g_sp%d" % i, [128, 128], F32) for i in range(4)]
            kin = [sb("g_kin%d" % i, [128, 128], BF16) for i in range(4)]
            kend = [sb("g_kend%d" % i, [128, 128], BF16) for i in range(4)]
            kendT = [sb("g_kendT%d" % i, [128, 128], BF16) for i in range(4)]
            qin = [sb("g_qin%d" % i, [128, 128], BF16) for i in range(4)]
            Am = [sb("g_Am%d" % i, [128, 128], BF16) for i in range(4)]
            osb = [sb("g_osb%d" % i, [128, 2, 128], F32) for i in range(4)]
            osq = [sb("g_osq%d" % i, [128, 2, 128], F32) for i in range(4)]
            mst = [sb("g_mst%d" % i, [128, 256], F32) for i in range(4)]
            t1 = [sb("g_t1%d" % i, [128, 128], F32) for i in range(4)]
            t2 = [sb("g_t2%d" % i, [128, 2, 128], F32) for i in range(4)]
            sr = [sb("g_sr%d" % i, [128, 2, 128], F32) for i in range(4)]
            on = [sb("g_on%d" % i, [128, 128], F32) for i in range(4)]
            on2 = [[sb("g_on2_%d_%d" % (i, e_), [128, 128], F32) for e_ in range(2)] for i in range(4)]
            TB2 = {"on": [[Buf(), Buf()] for _ in range(4)]}
            HB = {k: Buf() for k in ["q", "k", "r", "v", "gate", "S", "og", "bl"]}
            Sbfb = [Buf(), Buf(), Buf()]
            TB = {k: [Buf() for _ in range(4)] for k in ["tA", "sp", "kin", "kend", "kendT", "qin", "Am", "osb", "osq", "mst", "t1", "t2", "sr", "on"]}
            ogsem = P.new_dsem()
            LNS = math.log(128.0 ** -0.5)

            wg, wgb, wgs = wgring.next()
            P.dma("gpsimd", wg[:, :, :], wg_in[0].rearrange("(k p) n -> p k n", p=128), wgs, (), [wgb])
            for h in range(ngla):
                def g0(n, c):
                    c["bz"], c["bzb"] = nb(True)
                    mm(c["bz"][:, 0:128], glrT[0:17, n * 128:(n + 1) * 128], w2aug[0:17, h * 128:(h + 1) * 128], True, True, [GLR, WS], [c["bzb"]])

                def g1(n, c):
                    i4 = n % 4
                    act(tA[i4][:, :], c["bz"][:, 0:128], AF.Exp, [c["bzb"]], [TB["tA"][i4]], scale=-1.0)
                    act(spb[i4][:, :], tA[i4][:, :], AF.Ln, [TB["tA"][i4]], [TB["sp"][i4]], bias=1.0)
                    rel(c["bzb"])

                def g2(n, c):
                    i4 = n % 4
                    c["bt"], c["btb"] = nb(True)
                    mm(c["bt"][:, 0:128], spb[i4][:, :], uneg[:, :], True, True, [TB["sp"][i4], CB], [c["btb"]])

                def g3(n, c):
                    bt, btb = c["bt"], c["btb"]
                    cp("vector", blast[:, n:n + 1], bt[:, 127:128], [btb], [HB["bl"]])
                    act(enb[:, n * 128:(n + 1) * 128], bt[:, 0:128], AF.Exp, [btb], [HB["gate"]], scale=-1.0)
                    act(kef[:, n * 128:(n + 1) * 128], bt[:, 0:128], AF.Exp, [btb, HB["bl"]], [HB["gate"]], scale=-1.0, bias=blast[:, n:n + 1])
                    if n >= 15:
                        act(ebq[:, (n - 15) * 128:(n - 14) * 128], bt[:, 0:128], AF.Exp, [btb], [HB["gate"]], bias=LNS)
                    act(decay[:, n:n + 1], blast[:, n:n + 1], AF.Exp, [HB["bl"]], [HB["gate"]])
                    rel(btb)
                gsteps = pipeline_gen(32, [g0, g1, g2, g3])
                for blk in range(8):
                    for _ in range(5):
                        next(gsteps, None)
                    xb, xbb = load_x(xb_nat, XN, blk)
                    bk, bkb = nb()
                    with P.group("tensor"):
                        for kc in range(KC):
                            mm(bk[:, :], wg[:, kc, 128:256], xb[:, kc, :], kc == 0, kc == KC - 1, [xbb, wgb], [bkb])
                    cp("scalar", kT[:, blk * 512:(blk + 1) * 512], bk[:, :], [bkb], [HB["k"]])
                    if blk >= 3:
                        x0, nn_, q0 = (384, 128, 0) if blk == 3 else (0, 512, 128 + (blk - 4) * 512)
                        for (c0, dst, hb) in [(0, qT[:, q0:q0 + nn_], "q"), (256, rT[:, 0, q0:q0 + nn_], "r"), (384, rT[:, 1, q0:q0 + nn_], "r")]:
                            bq, bqb = nb()
                            with P.group("tensor"):
                                for kc in range(KC):
                                    mm(bq[:, 0:nn_], wg[:, kc, c0:c0 + 128], xb[:, kc, x0:x0 + nn_], kc == 0, kc == KC - 1, [xbb, wgb], [bqb])
                            cp("scalar" if hb == "q" else "vector", dst, bq[:, 0:nn_], [bqb], [HB[hb]])
                    for sub in range(4):
                        bv, bvb = nb()
                        with P.group("tensor"):
                            for kc in range(KC):
                                mm(bv[:, 0:256], xb[:, kc, sub * 128:(sub + 1) * 128], wg[:, kc, 512:768], kc == 0, kc == KC - 1, [xbb, wgb], [bvb])
                        cp("vector", vsb[:, blk * 4 + sub, :], bv[:, 0:256], [bvb], [HB["v"]])
                for _ in gsteps:
                    pass
                if h + 1 < ngla:
                    P.dma("gpsimd", wg[:, :, :], wg_in[h + 1].rearrange("(k p) n -> p k n", p=128), wgs, (), [wgb])
                act(rT[:, :, :], rT[:, :, :], AF.Silu, [], [HB["r"]])
                if stop == "proj":
                    finish()
                    return nc
                mset("vector", Sst[:, :], 0.0, [HB["S"]])
                mset("gpsimd", Sbf[2][:, :], 0.0, [Sbfb[2]])

                def c0_(n, c):
                    i4 = n % 4
                    tok = slice(n * 128, (n + 1) * 128)
                    tt("vector", kin[i4][:, :], kT[:, tok], enb[:, tok], ALU.mult, [HB["k"], HB["gate"]], [TB["kin"][i4]])
                    tt("gpsimd", kend[i4][:, :], kT[:, tok], kef[:, tok], ALU.mult, [HB["k"], HB["gate"]], [TB["kend"][i4]])
                    if n >= 15:
                        q0 = (n - 15) * 128
                        tt("vector", qin[i4][:, :], qT[:, q0:q0 + 128], ebq[:, q0:q0 + 128], ALU.mult, [HB["q"], HB["gate"]], [TB["qin"][i4]])

                def c1_(n, c):
                    i4 = n % 4
                    P.op("tensor", lambda e, i4=i4: e.transpose(out=ptb[:, i4 * 128:(i4 + 1) * 128], in_=kend[i4][:, :], identity=ident[:, :]),
                         [TB["kend"][i4], CB], [ptb_b])
                    if n >= 15:
                        c["ba"], c["bab"] = nb(True)
                        mm(c["ba"][:, 0:128], kin[i4][:, :], qin[i4][:, :], True, True, [TB["kin"][i4], TB["qin"][i4]], [c["bab"]])

                def c2_(n, c):
                    i4 = n % 4
                    cp("scalar", kendT[i4][:, :], ptb[:, i4 * 128:(i4 + 1) * 128], [ptb_b], [TB["kendT"][i4]])
                    if n >= 15:
                        q0 = (n - 15) * 128
                        tt("vector", Am[i4][:, :], c["ba"][:, 0:128], masks[:, 1, :], ALU.mult, [c["bab"], CB], [TB["Am"][i4]])
                        rel(c["bab"])

                def c3_(n, c):
                    i4 = n % 4
                    Sprev, Sprevb = Sbf[(n + 2) % 3], Sbfb[(n + 2) % 3]
                    if n >= 15:
                        c["bo"], c["bob"] = nb(True)
                        for e_ in range(2):
                            with P.group("tensor"):
                                mm(c["bo"][:, e_ * 128:(e_ + 1) * 128], vsb[:, n, e_ * 128:(e_ + 1) * 128], Am[i4][:, :], True, False, [HB["v"], TB["Am"][i4]], [c["bob"]])
                                mm(c["bo"][:, e_ * 128:(e_ + 1) * 128], Sprev[:, e_ * 128:(e_ + 1) * 128], qin[i4][:, :], False, True, [Sprevb, TB["qin"][i4]], [c["bob"]])
                    c["bs"], c["bsb"] = nb(True)
                    mm(c["bs"][:, 0:256], kendT[i4][:, :], vsb[:, n, :], True, True, [TB["kendT"][i4], HB["v"]], [c["bsb"]])

                def c4_(n, c):
                    i4 = n % 4
                    stt("vector", Sst[:, :], Sst[:, :], decay[:, n:n + 1], c["bs"][:, 0:256], ALU.mult, ALU.add, [c["bsb"], HB["gate"]], [HB["S"]])
                    cp("gpsimd", Sbf[n % 3][:, :], Sst[:, :], [HB["S"]], [Sbfb[n % 3]])
                    rel(c["bsb"])
                    if n >= 15:
                        cp("scalar", osb[i4][:, :, :], c["bo"][:, 0:256].rearrange("p (a b) -> p a b", a=2), [c["bob"]], [TB["osb"][i4]])
                        rel(c["bob"])

                def c5_(n, c):
                    i4 = n % 4
                    if n < 15:
                        return
                    tt("gpsimd", osq[i4][:, :, :], osb[i4][:, :, :], osb[i4][:, :, :], ALU.mult, [TB["osb"][i4]], [TB["osq"][i4]])
                    c["bm"], c["bmb"] = nb(True)
                    bm, bmb = c["bm"], c["bmb"]
                    with P.group("tensor"):
                        mm(bm[:, 0:128], ones256[:, :], osb[i4][:, 0, :], True, False, [CB, TB["osb"][i4]], [bmb])
                        mm(bm[:, 0:128], ones256[:, :], osb[i4][:, 1, :], False, True, [CB, TB["osb"][i4]], [bmb])
                    with P.group("tensor"):
                        mm(bm[:, 128:256], ones256[:, :], osq[i4][:, 0, :], True, False, [CB, TB["osq"][i4]], [bmb])
                        mm(bm[:, 128:256], ones256[:, :], osq[i4][:, 1, :], False, True, [CB, TB["osq"][i4]], [bmb])

                def c6_(n, c):
                    i4 = n % 4
                    if n < 15:
                        return
                    q0 = (n - 15) * 128
                    cp("scalar", mst[i4][:, :], c["bm"][:, 0:256], [c["bmb"]], [TB["mst"][i4]])
                    rel(c["bmb"])
                    tt("vector", t1[i4][:, :], mst[i4][:, 0:128], mst[i4][:, 0:128], ALU.mult, [TB["mst"][i4]], [TB["t1"][i4]])
                    tt("vector", t1[i4][:, :], mst[i4][:, 128:256], t1[i4][:, :], ALU.subtract, [TB["mst"][i4]], [TB["t1"][i4]])
                    act(t1[i4][:, :], t1[i4][:, :], AF.Ln, [], [TB["t1"][i4]], bias=LN_EPS)
                    act(t1[i4][:, :], t1[i4][:, :], AF.Exp, [], [TB["t1"][i4]], scale=-0.5)

                def c7_(n, c):
                    i4 = n % 4
                    if n < 15:
                        return
                    q0 = (n - 15) * 128
                    for e_ in range(2):
                        onb, ONB = on2[i4][e_], TB2["on"][i4][e_]
                        tt("vector", onb[:, :], osb[i4][:, e_, :], mst[i4][:, 0:128], ALU.subtract, [TB["osb"][i4], TB["mst"][i4]], [ONB])
                        tt("gpsimd", onb[:, :], onb[:, :], t1[i4][:, :], ALU.mult, [TB["t1"][i4]], [ONB])
                        if n == 15:
                            stt("vector", og[:, e_, 0:2], onb[:, 126:128], glag[:, 2 * h + e_:2 * h + e_ + 1], rT[:, e_, q0 + 126:q0 + 128],
                                ALU.mult, ALU.mult, [ONB, HB["r"], CB], [HB["og"]])
                        else:
                            o0 = 2 + (n - 16) * 128
                            stt("vector", og[:, e_, o0:o0 + 128], onb[:, :], glag[:, 2 * h + e_:2 * h + e_ + 1], rT[:, e_, q0:q0 + 128],
                                ALU.mult, ALU.mult, [ONB, HB["r"], CB], [HB["og"]])
                pipeline(32, [c0_, c1_, c2_, c3_, c4_, c5_, c6_, c7_])
                for e_ in range(2):
                    P.dma("sync", mixT[(2 * h + e_) * 128:(2 * h + e_ + 1) * 128, :], og[:, e_, :], ogsem, [HB["og"]], [MIXB])
            P.flush()

        P.barrier()
        if stop == "A":
            finish()
            return nc
        with ExitStack() as es:
            sb = lambda name, shape, dt: es.enter_context(nc.sbuf_tensor(name, list(shape), dt))
            xring = Ring(P, [sb("xblkb%d" % i, [128, KC, 512], BF16) for i in range(2)])

            def load_xp(blk):
                t, b, s = xring.next()
                P.dma("sync", t[:, :, :], xb_perm[:, blk * 512:(blk + 1) * 512].rearrange("(k p) n -> p k n", p=128), s, [XP[blk]], [b])
                return t, b

            cosT = sb("cosT", [32, NL], F32)
            sinT = sb("sinT", [32, NL], F32)
            ROT = Buf()
            with ExitStack() as es2:
                sb2 = lambda name, shape, dt: es2.enter_context(nc.sbuf_tensor(name, list(shape), dt))
                posi = sb2("posi", [32, NL], I32)
                ang = sb2("ang", [32, NL], F32)
                tf = sb2("tf", [32, NL], F32)
                rr = sb2("rr", [32, NL], F32)
                mk_ = sb2("mk_", [32, NL], F32)
                rotc = sb2("rotc_sb", [32, 2], F32)
                RB = Buf()
                rsem = P.new_dsem()
                P.dma("sync", posi[:, :], pos_in.partition_broadcast(32), rsem, (), [RB])
                P.dma("sync", rotc[:, :], rotc_in, rsem, (), [RB])
                defer = []

                def DF(fn, *a_, **k_):
                    defer.append(lambda: fn(*a_, **k_))
                DF(cp, "vector", ang[:, :], posi[:, :], [RB], [RB])
                DF(ts, "vector", ang[:, :], ang[:, :], rotc[:, 0:1], None, ALU.mult, None, [RB], [RB])
                DF(ts, "vector", tf[:, :], ang[:, :], 1.0 / (2 * PI), 0.5, ALU.mult, ALU.add, [RB], [RB])
                DF(cp, "vector", posi[:, :], tf[:, :], [RB], [RB])
                DF(cp, "vector", tf[:, :], posi[:, :], [RB], [RB])
                C1 = 6.28125
                C2 = 2 * PI - C1
                DF(stt, "vector", rr[:, :], tf[:, :], -C1, ang[:, :], ALU.mult, ALU.add, [RB], [RB])
                DF(stt, "vector", rr[:, :], tf[:, :], -C2, rr[:, :], ALU.mult, ALU.add, [RB], [RB])

                def wrap_clamp(r):
                    DF(ts, "vector", mk_[:, :], r[:, :], -PI, None, ALU.is_lt, None, [RB], [RB])
                    DF(stt, "vector", r[:, :], mk_[:, :], 2 * PI, r[:, :], ALU.mult, ALU.add, [RB], [RB])
                    DF(ts, "vector", mk_[:, :], r[:, :], PI, None, ALU.is_gt, None, [RB], [RB])
                    DF(stt, "vector", r[:, :], mk_[:, :], -2 * PI, r[:, :], ALU.mult, ALU.add, [RB], [RB])
                    DF(ts, "vector", r[:, :], r[:, :], -3.141592, 3.141592, ALU.max, ALU.min, [RB], [RB])
                wrap_clamp(rr)
                DF(act, sinT[:, :], rr[:, :], AF.Sin, [RB], [ROT], scale=rotc[:, 1:2])
                DF(ts, "vector", rr[:, :], rr[:, :], PI / 2, None, ALU.add, None, [RB], [RB])
                wrap_clamp(rr)
                DF(act, cosT[:, :], rr[:, :], AF.Sin, [RB], [ROT])

                wvd = sb2("wvd_sb", [128, KC, 1024], BF16)
                WV = Buf()
                wvs = P.new_dsem()
                for g in range(2):
                    P.dma("gpsimd", wvd[:, :, g * 512:(g + 1) * 512], wvd_in[:, g * 512:(g + 1) * 512].rearrange("(k p) n -> p k n", p=128), wvs, (), [WV])
                vst = Ring(P, [sb2("vst%d" % i, [128, 1024], BF16) for i in range(2)])
                for blk in range(8):
                    t_, b_, s__ = xring.next()
                    cols_ = slice(blk * 512, (blk + 1) * 512)
                    P.dma("gpsimd", t_[:, :, :], xT_perm[:, cols_].rearrange("(k p) n -> p k n", p=128), s__, (), [b_])
                    P.dma("sync", xb_perm[:, cols_].rearrange("(k p) n -> p k n", p=128), t_[:, :, :], s__, [b_], [XP[blk]])
                    xb, xbb = t_, b_
                    for sub in range(4):
                        if defer:
                            defer.pop(0)()
                        vt, vtb, vts = vst.next()
                        for g in range(2):
                            bv, bvb = nb()
                            with P.group("tensor"):
                                for kc in range(KC):
                                    mm(bv[:, :], xb[:, kc, sub * 128:(sub + 1) * 128], wvd[:, kc, g * 512:(g + 1) * 512], kc == 0, kc == KC - 1, [xbb, WV], [bvb])
                            cp("scalar" if g == 0 else "vector", vt[:, g * 512:(g + 1) * 512], bv[:, :], [bvb], [vtb])
                        r0 = (blk * 4 + sub) * 128
                        P.dma("scalar", vd_scr[r0:r0 + 128, :], vt[:, :], vts, [vtb], [VDB])
                        T_ = blk * 4 + sub
                        s_, r16 = T_ // 16, T_ % 16
                        a_, r4 = r16 // 4, r16 % 4
                        d4 = vd4_scr.rearrange("(t i) c -> t i c", i=128)[r4 * 8 + 4 * s_:r4 * 8 + 4 * s_ + 4, 32 * a_:32 * a_ + 32, :]
                        P.dma("scalar", d4, vt[:, :], vts, [vtb], [VDB])
                        d1 = vd1_scr.rearrange("(t i) c -> t i c", i=128)[16 * s_:16 * s_ + 16, 8 * r16:8 * r16 + 8, :]
                        P.dma("scalar", d1, vt[:, :], vts, [vtb], [VDB])
                while defer:
                    defer.pop(0)()
                P.flush()
            P.barrier()
            if stop == "V":
                finish()
                return nc

            wdring = Ring(P, [sb("wd%d" % i, [128, KC, 320], BF16) for i in range(2)])
            dq2 = [sb("d_qq%d" % i, [128, 2560], BF16) for i in range(2)]
            dk2 = [sb("d_kk%d" % i, [128, NL], BF16) for i in range(2)]
            DBQ, DBK = [Buf(), Buf()], [Buf(), Buf()]
            dk4 = sb("d_k4", [128, NL], BF16)
            dk1 = sb("d_k1", [128, NL], BF16)
            dq4 = sb("d_q4", [128, 2048], BF16)
            dq1 = sb("d_q1", [128, 2048], BF16)
            hq1 = sb("d_hq1", [128, 16], BF16)
            V16 = sb("d_v16", [128, 32, 128], BF16)
            V4 = sb("d_v4", [128, 32, 128], BF16)
            V1 = sb("d_v1", [128, 32, 128], BF16)
            acc = sb("d_acc", [128, 2560], F32)
            dacc = sb("d_dacc", [128, 2560], F32)
            odb = sb("d_od", [128, TQ], BF16)
            rt1 = [sb("d_rt1%d" % i, [32, 512], F32) for i in range(2)]
            rt2 = [sb("d_rt2%d" % i, [32, 512], F32) for i in range(2)]
            Pm = [sb("d_P%d" % i, [128, 2, 128], BF16) for i in range(4)]
            DB = {k: Buf() for k in ["q", "k", "v", "acc", "od", "k4", "k1", "q4", "q1"]}
            DBV = {16: Buf(), 4: Buf(), 1: Buf()}
            RTB = [[Buf(), Buf()], [Buf(), Buf()]]
            PmB = [Buf() for _ in range(4)]
            vsem = P.new_dsem()
            odsem = P.new_dsem()
            SC = 128.0 ** -0.5
            vd5 = vd_scr

            def kcols(t, off, kind, a, b_):
                if kind == 16:
                    p0 = 2048 * a + 128 * b_ - off
                    return t[:, p0:p0 + 128]
                if kind == 4:
                    r4, n = a, b_
                    s_, m = n // 4, n % 4
                    base = 2048 * s_ - off
                    return t[:, base:base + 2048].rearrange("p (a r u) -> p a r u", a=4, r=4, u=128)[:, :, r4, 32 * m:32 * m + 32]
                s_, m = a, b_
                base = 2048 * s_ - off
                return t[:, base:base + 2048].rearrange("p (r m u) -> p r m u", r=16, m=16, u=8)[:, :, m, :]

            def head_setup(h):
                wd, wdb, wds = wdring.next()
                P.dma("gpsimd", wd[:, :, :], wd_in[h].rearrange("(k p) n -> p k n", p=128), wds, (), [wdb])
                return wd, wdb

            def proj_gen(h, wd, wdb):
                dq, dk = dq2[h % 2], dk2[h % 2]
                DBq = {"q": DBQ[h % 2], "k": DBK[h % 2]}
                for blk in range(8):
                    xb, xbb = load_xp(blk)
                    cols = slice(blk * 512, (blk + 1) * 512)
                    todo = [(128, 288, dk[:, cols], "k")]
                    if blk >= 3:
                        todo.append((0, 256, dq[:, (blk - 3) * 512:(blk - 2) * 512], "q"))
                    for ti, (c0, cs, dst, hb) in enumerate(todo):
                        bk, bkb = nb()
                        with P.group("tensor"):
                            for kc in range(KC):
                                mm(bk[:, :], wd[:, kc, c0:c0 + 128], xb[:, kc, :], kc == 0, kc == KC - 1, [xbb, wdb], [bkb])
                        bs_, bsb_ = nb()
                        with P.group("tensor"):
                            for kc in range(KC):
                                mm(bs_[0:32, :], wd[:, kc, cs:cs + 32], xb[:, kc, :], kc == 0, kc == KC - 1, [xbb, wdb], [bsb_])
                        cp("scalar", dst, bk[:, :], [bkb], [DBq[hb]])
                        tt("vector", rt1[ti][:, :], bk[0:32, :], cosT[:, cols], ALU.mult, [bkb, ROT], [RTB[ti][0]])
                        tt("vector", rt2[ti][:, :], bs_[0:32, :], sinT[:, cols], ALU.mult, [bsb_, ROT], [RTB[ti][1]])
                        P.op("vector", lambda e, dst=dst, ti=ti: e.tensor_tensor(out=dst[0:32], in0=rt1[ti][:, :], in1=rt2[ti][:, :], op=ALU.add),
                             [RTB[ti][0], RTB[ti][1]], [DBq[hb]])
                    yield blk

            vdone = set()
            vsems = {16: P.new_dsem(), 4: P.new_dsem(), 1: P.new_dsem()}

            def vload(h, kind_):
                if (h, kind_) in vdone:
                    return
                vdone.add((h, kind_))
                hc = slice(h * 128, (h + 1) * 128)
                dst, src = {16: (V16, vd5), 4: (V4, vd4_scr), 1: (V1, vd1_scr)}[kind_]
                P.dma("gpsimd", dst[:, :, :], src[:, hc].rearrange("(t p) c -> p t c", p=128), vsems[kind_], [VDB], [DBV[kind_]])

            def post_proj(h):
                dq, dk = dq2[h % 2], dk2[h % 2]
                DBq = {"q": DBQ[h % 2], "k": DBK[h % 2]}
                for kind_ in (16, 4, 1):
                    vload(h, kind_)
                for s_ in range(2):
                    srck = dk[:, 2048 * s_:2048 * s_ + 2048]
                    for r4 in range(4):
                        P.op("vector" if r4 % 2 == 0 else "gpsimd", lambda e, s_=s_, r4=r4, srck=srck: e.tensor_copy(
                            out=dk4[:, (r4 * 8 + s_ * 4) * 128:(r4 * 8 + s_ * 4 + 4) * 128].rearrange("p (m a u) -> p m a u", m=4, a=4, u=32),
                            in_=srck.rearrange("p (a r m u) -> p r m a u", a=4, r=4, m=4, u=32)[:, r4]), [DBq["k"]], [DB["k4"]])
                    P.op("vector", lambda e, s_=s_, srck=srck: e.tensor_copy(
                        out=dk1[:, 2048 * s_:2048 * s_ + 2048].rearrange("p (m r u) -> p m r u", m=16, r=16, u=8),
                        in_=srck.rearrange("p (r m u) -> p m r u", r=16, m=16, u=8)), [DBq["k"]], [DB["k1"]])
                srcq = dq[:, 512:2560]
                for r4 in range(4):
                    P.op("scalar", lambda e, r4=r4: e.activation(
                        out=dq4[:, r4 * 512:(r4 + 1) * 512].rearrange("p (m a u) -> p m a u", m=4, a=4, u=32),
                        in_=srcq.rearrange("p (a r m u) -> p r m a u", a=4, r=4, m=4, u=32)[:, r4], func=AF.Copy), [DBq["q"]], [DB["q4"]])
                P.op("scalar", lambda e: e.activation(
                    out=dq1[:, :].rearrange("p (m r u) -> p m r u", m=16, r=16, u=8),
                    in_=srcq.rearrange("p (r m u) -> p m r u", r=16, m=16, u=8), func=AF.Copy), [DBq["q"]], [DB["q1"]])
                P.op("scalar", lambda e: e.activation(
                    out=hq1[:, :].rearrange("p (r u) -> p r u", r=2),
                    in_=dq[:, 256:512].rearrange("p (r u) -> p r u", r=2)[:, :, 120:128], func=AF.Copy), [DBq["q"]], [DB["q1"]])


            def att_gen(h):
                dq, dk = dq2[h % 2], dk2[h % 2]
                DBq = {"q": DBQ[h % 2], "k": DBK[h % 2]}

                def kap(kb):
                    kind, a_, b_ = kb
                    if kind == 16:
                        p0 = 2048 * a_ + 128 * b_
                        return dk[:, p0:p0 + 128], DBq["k"]
                    if kind == 4:
                        p0 = (a_ * 8 + b_) * 128
                        return dk4[:, p0:p0 + 128], DB["k4"]
                    p0 = (16 * a_ + b_) * 128
                    return dk1[:, p0:p0 + 128], DB["k1"]
                mset("gpsimd", acc[:, :], 0.0, [DB["acc"]])
                mset("gpsimd", dacc[:, :], 0.0, [DB["acc"]])
                blocks = []
                for r in range(16):
                    blocks.append((16, (kcols(dq, 1536, 16, 1, r), DBq["q"]), kcols(acc, 1536, 16, 1, r), kcols(dacc, 1536, 16, 1, r), 128, None,
                                   [((16, 0, r), V16[:, r, :], masksc[:, 0, :]), ((16, 1, r), V16[:, 16 + r, :], masks[:, 1, :])]))
                for r in (14, 15):
                    blocks.append((16, (kcols(dq, 1536, 16, 0, r), DBq["q"]), kcols(acc, 1536, 16, 0, r), kcols(dacc, 1536, 16, 0, r), 128, None,
                                   [((16, 0, r), V16[:, r, :], masksc[:, 1, :])]))
                for r4 in range(4):
                    for n in range(4, 8):
                        pm = masksc[:, 2, :] if n == 4 else masks[:, 2, :]
                        blocks.append((4, (dq4[:, (r4 * 4 + n - 4) * 128:(r4 * 4 + n - 3) * 128], DB["q4"]), kcols(acc, 1536, 4, r4, n), kcols(dacc, 1536, 4, r4, n), 128, [4, 32],
                                       [((4, r4, n - 1), V4[:, r4 * 8 + n - 1, :], pm), ((4, r4, n), V4[:, r4 * 8 + n, :], masks[:, 3, :])]))
                for r4 in (2, 3):
                    p0 = (12 + r4) * 128 + 96 - 1536
                    blocks.append((4, (dq[:, p0:p0 + 32], DBq["q"]), acc[:, p0:p0 + 32], dacc[:, p0:p0 + 32], 32, None,
                                   [((4, r4, 2), V4[:, r4 * 8 + 2, :], masksc[:, 2, 96:128]), ((4, r4, 3), V4[:, r4 * 8 + 3, :], masksc[:, 3, 96:128])]))
                for m in range(16):
                    pk = (1, 0, 15) if m == 0 else (1, 1, m - 1)
                    pm = masksc[:, 4, :] if m == 0 else masks[:, 4, :]
                    blocks.append((1, (dq1[:, m * 128:(m + 1) * 128], DB["q1"]), kcols(acc, 1536, 1, 1, m), kcols(dacc, 1536, 1, 1, m), 128, [16, 8],
                                   [(pk, V1[:, 16 * pk[1] + pk[2], :], pm), ((1, 1, m), V1[:, 16 + m, :], masks[:, 5, :])]))
                hq = lambda t: t[:, 14 * 128 - 1536:16 * 128 - 1536].rearrange("p (r u) -> p r u", r=2)[:, :, 120:128]
                blocks.append((1, (hq1[:, :], DB["q1"]), hq(acc), hq(dacc), 16, [2, 8],
                               [((1, 0, 14), V1[:, 14, :], masksc[:, 4, 112:128]), ((1, 0, 15), V1[:, 15, :], masksc[:, 5, 112:128])]))
                def a0(i, c):
                    kind, (qap, qbuf), accap, daccap, nq, qshape, keys = blocks[i]
                    c["bsc"], c["bscb"] = nb(True)
                    for ki, (kb, vt, mk) in enumerate(keys):
                        ka, kbuf = kap(kb)
                        mm(c["bsc"][:, ki * 128:ki * 128 + nq], ka, qap, True, True, [kbuf, qbuf], [c["bscb"]])

                def a1(i, c):
                    kind, (qap, qbuf), accap, daccap, nq, qshape, keys = blocks[i]
                    pi = i % 4
                    nk = len(keys)
                    act(Pm[pi][:, 0:nk, 0:nq], c["bsc"][:, 0:nk * 128].rearrange("p (a b) -> p a b", a=nk)[:, :, 0:nq], AF.Exp, [c["bscb"]], [PmB[pi]], scale=SC)
                    rel(c["bscb"])
                    for ki, (kb, vt, mk) in enumerate(keys):
                        tt("gpsimd", Pm[pi][:, ki, 0:nq], Pm[pi][:, ki, 0:nq], mk, ALU.mult, [CB], [PmB[pi]])

                def a2(i, c):
                    kind, (qap, qbuf), accap, daccap, nq, qshape, keys = blocks[i]
                    pi = i % 4
                    nk = len(keys)
                    c["bo"], c["bob"] = nb(True)
                    with P.group("tensor"):
                        for ki, (kb, vt, mk) in enumerate(keys):
                            mm(c["bo"][:, 0:nq], vt, Pm[pi][:, ki, 0:nq], ki == 0, ki == nk - 1, [DBV[kind], PmB[pi]], [c["bob"]])
                    with P.group("tensor"):
                        for ki, (kb, vt, mk) in enumerate(keys):
                            mm(c["bo"][:, 128:128 + nq], ones_bf[:, :], Pm[pi][:, ki, 0:nq], ki == 0, ki == nk - 1, [CB, PmB[pi]], [c["bob"]])

                def a3(i, c):
                    kind, (qap, qbuf), accap, daccap, nq, qshape, keys = blocks[i]
                    o_in = c["bo"][:, 0:nq]
                    d_in = c["bo"][:, 128:128 + nq]
                    if qshape is not None:
                        o_in = o_in.rearrange("p (a b) -> p a b", a=qshape[0])
                        d_in = d_in.rearrange("p (a b) -> p a b", a=qshape[0])
                    tt("vector", accap, accap, o_in, ALU.add, [c["bob"]], [DB["acc"]])
                    tt("vector", daccap, daccap, d_in, ALU.add, [c["bob"]], [DB["acc"]])
                    rel(c["bob"])
                for t_ in pipeline_gen(len(blocks), [a0, a1, a2, a3]):
                    yield t_
                ts("vector", dacc[:, 256:2560], dacc[:, 256:2560], 1e-30, None, ALU.max, None, [], [DB["acc"]])
                act(dacc[:, 256:2560], dacc[:, 256:2560], AF.Ln, [], [DB["acc"]])
                act(dacc[:, 256:2560], dacc[:, 256:2560], AF.Exp, [], [DB["acc"]], scale=-1.0)
                P.op("vector", lambda e: e.tensor_tensor(out=odb[:, 2:TQ].rearrange("p (u r) -> p r u", r=16),
                                                         in0=acc[:, 512:2560].rearrange("p (r u) -> p r u", r=16),
                                                         in1=dacc[:, 512:2560].rearrange("p (r u) -> p r u", r=16), op=ALU.mult), [DB["acc"]], [DB["od"]])
                tt("vector", odb[:, 0:1], acc[:, 383:384], dacc[:, 383:384], ALU.mult, [DB["acc"]], [DB["od"]])
                tt("vector", odb[:, 1:2], acc[:, 511:512], dacc[:, 511:512], ALU.mult, [DB["acc"]], [DB["od"]])
                P.dma("sync", mixT[(8 + h) * 128:(9 + h) * 128, :], odb[:, :], odsem, [DB["od"]], [MIXB])
                yield -1

            prev_att = None
            wd_next = head_setup(0)
            for h in range(ndil):
                wd, wdb = wd_next
                pg = proj_gen(h, wd, wdb)
                nst = 0
                for _blk in pg:
                    if prev_att is not None:
                        for _ in range(8):
                            if next(prev_att, None) is None:
                                break
                            nst += 1
                            if nst == 23:
                                vload(h, 16)
                            if nst == 41:
                                vload(h, 4)
                if h + 1 < ndil:
                    wd_next = head_setup(h + 1)
                if prev_att is not None:
                    for _ in prev_att:
                        pass
                post_proj(h)
                prev_att = att_gen(h)
            for _ in prev_att:
                pass
            P.flush()
        P.barrier()

        def gemm_ln_phase(tag, w_dram, a_dram, a_cast, resid_dram, lnidx, out_dram, out_col0, chunks, ABUF, RBUF, OBUF):
            with ExitStack() as es:
                sb = lambda name, shape, dt: es.enter_context(nc.sbuf_tensor(tag + name, list(shape), dt))
                W = sb("W", [128, KC, D], BF16)
                WB = [Buf() for _ in range(4)]
                for g in range(4):
                    P.dma("gpsimd", W[:, :, g * 512:(g + 1) * 512], w_dram[:, g * 512:(g + 1) * 512].rearrange("(k p) n -> p k n", p=128), P.new_dsem(), (), [WB[g]])
                aring = Ring(P, [sb("a%d" % i, [128, KC, 512], BF16) for i in range(2)])
                ln_chunks(tag, sb, chunks, KC,
                          lambda dc, kc: W[:, kc, dc * 128:(dc + 1) * 128], WB,
                          a_dram, a_cast, aring, ABUF, resid_dram, RBUF, lnidx, out_dram, out_col0, OBUF)
                P.flush()
            P.barrier()

        def ln_chunks(tag, sb, chunks, nk, wfn, wbufs, a_dram, a_cast, aring, ABUF, resid_dram, RBUF, lnidx, out_dram, out_col0, OBUF, wstream=None):
            ny = 2
            y = [sb("y%d" % i, [128, KC, 512], F32) for i in range(ny)]
            YB = [Buf() for _ in range(ny)]
            s1 = [sb("s1_%d" % i, [128, 512], F32) for i in range(2)]
            s2 = [sb("s2_%d" % i, [128, 512], F32) for i in range(2)]
            SB1, SB2 = [Buf(), Buf()], [Buf(), Buf()]
            ysq = [sb("ysq%d" % i, [128, 512], F32) for i in range(2)]
            YSQ = [Buf(), Buf()]
            rres = Ring(P, [sb("res%d" % i, [128, 512], F32) for i in range(3)])
            mean = sb("mean", [128, 512], F32)
            rstd = sb("rstd", [128, 512], F32)
            MB = Buf()
            tn = [sb("tn%d" % i, [128, 512], F32) for i in range(2)]
            TN = [Buf(), Buf()]
            oring = Ring(P, [sb("o%d" % i, [128, 512], F32) for i in range(3)])
            pend = iter(())
            for ci, (c0, c1) in enumerate(chunks):
                n = c1 - c0
                yi = ci % ny
                yc, ycb, s1c, s2c, S1B, S2B = y[yi], YB[yi], s1[ci % 2], s2[ci % 2], SB1[ci % 2], SB2[ci % 2]
                if ci == 0:
                    nxt = aring.next()
                    P.dma("gpsimd" if a_cast else "sync", nxt[0][:, 0:nk, 0:n], a_dram[:, c0:c1].rearrange("(k p) n -> p k n", p=128), nxt[2], [ABUF], [nxt[1]])
                a, ab, asem = nxt
                if ci + 1 < len(chunks):
                    d0, d1 = chunks[ci + 1]
                    nxt = aring.next()
                    P.dma("gpsimd" if a_cast else "sync", nxt[0][:, 0:nk, 0:d1 - d0], a_dram[:, d0:d1].rearrange("(k p) n -> p k n", p=128), nxt[2], [ABUF], [nxt[1]])

                def epi(dc, bk, bkb):
                    rt, rtb, rsem_ = rres.next()
                    P.dma("sync", rt[:, 0:n], resid_dram(dc, c0, c1), rsem_, [RBUF], [rtb])
                    stt("vector", yc[:, dc, 0:n], rt[:, 0:n], ALPHA, bk[:, 0:n], ALU.mult, ALU.add, [rtb, bkb], [ycb])
                    i2 = dc % 2
                    if dc == 0:
                        cp("vector", s1c[:, 0:n], yc[:, dc, 0:n], [ycb], [S1B])
                        act(s2c[:, 0:n], yc[:, dc, 0:n], AF.Square, [ycb], [S2B])
                    else:
                        tt("vector", s1c[:, 0:n], s1c[:, 0:n], yc[:, dc, 0:n], ALU.add, [ycb], [S1B])
                        act(ysq[i2][:, 0:n], yc[:, dc, 0:n], AF.Square, [ycb], [YSQ[i2]])
                        tt("gpsimd", s2c[:, 0:n], s2c[:, 0:n], ysq[i2][:, 0:n], ALU.add, [YSQ[i2]], [S2B])

                if wstream is None:
                    for dc in range(KC):
                        bk, bkb = nb()
                        with P.group("tensor"):
                            for kc in range(nk):
                                mm(bk[:, 0:n], wfn(dc, kc), a[:, kc, 0:n], kc == 0, kc == nk - 1, [ab, wbufs[dc // 4]], [bkb])
                        epi(dc, bk, bkb)
                        if dc >= 2:
                            next(pend, None)
                else:
                    for qd_ in range(4):
                        acc4 = [nb(True) for _ in range(4)]
                        for j in range(nk):
                            wt, wtb = wstream(j, qd_)
                            with P.group("tensor"):
                                for i_ in range(4):
                                    mm(acc4[i_][0][:, 0:n], wt[:, i_ * 128:(i_ + 1) * 128], a[:, j, 0:n], j == 0, j == nk - 1, [ab, wtb], [acc4[i_][1]])
                        for i_ in range(4):
                            epi(4 * qd_ + i_, acc4[i_][0], acc4[i_][1])
                            rel(acc4[i_][1])
                            if 4 * qd_ + i_ >= 2:
                                next(pend, None)
                def part2(n=n, c0=c0, c1=c1, yc=yc, ycb=ycb, s1c=s1c, s2c=s2c, S1B=S1B, S2B=S2B):
                    yield 0
                    bm, bmb = nb()
                    mm(bm[:, 0:n], ones2048[:, :], s1c[:, 0:n], True, True, [CB, S1B], [bmb])
                    bm2, bm2b = nb()
                    mm(bm2[:, 0:n], ones2048[:, :], s2c[:, 0:n], True, True, [CB, S2B], [bm2b])
                    cp("scalar", mean[:, 0:n], bm[:, 0:n], [bmb], [MB])
                    tt("vector", rstd[:, 0:n], mean[:, 0:n], mean[:, 0:n], ALU.mult, [MB], [MB])
                    tt("vector", rstd[:, 0:n], bm2[:, 0:n], rstd[:, 0:n], ALU.subtract, [bm2b], [MB])
                    act(rstd[:, 0:n], rstd[:, 0:n], AF.Ln, [], [MB], bias=LN_EPS)
                    act(rstd[:, 0:n], rstd[:, 0:n], AF.Exp, [], [MB], scale=-0.5)
                    for dc in range(KC):
                        i2 = dc % 2
                        tt("vector", tn[i2][:, 0:n], yc[:, dc, 0:n], mean[:, 0:n], ALU.subtract, [ycb, MB], [TN[i2]])
                        tt("vector" if dc % 2 == 0 else "gpsimd", tn[i2][:, 0:n], tn[i2][:, 0:n], rstd[:, 0:n], ALU.mult, [MB], [TN[i2]])
                        ot, otb, osem_ = oring.next()
                        act(ot[:, 0:n], tn[i2][:, 0:n], AF.Identity, [TN[i2], CB], [otb], scale=lnp[:, 2 * lnidx, dc:dc + 1], bias=lnp[:, 2 * lnidx + 1, dc:dc + 1])
                        P.dma("scalar", out_dram[dc * 128:(dc + 1) * 128, c0 - out_col0:c1 - out_col0], ot[:, 0:n], osem_, [otb], [OBUF])
                        yield dc + 1
                for _ in pend:
                    pass
                pend = part2()
            for _ in pend:
                pass

        if stop == "B":
            finish()
            return nc
        X1B, OCB, X2B, HTB, OUTB = Buf(), Buf(), Buf(), Buf(), Buf()
        W2B = Buf()
        w2sem = P.new_dsem()
        NOB = Buf()
        xres = lambda dc, c0, c1: xT_nat[dc * 128:(dc + 1) * 128, 2046 + c0:2046 + c1]
        gemm_ln_phase("C", wout_in, mixT, False, xres, 0, x1T, 0, TCH, MIXB, NOB, X1B)
        if stop == "C":
            finish()
            return nc

        with ExitStack() as es:
            sb = lambda name, shape, dt: es.enter_context(nc.sbuf_tensor("D1" + name, list(shape), dt))
            mkT = sb("mkT", [128, 16, 256], BF16)
            mv = sb("mv", [128, 2, D], BF16)
            MKB, MVB, MTB = Buf(), Buf(), Buf()
            Wq = sb("Wq", [128, KC, D], BF16)
            WQB = Buf()
            wqs = P.new_dsem()
            es2 = ExitStack()
            sb2 = lambda name, shape, dt: es2.enter_context(nc.sbuf_tensor("D0" + name, list(shape), dt))
            mT = sb2("memT", [128, KC, 256], BF16)
            P.dma("gpsimd", mT[:, :, :], memT.rearrange("(k p) n -> p k n", p=128), P.new_dsem(), (), [MTB])
            wkvr = Ring(P, [sb2("wkv%d" % i, [128, KC, 512], BF16) for i in range(2)])
            for g in range(8):
                wt, wtb, wts = wkvr.next()
                P.dma("gpsimd", wt[:, :, :], wkv_in[:, g * 512:(g + 1) * 512].rearrange("(k p) n -> p k n", p=128), wts, (), [wtb])
                if g < 4:
                    for j in range(4):
                        bk, bkb = nb()
                        with P.group("tensor"):
                            for kc in range(KC):
                                mm(bk[:, 0:256], wt[:, kc, j * 128:(j + 1) * 128], mT[:, kc, :], kc == 0, kc == KC - 1, [wtb, MTB], [bkb])
                        cp("scalar", mkT[:, 4 * g + j, :], bk[:, 0:256], [bkb], [MKB])
                else:
                    for mt in range(2):
                        bk, bkb = nb()
                        with P.group("tensor"):
                            for kc in range(KC):
                                mm(bk[:, :], mT[:, kc, mt * 128:(mt + 1) * 128], wt[:, kc, :], kc == 0, kc == KC - 1, [wtb, MTB], [bkb])
                        cp("vector", mv[:, mt, (g - 4) * 512:(g - 3) * 512], bk[:, :], [bkb], [MVB])
            for g in range(4):
                P.dma("gpsimd", Wq[:, :, g * 512:(g + 1) * 512], wq_in[:, g * 512:(g + 1) * 512].rearrange("(k p) n -> p k n", p=128), wqs, (), [WQB])
            for g in range(8):
                P.dma("gpsimd", w2b[g * 688:(g + 1) * 688, :], fwout_in[g * 688:(g + 1) * 688, :], w2sem, (), [W2B])
            w2sem["gpsimd"].append(True)
            P.flush()
            es2.close()
            P.barrier()
            aring = Ring(P, [sb("a%d" % i, [128, KC, 512], BF16) for i in range(2)])
            qc = sb("qc", [128, KC, 512], BF16)
            QCB = Buf()
            ocr = Ring(P, [sb("oc%d" % i, [128, KC, 512], BF16) for i in range(2)])
            Pc = [sb("Pc%d" % i, [128, 2, 512], BF16) for i in range(2)]
            PCB = [Buf(), Buf()]
            rden = [sb("rden%d" % i, [128, 512], F32) for i in range(2)]
            RDB = [Buf(), Buf()]
            SCC = 512.0 ** -0.5
            for (c0, c1) in TCH:
                n = c1 - c0
                a, ab, asem = aring.next()
                P.dma("gpsimd", a[:, :, 0:n], x1T[:, c0:c1].rearrange("(k p) n -> p k n", p=128), asem, [X1B], [ab])
                for dc in range(KC):
                    bk, bkb = nb()
                    with P.group("tensor"):
                        for kc in range(KC):
                            mm(bk[:, 0:n], Wq[:, kc, dc * 128:(dc + 1) * 128], a[:, kc, 0:n], kc == 0, kc == KC - 1, [ab, WQB], [bkb])
                    cp("scalar" if dc % 2 == 0 else "vector", qc[:, dc, 0:n], bk[:, 0:n], [bkb], [QCB])
                oc, ocb, ocs = ocr.next()
                for hh in range(4):
                    i2 = hh % 2
                    for mt in range(2):
                        bsx, bsxb = nb()
                        with P.group("tensor"):
                            for c in range(4):
                                mm(bsx[:, 0:n], mkT[:, 4 * hh + c, mt * 128:(mt + 1) * 128], qc[:, 4 * hh + c, 0:n], c == 0, c == 3, [MKB, QCB], [bsxb])
                        act(Pc[i2][:, mt, 0:n], bsx[:, 0:n], AF.Exp, [bsxb], [PCB[i2]], scale=SCC)
                    bd, bdb = nb()
                    with P.group("tensor"):
                        for mt in range(2):
                            mm(bd[:, 0:n], ones_bf[:, :], Pc[i2][:, mt, 0:n], mt == 0, mt == 1, [CB, PCB[i2]], [bdb])
                    recip("vector", rden[i2][:, 0:n], bd[:, 0:n], [bdb], [RDB[i2]])
                    for c in range(4):
                        bo, bob = nb()
                        with P.group("tensor"):
                            for mt in range(2):
                                mm(bo[:, 0:n], mv[:, mt, (4 * hh + c) * 128:(4 * hh + c + 1) * 128], Pc[i2][:, mt, 0:n], mt == 0, mt == 1, [MVB, PCB[i2]], [bob])
                        tt("vector", oc[:, 4 * hh + c, 0:n], bo[:, 0:n], rden[i2][:, 0:n], ALU.mult, [bob, RDB[i2]], [ocb])
                P.dma("sync", ocT[:, c0:c1].rearrange("(k p) n -> p k n", p=128), oc[:, :, 0:n], ocs, [ocb], [OCB])
            P.flush()
        P.barrier()

        if stop == "D1":
            finish()
            return nc
        x1res = lambda dc, c0, c1: x1T[dc * 128:(dc + 1) * 128, c0:c1]
        gemm_ln_phase("D2", wo_in, ocT, False, x1res, 1, x2T, 0, TCH, OCB, X1B, X2B)
        if stop == "D2":
            finish()
            return nc

        with ExitStack() as es:
            sb = lambda name, shape, dt: es.enter_context(nc.sbuf_tensor("E" + name, list(shape), dt))
            x2b = sb("x2b", [128, KC, TQ], BF16)
            X2S = Buf()
            xs = P.new_dsem()
            for (c0, c1) in TCH[1:]:
                P.dma("gpsimd", x2b[:, :, c0:c1], x2T[:, c0:c1].rearrange("(k p) n -> p k n", p=128), xs, [X2B], [X2S])
            P.dma("gpsimd", x2b[:, :, 0:2], x2T[:, 0:2].rearrange("(k p) n -> p k n", p=128), xs, [X2B], [X2S])
            ts("vector", x2b[:, :, 0:2], x2b[:, :, 0:2], flag[:, 0:1], None, ALU.mult, None, [CB], [X2S])
            wr = Ring(P, [sb("w%d" % i, [128, KC, 256], BF16) for i in range(3)])
            ug = [sb("ug%d" % i, [128, TQ], F32) for i in range(2)]
            uu = [sb("uu%d" % i, [128, TQ], F32) for i in range(2)]
            yg = [sb("yg%d" % i, [128, 2048], F32) for i in range(2)]
            yu = [sb("yu%d" % i, [128, 2048], F32) for i in range(2)]
            hr = Ring(P, [sb("h%d" % i, [128, 2048], BF16) for i in range(2)])
            UG, UU, YG, YU = [Buf(), Buf()], [Buf(), Buf()], [Buf(), Buf()], [Buf(), Buf()]
            for j in range(NJ):
                i2 = j % 2
                wt, wtb, wts = wr.next()
                P.dma("gpsimd", wt[:, :, 0:128], fwin_in[:, j * 128:(j + 1) * 128].rearrange("(k p) n -> p k n", p=128), wts, (), [wtb])
                P.dma("gpsimd", wt[:, :, 128:256], fwin_in[:, DFF + j * 128:DFF + (j + 1) * 128].rearrange("(k p) n -> p k n", p=128), wts, (), [wtb])
                for part, (ubuf, UB) in enumerate([(ug[i2], UG[i2]), (uu[i2], UU[i2])]):
                    for ci, (c0, c1) in enumerate(TCH):
                        n = c1 - c0
                        bk, bkb = nb()
                        with P.group("tensor"):
                            for kc in range(KC):
                                mm(bk[:, 0:n], wt[:, kc, part * 128:(part + 1) * 128], x2b[:, kc, c0:c1], kc == 0, kc == KC - 1, [wtb, X2S], [bkb])
                        cp("scalar", ubuf[:, c0:c1], bk[:, 0:n], [bkb], [UB])
                for part, (eng, ubuf, UB, ybuf, YB_) in enumerate([("vector", ug[i2], UG[i2], yg[i2], YG[i2]), ("gpsimd", uu[i2], UU[i2], yu[i2], YU[i2])]):
                    cj = part * NJ + j
                    act(ybuf[:, :], ubuf[:, 2:TQ], AF.Identity, [UB, CB], [YB_], scale=cw[:, cj, 2:3], bias=cb[:, cj:cj + 1])
                    stt("vector", ybuf[:, :], ubuf[:, 1:TQ - 1], cw[:, cj, 1:2], ybuf[:, :], ALU.mult, ALU.add, [UB, CB], [YB_])
                    stt("vector", ybuf[:, :], ubuf[:, 0:TQ - 2], cw[:, cj, 0:1], ybuf[:, :], ALU.mult, ALU.add, [UB, CB], [YB_])
                act(yg[i2][:, :], yg[i2][:, :], AF.Silu, [], [YG[i2]])
                ht, htb, hts = hr.next()
                tt("vector", ht[:, :], yg[i2][:, :], yu[i2][:, :], ALU.mult, [YG[i2], YU[i2]], [htb])
                P.dma("sync", hT[j * 128:(j + 1) * 128, :], ht[:, :], hts, [htb], [HTB])
            P.flush()
        P.barrier()

        if stop == "E":
            finish()
            return nc
        with ExitStack() as es:
            sb = lambda name, shape, dt: es.enter_context(nc.sbuf_tensor("F" + name, list(shape), dt))
            aring = Ring(P, [sb("a%d" % i, [128, NJ, 512], BF16) for i in range(2)])
            w2r = Ring(P, [sb("w%d" % i, [128, 512], BF16) for i in range(8)])

            def wstream(j, qd_):
                wt, wtb, wts = w2r.next()
                P.dma("sync", wt[:, :], w2b[j * 128:(j + 1) * 128, qd_ * 512:(qd_ + 1) * 512], wts, [W2B], [wtb])
                return wt, wtb
            x2res = lambda dc, c0, c1: x2T[dc * 128:(dc + 1) * 128, 2 + c0:2 + c1]
            ln_chunks("F", sb, [(i * 512, (i + 1) * 512) for i in range(4)], NJ, None, None,
                      hT, False, aring, HTB, x2res, X2B, 2, outT, 0, OUTB, wstream=wstream)
            final = [(s[0], s[1], "dma") for s in P.dsems if s[1] > 0]
            P.flush(final)
    return nc


def _perm_idx():
    idx = np.empty(NL, np.int64)
    for s in range(2):
        for r in range(16):
            idx[s * 2048 + r * 128:s * 2048 + (r + 1) * 128] = s * 2048 + 16 * np.arange(128) + r
    return idx


def _masks():
    m = np.zeros((128, 6, 128), np.float32)
    j = np.arange(128)[:, None]
    i = np.arange(128)[None, :]
    for pi_, nat in enumerate([lambda x: x, lambda x: 4 * (x % 32) + x // 32, lambda x: 16 * (x % 8) + x // 8]):
        nj, ni = nat(j), nat(i)
        m[:, 2 * pi_, :] = (nj >= ni)
        m[:, 2 * pi_ + 1, :] = (nj <= ni)
    return m.reshape(128, 768)


def _fm(v, nchunk):
    return np.ascontiguousarray(v.reshape(nchunk, 128).T)


_CACHE = {}


def make_in_maps(x, mem, positions, w_in, gla_gate_w2, gla_gate_b, gla_norm_g, w_out, ln1_g, ln1_b,
                 ca_wq, ca_wkv, ca_wo, ln2_g, ln2_b, ffn_w_in, ffn_conv_w, ffn_conv_b, ffn_w_out, ln3_g, ln3_b):
    f32 = np.float32
    x = np.asarray(x, f32)
    mem = np.asarray(mem, f32)
    positions = np.asarray(positions, np.int32)
    w_in = np.asarray(w_in, f32)[0]
    pidx = _perm_idx()
    o = 0
    cols = {}
    for name, w in zip(["qg", "kg", "vg", "rg", "glr", "qd", "kd", "vd"], [512, 512, 1024, 1024, 16, 1024, 1024, 1024]):
        cols[name] = w_in[:, o:o + w]
        o += w
    wg = np.stack([np.concatenate([cols["qg"][:, h * 128:(h + 1) * 128], cols["kg"][:, h * 128:(h + 1) * 128],
                                   cols["rg"][:, h * 256:(h + 1) * 256], cols["vg"][:, h * 256:(h + 1) * 256]], axis=1) for h in range(4)])
    swap = np.concatenate([np.arange(16, 32), np.arange(0, 16)])
    wd = np.stack([np.concatenate([cols["qd"][:, h * 128:(h + 1) * 128], cols["kd"][:, h * 128:(h + 1) * 128],
                                   cols["qd"][:, h * 128 + swap], cols["kd"][:, h * 128 + swap]], axis=1) for h in range(8)])
    w2aug = np.concatenate([np.asarray(gla_gate_w2, f32)[0], np.asarray(gla_gate_b, f32)[0][None, :]], axis=0)
    lnp = np.stack([_fm(np.asarray(v, f32)[0], 16) for v in [ln1_g, ln1_b, ln2_g, ln2_b, ln3_g, ln3_b]], axis=1).reshape(128, 96)
    convw = np.ascontiguousarray(np.asarray(ffn_conv_w, f32)[0].T.reshape(86, 128, 3).transpose(1, 0, 2)).reshape(128, 258)
    convb = _fm(np.asarray(ffn_conv_b, f32)[0], 86)
    jj = np.arange(128)[:, None]
    ii = np.arange(128)[None, :]
    uneg = np.where(jj <= ii, f32(-1.0 / 16.0), f32(0.0)).astype(f32)
    invf = (500000.0 ** (-(np.arange(0, 32, 2, dtype=np.float32)) / 32.0)).astype(f32)
    rotc = np.stack([np.concatenate([invf, invf]), np.concatenate([-np.ones(16, f32), np.ones(16, f32)])], axis=1).astype(f32)
    shared = dict(masks=_masks(), uneg=uneg, rotc=rotc, wg=np.ascontiguousarray(wg), wglr=np.ascontiguousarray(cols["glr"]),
                  w2aug=np.ascontiguousarray(w2aug), glag=_fm(np.asarray(gla_norm_g, f32)[0], 8), wd=np.ascontiguousarray(wd),
                  wvd=np.ascontiguousarray(cols["vd"]), w_out=np.asarray(w_out, f32)[0], lnp=np.ascontiguousarray(lnp),
                  ca_wq=np.asarray(ca_wq, f32)[0], ca_wkv=np.asarray(ca_wkv, f32)[0], ca_wo=np.asarray(ca_wo, f32)[0],
                  ffn_w_in=np.asarray(ffn_w_in, f32)[0], convw=convw, convb=convb, ffn_w_out=np.asarray(ffn_w_out, f32)[0])
    in_maps = []
    for c in range(8):
        b, hf = c // 2, c % 2
        xl = np.zeros((NL, D), f32)
        pl = np.zeros((NL,), np.int32)
        if hf == 1:
            xl[:] = x[b]
            pl[:] = positions[b]
        else:
            xl[2048:] = x[b, :2048]
            pl[2048:] = positions[b, :2048]
        xTn = np.ascontiguousarray(xl.T)
        m = dict(shared)
        m.update(xT_nat=xTn, xT_perm=np.ascontiguousarray(xTn[:, pidx]), memT=np.ascontiguousarray(mem[b].T),
                 pos_perm=np.ascontiguousarray(pl[pidx][None, :]), flag=np.full((128, 1), float(hf), f32))
        in_maps.append(m)
    return in_maps


def kernel(**inputs):
    if "nc" not in _CACHE:
        _CACHE["nc"] = build(False)
    nc = _CACHE["nc"]
    in_maps = make_in_maps(**inputs)
    res = run_bass_kernel_spmd(nc, in_maps, core_ids=list(range(8)))
    out = np.empty((4, 4096, D), np.float32)
    for c in range(8):
        b, hf = c // 2, c % 2
        out[b, hf * 2048:(hf + 1) * 2048, :] = np.asarray(res.results[c]["outT"]).T
    return out
```

```python
import math
from contextlib import ExitStack, contextmanager
import numpy as np
import concourse.bass as bass
import concourse.mybir as mybir
from concourse.bass_utils import run_bass_kernel_spmd

F32 = mybir.dt.float32
BF16 = mybir.dt.bfloat16
I32 = mybir.dt.int32
AF = mybir.ActivationFunctionType
ALU = mybir.AluOpType

D = 2048
KC = 16
NL = 4096
TQ = 2050
QN = 2176
DFF = 5504
NJ = 43
ALPHA = 2.0 ** 0.25
LN_EPS = 1e-5
TCH = [(0, 2), (2, 514), (514, 1026), (1026, 1538), (1538, 2050)]
ENGS = ["tensor", "vector", "scalar", "gpsimd", "sync"]
PI = math.pi


class Buf:
    __slots__ = ("w", "r", "excl")

    def __init__(self, excl=False):
        self.w = None
        self.r = {}
        self.excl = excl


class Prog:
    def __init__(self, nc, es):
        self.nc = nc
        self.q = {e: [] for e in ENGS}
        self.cnt = {e: 0 for e in ENGS}
        self.sem = {e: es.enter_context(nc.semaphore("s_" + e)) for e in ENGS}
        self.grp = {e: None for e in ENGS}
        self.es = es
        self.dsems = []
        self.bar = {e: [] for e in ENGS}
        self.waited = {e: {} for e in ENGS}

    def _deps(self, reads, writes, extra):
        d = [x for x in extra if x is not None]
        if any(b.excl for b in reads):
            writes = list(writes) + [b for b in reads if b.excl]
            reads = [b for b in reads if not b.excl]
        for b in reads:
            if b.w is not None:
                d.append(b.w)
        for b in writes:
            if b.w is not None:
                d.append(b.w)
            d.extend(b.r.values())
        return d

    def _mark(self, tok, reads, writes):
        if any(b.excl for b in reads):
            writes = list(writes) + [b for b in reads if b.excl]
            reads = [b for b in reads if not b.excl]
        for b in reads:
            b.r[id(tok[0])] = tok
        for b in writes:
            b.w = tok
            b.r = {}

    def op(self, eng, fn, reads=(), writes=(), deps=()):
        d = self._deps(reads, writes, deps)
        if self.bar[eng]:
            d.extend(self.bar[eng])
            self.bar[eng] = []
        if eng == "tensor":
            d = [t for t in d if t[2] != "tensor"]
        if self.grp[eng] is not None:
            tok = self.grp[eng]
            d = [t for t in d if not (t[0] is tok[0] and t[1] == tok[1])]
            self.q[eng].append(["op", fn, d, False])
        else:
            self.cnt[eng] += 1
            tok = (self.sem[eng], self.cnt[eng], eng)
            self.q[eng].append(["op", fn, d, True])
        self._mark(tok, reads, writes)
        return tok

    @contextmanager
    def group(self, eng):
        tok = (self.sem[eng], self.cnt[eng] + 1, eng)
        self.grp[eng] = tok
        n0 = len(self.q[eng])
        yield tok
        self.grp[eng] = None
        assert len(self.q[eng]) > n0
        self.q[eng][-1][3] = True
        self.cnt[eng] += 1

    def _raw_dsem(self):
        s = [self.es.enter_context(self.nc.semaphore("d%d" % len(self.dsems))), 0]
        self.dsems.append(s)
        return s

    def new_dsem(self):
        return {}

    def dma(self, eng, out, in_, sem, reads=(), writes=(), deps=()):
        d = self._deps(reads, writes, deps)
        if self.bar[eng]:
            d.extend(self.bar[eng])
            self.bar[eng] = []
        if eng not in sem:
            sem[eng] = self._raw_dsem()
        sem = sem[eng]
        sem[1] += 16
        tok = (sem[0], sem[1], "dma")
        self.q[eng].append(["dma", (out, in_, sem[0]), d, True])
        self._mark(tok, reads, writes)
        return tok

    def barrier(self):
        toks = [(self.sem[e], self.cnt[e], "bar") for e in ENGS if self.cnt[e] > 0]
        toks += [(s[0], s[1], "dma") for s in self.dsems if s[1] > 0 and not (len(s) > 2 and s[2])]
        for e in ENGS:
            self.bar[e] = list(toks)

    def flush(self, final=()):
        with self.nc.Block() as block:
            def mk(ename):
                def body(eng):
                    waited = self.waited[ename]
                    for kind, payload, deps, sig in self.q[ename]:
                        for (s, v, src) in deps:
                            key = id(s)
                            if waited.get(key, 0) >= v:
                                continue
                            eng.wait_ge(s, v)
                            waited[key] = v
                        if kind == "op":
                            ins = payload(eng)
                            if sig:
                                ins.then_inc(self.sem[ename], 1)
                        else:
                            out, in_, s = payload
                            eng.dma_start(out=out, in_=in_).then_inc(s, 16)
                    if ename == "sync":
                        for (s, v, src) in final:
                            eng.wait_ge(s, v)
                    self.q[ename] = []
                return body
            block.tensor(mk("tensor"))
            block.vector(mk("vector"))
            block.scalar(mk("scalar"))
            block.gpsimd(mk("gpsimd"))
            block.sync(mk("sync"))


def pipeline(n, stages):
    S = len(stages)
    ctx = [dict() for _ in range(n)]
    for t in range(n + S - 1):
        for s_ in reversed(range(S)):
            i = t - s_
            if 0 <= i < n:
                stages[s_](i, ctx[i])


def pipeline_gen(n, stages):
    S = len(stages)
    ctx = [dict() for _ in range(n)]
    for t in range(n + S - 1):
        for s_ in reversed(range(S)):
            i = t - s_
            if 0 <= i < n:
                stages[s_](i, ctx[i])
        yield t


class Ring:
    def __init__(self, P, tensors):
        self.t = tensors
        self.b = [Buf() for _ in tensors]
        self.s = [P.new_dsem() for _ in tensors]
        self.i = -1

    def next(self):
        self.i = (self.i + 1) % len(self.t)
        return self.t[self.i], self.b[self.i], self.s[self.i]


def build(debug=False, stop=None, ngla=4, ndil=8):
    nc = bass.Bass("TRN2", target_bir_lowering=False)
    din = lambda name, shape, dt=F32: nc.dram_tensor(name, list(shape), dt, kind="ExternalInput").ap()
    okind = "ExternalOutput" if debug else "Internal"
    dscr = lambda name, shape, dt: nc.dram_tensor(name, list(shape), dt, kind=okind).ap()
    xT_nat = din("xT_nat", [D, NL])
    xT_perm = din("xT_perm", [D, NL])
    memT = din("memT", [D, 256])
    pos_in = din("pos_perm", [1, NL], I32)
    flag_in = din("flag", [128, 1])
    masks_in = din("masks", [128, 6 * 128])
    uneg_in = din("uneg", [128, 128])
    rotc_in = din("rotc", [32, 2])
    wg_in = din("wg", [4, D, 768])
    wglr_in = din("wglr", [D, 16])
    w2aug_in = din("w2aug", [17, 512])
    glag_in = din("glag", [128, 8])
    wd_in = din("wd", [8, D, 320])
    wvd_in = din("wvd", [D, 1024])
    wout_in = din("w_out", [D, D])
    lnp_in = din("lnp", [128, 6 * 16])
    wq_in = din("ca_wq", [D, D])
    wkv_in = din("ca_wkv", [D, 2 * D])
    wo_in = din("ca_wo", [D, D])
    fwin_in = din("ffn_w_in", [D, 2 * DFF])
    cw_in = din("convw", [128, 86 * 3])
    cb_in = din("convb", [128, 86])
    fwout_in = din("ffn_w_out", [DFF, D])
    outT = nc.dram_tensor("outT", [D, 2048], F32, kind="ExternalOutput").ap()

    xb_nat = dscr("xb_nat", [D, NL], BF16)
    xb_perm = dscr("xb_perm", [D, NL], BF16)
    vd_scr = dscr("vd_scr", [NL, 1024], BF16)
    vd4_scr = dscr("vd4_scr", [NL, 1024], BF16)
    vd1_scr = dscr("vd1_scr", [NL, 1024], BF16)
    mixT = dscr("mixT", [D, TQ], BF16)
    x1T = dscr("x1T", [D, TQ], F32)
    ocT = dscr("ocT", [D, TQ], BF16)
    x2T = dscr("x2T", [D, TQ], F32)
    hT = dscr("hT", [DFF, 2048], BF16)
    w2b = dscr("w2b", [DFF, D], BF16)

    with ExitStack() as ges:
        P = Prog(nc, ges)
        gsb = lambda name, shape, dt: ges.enter_context(nc.sbuf_tensor(name, list(shape), dt))
        banks = [ges.enter_context(nc.psum_tensor("pb%d" % i, [128, 512], F32)) for i in range(7)]
        bankb = [Buf(True) for _ in range(7)]
        ptb = ges.enter_context(nc.psum_tensor("ptb", [128, 1024], BF16))
        ptb_b = Buf(True)
        bi = [0]

        def finish():
            final = [(s_[0], s_[1], "dma") for s_ in P.dsems if s_[1] > 0]
            final += [(P.sem[e_], P.cnt[e_], "bar") for e_ in ENGS if P.cnt[e_] > 0 and e_ != "sync"]
            P.flush(final)

        busy = set()

        def nb(reserve=False):
            for _ in range(8):
                bi[0] = (bi[0] + 1) % 7
                if bi[0] not in busy:
                    break
            else:
                raise RuntimeError("no free PSUM bank")
            if reserve:
                busy.add(bi[0])
            return banks[bi[0]], bankb[bi[0]]

        def rel(bb):
            busy.discard(bankb.index(bb))

        def mm(out, lhsT, rhs, start, stop, reads, writes):
            return P.op("tensor", lambda e: e.matmul(out, lhsT=lhsT, rhs=rhs, start=start, stop=stop), reads, writes)

        def act(out, in_, func, reads, writes, bias=None, scale=None):
            kw = {}
            if bias is not None:
                kw["bias"] = bias
            if scale is not None:
                kw["scale"] = scale
            return P.op("scalar", lambda e: e.activation(out=out, in_=in_, func=func, **kw), reads, writes)

        def tt(eng, out, in0, in1, op, reads, writes):
            return P.op(eng, lambda e: e.tensor_tensor(out=out, in0=in0, in1=in1, op=op), reads, writes)

        def ts(eng, out, in0, s1, s2, op0, op1, reads, writes):
            if op1 is None:
                return P.op(eng, lambda e: e.tensor_scalar(out=out, in0=in0, scalar1=s1, scalar2=None, op0=op0), reads, writes)
            return P.op(eng, lambda e: e.tensor_scalar(out=out, in0=in0, scalar1=s1, scalar2=s2, op0=op0, op1=op1), reads, writes)

        def stt(eng, out, in0, scalar, in1, op0, op1, reads, writes):
            return P.op(eng, lambda e: e.scalar_tensor_tensor(out=out, in0=in0, scalar=scalar, in1=in1, op0=op0, op1=op1), reads, writes)

        def cp(eng, out, in_, reads, writes):
            if eng == "scalar":
                return P.op(eng, lambda e: e.activation(out=out, in_=in_, func=AF.Copy), reads, writes)
            return P.op(eng, lambda e: e.tensor_copy(out=out, in_=in_), reads, writes)

        def recip(eng, out, in_, reads, writes):
            return P.op(eng, lambda e: e.reciprocal(out=out, in_=in_), reads, writes)

        def mset(eng, ap, val, writes):
            return P.op(eng, lambda e: e.memset(ap, val), (), writes)

        ident = gsb("ident", [128, 128], BF16)
        ones_bf = gsb("ones_bf", [128, 128], BF16)
        ones256 = gsb("ones256", [128, 128], F32)
        ones2048 = gsb("ones2048", [128, 128], F32)
        masks = gsb("masks_sb", [128, 6, 128], BF16)
        masksc = gsb("masksc_sb", [128, 6, 128], BF16)
        uneg = gsb("uneg_sb", [128, 128], F32)
        flag = gsb("flag_sb", [128, 1], F32)
        lnp = gsb("lnp_sb", [128, 6, 16], F32)
        glag = gsb("glag_sb", [128, 8], F32)
        cw = gsb("cw_sb", [128, 86, 3], F32)
        cb = gsb("cb_sb", [128, 86], F32)
        mtmp = gsb("mtmp", [128, 6, 128], F32)
        CB = Buf()
        csem = P.new_dsem()
        P.dma("sync", mtmp[:, :, :], masks_in.rearrange("p (a b) -> p a b", a=6), csem, (), [CB])
        P.dma("sync", uneg[:, :], uneg_in, csem, (), [CB])
        P.dma("sync", flag[:, :], flag_in, csem, (), [CB])
        P.dma("sync", lnp[:, :, :], lnp_in.rearrange("p (a b) -> p a b", a=6), csem, (), [CB])
        P.dma("sync", glag[:, :], glag_in, csem, (), [CB])
        P.dma("sync", cw[:, :, :], cw_in.rearrange("p (a b) -> p a b", b=3), csem, (), [CB])
        P.dma("sync", cb[:, :], cb_in, csem, (), [CB])
        mset("gpsimd", ident[:, :], 0.0, [CB])
        P.op("gpsimd", lambda e: e.affine_select(out=ident[:, :], in_=ident[:, :], pattern=[[-1, 128]],
                                                 compare_op=ALU.not_equal, fill=1.0, base=0, channel_multiplier=1), [CB], [CB])
        mset("gpsimd", ones_bf[:, :], 1.0, [CB])
        mset("gpsimd", ones256[:, :], 1.0 / 256.0, [CB])
        mset("gpsimd", ones2048[:, :], 1.0 / 2048.0, [CB])
        cp("vector", masks[:, :, :], mtmp[:, :, :], [CB], [CB])
        ts("vector", masksc[:, :, :], mtmp[:, :, :], flag[:, 0:1], None, ALU.mult, None, [CB], [CB])

        XN = [Buf() for _ in range(8)]
        XP = [Buf() for _ in range(8)]
        P.flush()
        if stop == "pre":
            finish()
            return nc

        MIXB = Buf()
        VDB = Buf()

        with ExitStack() as es:
            sb = lambda name, shape, dt: es.enter_context(nc.sbuf_tensor(name, list(shape), dt))
            xring = Ring(P, [sb("xblk%d" % i, [128, KC, 512], BF16) for i in range(2)])

            def load_x(src, srcbufs, blk):
                t, b, s = xring.next()
                P.dma("sync", t[:, :, :], src[:, blk * 512:(blk + 1) * 512].rearrange("(k p) n -> p k n", p=128), s, [srcbufs[blk]], [b])
                return t, b

            glrT = sb("glrT", [32, NL], F32)
            GLR = Buf()
            wglr = sb("wglr_sb", [128, KC, 16], BF16)
            w2aug = sb("w2aug_sb", [17, 512], F32)
            WS = Buf()
            wsem = P.new_dsem()
            P.dma("gpsimd", wglr[:, :, :], wglr_in.rearrange("(k p) n -> p k n", p=128), wsem, (), [WS])
            P.dma("sync", w2aug[:, :], w2aug_in, wsem, (), [WS])
            def load_first(ring, src32, dst16, dbufs, blk):
                t, b, s_ = ring.next()
                cols = slice(blk * 512, (blk + 1) * 512)
                P.dma("gpsimd", t[:, :, :], src32[:, cols].rearrange("(k p) n -> p k n", p=128), s_, (), [b])
                P.dma("sync", dst16[:, cols].rearrange("(k p) n -> p k n", p=128), t[:, :, :], s_, [b], [dbufs[blk]])
                return t, b
            mset("vector", glrT[:, :], 1.0, [GLR])
            for blk in range(8):
                xb, xbb = load_first(xring, xT_nat, xb_nat, XN, blk)
                bk, bkb = nb()
                with P.group("tensor"):
                    for kc in range(KC):
                        mm(bk[0:16, :], wglr[:, kc, :], xb[:, kc, :], kc == 0, kc == KC - 1, [xbb, WS], [bkb])
                cp("scalar", glrT[0:16, blk * 512:(blk + 1) * 512], bk[0:16, :], [bkb], [GLR])

            if stop == "glr":
                finish()
                return nc
            wgring = Ring(P, [sb("wg%d" % i, [128, KC, 768], BF16) for i in range(1)])
            qT = sb("g_qT", [128, QN], BF16)
            kT = sb("g_kT", [128, NL], BF16)
            rT = sb("g_rT", [128, 2, QN], BF16)
            vsb = sb("g_v", [128, 32, 256], BF16)
            GB = [(sb("g_enb%d" % i, [128, NL], BF16), sb("g_kef%d" % i, [128, NL], BF16), sb("g_ebq%d" % i, [128, QN], BF16),
                   sb("g_decay%d" % i, [128, 32], F32), sb("g_blast%d" % i, [128, 32], F32)) for i in range(2)]
            HGB = [(Buf(), Buf()) for _ in range(2)]
            Sst = sb("g_S", [128, 256], F32)
            Sbf = [sb("g_Sbf%d" % i, [128, 256], BF16) for i in range(3)]
            og = sb("g_og", [128, 2, TQ], BF16)
            tA = [sb("g_tA%d" % i, [128, 128], F32) for i in range(4)]
            spb = [sb("g_sp%d" % i, [128, 128], F32) for i in range(4)]
            kin = [sb("g_kin%d" % i, [128, 128], BF16) for i in range(4)]
            kend = [sb("g_kend%d" % i, [128, 128], BF16) for i in range(4)]
            kendT = [sb("g_kendT%d" % i, [128, 128], BF16) for i in range(4)]
            qin = [sb("g_qin%d" % i, [128, 128], BF16) for i in range(4)]
            Am = [sb("g_Am%d" % i, [128, 128], BF16) for i in range(4)]
            osb = [sb("g_osb%d" % i, [128, 2, 128], F32) for i in range(4)]
            osq = [sb("g_osq%d" % i, [128, 2, 128], F32) for i in range(4)]
            mst = [sb("g_mst%d" % i, [128, 256], F32) for i in range(4)]
            t1 = [sb("g_t1%d" % i, [128, 128], F32) for i in range(4)]
            on2 = [[sb("g_on2_%d_%d" % (i, e_), [128, 128], F32) for e_ in range(2)] for i in range(4)]
            TB2 = {"on": [[Buf(), Buf()] for _ in range(4)]}
            HB = {k: Buf() for k in ["q", "k", "r", "v", "gate", "S", "og", "bl"]}
            Sbfb = [Buf(), Buf(), Buf()]
            TB = {k: [Buf() for _ in range(4)] for k in ["tA", "sp", "kin", "kend", "kendT", "qin", "Am", "osb", "osq", "mst", "t1", "t2", "sr", "on"]}
            ogsem = P.new_dsem()
            LNS = math.log(128.0 ** -0.5)

            wg, wgb, wgs = wgring.next()
            P.dma("gpsimd", wg[:, :, :], wg_in[0].rearrange("(k p) n -> p k n", p=128), wgs, (), [wgb])
            def make_gates(hh):
                enb, kef, ebq, decay, blast = GB[hh % 2]
                HBg, HBbl = HGB[hh % 2]

                def g0(n, c):
                    c["bz"], c["bzb"] = nb(True)
                    mm(c["bz"][:, 0:128], glrT[0:17, n * 128:(n + 1) * 128], w2aug[0:17, hh * 128:(hh + 1) * 128], True, True, [GLR, WS], [c["bzb"]])

                def g1(n, c):
                    i4 = n % 4
                    act(tA[i4][:, :], c["bz"][:, 0:128], AF.Exp, [c["bzb"]], [TB["tA"][i4]], scale=-1.0)
                    act(spb[i4][:, :], tA[i4][:, :], AF.Ln, [TB["tA"][i4]], [TB["sp"][i4]], bias=1.0)
                    rel(c["bzb"])

                def g2(n, c):
                    i4 = n % 4
                    c["bt"], c["btb"] = nb(True)
                    mm(c["bt"][:, 0:128], spb[i4][:, :], uneg[:, :], True, True, [TB["sp"][i4], CB], [c["btb"]])

                def g3(n, c):
                    bt, btb = c["bt"], c["btb"]
                    cp("vector", blast[:, n:n + 1], bt[:, 127:128], [btb], [HBbl])
                    act(enb[:, n * 128:(n + 1) * 128], bt[:, 0:128], AF.Exp, [btb], [HBg], scale=-1.0)
                    act(kef[:, n * 128:(n + 1) * 128], bt[:, 0:128], AF.Exp, [btb, HBbl], [HBg], scale=-1.0, bias=blast[:, n:n + 1])
                    if n >= 15:
                        act(ebq[:, (n - 15) * 128:(n - 14) * 128], bt[:, 0:128], AF.Exp, [btb], [HBg], bias=LNS)
                    act(decay[:, n:n + 1], blast[:, n:n + 1], AF.Exp, [HBbl], [HBg])
                    rel(btb)
                return pipeline_gen(32, [g0, g1, g2, g3])

            gsteps = make_gates(0)
            for h in range(ngla):
                enb, kef, ebq, decay, blast = GB[h % 2]
                HB["gate"], HB["bl"] = HGB[h % 2]
                for blk in range(8):
                    if h == 0:
                        for _ in range(5):
                            next(gsteps, None)
                    xb, xbb = load_x(xb_nat, XN, blk)
                    bk, bkb = nb()
                    with P.group("tensor"):
                        for kc in range(KC):
                            mm(bk[:, :], wg[:, kc, 128:256], xb[:, kc, :], kc == 0, kc == KC - 1, [xbb, wgb], [bkb])
                    cp("scalar", kT[:, blk * 512:(blk + 1) * 512], bk[:, :], [bkb], [HB["k"]])
                    if blk >= 3:
                        x0, nn_, q0 = (384, 128, 0) if blk == 3 else (0, 512, 128 + (blk - 4) * 512)
                        for (c0, dst, hb) in [(0, qT[:, q0:q0 + nn_], "q"), (256, rT[:, 0, q0:q0 + nn_], "r"), (384, rT[:, 1, q0:q0 + nn_], "r")]:
                            bq, bqb = nb()
                            with P.group("tensor"):
                                for kc in range(KC):
                                    mm(bq[:, 0:nn_], wg[:, kc, c0:c0 + 128], xb[:, kc, x0:x0 + nn_], kc == 0, kc == KC - 1, [xbb, wgb], [bqb])
                            cp("scalar" if hb == "q" else "vector", dst, bq[:, 0:nn_], [bqb], [HB[hb]])
                    for sub in range(4):
                        bv, bvb = nb()
                        with P.group("tensor"):
                            for kc in range(KC):
                                mm(bv[:, 0:256], xb[:, kc, sub * 128:(sub + 1) * 128], wg[:, kc, 512:768], kc == 0, kc == KC - 1, [xbb, wgb], [bvb])
                        cp("vector", vsb[:, blk * 4 + sub, :], bv[:, 0:256], [bvb], [HB["v"]])
                for _ in gsteps:
                    pass
                if h + 1 < ngla:
                    P.dma("gpsimd", wg[:, :, :], wg_in[h + 1].rearrange("(k p) n -> p k n", p=128), wgs, (), [wgb])
                act(rT[:, :, :], rT[:, :, :], AF.Silu, [], [HB["r"]])
                if stop == "proj":
                    finish()
                    return nc
                mset("vector", Sst[:, :], 0.0, [HB["S"]])
                mset("gpsimd", Sbf[2][:, :], 0.0, [Sbfb[2]])

                def c0_(n, c):
                    i4 = n % 4
                    tok = slice(n * 128, (n + 1) * 128)
                    tt("vector", kin[i4][:, :], kT[:, tok], enb[:, tok], ALU.mult, [HB["k"], HB["gate"]], [TB["kin"][i4]])
                    tt("gpsimd", kend[i4][:, :], kT[:, tok], kef[:, tok], ALU.mult, [HB["k"], HB["gate"]], [TB["kend"][i4]])
                    if n >= 15:
                        q0 = (n - 15) * 128
                        tt("vector", qin[i4][:, :], qT[:, q0:q0 + 128], ebq[:, q0:q0 + 128], ALU.mult, [HB["q"], HB["gate"]], [TB["qin"][i4]])

                def c1_(n, c):
                    i4 = n % 4
                    P.op("tensor", lambda e, i4=i4: e.transpose(out=ptb[:, i4 * 128:(i4 + 1) * 128], in_=kend[i4][:, :], identity=ident[:, :]),
                         [TB["kend"][i4], CB], [ptb_b])
                    if n >= 15:
                        c["ba"], c["bab"] = nb(True)
                        mm(c["ba"][:, 0:128], kin[i4][:, :], qin[i4][:, :], True, True, [TB["kin"][i4], TB["qin"][i4]], [c["bab"]])

                def c2_(n, c):
                    i4 = n % 4
                    cp("scalar", kendT[i4][:, :], ptb[:, i4 * 128:(i4 + 1) * 128], [ptb_b], [TB["kendT"][i4]])
                    if n >= 15:
                        q0 = (n - 15) * 128
                        tt("vector", Am[i4][:, :], c["ba"][:, 0:128], masks[:, 1, :], ALU.mult, [c["bab"], CB], [TB["Am"][i4]])
                        rel(c["bab"])

                def c3_(n, c):
                    i4 = n % 4
                    Sprev, Sprevb = Sbf[(n + 2) % 3], Sbfb[(n + 2) % 3]
                    if n >= 15:
                        c["bo"], c["bob"] = nb(True)
                        for e_ in range(2):
                            with P.group("tensor"):
                                mm(c["bo"][:, e_ * 128:(e_ + 1) * 128], vsb[:, n, e_ * 128:(e_ + 1) * 128], Am[i4][:, :], True, False, [HB["v"], TB["Am"][i4]], [c["bob"]])
                                mm(c["bo"][:, e_ * 128:(e_ + 1) * 128], Sprev[:, e_ * 128:(e_ + 1) * 128], qin[i4][:, :], False, True, [Sprevb, TB["qin"][i4]], [c["bob"]])
                    c["bs"], c["bsb"] = nb(True)
                    mm(c["bs"][:, 0:256], kendT[i4][:, :], vsb[:, n, :], True, True, [TB["kendT"][i4], HB["v"]], [c["bsb"]])

                def c4_(n, c):
                    i4 = n % 4
                    stt("vector", Sst[:, :], Sst[:, :], decay[:, n:n + 1], c["bs"][:, 0:256], ALU.mult, ALU.add, [c["bsb"], HB["gate"]], [HB["S"]])
                    cp("gpsimd", Sbf[n % 3][:, :], Sst[:, :], [HB["S"]], [Sbfb[n % 3]])
                    rel(c["bsb"])
                    if n >= 15:
                        cp("scalar", osb[i4][:, :, :], c["bo"][:, 0:256].rearrange("p (a b) -> p a b", a=2), [c["bob"]], [TB["osb"][i4]])
                        rel(c["bob"])

                def c5_(n, c):
                    i4 = n % 4
                    if n < 15:
                        return
                    tt("gpsimd", osq[i4][:, :, :], osb[i4][:, :, :], osb[i4][:, :, :], ALU.mult, [TB["osb"][i4]], [TB["osq"][i4]])
                    c["bm"], c["bmb"] = nb(True)
                    bm, bmb = c["bm"], c["bmb"]
                    with P.group("tensor"):
                        mm(bm[:, 0:128], ones256[:, :], osb[i4][:, 0, :], True, False, [CB, TB["osb"][i4]], [bmb])
                        mm(bm[:, 0:128], ones256[:, :], osb[i4][:, 1, :], False, True, [CB, TB["osb"][i4]], [bmb])
                    with P.group("tensor"):
                        mm(bm[:, 128:256], ones256[:, :], osq[i4][:, 0, :], True, False, [CB, TB["osq"][i4]], [bmb])
                        mm(bm[:, 128:256], ones256[:, :], osq[i4][:, 1, :], False, True, [CB, TB["osq"][i4]], [bmb])

                def c6_(n, c):
                    i4 = n % 4
                    if n < 15:
                        return
                    q0 = (n - 15) * 128
                    cp("scalar", mst[i4][:, :], c["bm"][:, 0:256], [c["bmb"]], [TB["mst"][i4]])
                    rel(c["bmb"])
                    tt("vector", t1[i4][:, :], mst[i4][:, 0:128], mst[i4][:, 0:128], ALU.mult, [TB["mst"][i4]], [TB["t1"][i4]])
                    tt("vector", t1[i4][:, :], mst[i4][:, 128:256], t1[i4][:, :], ALU.subtract, [TB["mst"][i4]], [TB["t1"][i4]])
                    act(t1[i4][:, :], t1[i4][:, :], AF.Ln, [], [TB["t1"][i4]], bias=LN_EPS)
                    act(t1[i4][:, :], t1[i4][:, :], AF.Exp, [], [TB["t1"][i4]], scale=-0.5)

                def c7_(n, c):
                    i4 = n % 4
                    if n < 15:
                        return
                    q0 = (n - 15) * 128
                    for e_ in range(2):
                        onb, ONB = on2[i4][e_], TB2["on"][i4][e_]
                        tt("vector", onb[:, :], osb[i4][:, e_, :], mst[i4][:, 0:128], ALU.subtract, [TB["osb"][i4], TB["mst"][i4]], [ONB])
                        tt("gpsimd", onb[:, :], onb[:, :], t1[i4][:, :], ALU.mult, [TB["t1"][i4]], [ONB])
                        if n == 15:
                            stt("vector", og[:, e_, 0:2], onb[:, 126:128], glag[:, 2 * h + e_:2 * h + e_ + 1], rT[:, e_, q0 + 126:q0 + 128],
                                ALU.mult, ALU.mult, [ONB, HB["r"], CB], [HB["og"]])
                        else:
                            o0 = 2 + (n - 16) * 128
                            stt("vector", og[:, e_, o0:o0 + 128], onb[:, :], glag[:, 2 * h + e_:2 * h + e_ + 1], rT[:, e_, q0:q0 + 128],
                                ALU.mult, ALU.mult, [ONB, HB["r"], CB], [HB["og"]])
                gsteps = make_gates(h + 1) if h + 1 < ngla else iter(())
                for _ in pipeline_gen(32, [c0_, c1_, c2_, c3_, c4_, c5_, c6_, c7_]):
                    next(gsteps, None)
                for _ in gsteps:
                    pass
                for e_ in range(2):
                    P.dma("sync", mixT[(2 * h + e_) * 128:(2 * h + e_ + 1) * 128, :], og[:, e_, :], ogsem, [HB["og"]], [MIXB])
            P.flush()

        P.barrier()
        if stop == "A":
            finish()
            return nc
        with ExitStack() as es:
            sb = lambda name, shape, dt: es.enter_context(nc.sbuf_tensor(name, list(shape), dt))
            xring = Ring(P, [sb("xblkb%d" % i, [128, KC, 512], BF16) for i in range(2)])

            def load_xp(blk):
                t, b, s = xring.next()
                P.dma("sync", t[:, :, :], xb_perm[:, blk * 512:(blk + 1) * 512].rearrange("(k p) n -> p k n", p=128), s, [XP[blk]], [b])
                return t, b

            cosT = sb("cosT", [32, NL], F32)
            sinT = sb("sinT", [32, NL], F32)
            ROT = Buf()
            with ExitStack() as es2:
                sb2 = lambda name, shape, dt: es2.enter_context(nc.sbuf_tensor(name, list(shape), dt))
                posi = sb2("posi", [32, NL], I32)
                ang = sb2("ang", [32, NL], F32)
                tf = sb2("tf", [32, NL], F32)
                rr = sb2("rr", [32, NL], F32)
                mk_ = sb2("mk_", [32, NL], F32)
                rotc = sb2("rotc_sb", [32, 2], F32)
                RB = Buf()
                rsem = P.new_dsem()
                P.dma("sync", posi[:, :], pos_in.partition_broadcast(32), rsem, (), [RB])
                P.dma("sync", rotc[:, :], rotc_in, rsem, (), [RB])
                defer = []

                def DF(fn, *a_, **k_):
                    defer.append(lambda: fn(*a_, **k_))
                DF(cp, "vector", ang[:, :], posi[:, :], [RB], [RB])
                DF(ts, "vector", ang[:, :], ang[:, :], rotc[:, 0:1], None, ALU.mult, None, [RB], [RB])
                DF(ts, "vector", tf[:, :], ang[:, :], 1.0 / (2 * PI), 0.5, ALU.mult, ALU.add, [RB], [RB])
                DF(cp, "vector", posi[:, :], tf[:, :], [RB], [RB])
                DF(cp, "vector", tf[:, :], posi[:, :], [RB], [RB])
                C1 = 6.28125
                C2 = 2 * PI - C1
                DF(stt, "vector", rr[:, :], tf[:, :], -C1, ang[:, :], ALU.mult, ALU.add, [RB], [RB])
                DF(stt, "vector", rr[:, :], tf[:, :], -C2, rr[:, :], ALU.mult, ALU.add, [RB], [RB])

                def wrap_clamp(r):
                    DF(ts, "vector", mk_[:, :], r[:, :], -PI, None, ALU.is_lt, None, [RB], [RB])
                    DF(stt, "vector", r[:, :], mk_[:, :], 2 * PI, r[:, :], ALU.mult, ALU.add, [RB], [RB])
                    DF(ts, "vector", mk_[:, :], r[:, :], PI, None, ALU.is_gt, None, [RB], [RB])
                    DF(stt, "vector", r[:, :], mk_[:, :], -2 * PI, r[:, :], ALU.mult, ALU.add, [RB], [RB])
                    DF(ts, "vector", r[:, :], r[:, :], -3.141592, 3.141592, ALU.max, ALU.min, [RB], [RB])
                wrap_clamp(rr)
                DF(act, sinT[:, :], rr[:, :], AF.Sin, [RB], [ROT], scale=rotc[:, 1:2])
                DF(ts, "vector", rr[:, :], rr[:, :], PI / 2, None, ALU.add, None, [RB], [RB])
                wrap_clamp(rr)
                DF(act, cosT[:, :], rr[:, :], AF.Sin, [RB], [ROT])

                wvd = sb2("wvd_sb", [128, KC, 1024], BF16)
                WV = Buf()
                wvs = P.new_dsem()
                for g in range(2):
                    P.dma("gpsimd", wvd[:, :, g * 512:(g + 1) * 512], wvd_in[:, g * 512:(g + 1) * 512].rearrange("(k p) n -> p k n", p=128), wvs, (), [WV])
                vst = Ring(P, [sb2("vst%d" % i, [128, 1024], BF16) for i in range(2)])
                for blk in range(8):
                    t_, b_, s__ = xring.next()
                    cols_ = slice(blk * 512, (blk + 1) * 512)
                    P.dma("gpsimd", t_[:, :, :], xT_perm[:, cols_].rearrange("(k p) n -> p k n", p=128), s__, (), [b_])
                    P.dma("sync", xb_perm[:, cols_].rearrange("(k p) n -> p k n", p=128), t_[:, :, :], s__, [b_], [XP[blk]])
                    xb, xbb = t_, b_
                    for sub in range(4):
                        if defer:
                            defer.pop(0)()
                        vt, vtb, vts = vst.next()
                        for g in range(2):
                            bv, bvb = nb()
                            with P.group("tensor"):
                                for kc in range(KC):
                                    mm(bv[:, :], xb[:, kc, sub * 128:(sub + 1) * 128], wvd[:, kc, g * 512:(g + 1) * 512], kc == 0, kc == KC - 1, [xbb, WV], [bvb])
                            cp("scalar" if g == 0 else "vector", vt[:, g * 512:(g + 1) * 512], bv[:, :], [bvb], [vtb])
                        r0 = (blk * 4 + sub) * 128
                        P.dma("scalar", vd_scr[r0:r0 + 128, :], vt[:, :], vts, [vtb], [VDB])
                        T_ = blk * 4 + sub
                        s_, r16 = T_ // 16, T_ % 16
                        a_, r4 = r16 // 4, r16 % 4
                        d4 = vd4_scr.rearrange("(t i) c -> t i c", i=128)[r4 * 8 + 4 * s_:r4 * 8 + 4 * s_ + 4, 32 * a_:32 * a_ + 32, :]
                        P.dma("scalar", d4, vt[:, :], vts, [vtb], [VDB])
                        d1 = vd1_scr.rearrange("(t i) c -> t i c", i=128)[16 * s_:16 * s_ + 16, 8 * r16:8 * r16 + 8, :]
                        P.dma("scalar", d1, vt[:, :], vts, [vtb], [VDB])
                while defer:
                    defer.pop(0)()
                P.flush()
            P.barrier()
            if stop == "V":
                finish()
                return nc

            wdring = Ring(P, [sb("wd%d" % i, [128, KC, 320], BF16) for i in range(2)])
            dq2 = [sb("d_qq%d" % i, [128, 2560], BF16) for i in range(2)]
            dk2 = [sb("d_kk%d" % i, [128, NL], BF16) for i in range(2)]
            DBQ, DBK = [Buf(), Buf()], [Buf(), Buf()]
            dk4 = sb("d_k4", [128, NL], BF16)
            dk1 = sb("d_k1", [128, NL], BF16)
            dq4 = sb("d_q4", [128, 2048], BF16)
            dq1 = sb("d_q1", [128, 2048], BF16)
            hq1 = sb("d_hq1", [128, 16], BF16)
            V16 = sb("d_v16", [128, 32, 128], BF16)
            V4 = sb("d_v4", [128, 32, 128], BF16)
            V1 = sb("d_v1", [128, 32, 128], BF16)
            acc = sb("d_acc", [128, 2560], F32)
            dacc = sb("d_dacc", [128, 2560], F32)
            odb = sb("d_od", [128, TQ], BF16)
            rt1 = [sb("d_rt1%d" % i, [32, 512], F32) for i in range(2)]
            rt2 = [sb("d_rt2%d" % i, [32, 512], F32) for i in range(2)]
            Pm = [sb("d_P%d" % i, [128, 2, 128], BF16) for i in range(4)]
            DB = {k: Buf() for k in ["q", "k", "v", "acc", "od", "k4", "k1", "q4", "q1"]}
            DBV = {16: Buf(), 4: Buf(), 1: Buf()}
            RTB = [[Buf(), Buf()], [Buf(), Buf()]]
            PmB = [Buf() for _ in range(4)]
            vsem = P.new_dsem()
            odsem = P.new_dsem()
            SC = 128.0 ** -0.5
            vd5 = vd_scr

            def kcols(t, off, kind, a, b_):
                if kind == 16:
                    p0 = 2048 * a + 128 * b_ - off
                    return t[:, p0:p0 + 128]
                if kind == 4:
                    r4, n = a, b_
                    s_, m = n // 4, n % 4
                    base = 2048 * s_ - off
                    return t[:, base:base + 2048].rearrange("p (a r u) -> p a r u", a=4, r=4, u=128)[:, :, r4, 32 * m:32 * m + 32]
                s_, m = a, b_
                base = 2048 * s_ - off
                return t[:, base:base + 2048].rearrange("p (r m u) -> p r m u", r=16, m=16, u=8)[:, :, m, :]

            def head_setup(h):
                wd, wdb, wds = wdring.next()
                P.dma("gpsimd", wd[:, :, :], wd_in[h].rearrange("(k p) n -> p k n", p=128), wds, (), [wdb])
                return wd, wdb

            def proj_gen(h, wd, wdb):
                dq, dk = dq2[h % 2], dk2[h % 2]
                DBq = {"q": DBQ[h % 2], "k": DBK[h % 2]}
                for blk in range(8):
                    xb, xbb = load_xp(blk)
                    cols = slice(blk * 512, (blk + 1) * 512)
                    todo = [(128, 288, dk[:, cols], "k")]
                    if blk >= 3:
                        todo.append((0, 256, dq[:, (blk - 3) * 512:(blk - 2) * 512], "q"))
                    for ti, (c0, cs, dst, hb) in enumerate(todo):
                        bk, bkb = nb(True)
                        with P.group("tensor"):
                            for kc in range(KC):
                                mm(bk[:, :], wd[:, kc, c0:c0 + 128], xb[:, kc, :], kc == 0, kc == KC - 1, [xbb, wdb], [bkb])
                        yield blk
                        bs_, bsb_ = nb(True)
                        with P.group("tensor"):
                            for kc in range(KC):
                                mm(bs_[0:32, :], wd[:, kc, cs:cs + 32], xb[:, kc, :], kc == 0, kc == KC - 1, [xbb, wdb], [bsb_])
                        cp("scalar", dst, bk[:, :], [bkb], [DBq[hb]])
                        tt("vector", rt1[ti][:, :], bk[0:32, :], cosT[:, cols], ALU.mult, [bkb, ROT], [RTB[ti][0]])
                        tt("vector", rt2[ti][:, :], bs_[0:32, :], sinT[:, cols], ALU.mult, [bsb_, ROT], [RTB[ti][1]])
                        P.op("vector", lambda e, dst=dst, ti=ti: e.tensor_tensor(out=dst[0:32], in0=rt1[ti][:, :], in1=rt2[ti][:, :], op=ALU.add),
                             [RTB[ti][0], RTB[ti][1]], [DBq[hb]])
                        rel(bkb)
                        rel(bsb_)
                        yield blk

            vdone = set()
            vsems = {16: P.new_dsem(), 4: P.new_dsem(), 1: P.new_dsem()}

            def vload(h, kind_):
                if (h, kind_) in vdone:
                    return
                vdone.add((h, kind_))
                hc = slice(h * 128, (h + 1) * 128)
                dst, src = {16: (V16, vd5), 4: (V4, vd4_scr), 1: (V1, vd1_scr)}[kind_]
                P.dma("gpsimd", dst[:, :, :], src[:, hc].rearrange("(t p) c -> p t c", p=128), vsems[kind_], [VDB], [DBV[kind_]])

            def post_proj(h):
                dq, dk = dq2[h % 2], dk2[h % 2]
                DBq = {"q": DBQ[h % 2], "k": DBK[h % 2]}
                for kind_ in (16, 4, 1):
                    vload(h, kind_)
                for s_ in range(2):
                    srck = dk[:, 2048 * s_:2048 * s_ + 2048]
                    for r4 in range(4):
                        P.op("vector" if r4 % 2 == 0 else "gpsimd", lambda e, s_=s_, r4=r4, srck=srck: e.tensor_copy(
                            out=dk4[:, (r4 * 8 + s_ * 4) * 128:(r4 * 8 + s_ * 4 + 4) * 128].rearrange("p (m a u) -> p m a u", m=4, a=4, u=32),
                            in_=srck.rearrange("p (a r m u) -> p r m a u", a=4, r=4, m=4, u=32)[:, r4]), [DBq["k"]], [DB["k4"]])
                    P.op("vector", lambda e, s_=s_, srck=srck: e.tensor_copy(
                        out=dk1[:, 2048 * s_:2048 * s_ + 2048].rearrange("p (m r u) -> p m r u", m=16, r=16, u=8),
                        in_=srck.rearrange("p (r m u) -> p m r u", r=16, m=16, u=8)), [DBq["k"]], [DB["k1"]])
                srcq = dq[:, 512:2560]
                for r4 in range(4):
                    P.op("scalar", lambda e, r4=r4: e.activation(
                        out=dq4[:, r4 * 512:(r4 + 1) * 512].rearrange("p (m a u) -> p m a u", m=4, a=4, u=32),
                        in_=srcq.rearrange("p (a r m u) -> p r m a u", a=4, r=4, m=4, u=32)[:, r4], func=AF.Copy), [DBq["q"]], [DB["q4"]])
                P.op("scalar", lambda e: e.activation(
                    out=dq1[:, :].rearrange("p (m r u) -> p m r u", m=16, r=16, u=8),
                    in_=srcq.rearrange("p (r m u) -> p m r u", r=16, m=16, u=8), func=AF.Copy), [DBq["q"]], [DB["q1"]])
                P.op("scalar", lambda e: e.activation(
                    out=hq1[:, :].rearrange("p (r u) -> p r u", r=2),
                    in_=dq[:, 256:512].rearrange("p (r u) -> p r u", r=2)[:, :, 120:128], func=AF.Copy), [DBq["q"]], [DB["q1"]])


            def att_gen(h):
                dq, dk = dq2[h % 2], dk2[h % 2]
                DBq = {"q": DBQ[h % 2], "k": DBK[h % 2]}

                def kap(kb):
                    kind, a_, b_ = kb
                    if kind == 16:
                        p0 = 2048 * a_ + 128 * b_
                        return dk[:, p0:p0 + 128], DBq["k"]
                    if kind == 4:
                        p0 = (a_ * 8 + b_) * 128
                        return dk4[:, p0:p0 + 128], DB["k4"]
                    p0 = (16 * a_ + b_) * 128
                    return dk1[:, p0:p0 + 128], DB["k1"]
                mset("gpsimd", acc[:, :], 0.0, [DB["acc"]])
                mset("gpsimd", dacc[:, :], 0.0, [DB["acc"]])
                blocks = []
                for r in range(16):
                    blocks.append((16, (kcols(dq, 1536, 16, 1, r), DBq["q"]), kcols(acc, 1536, 16, 1, r), kcols(dacc, 1536, 16, 1, r), 128, None,
                                   [((16, 0, r), V16[:, r, :], masksc[:, 0, :]), ((16, 1, r), V16[:, 16 + r, :], masks[:, 1, :])]))
                for r in (14, 15):
                    blocks.append((16, (kcols(dq, 1536, 16, 0, r), DBq["q"]), kcols(acc, 1536, 16, 0, r), kcols(dacc, 1536, 16, 0, r), 128, None,
                                   [((16, 0, r), V16[:, r, :], masksc[:, 1, :])]))
                for r4 in range(4):
                    for n in range(4, 8):
                        pm = masksc[:, 2, :] if n == 4 else masks[:, 2, :]
                        blocks.append((4, (dq4[:, (r4 * 4 + n - 4) * 128:(r4 * 4 + n - 3) * 128], DB["q4"]), kcols(acc, 1536, 4, r4, n), kcols(dacc, 1536, 4, r4, n), 128, [4, 32],
                                       [((4, r4, n - 1), V4[:, r4 * 8 + n - 1, :], pm), ((4, r4, n), V4[:, r4 * 8 + n, :], masks[:, 3, :])]))
                for r4 in (2, 3):
                    p0 = (12 + r4) * 128 + 96 - 1536
                    blocks.append((4, (dq[:, p0:p0 + 32], DBq["q"]), acc[:, p0:p0 + 32], dacc[:, p0:p0 + 32], 32, None,
                                   [((4, r4, 2), V4[:, r4 * 8 + 2, :], masksc[:, 2, 96:128]), ((4, r4, 3), V4[:, r4 * 8 + 3, :], masksc[:, 3, 96:128])]))
                for m in range(16):
                    pk = (1, 0, 15) if m == 0 else (1, 1, m - 1)
                    pm = masksc[:, 4, :] if m == 0 else masks[:, 4, :]
                    blocks.append((1, (dq1[:, m * 128:(m + 1) * 128], DB["q1"]), kcols(acc, 1536, 1, 1, m), kcols(dacc, 1536, 1, 1, m), 128, [16, 8],
                                   [(pk, V1[:, 16 * pk[1] + pk[2], :], pm), ((1, 1, m), V1[:, 16 + m, :], masks[:, 5, :])]))
                hq = lambda t: t[:, 14 * 128 - 1536:16 * 128 - 1536].rearrange("p (r u) -> p r u", r=2)[:, :, 120:128]
                blocks.append((1, (hq1[:, :], DB["q1"]), hq(acc), hq(dacc), 16, [2, 8],
                               [((1, 0, 14), V1[:, 14, :], masksc[:, 4, 112:128]), ((1, 0, 15), V1[:, 15, :], masksc[:, 5, 112:128])]))
                def a0(i, c):
                    kind, (qap, qbuf), accap, daccap, nq, qshape, keys = blocks[i]
                    c["bsc"], c["bscb"] = nb(True)
                    for ki, (kb, vt, mk) in enumerate(keys):
                        ka, kbuf = kap(kb)
                        mm(c["bsc"][:, ki * 128:ki * 128 + nq], ka, qap, True, True, [kbuf, qbuf], [c["bscb"]])

                def a1(i, c):
                    kind, (qap, qbuf), accap, daccap, nq, qshape, keys = blocks[i]
                    pi = i % 4
                    nk = len(keys)
                    act(Pm[pi][:, 0:nk, 0:nq], c["bsc"][:, 0:nk * 128].rearrange("p (a b) -> p a b", a=nk)[:, :, 0:nq], AF.Exp, [c["bscb"]], [PmB[pi]], scale=SC)
                    rel(c["bscb"])
                    for ki, (kb, vt, mk) in enumerate(keys):
                        tt("gpsimd", Pm[pi][:, ki, 0:nq], Pm[pi][:, ki, 0:nq], mk, ALU.mult, [CB], [PmB[pi]])

                def a2(i, c):
                    kind, (qap, qbuf), accap, daccap, nq, qshape, keys = blocks[i]
                    pi = i % 4
                    nk = len(keys)
                    c["bo"], c["bob"] = nb(True)
                    with P.group("tensor"):
                        for ki, (kb, vt, mk) in enumerate(keys):
                            mm(c["bo"][:, 0:nq], vt, Pm[pi][:, ki, 0:nq], ki == 0, ki == nk - 1, [DBV[kind], PmB[pi]], [c["bob"]])
                    with P.group("tensor"):
                        for ki, (kb, vt, mk) in enumerate(keys):
                            mm(c["bo"][:, 128:128 + nq], ones_bf[:, :], Pm[pi][:, ki, 0:nq], ki == 0, ki == nk - 1, [CB, PmB[pi]], [c["bob"]])

                def a3(i, c):
                    kind, (qap, qbuf), accap, daccap, nq, qshape, keys = blocks[i]
                    o_in = c["bo"][:, 0:nq]
                    d_in = c["bo"][:, 128:128 + nq]
                    if qshape is not None:
                        o_in = o_in.rearrange("p (a b) -> p a b", a=qshape[0])
                        d_in = d_in.rearrange("p (a b) -> p a b", a=qshape[0])
                    tt("vector", accap, accap, o_in, ALU.add, [c["bob"]], [DB["acc"]])
                    tt("vector", daccap, daccap, d_in, ALU.add, [c["bob"]], [DB["acc"]])
                    rel(c["bob"])
                for t_ in pipeline_gen(len(blocks), [a0, a1, a2, a3]):
                    yield t_
                ts("vector", dacc[:, 256:2560], dacc[:, 256:2560], 1e-30, None, ALU.max, None, [], [DB["acc"]])
                act(dacc[:, 256:2560], dacc[:, 256:2560], AF.Ln, [], [DB["acc"]])
                act(dacc[:, 256:2560], dacc[:, 256:2560], AF.Exp, [], [DB["acc"]], scale=-1.0)
                P.op("vector", lambda e: e.tensor_tensor(out=odb[:, 2:TQ].rearrange("p (u r) -> p r u", r=16),
                                                         in0=acc[:, 512:2560].rearrange("p (r u) -> p r u", r=16),
                                                         in1=dacc[:, 512:2560].rearrange("p (r u) -> p r u", r=16), op=ALU.mult), [DB["acc"]], [DB["od"]])
                tt("vector", odb[:, 0:1], acc[:, 383:384], dacc[:, 383:384], ALU.mult, [DB["acc"]], [DB["od"]])
                tt("vector", odb[:, 1:2], acc[:, 511:512], dacc[:, 511:512], ALU.mult, [DB["acc"]], [DB["od"]])
                P.dma("sync", mixT[(8 + h) * 128:(9 + h) * 128, :], odb[:, :], odsem, [DB["od"]], [MIXB])
                yield -1

            prev_att = None
            wd_next = head_setup(0)
            for h in range(ndil):
                wd, wdb = wd_next
                pg = proj_gen(h, wd, wdb)
                nst = 0
                for _blk in pg:
                    if prev_att is not None:
                        for _ in range(3):
                            if next(prev_att, None) is None:
                                break
                            nst += 1
                            if nst == 23:
                                vload(h, 16)
                            if nst == 41:
                                vload(h, 4)
                if h + 1 < ndil:
                    wd_next = head_setup(h + 1)
                if prev_att is not None:
                    for _ in prev_att:
                        pass
                post_proj(h)
                prev_att = att_gen(h)
            for _ in prev_att:
                pass
            P.flush()
        P.barrier()

        def gemm_ln_phase(tag, w_dram, a_dram, a_cast, resid_dram, lnidx, out_dram, out_col0, chunks, ABUF, RBUF, OBUF):
            with ExitStack() as es:
                sb = lambda name, shape, dt: es.enter_context(nc.sbuf_tensor(tag + name, list(shape), dt))
                W = sb("W", [128, KC, D], BF16)
                WB = [Buf() for _ in range(4)]
                for g in range(4):
                    P.dma("gpsimd", W[:, :, g * 512:(g + 1) * 512], w_dram[:, g * 512:(g + 1) * 512].rearrange("(k p) n -> p k n", p=128), P.new_dsem(), (), [WB[g]])
                aring = Ring(P, [sb("a%d" % i, [128, KC, 512], BF16) for i in range(2)])
                ln_chunks(tag, sb, chunks, KC,
                          lambda dc, kc: W[:, kc, dc * 128:(dc + 1) * 128], WB,
                          a_dram, a_cast, aring, ABUF, resid_dram, RBUF, lnidx, out_dram, out_col0, OBUF)
                P.flush()
            P.barrier()

        def ln_chunks(tag, sb, chunks, nk, wfn, wbufs, a_dram, a_cast, aring, ABUF, resid_dram, RBUF, lnidx, out_dram, out_col0, OBUF, wstream=None):
            ny = 2
            y = [sb("y%d" % i, [128, KC, 512], F32) for i in range(ny)]
            YB = [Buf() for _ in range(ny)]
            s1 = [sb("s1_%d" % i, [128, 512], F32) for i in range(2)]
            s2 = [sb("s2_%d" % i, [128, 512], F32) for i in range(2)]
            SB1, SB2 = [Buf(), Buf()], [Buf(), Buf()]
            ysq = [sb("ysq%d" % i, [128, 512], F32) for i in range(2)]
            YSQ = [Buf(), Buf()]
            rres = Ring(P, [sb("res%d" % i, [128, 512], F32) for i in range(3)])
            mean = sb("mean", [128, 512], F32)
            rstd = sb("rstd", [128, 512], F32)
            MB = Buf()
            tn = [sb("tn%d" % i, [128, 512], F32) for i in range(2)]
            TN = [Buf(), Buf()]
            oring = Ring(P, [sb("o%d" % i, [128, 512], F32) for i in range(3)])
            pend = iter(())
            for ci, (c0, c1) in enumerate(chunks):
                n = c1 - c0
                yi = ci % ny
                yc, ycb, s1c, s2c, S1B, S2B = y[yi], YB[yi], s1[ci % 2], s2[ci % 2], SB1[ci % 2], SB2[ci % 2]
                if ci == 0:
                    nxt = aring.next()
                    P.dma("gpsimd" if a_cast else "sync", nxt[0][:, 0:nk, 0:n], a_dram[:, c0:c1].rearrange("(k p) n -> p k n", p=128), nxt[2], [ABUF], [nxt[1]])
                a, ab, asem = nxt
                if ci + 1 < len(chunks):
                    d0, d1 = chunks[ci + 1]
                    nxt = aring.next()
                    P.dma("gpsimd" if a_cast else "sync", nxt[0][:, 0:nk, 0:d1 - d0], a_dram[:, d0:d1].rearrange("(k p) n -> p k n", p=128), nxt[2], [ABUF], [nxt[1]])

                def epi(dc, bk, bkb):
                    rt, rtb, rsem_ = rres.next()
                    P.dma("sync", rt[:, 0:n], resid_dram(dc, c0, c1), rsem_, [RBUF], [rtb])
                    stt("vector", yc[:, dc, 0:n], rt[:, 0:n], ALPHA, bk[:, 0:n], ALU.mult, ALU.add, [rtb, bkb], [ycb])
                    i2 = dc % 2
                    if dc == 0:
                        cp("vector", s1c[:, 0:n], yc[:, dc, 0:n], [ycb], [S1B])
                        act(s2c[:, 0:n], yc[:, dc, 0:n], AF.Square, [ycb], [S2B])
                    else:
                        tt("vector", s1c[:, 0:n], s1c[:, 0:n], yc[:, dc, 0:n], ALU.add, [ycb], [S1B])
                        act(ysq[i2][:, 0:n], yc[:, dc, 0:n], AF.Square, [ycb], [YSQ[i2]])
                        tt("gpsimd", s2c[:, 0:n], s2c[:, 0:n], ysq[i2][:, 0:n], ALU.add, [YSQ[i2]], [S2B])

                if wstream is None:
                    for dc in range(KC):
                        bk, bkb = nb()
                        with P.group("tensor"):
                            for kc in range(nk):
                                mm(bk[:, 0:n], wfn(dc, kc), a[:, kc, 0:n], kc == 0, kc == nk - 1, [ab, wbufs[dc // 4]], [bkb])
                        epi(dc, bk, bkb)
                        if dc >= 2:
                            next(pend, None)
                else:
                    for qd_ in range(4):
                        acc4 = [nb(True) for _ in range(4)]
                        for j in range(nk):
                            wt, wtb = wstream(j, qd_)
                            with P.group("tensor"):
                                for i_ in range(4):
                                    mm(acc4[i_][0][:, 0:n], wt[:, i_ * 128:(i_ + 1) * 128], a[:, j, 0:n], j == 0, j == nk - 1, [ab, wtb], [acc4[i_][1]])
                        for i_ in range(4):
                            epi(4 * qd_ + i_, acc4[i_][0], acc4[i_][1])
                            rel(acc4[i_][1])
                            if 4 * qd_ + i_ >= 2:
                                next(pend, None)
                def part2(n=n, c0=c0, c1=c1, yc=yc, ycb=ycb, s1c=s1c, s2c=s2c, S1B=S1B, S2B=S2B):
                    yield 0
                    bm, bmb = nb()
                    mm(bm[:, 0:n], ones2048[:, :], s1c[:, 0:n], True, True, [CB, S1B], [bmb])
                    bm2, bm2b = nb()
                    mm(bm2[:, 0:n], ones2048[:, :], s2c[:, 0:n], True, True, [CB, S2B], [bm2b])
                    cp("scalar", mean[:, 0:n], bm[:, 0:n], [bmb], [MB])
                    tt("vector", rstd[:, 0:n], mean[:, 0:n], mean[:, 0:n], ALU.mult, [MB], [MB])
                    tt("vector", rstd[:, 0:n], bm2[:, 0:n], rstd[:, 0:n], ALU.subtract, [bm2b], [MB])
                    act(rstd[:, 0:n], rstd[:, 0:n], AF.Ln, [], [MB], bias=LN_EPS)
                    act(rstd[:, 0:n], rstd[:, 0:n], AF.Exp, [], [MB], scale=-0.5)
                    for dc in range(KC):
                        i2 = dc % 2
                        tt("vector", tn[i2][:, 0:n], yc[:, dc, 0:n], mean[:, 0:n], ALU.subtract, [ycb, MB], [TN[i2]])
                        tt("vector" if dc % 2 == 0 else "gpsimd", tn[i2][:, 0:n], tn[i2][:, 0:n], rstd[:, 0:n], ALU.mult, [MB], [TN[i2]])
                        ot, otb, osem_ = oring.next()
                        act(ot[:, 0:n], tn[i2][:, 0:n], AF.Identity, [TN[i2], CB], [otb], scale=lnp[:, 2 * lnidx, dc:dc + 1], bias=lnp[:, 2 * lnidx + 1, dc:dc + 1])
                        P.dma("scalar", out_dram[dc * 128:(dc + 1) * 128, c0 - out_col0:c1 - out_col0], ot[:, 0:n], osem_, [otb], [OBUF])
                        yield dc + 1
                for _ in pend:
                    pass
                pend = part2()
            for _ in pend:
                pass

        if stop == "B":
            finish()
            return nc
        X1B, OCB, X2B, HTB, OUTB = Buf(), Buf(), Buf(), Buf(), Buf()
        W2B = Buf()
        w2sem = P.new_dsem()
        NOB = Buf()
        xres = lambda dc, c0, c1: xT_nat[dc * 128:(dc + 1) * 128, 2046 + c0:2046 + c1]
        gemm_ln_phase("C", wout_in, mixT, False, xres, 0, x1T, 0, TCH, MIXB, NOB, X1B)
        if stop == "C":
            finish()
            return nc

        with ExitStack() as es:
            sb = lambda name, shape, dt: es.enter_context(nc.sbuf_tensor("D1" + name, list(shape), dt))
            mkT = sb("mkT", [128, 16, 256], BF16)
            mv = sb("mv", [128, 2, D], BF16)
            MKB, MVB, MTB = Buf(), Buf(), Buf()
            Wq = sb("Wq", [128, KC, D], BF16)
            WQB = Buf()
            wqs = P.new_dsem()
            es2 = ExitStack()
            sb2 = lambda name, shape, dt: es2.enter_context(nc.sbuf_tensor("D0" + name, list(shape), dt))
            mT = sb2("memT", [128, KC, 256], BF16)
            P.dma("gpsimd", mT[:, :, :], memT.rearrange("(k p) n -> p k n", p=128), P.new_dsem(), (), [MTB])
            wkvr = Ring(P, [sb2("wkv%d" % i, [128, KC, 512], BF16) for i in range(2)])
            for g in range(8):
                wt, wtb, wts = wkvr.next()
                P.dma("gpsimd", wt[:, :, :], wkv_in[:, g * 512:(g + 1) * 512].rearrange("(k p) n -> p k n", p=128), wts, (), [wtb])
                if g < 4:
                    for j in range(4):
                        bk, bkb = nb()
                        with P.group("tensor"):
                            for kc in range(KC):
                                mm(bk[:, 0:256], wt[:, kc, j * 128:(j + 1) * 128], mT[:, kc, :], kc == 0, kc == KC - 1, [wtb, MTB], [bkb])
                        cp("scalar", mkT[:, 4 * g + j, :], bk[:, 0:256], [bkb], [MKB])
                else:
                    for mt in range(2):
                        bk, bkb = nb()
                        with P.group("tensor"):
                            for kc in range(KC):
                                mm(bk[:, :], mT[:, kc, mt * 128:(mt + 1) * 128], wt[:, kc, :], kc == 0, kc == KC - 1, [wtb, MTB], [bkb])
                        cp("vector", mv[:, mt, (g - 4) * 512:(g - 3) * 512], bk[:, :], [bkb], [MVB])
            for g in range(4):
                P.dma("gpsimd", Wq[:, :, g * 512:(g + 1) * 512], wq_in[:, g * 512:(g + 1) * 512].rearrange("(k p) n -> p k n", p=128), wqs, (), [WQB])
            P.flush()
            es2.close()
            P.barrier()
            aring = Ring(P, [sb("a%d" % i, [128, KC, 512], BF16) for i in range(2)])
            qc = sb("qc", [128, KC, 512], BF16)
            QCB = Buf()
            ocr = Ring(P, [sb("oc%d" % i, [128, KC, 512], BF16) for i in range(2)])
            Pc = [sb("Pc%d" % i, [128, 2, 512], BF16) for i in range(2)]
            PCB = [Buf(), Buf()]
            rden = [sb("rden%d" % i, [128, 512], F32) for i in range(2)]
            RDB = [Buf(), Buf()]
            SCC = 512.0 ** -0.5
            for (c0, c1) in TCH:
                n = c1 - c0
                a, ab, asem = aring.next()
                P.dma("gpsimd", a[:, :, 0:n], x1T[:, c0:c1].rearrange("(k p) n -> p k n", p=128), asem, [X1B], [ab])
                for dc in range(KC):
                    bk, bkb = nb()
                    with P.group("tensor"):
                        for kc in range(KC):
                            mm(bk[:, 0:n], Wq[:, kc, dc * 128:(dc + 1) * 128], a[:, kc, 0:n], kc == 0, kc == KC - 1, [ab, WQB], [bkb])
                    cp("scalar" if dc % 2 == 0 else "vector", qc[:, dc, 0:n], bk[:, 0:n], [bkb], [QCB])
                oc, ocb, ocs = ocr.next()
                for hh in range(4):
                    i2 = hh % 2
                    for mt in range(2):
                        bsx, bsxb = nb()
                        with P.group("tensor"):
                            for c in range(4):
                                mm(bsx[:, 0:n], mkT[:, 4 * hh + c, mt * 128:(mt + 1) * 128], qc[:, 4 * hh + c, 0:n], c == 0, c == 3, [MKB, QCB], [bsxb])
                        act(Pc[i2][:, mt, 0:n], bsx[:, 0:n], AF.Exp, [bsxb], [PCB[i2]], scale=SCC)
                    bd, bdb = nb()
                    with P.group("tensor"):
                        for mt in range(2):
                            mm(bd[:, 0:n], ones_bf[:, :], Pc[i2][:, mt, 0:n], mt == 0, mt == 1, [CB, PCB[i2]], [bdb])
                    recip("vector", rden[i2][:, 0:n], bd[:, 0:n], [bdb], [RDB[i2]])
                    for c in range(4):
                        bo, bob = nb()
                        with P.group("tensor"):
                            for mt in range(2):
                                mm(bo[:, 0:n], mv[:, mt, (4 * hh + c) * 128:(4 * hh + c + 1) * 128], Pc[i2][:, mt, 0:n], mt == 0, mt == 1, [MVB, PCB[i2]], [bob])
                        tt("vector", oc[:, 4 * hh + c, 0:n], bo[:, 0:n], rden[i2][:, 0:n], ALU.mult, [bob, RDB[i2]], [ocb])
                P.dma("sync", ocT[:, c0:c1].rearrange("(k p) n -> p k n", p=128), oc[:, :, 0:n], ocs, [ocb], [OCB])
            P.flush()
        P.barrier()

        if stop == "D1":
            finish()
            return nc
        x1res = lambda dc, c0, c1: x1T[dc * 128:(dc + 1) * 128, c0:c1]
        gemm_ln_phase("D2", wo_in, ocT, False, x1res, 1, x2T, 0, TCH, OCB, X1B, X2B)
        if stop == "D2":
            finish()
            return nc

        with ExitStack() as es:
            sb = lambda name, shape, dt: es.enter_context(nc.sbuf_tensor("E" + name, list(shape), dt))
            x2b = sb("x2b", [128, KC, TQ], BF16)
            X2S = Buf()
            xs = P.new_dsem()
            for (c0, c1) in TCH[1:]:
                P.dma("gpsimd", x2b[:, :, c0:c1], x2T[:, c0:c1].rearrange("(k p) n -> p k n", p=128), xs, [X2B], [X2S])
            P.dma("gpsimd", x2b[:, :, 0:2], x2T[:, 0:2].rearrange("(k p) n -> p k n", p=128), xs, [X2B], [X2S])
            ts("vector", x2b[:, :, 0:2], x2b[:, :, 0:2], flag[:, 0:1], None, ALU.mult, None, [CB], [X2S])
            wr = Ring(P, [sb("w%d" % i, [128, KC, 256], BF16) for i in range(3)])
            ug = [sb("ug%d" % i, [128, TQ], F32) for i in range(2)]
            uu = [sb("uu%d" % i, [128, TQ], F32) for i in range(2)]
            yg = [sb("yg%d" % i, [128, 2048], F32) for i in range(2)]
            yu = [sb("yu%d" % i, [128, 2048], F32) for i in range(2)]
            hr = Ring(P, [sb("h%d" % i, [128, 2048], BF16) for i in range(2)])
            UG, UU, YG, YU = [Buf(), Buf()], [Buf(), Buf()], [Buf(), Buf()], [Buf(), Buf()]
            for j in range(NJ):
                i2 = j % 2
                wt, wtb, wts = wr.next()
                P.dma("gpsimd", wt[:, :, 0:128], fwin_in[:, j * 128:(j + 1) * 128].rearrange("(k p) n -> p k n", p=128), wts, (), [wtb])
                P.dma("gpsimd", wt[:, :, 128:256], fwin_in[:, DFF + j * 128:DFF + (j + 1) * 128].rearrange("(k p) n -> p k n", p=128), wts, (), [wtb])
                P.dma("gpsimd", w2b[j * 128:(j + 1) * 128, :], fwout_in[j * 128:(j + 1) * 128, :], w2sem, (), [W2B])
                for part, (ubuf, UB) in enumerate([(ug[i2], UG[i2]), (uu[i2], UU[i2])]):
                    for ci, (c0, c1) in enumerate(TCH):
                        n = c1 - c0
                        bk, bkb = nb()
                        with P.group("tensor"):
                            for kc in range(KC):
                                mm(bk[:, 0:n], wt[:, kc, part * 128:(part + 1) * 128], x2b[:, kc, c0:c1], kc == 0, kc == KC - 1, [wtb, X2S], [bkb])
                        cp("scalar", ubuf[:, c0:c1], bk[:, 0:n], [bkb], [UB])
                for part, (eng, ubuf, UB, ybuf, YB_) in enumerate([("vector", ug[i2], UG[i2], yg[i2], YG[i2]), ("gpsimd", uu[i2], UU[i2], yu[i2], YU[i2])]):
                    cj = part * NJ + j
                    act(ybuf[:, :], ubuf[:, 2:TQ], AF.Identity, [UB, CB], [YB_], scale=cw[:, cj, 2:3], bias=cb[:, cj:cj + 1])
                    stt("vector", ybuf[:, :], ubuf[:, 1:TQ - 1], cw[:, cj, 1:2], ybuf[:, :], ALU.mult, ALU.add, [UB, CB], [YB_])
                    stt("vector", ybuf[:, :], ubuf[:, 0:TQ - 2], cw[:, cj, 0:1], ybuf[:, :], ALU.mult, ALU.add, [UB, CB], [YB_])
                act(yg[i2][:, :], yg[i2][:, :], AF.Silu, [], [YG[i2]])
                ht, htb, hts = hr.next()
                tt("vector", ht[:, :], yg[i2][:, :], yu[i2][:, :], ALU.mult, [YG[i2], YU[i2]], [htb])
                P.dma("sync", hT[j * 128:(j + 1) * 128, :], ht[:, :], hts, [htb], [HTB])
            P.flush()
        P.barrier()

        if stop == "E":
            finish()
            return nc
        with ExitStack() as es:
            sb = lambda name, shape, dt: es.enter_context(nc.sbuf_tensor("F" + name, list(shape), dt))
            aring = Ring(P, [sb("a%d" % i, [128, NJ, 512], BF16) for i in range(2)])
            w2r = Ring(P, [sb("w%d" % i, [128, 512], BF16) for i in range(8)])

            def wstream(j, qd_):
                wt, wtb, wts = w2r.next()
                P.dma("sync", wt[:, :], w2b[j * 128:(j + 1) * 128, qd_ * 512:(qd_ + 1) * 512], wts, [W2B], [wtb])
                return wt, wtb
            x2res = lambda dc, c0, c1: x2T[dc * 128:(dc + 1) * 128, 2 + c0:2 + c1]
            ln_chunks("F", sb, [(i * 512, (i + 1) * 512) for i in range(4)], NJ, None, None,
                      hT, False, aring, HTB, x2res, X2B, 2, outT, 0, OUTB, wstream=wstream)
            final = [(s[0], s[1], "dma") for s in P.dsems if s[1] > 0]
            P.flush(final)
    return nc


def _perm_idx():
    idx = np.empty(NL, np.int64)
    for s in range(2):
        for r in range(16):
            idx[s * 2048 + r * 128:s * 2048 + (r + 1) * 128] = s * 2048 + 16 * np.arange(128) + r
    return idx


def _masks():
    m = np.zeros((128, 6, 128), np.float32)
    j = np.arange(128)[:, None]
    i = np.arange(128)[None, :]
    for pi_, nat in enumerate([lambda x: x, lambda x: 4 * (x % 32) + x // 32, lambda x: 16 * (x % 8) + x // 8]):
        nj, ni = nat(j), nat(i)
        m[:, 2 * pi_, :] = (nj >= ni)
        m[:, 2 * pi_ + 1, :] = (nj <= ni)
    return m.reshape(128, 768)


def _fm(v, nchunk):
    return np.ascontiguousarray(v.reshape(nchunk, 128).T)


_CACHE = {}


def make_in_maps(x, mem, positions, w_in, gla_gate_w2, gla_gate_b, gla_norm_g, w_out, ln1_g, ln1_b,
                 ca_wq, ca_wkv, ca_wo, ln2_g, ln2_b, ffn_w_in, ffn_conv_w, ffn_conv_b, ffn_w_out, ln3_g, ln3_b):
    f32 = np.float32
    x = np.asarray(x, f32)
    mem = np.asarray(mem, f32)
    positions = np.asarray(positions, np.int32)
    w_in = np.asarray(w_in, f32)[0]
    pidx = _perm_idx()
    o = 0
    cols = {}
    for name, w in zip(["qg", "kg", "vg", "rg", "glr", "qd", "kd", "vd"], [512, 512, 1024, 1024, 16, 1024, 1024, 1024]):
        cols[name] = w_in[:, o:o + w]
        o += w
    wg = np.stack([np.concatenate([cols["qg"][:, h * 128:(h + 1) * 128], cols["kg"][:, h * 128:(h + 1) * 128],
                                   cols["rg"][:, h * 256:(h + 1) * 256], cols["vg"][:, h * 256:(h + 1) * 256]], axis=1) for h in range(4)])
    swap = np.concatenate([np.arange(16, 32), np.arange(0, 16)])
    wd = np.stack([np.concatenate([cols["qd"][:, h * 128:(h + 1) * 128], cols["kd"][:, h * 128:(h + 1) * 128],
                                   cols["qd"][:, h * 128 + swap], cols["kd"][:, h * 128 + swap]], axis=1) for h in range(8)])
    w2aug = np.concatenate([np.asarray(gla_gate_w2, f32)[0], np.asarray(gla_gate_b, f32)[0][None, :]], axis=0)
    lnp = np.stack([_fm(np.asarray(v, f32)[0], 16) for v in [ln1_g, ln1_b, ln2_g, ln2_b, ln3_g, ln3_b]], axis=1).reshape(128, 96)
    convw = np.ascontiguousarray(np.asarray(ffn_conv_w, f32)[0].T.reshape(86, 128, 3).transpose(1, 0, 2)).reshape(128, 258)
    convb = _fm(np.asarray(ffn_conv_b, f32)[0], 86)
    jj = np.arange(128)[:, None]
    ii = np.arange(128)[None, :]
    uneg = np.where(jj <= ii, f32(-1.0 / 16.0), f32(0.0)).astype(f32)
    invf = (500000.0 ** (-(np.arange(0, 32, 2, dtype=np.float32)) / 32.0)).astype(f32)
    rotc = np.stack([np.concatenate([invf, invf]), np.concatenate([-np.ones(16, f32), np.ones(16, f32)])], axis=1).astype(f32)
    shared = dict(masks=_masks(), uneg=uneg, rotc=rotc, wg=np.ascontiguousarray(wg), wglr=np.ascontiguousarray(cols["glr"]),
                  w2aug=np.ascontiguousarray(w2aug), glag=_fm(np.asarray(gla_norm_g, f32)[0], 8), wd=np.ascontiguousarray(wd),
                  wvd=np.ascontiguousarray(cols["vd"]), w_out=np.asarray(w_out, f32)[0], lnp=np.ascontiguousarray(lnp),
                  ca_wq=np.asarray(ca_wq, f32)[0], ca_wkv=np.asarray(ca_wkv, f32)[0], ca_wo=np.asarray(ca_wo, f32)[0],
                  ffn_w_in=np.asarray(ffn_w_in, f32)[0], convw=convw, convb=convb, ffn_w_out=np.asarray(ffn_w_out, f32)[0])
    in_maps = []
    for c in range(8):
        b, hf = c // 2, c % 2
        xl = np.zeros((NL, D), f32)
        pl = np.zeros((NL,), np.int32)
        if hf == 1:
            xl[:] = x[b]
            pl[:] = positions[b]
        else:
            xl[2048:] = x[b, :2048]
            pl[2048:] = positions[b, :2048]
        xTn = np.ascontiguousarray(xl.T)
        m = dict(shared)
        m.update(xT_nat=xTn, xT_perm=np.ascontiguousarray(xTn[:, pidx]), memT=np.ascontiguousarray(mem[b].T),
                 pos_perm=np.ascontiguousarray(pl[pidx][None, :]), flag=np.full((128, 1), float(hf), f32))
        in_maps.append(m)
    return in_maps


def kernel(**inputs):
    if "nc" not in _CACHE:
        _CACHE["nc"] = build(False)
    nc = _CACHE["nc"]
    in_maps = make_in_maps(**inputs)
    res = run_bass_kernel_spmd(nc, in_maps, core_ids=list(range(8)))
    out = np.empty((4, 4096, D), np.float32)
    for c in range(8):
        b, hf = c // 2, c % 2
        out[b, hf * 2048:(hf + 1) * 2048, :] = np.asarray(res.results[c]["outT"]).T
    return out
```

```python
import math
from contextlib import ExitStack, contextmanager
import numpy as np
import concourse.bass as bass
import concourse.mybir as mybir
from concourse.bass_utils import run_bass_kernel_spmd

F32 = mybir.dt.float32
BF16 = mybir.dt.bfloat16
I32 = mybir.dt.int32
AF = mybir.ActivationFunctionType
ALU = mybir.AluOpType

D = 2048
KC = 16
NL = 4096
TQ = 2050
QN = 2176
DFF = 5504
NJ = 43
ALPHA = 2.0 ** 0.25
LN_EPS = 1e-5
TCH = [(0, 2), (2, 514), (514, 1026), (1026, 1538), (1538, 2050)]
ENGS = ["tensor", "vector", "scalar", "gpsimd", "sync"]
PI = math.pi


class Buf:
    __slots__ = ("w", "r", "excl")

    def __init__(self, excl=False):
        self.w = None
        self.r = {}
        self.excl = excl


class Prog:
    def __init__(self, nc, es):
        self.nc = nc
        self.q = {e: [] for e in ENGS}
        self.cnt = {e: 0 for e in ENGS}
        self.sem = {e: es.enter_context(nc.semaphore("s_" + e)) for e in ENGS}
        self.grp = {e: None for e in ENGS}
        self.es = es
        self.dsems = []
        self.bar = {e: [] for e in ENGS}
        self.waited = {e: {} for e in ENGS}

    def _deps(self, reads, writes, extra):
        d = [x for x in extra if x is not None]
        if any(b.excl for b in reads):
            writes = list(writes) + [b for b in reads if b.excl]
            reads = [b for b in reads if not b.excl]
        for b in reads:
            if b.w is not None:
                d.append(b.w)
        for b in writes:
            if b.w is not None:
                d.append(b.w)
            d.extend(b.r.values())
        return d

    def _mark(self, tok, reads, writes):
        if any(b.excl for b in reads):
            writes = list(writes) + [b for b in reads if b.excl]
            reads = [b for b in reads if not b.excl]
        for b in reads:
            b.r[id(tok[0])] = tok
        for b in writes:
            b.w = tok
            b.r = {}

    def op(self, eng, fn, reads=(), writes=(), deps=()):
        d = self._deps(reads, writes, deps)
        if self.bar[eng]:
            d.extend(self.bar[eng])
            self.bar[eng] = []
        if eng == "tensor":
            d = [t for t in d if t[2] != "tensor"]
        if self.grp[eng] is not None:
            tok = self.grp[eng]
            d = [t for t in d if not (t[0] is tok[0] and t[1] == tok[1])]
            self.q[eng].append(["op", fn, d, False])
        else:
            self.cnt[eng] += 1
            tok = (self.sem[eng], self.cnt[eng], eng)
            self.q[eng].append(["op", fn, d, True])
        self._mark(tok, reads, writes)
        return tok

    @contextmanager
    def group(self, eng):
        tok = (self.sem[eng], self.cnt[eng] + 1, eng)
        self.grp[eng] = tok
        n0 = len(self.q[eng])
        yield tok
        self.grp[eng] = None
        assert len(self.q[eng]) > n0
        self.q[eng][-1][3] = True
        self.cnt[eng] += 1

    def _raw_dsem(self):
        s = [self.es.enter_context(self.nc.semaphore("d%d" % len(self.dsems))), 0]
        self.dsems.append(s)
        return s

    def new_dsem(self):
        return {}

    def dma(self, eng, out, in_, sem, reads=(), writes=(), deps=()):
        d = self._deps(reads, writes, deps)
        if self.bar[eng]:
            d.extend(self.bar[eng])
            self.bar[eng] = []
        if eng not in sem:
            sem[eng] = self._raw_dsem()
        sem = sem[eng]
        sem[1] += 16
        tok = (sem[0], sem[1], "dma")
        self.q[eng].append(["dma", (out, in_, sem[0]), d, True])
        self._mark(tok, reads, writes)
        return tok

    def barrier(self):
        toks = [(self.sem[e], self.cnt[e], "bar") for e in ENGS if self.cnt[e] > 0]
        toks += [(s[0], s[1], "dma") for s in self.dsems if s[1] > 0 and not (len(s) > 2 and s[2])]
        for e in ENGS:
            self.bar[e] = list(toks)

    def flush(self, final=()):
        with self.nc.Block() as block:
            def mk(ename):
                def body(eng):
                    waited = self.waited[ename]
                    for kind, payload, deps, sig in self.q[ename]:
                        for (s, v, src) in deps:
                            key = id(s)
                            if waited.get(key, 0) >= v:
                                continue
                            eng.wait_ge(s, v)
                            waited[key] = v
                        if kind == "op":
                            ins = payload(eng)
                            if sig:
                                ins.then_inc(self.sem[ename], 1)
                        else:
                            out, in_, s = payload
                            eng.dma_start(out=out, in_=in_).then_inc(s, 16)
                    if ename == "sync":
                        for (s, v, src) in final:
                            eng.wait_ge(s, v)
                    self.q[ename] = []
                return body
            block.tensor(mk("tensor"))
            block.vector(mk("vector"))
            block.scalar(mk("scalar"))
            block.gpsimd(mk("gpsimd"))
            block.sync(mk("sync"))


def pipeline(n, stages):
    S = len(stages)
    ctx = [dict() for _ in range(n)]
    for t in range(n + S - 1):
        for s_ in reversed(range(S)):
            i = t - s_
            if 0 <= i < n:
                stages[s_](i, ctx[i])


def pipeline_gen(n, stages):
    S = len(stages)
    ctx = [dict() for _ in range(n)]
    for t in range(n + S - 1):
        for s_ in reversed(range(S)):
            i = t - s_
            if 0 <= i < n:
                stages[s_](i, ctx[i])
        yield t


class Ring:
    def __init__(self, P, tensors):
        self.t = tensors
        self.b = [Buf() for _ in tensors]
        self.s = [P.new_dsem() for _ in tensors]
        self.i = -1

    def next(self):
        self.i = (self.i + 1) % len(self.t)
        return self.t[self.i], self.b[self.i], self.s[self.i]


def build(debug=False, stop=None, ngla=4, ndil=8):
    nc = bass.Bass("TRN2", target_bir_lowering=False)
    din = lambda name, shape, dt=F32: nc.dram_tensor(name, list(shape), dt, kind="ExternalInput").ap()
    okind = "ExternalOutput" if debug else "Internal"
    dscr = lambda name, shape, dt: nc.dram_tensor(name, list(shape), dt, kind=okind).ap()
    xT_nat = din("xT_nat", [D, NL])
    xT_perm = din("xT_perm", [D, NL])
    memT = din("memT", [D, 256])
    pos_in = din("pos_perm", [1, NL], I32)
    flag_in = din("flag", [128, 1])
    masks_in = din("masks", [128, 6 * 128])
    uneg_in = din("uneg", [128, 128])
    rotc_in = din("rotc", [32, 2])
    wg_in = din("wg", [4, D, 768])
    wglr_in = din("wglr", [D, 16])
    w2aug_in = din("w2aug", [17, 512])
    glag_in = din("glag", [128, 8])
    wd_in = din("wd", [8, D, 320])
    wvd_in = din("wvd", [D, 1024])
    wout_in = din("w_out", [D, D])
    lnp_in = din("lnp", [128, 6 * 16])
    wq_in = din("ca_wq", [D, D])
    wkv_in = din("ca_wkv", [D, 2 * D])
    wo_in = din("ca_wo", [D, D])
    fwin_in = din("ffn_w_in", [D, 2 * DFF])
    cw_in = din("convw", [128, 86 * 3])
    cb_in = din("convb", [128, 86])
    fwout_in = din("ffn_w_out", [DFF, D])
    outT = nc.dram_tensor("outT", [D, 2048], F32, kind="ExternalOutput").ap()

    xb_nat = dscr("xb_nat", [D, NL], BF16)
    xb_perm = dscr("xb_perm", [D, NL], BF16)
    vd_scr = dscr("vd_scr", [NL, 1024], BF16)
    vd4_scr = dscr("vd4_scr", [NL, 1024], BF16)
    vd1_scr = dscr("vd1_scr", [NL, 1024], BF16)
    mixT = dscr("mixT", [D, TQ], BF16)
    x1T = dscr("x1T", [D, TQ], F32)
    ocT = dscr("ocT", [D, TQ], BF16)
    x2T = dscr("x2T", [D, TQ], F32)
    hT = dscr("hT", [DFF, 2048], BF16)
    w2b = dscr("w2b", [DFF, D], BF16)

    with ExitStack() as ges:
        P = Prog(nc, ges)
        gsb = lambda name, shape, dt: ges.enter_context(nc.sbuf_tensor(name, list(shape), dt))
        banks = [ges.enter_context(nc.psum_tensor("pb%d" % i, [128, 512], F32)) for i in range(7)]
        bankb = [Buf(True) for _ in range(7)]
        ptb = ges.enter_context(nc.psum_tensor("ptb", [128, 1024], BF16))
        ptb_b = Buf(True)
        bi = [0]

        def finish():
            final = [(s_[0], s_[1], "dma") for s_ in P.dsems if s_[1] > 0]
            final += [(P.sem[e_], P.cnt[e_], "bar") for e_ in ENGS if P.cnt[e_] > 0 and e_ != "sync"]
            P.flush(final)

        busy = set()

        def nb(reserve=False):
            for _ in range(8):
                bi[0] = (bi[0] + 1) % 7
                if bi[0] not in busy:
                    break
            else:
                raise RuntimeError("no free PSUM bank")
            if reserve:
                busy.add(bi[0])
            return banks[bi[0]], bankb[bi[0]]

        def rel(bb):
            busy.discard(bankb.index(bb))

        def mm(out, lhsT, rhs, start, stop, reads, writes):
            return P.op("tensor", lambda e: e.matmul(out, lhsT=lhsT, rhs=rhs, start=start, stop=stop), reads, writes)

        def act(out, in_, func, reads, writes, bias=None, scale=None):
            kw = {}
            if bias is not None:
                kw["bias"] = bias
            if scale is not None:
                kw["scale"] = scale
            return P.op("scalar", lambda e: e.activation(out=out, in_=in_, func=func, **kw), reads, writes)

        def tt(eng, out, in0, in1, op, reads, writes):
            return P.op(eng, lambda e: e.tensor_tensor(out=out, in0=in0, in1=in1, op=op), reads, writes)

        def ts(eng, out, in0, s1, s2, op0, op1, reads, writes):
            if op1 is None:
                return P.op(eng, lambda e: e.tensor_scalar(out=out, in0=in0, scalar1=s1, scalar2=None, op0=op0), reads, writes)
            return P.op(eng, lambda e: e.tensor_scalar(out=out, in0=in0, scalar1=s1, scalar2=s2, op0=op0, op1=op1), reads, writes)

        def stt(eng, out, in0, scalar, in1, op0, op1, reads, writes):
            return P.op(eng, lambda e: e.scalar_tensor_tensor(out=out, in0=in0, scalar=scalar, in1=in1, op0=op0, op1=op1), reads, writes)

        def cp(eng, out, in_, reads, writes):
            if eng == "scalar":
                return P.op(eng, lambda e: e.activation(out=out, in_=in_, func=AF.Copy), reads, writes)
            return P.op(eng, lambda e: e.tensor_copy(out=out, in_=in_), reads, writes)

        def recip(eng, out, in_, reads, writes):
            return P.op(eng, lambda e: e.reciprocal(out=out, in_=in_), reads, writes)

        def mset(eng, ap, val, writes):
            return P.op(eng, lambda e: e.memset(ap, val), (), writes)

        ident = gsb("ident", [128, 128], BF16)
        ones_bf = gsb("ones_bf", [128, 128], BF16)
        ones256 = gsb("ones256", [128, 128], F32)
        ones2048 = gsb("ones2048", [128, 128], F32)
        masks = gsb("masks_sb", [128, 6, 128], BF16)
        masksc = gsb("masksc_sb", [128, 6, 128], BF16)
        uneg = gsb("uneg_sb", [128, 128], F32)
        flag = gsb("flag_sb", [128, 1], F32)
        lnp = gsb("lnp_sb", [128, 6, 16], F32)
        glag = gsb("glag_sb", [128, 8], F32)
        cw = gsb("cw_sb", [128, 86, 3], F32)
        cb = gsb("cb_sb", [128, 86], F32)
        mtmp = gsb("mtmp", [128, 6, 128], F32)
        CB = Buf()
        csem = P.new_dsem()
        P.dma("sync", mtmp[:, :, :], masks_in.rearrange("p (a b) -> p a b", a=6), csem, (), [CB])
        P.dma("sync", uneg[:, :], uneg_in, csem, (), [CB])
        P.dma("sync", flag[:, :], flag_in, csem, (), [CB])
        P.dma("sync", lnp[:, :, :], lnp_in.rearrange("p (a b) -> p a b", a=6), csem, (), [CB])
        P.dma("sync", glag[:, :], glag_in, csem, (), [CB])
        P.dma("sync", cw[:, :, :], cw_in.rearrange("p (a b) -> p a b", b=3), csem, (), [CB])
        P.dma("sync", cb[:, :], cb_in, csem, (), [CB])
        mset("gpsimd", ident[:, :], 0.0, [CB])
        P.op("gpsimd", lambda e: e.affine_select(out=ident[:, :], in_=ident[:, :], pattern=[[-1, 128]],
                                                 compare_op=ALU.not_equal, fill=1.0, base=0, channel_multiplier=1), [CB], [CB])
        mset("gpsimd", ones_bf[:, :], 1.0, [CB])
        mset("gpsimd", ones256[:, :], 1.0 / 256.0, [CB])
        mset("gpsimd", ones2048[:, :], 1.0 / 2048.0, [CB])
        cp("vector", masks[:, :, :], mtmp[:, :, :], [CB], [CB])
        ts("vector", masksc[:, :, :], mtmp[:, :, :], flag[:, 0:1], None, ALU.mult, None, [CB], [CB])

        XN = [Buf() for _ in range(8)]
        XP = [Buf() for _ in range(8)]
        P.flush()
        if stop == "pre":
            finish()
            return nc

        MIXB = Buf()
        VDB = Buf()

        with ExitStack() as es:
            sb = lambda name, shape, dt: es.enter_context(nc.sbuf_tensor(name, list(shape), dt))
            xring = Ring(P, [sb("xblk%d" % i, [128, KC, 512], BF16) for i in range(2)])

            def load_x(src, srcbufs, blk):
                t, b, s = xring.next()
                P.dma("sync", t[:, :, :], src[:, blk * 512:(blk + 1) * 512].rearrange("(k p) n -> p k n", p=128), s, [srcbufs[blk]], [b])
                return t, b

            glrT = sb("glrT", [32, NL], F32)
            GLR = Buf()
            wglr = sb("wglr_sb", [128, KC, 16], BF16)
            w2aug = sb("w2aug_sb", [17, 512], F32)
            WS = Buf()
            wsem = P.new_dsem()
            P.dma("gpsimd", wglr[:, :, :], wglr_in.rearrange("(k p) n -> p k n", p=128), wsem, (), [WS])
            P.dma("sync", w2aug[:, :], w2aug_in, wsem, (), [WS])
            def load_first(ring, src32, dst16, dbufs, blk):
                t, b, s_ = ring.next()
                cols = slice(blk * 512, (blk + 1) * 512)
                P.dma("gpsimd", t[:, :, :], src32[:, cols].rearrange("(k p) n -> p k n", p=128), s_, (), [b])
                P.dma("sync", dst16[:, cols].rearrange("(k p) n -> p k n", p=128), t[:, :, :], s_, [b], [dbufs[blk]])
                return t, b
            mset("vector", glrT[:, :], 1.0, [GLR])
            for blk in range(8):
                xb, xbb = load_first(xring, xT_nat, xb_nat, XN, blk)
                bk, bkb = nb()
                with P.group("tensor"):
                    for kc in range(KC):
                        mm(bk[0:16, :], wglr[:, kc, :], xb[:, kc, :], kc == 0, kc == KC - 1, [xbb, WS], [bkb])
                cp("scalar", glrT[0:16, blk * 512:(blk + 1) * 512], bk[0:16, :], [bkb], [GLR])

            if stop == "glr":
                finish()
                return nc
            wgring = Ring(P, [sb("wg%d" % i, [128, KC, 768], BF16) for i in range(1)])
            qT = sb("g_qT", [128, QN], BF16)
            kT = sb("g_kT", [128, NL], BF16)
            rT = sb("g_rT", [128, 2, QN], BF16)
            vsb = sb("g_v", [128, 32, 256], BF16)
            GB = [(sb("g_enb%d" % i, [128, NL], BF16), sb("g_kef%d" % i, [128, NL], BF16), sb("g_ebq%d" % i, [128, QN], BF16),
                   sb("g_decay%d" % i, [128, 32], F32), sb("g_blast%d" % i, [128, 32], F32)) for i in range(2)]
            HGB = [(Buf(), Buf()) for _ in range(2)]
            Sst = sb("g_S", [128, 256], F32)
            Sbf = [sb("g_Sbf%d" % i, [128, 256], BF16) for i in range(3)]
            og = sb("g_og", [128, 2, TQ], BF16)
            tA = [sb("g_tA%d" % i, [128, 128], F32) for i in range(4)]
            spb = [sb("g_sp%d" % i, [128, 128], F32) for i in range(4)]
            kin = [sb("g_kin%d" % i, [128, 128], BF16) for i in range(4)]
            kend = [sb("g_kend%d" % i, [128, 128], BF16) for i in range(4)]
            kendT = [sb("g_kendT%d" % i, [128, 128], BF16) for i in range(4)]
            qin = [sb("g_qin%d" % i, [128, 128], BF16) for i in range(4)]
            Am = [sb("g_Am%d" % i, [128, 128], BF16) for i in range(4)]
            osb = [sb("g_osb%d" % i, [128, 2, 128], F32) for i in range(4)]
            osq = [sb("g_osq%d" % i, [128, 2, 128], F32) for i in range(4)]
            mst = [sb("g_mst%d" % i, [128, 256], F32) for i in range(4)]
            t1 = [sb("g_t1%d" % i, [128, 128], F32) for i in range(4)]
            on2 = [[sb("g_on2_%d_%d" % (i, e_), [128, 128], F32) for e_ in range(2)] for i in range(4)]
            TB2 = {"on": [[Buf(), Buf()] for _ in range(4)]}
            HB = {k: Buf() for k in ["q", "k", "r", "v", "gate", "S", "og", "bl"]}
            Sbfb = [Buf(), Buf(), Buf()]
            TB = {k: [Buf() for _ in range(4)] for k in ["tA", "sp", "kin", "kend", "kendT", "qin", "Am", "osb", "osq", "mst", "t1", "t2", "sr", "on"]}
            ogsem = P.new_dsem()
            LNS = math.log(128.0 ** -0.5)

            wg, wgb, wgs = wgring.next()
            P.dma("gpsimd", wg[:, :, :], wg_in[0].rearrange("(k p) n -> p k n", p=128), wgs, (), [wgb])
            def make_gates(hh):
                enb, kef, ebq, decay, blast = GB[hh % 2]
                HBg, HBbl = HGB[hh % 2]

                def g0(n, c):
                    c["bz"], c["bzb"] = nb(True)
                    mm(c["bz"][:, 0:128], glrT[0:17, n * 128:(n + 1) * 128], w2aug[0:17, hh * 128:(hh + 1) * 128], True, True, [GLR, WS], [c["bzb"]])

                def g1(n, c):
                    i4 = n % 4
                    act(tA[i4][:, :], c["bz"][:, 0:128], AF.Exp, [c["bzb"]], [TB["tA"][i4]], scale=-1.0)
                    act(spb[i4][:, :], tA[i4][:, :], AF.Ln, [TB["tA"][i4]], [TB["sp"][i4]], bias=1.0)
                    rel(c["bzb"])

                def g2(n, c):
                    i4 = n % 4
                    c["bt"], c["btb"] = nb(True)
                    mm(c["bt"][:, 0:128], spb[i4][:, :], uneg[:, :], True, True, [TB["sp"][i4], CB], [c["btb"]])

                def g3(n, c):
                    bt, btb = c["bt"], c["btb"]
                    cp("vector", blast[:, n:n + 1], bt[:, 127:128], [btb], [HBbl])
                    act(enb[:, n * 128:(n + 1) * 128], bt[:, 0:128], AF.Exp, [btb], [HBg], scale=-1.0)
                    act(kef[:, n * 128:(n + 1) * 128], bt[:, 0:128], AF.Exp, [btb, HBbl], [HBg], scale=-1.0, bias=blast[:, n:n + 1])
                    if n >= 15:
                        act(ebq[:, (n - 15) * 128:(n - 14) * 128], bt[:, 0:128], AF.Exp, [btb], [HBg], bias=LNS)
                    act(decay[:, n:n + 1], blast[:, n:n + 1], AF.Exp, [HBbl], [HBg])
                    rel(btb)
                return pipeline_gen(32, [g0, g1, g2, g3])

            gsteps = make_gates(0)
            for h in range(ngla):
                enb, kef, ebq, decay, blast = GB[h % 2]
                HB["gate"], HB["bl"] = HGB[h % 2]
                for blk in range(8):
                    if h == 0:
                        for _ in range(5):
                            next(gsteps, None)
                    if ngla >= 3 and h in (1, 2) and blk % 2 == 0:
                        pb_ = (h - 1) * 4 + blk // 2
                        P.dma("gpsimd", xb_perm[:, pb_ * 512:(pb_ + 1) * 512], xT_perm[:, pb_ * 512:(pb_ + 1) * 512], P.new_dsem(), (), [XP[pb_]])
                    xb, xbb = load_x(xb_nat, XN, blk)
                    bk, bkb = nb()
                    with P.group("tensor"):
                        for kc in range(KC):
                            mm(bk[:, :], wg[:, kc, 128:256], xb[:, kc, :], kc == 0, kc == KC - 1, [xbb, wgb], [bkb])
                    cp("scalar", kT[:, blk * 512:(blk + 1) * 512], bk[:, :], [bkb], [HB["k"]])
                    if blk >= 3:
                        x0, nn_, q0 = (384, 128, 0) if blk == 3 else (0, 512, 128 + (blk - 4) * 512)
                        for (c0, dst, hb) in [(0, qT[:, q0:q0 + nn_], "q"), (256, rT[:, 0, q0:q0 + nn_], "r"), (384, rT[:, 1, q0:q0 + nn_], "r")]:
                            bq, bqb = nb()
                            with P.group("tensor"):
                                for kc in range(KC):
                                    mm(bq[:, 0:nn_], wg[:, kc, c0:c0 + 128], xb[:, kc, x0:x0 + nn_], kc == 0, kc == KC - 1, [xbb, wgb], [bqb])
                            cp("scalar" if hb == "q" else "vector", dst, bq[:, 0:nn_], [bqb], [HB[hb]])
                    for sub in range(4):
                        bv, bvb = nb()
                        with P.group("tensor"):
                            for kc in range(KC):
                                mm(bv[:, 0:256], xb[:, kc, sub * 128:(sub + 1) * 128], wg[:, kc, 512:768], kc == 0, kc == KC - 1, [xbb, wgb], [bvb])
                        cp("vector", vsb[:, blk * 4 + sub, :], bv[:, 0:256], [bvb], [HB["v"]])
                for _ in gsteps:
                    pass
                if h + 1 < ngla:
                    P.dma("gpsimd", wg[:, :, :], wg_in[h + 1].rearrange("(k p) n -> p k n", p=128), wgs, (), [wgb])
                act(rT[:, :, :], rT[:, :, :], AF.Silu, [], [HB["r"]])
                if stop == "proj":
                    finish()
                    return nc
                mset("vector", Sst[:, :], 0.0, [HB["S"]])
                mset("gpsimd", Sbf[2][:, :], 0.0, [Sbfb[2]])

                def c0_(n, c):
                    i4 = n % 4
                    tok = slice(n * 128, (n + 1) * 128)
                    tt("vector", kin[i4][:, :], kT[:, tok], enb[:, tok], ALU.mult, [HB["k"], HB["gate"]], [TB["kin"][i4]])
                    tt("gpsimd", kend[i4][:, :], kT[:, tok], kef[:, tok], ALU.mult, [HB["k"], HB["gate"]], [TB["kend"][i4]])
                    if n >= 15:
                        q0 = (n - 15) * 128
                        tt("vector", qin[i4][:, :], qT[:, q0:q0 + 128], ebq[:, q0:q0 + 128], ALU.mult, [HB["q"], HB["gate"]], [TB["qin"][i4]])

                def c1_(n, c):
                    i4 = n % 4
                    P.op("tensor", lambda e, i4=i4: e.transpose(out=ptb[:, i4 * 128:(i4 + 1) * 128], in_=kend[i4][:, :], identity=ident[:, :]),
                         [TB["kend"][i4], CB], [ptb_b])
                    if n >= 15:
                        c["ba"], c["bab"] = nb(True)
                        mm(c["ba"][:, 0:128], kin[i4][:, :], qin[i4][:, :], True, True, [TB["kin"][i4], TB["qin"][i4]], [c["bab"]])

                def c2_(n, c):
                    i4 = n % 4
                    cp("scalar", kendT[i4][:, :], ptb[:, i4 * 128:(i4 + 1) * 128], [ptb_b], [TB["kendT"][i4]])
                    if n >= 15:
                        q0 = (n - 15) * 128
                        tt("vector", Am[i4][:, :], c["ba"][:, 0:128], masks[:, 1, :], ALU.mult, [c["bab"], CB], [TB["Am"][i4]])
                        rel(c["bab"])

                def c3_(n, c):
                    i4 = n % 4
                    Sprev, Sprevb = Sbf[(n + 2) % 3], Sbfb[(n + 2) % 3]
                    if n >= 15:
                        c["bo"], c["bob"] = nb(True)
                        for e_ in range(2):
                            with P.group("tensor"):
                                mm(c["bo"][:, e_ * 128:(e_ + 1) * 128], vsb[:, n, e_ * 128:(e_ + 1) * 128], Am[i4][:, :], True, False, [HB["v"], TB["Am"][i4]], [c["bob"]])
                                mm(c["bo"][:, e_ * 128:(e_ + 1) * 128], Sprev[:, e_ * 128:(e_ + 1) * 128], qin[i4][:, :], False, True, [Sprevb, TB["qin"][i4]], [c["bob"]])
                    c["bs"], c["bsb"] = nb(True)
                    mm(c["bs"][:, 0:256], kendT[i4][:, :], vsb[:, n, :], True, True, [TB["kendT"][i4], HB["v"]], [c["bsb"]])

                def c4_(n, c):
                    i4 = n % 4
                    stt("vector", Sst[:, :], Sst[:, :], decay[:, n:n + 1], c["bs"][:, 0:256], ALU.mult, ALU.add, [c["bsb"], HB["gate"]], [HB["S"]])
                    cp("gpsimd", Sbf[n % 3][:, :], Sst[:, :], [HB["S"]], [Sbfb[n % 3]])
                    rel(c["bsb"])
                    if n >= 15:
                        cp("scalar", osb[i4][:, :, :], c["bo"][:, 0:256].rearrange("p (a b) -> p a b", a=2), [c["bob"]], [TB["osb"][i4]])
                        rel(c["bob"])

                def c5_(n, c):
                    i4 = n % 4
                    if n < 15:
                        return
                    tt("gpsimd", osq[i4][:, :, :], osb[i4][:, :, :], osb[i4][:, :, :], ALU.mult, [TB["osb"][i4]], [TB["osq"][i4]])
                    c["bm"], c["bmb"] = nb(True)
                    bm, bmb = c["bm"], c["bmb"]
                    with P.group("tensor"):
                        mm(bm[:, 0:128], ones256[:, :], osb[i4][:, 0, :], True, False, [CB, TB["osb"][i4]], [bmb])
                        mm(bm[:, 0:128], ones256[:, :], osb[i4][:, 1, :], False, True, [CB, TB["osb"][i4]], [bmb])
                    with P.group("tensor"):
                        mm(bm[:, 128:256], ones256[:, :], osq[i4][:, 0, :], True, False, [CB, TB["osq"][i4]], [bmb])
                        mm(bm[:, 128:256], ones256[:, :], osq[i4][:, 1, :], False, True, [CB, TB["osq"][i4]], [bmb])

                def c6_(n, c):
                    i4 = n % 4
                    if n < 15:
                        return
                    q0 = (n - 15) * 128
                    cp("scalar", mst[i4][:, :], c["bm"][:, 0:256], [c["bmb"]], [TB["mst"][i4]])
                    rel(c["bmb"])
                    tt("vector", t1[i4][:, :], mst[i4][:, 0:128], mst[i4][:, 0:128], ALU.mult, [TB["mst"][i4]], [TB["t1"][i4]])
                    tt("vector", t1[i4][:, :], mst[i4][:, 128:256], t1[i4][:, :], ALU.subtract, [TB["mst"][i4]], [TB["t1"][i4]])
                    act(t1[i4][:, :], t1[i4][:, :], AF.Ln, [], [TB["t1"][i4]], bias=LN_EPS)
                    act(t1[i4][:, :], t1[i4][:, :], AF.Exp, [], [TB["t1"][i4]], scale=-0.5)

                def c7_(n, c):
                    i4 = n % 4
                    if n < 15:
                        return
                    q0 = (n - 15) * 128
                    for e_ in range(2):
                        onb, ONB = on2[i4][e_], TB2["on"][i4][e_]
                        tt("vector", onb[:, :], osb[i4][:, e_, :], mst[i4][:, 0:128], ALU.subtract, [TB["osb"][i4], TB["mst"][i4]], [ONB])
                        tt("gpsimd", onb[:, :], onb[:, :], t1[i4][:, :], ALU.mult, [TB["t1"][i4]], [ONB])
                        if n == 15:
                            stt("vector", og[:, e_, 0:2], onb[:, 126:128], glag[:, 2 * h + e_:2 * h + e_ + 1], rT[:, e_, q0 + 126:q0 + 128],
                                ALU.mult, ALU.mult, [ONB, HB["r"], CB], [HB["og"]])
                        else:
                            o0 = 2 + (n - 16) * 128
                            stt("vector", og[:, e_, o0:o0 + 128], onb[:, :], glag[:, 2 * h + e_:2 * h + e_ + 1], rT[:, e_, q0:q0 + 128],
                                ALU.mult, ALU.mult, [ONB, HB["r"], CB], [HB["og"]])
                gsteps = make_gates(h + 1) if h + 1 < ngla else iter(())
                for _ in pipeline_gen(32, [c0_, c1_, c2_, c3_, c4_, c5_, c6_, c7_]):
                    next(gsteps, None)
                for _ in gsteps:
                    pass
                for e_ in range(2):
                    P.dma("sync", mixT[(2 * h + e_) * 128:(2 * h + e_ + 1) * 128, :], og[:, e_, :], ogsem, [HB["og"]], [MIXB])
            P.flush()

        P.barrier()
        if stop == "A":
            finish()
            return nc
        with ExitStack() as es:
            sb = lambda name, shape, dt: es.enter_context(nc.sbuf_tensor(name, list(shape), dt))
            xring = Ring(P, [sb("xblkb%d" % i, [128, KC, 512], BF16) for i in range(2)])

            def load_xp(blk):
                t, b, s = xring.next()
                P.dma("sync", t[:, :, :], xb_perm[:, blk * 512:(blk + 1) * 512].rearrange("(k p) n -> p k n", p=128), s, [XP[blk]], [b])
                return t, b

            cosT = sb("cosT", [32, NL], F32)
            sinT = sb("sinT", [32, NL], F32)
            ROT = Buf()
            with ExitStack() as es2:
                sb2 = lambda name, shape, dt: es2.enter_context(nc.sbuf_tensor(name, list(shape), dt))
                posi = sb2("posi", [32, NL], I32)
                ang = sb2("ang", [32, NL], F32)
                tf = sb2("tf", [32, NL], F32)
                rr = sb2("rr", [32, NL], F32)
                mk_ = sb2("mk_", [32, NL], F32)
                rotc = sb2("rotc_sb", [32, 2], F32)
                RB = Buf()
                rsem = P.new_dsem()
                P.dma("sync", posi[:, :], pos_in.partition_broadcast(32), rsem, (), [RB])
                P.dma("sync", rotc[:, :], rotc_in, rsem, (), [RB])
                defer = []

                def DF(fn, *a_, **k_):
                    defer.append(lambda: fn(*a_, **k_))
                DF(cp, "vector", ang[:, :], posi[:, :], [RB], [RB])
                DF(ts, "vector", ang[:, :], ang[:, :], rotc[:, 0:1], None, ALU.mult, None, [RB], [RB])
                DF(ts, "vector", tf[:, :], ang[:, :], 1.0 / (2 * PI), 0.5, ALU.mult, ALU.add, [RB], [RB])
                DF(cp, "vector", posi[:, :], tf[:, :], [RB], [RB])
                DF(cp, "vector", tf[:, :], posi[:, :], [RB], [RB])
                C1 = 6.28125
                C2 = 2 * PI - C1
                DF(stt, "vector", rr[:, :], tf[:, :], -C1, ang[:, :], ALU.mult, ALU.add, [RB], [RB])
                DF(stt, "vector", rr[:, :], tf[:, :], -C2, rr[:, :], ALU.mult, ALU.add, [RB], [RB])

                def wrap_clamp(r):
                    DF(ts, "vector", mk_[:, :], r[:, :], -PI, None, ALU.is_lt, None, [RB], [RB])
                    DF(stt, "vector", r[:, :], mk_[:, :], 2 * PI, r[:, :], ALU.mult, ALU.add, [RB], [RB])
                    DF(ts, "vector", mk_[:, :], r[:, :], PI, None, ALU.is_gt, None, [RB], [RB])
                    DF(stt, "vector", r[:, :], mk_[:, :], -2 * PI, r[:, :], ALU.mult, ALU.add, [RB], [RB])
                    DF(ts, "vector", r[:, :], r[:, :], -3.141592, 3.141592, ALU.max, ALU.min, [RB], [RB])
                wrap_clamp(rr)
                DF(act, sinT[:, :], rr[:, :], AF.Sin, [RB], [ROT], scale=rotc[:, 1:2])
                DF(ts, "vector", rr[:, :], rr[:, :], PI / 2, None, ALU.add, None, [RB], [RB])
                wrap_clamp(rr)
                DF(act, cosT[:, :], rr[:, :], AF.Sin, [RB], [ROT])

                wvd = sb2("wvd_sb", [128, KC, 1024], BF16)
                WV = Buf()
                wvs = P.new_dsem()
                for g in range(2):
                    P.dma("gpsimd", wvd[:, :, g * 512:(g + 1) * 512], wvd_in[:, g * 512:(g + 1) * 512].rearrange("(k p) n -> p k n", p=128), wvs, (), [WV])
                vst = Ring(P, [sb2("vst%d" % i, [128, 1024], BF16) for i in range(2)])
                for blk in range(8):
                    if ngla >= 3:
                        xb, xbb = load_xp(blk)
                    else:
                        t_, b_, s__ = xring.next()
                        cols_ = slice(blk * 512, (blk + 1) * 512)
                        P.dma("gpsimd", t_[:, :, :], xT_perm[:, cols_].rearrange("(k p) n -> p k n", p=128), s__, (), [b_])
                        P.dma("sync", xb_perm[:, cols_].rearrange("(k p) n -> p k n", p=128), t_[:, :, :], s__, [b_], [XP[blk]])
                        xb, xbb = t_, b_
                    for sub in range(4):
                        if defer:
                            defer.pop(0)()
                        vt, vtb, vts = vst.next()
                        for g in range(2):
                            bv, bvb = nb()
                            with P.group("tensor"):
                                for kc in range(KC):
                                    mm(bv[:, :], xb[:, kc, sub * 128:(sub + 1) * 128], wvd[:, kc, g * 512:(g + 1) * 512], kc == 0, kc == KC - 1, [xbb, WV], [bvb])
                            cp("scalar" if g == 0 else "vector", vt[:, g * 512:(g + 1) * 512], bv[:, :], [bvb], [vtb])
                        r0 = (blk * 4 + sub) * 128
                        P.dma("scalar", vd_scr[r0:r0 + 128, :], vt[:, :], vts, [vtb], [VDB])
                        T_ = blk * 4 + sub
                        s_, r16 = T_ // 16, T_ % 16
                        a_, r4 = r16 // 4, r16 % 4
                        d4 = vd4_scr.rearrange("(t i) c -> t i c", i=128)[r4 * 8 + 4 * s_:r4 * 8 + 4 * s_ + 4, 32 * a_:32 * a_ + 32, :]
                        P.dma("scalar", d4, vt[:, :], vts, [vtb], [VDB])
                        d1 = vd1_scr.rearrange("(t i) c -> t i c", i=128)[16 * s_:16 * s_ + 16, 8 * r16:8 * r16 + 8, :]
                        P.dma("scalar", d1, vt[:, :], vts, [vtb], [VDB])
                while defer:
                    defer.pop(0)()
                P.flush()
            P.barrier()
            if stop == "V":
                finish()
                return nc

            wdring = Ring(P, [sb("wd%d" % i, [128, KC, 320], BF16) for i in range(2)])
            dq2 = [sb("d_qq%d" % i, [128, 2560], BF16) for i in range(2)]
            dk2 = [sb("d_kk%d" % i, [128, NL], BF16) for i in range(2)]
            DBQ, DBK = [Buf(), Buf()], [Buf(), Buf()]
            dk4 = sb("d_k4", [128, NL], BF16)
            dk1 = sb("d_k1", [128, NL], BF16)
            dq4 = sb("d_q4", [128, 2048], BF16)
            dq1 = sb("d_q1", [128, 2048], BF16)
            hq1 = sb("d_hq1", [128, 16], BF16)
            V16 = sb("d_v16", [128, 32, 128], BF16)
            V4 = sb("d_v4", [128, 32, 128], BF16)
            V1 = sb("d_v1", [128, 32, 128], BF16)
            acc = sb("d_acc", [128, 2560], F32)
            dacc = sb("d_dacc", [128, 2560], F32)
            odb = sb("d_od", [128, TQ], BF16)
            rt1 = [sb("d_rt1%d" % i, [32, 512], F32) for i in range(2)]
            rt2 = [sb("d_rt2%d" % i, [32, 512], F32) for i in range(2)]
            Pm = [sb("d_P%d" % i, [128, 2, 128], BF16) for i in range(4)]
            DB = {k: Buf() for k in ["q", "k", "v", "acc", "od", "k4", "k1", "q4", "q1"]}
            DBV = {16: Buf(), 4: Buf(), 1: Buf()}
            RTB = [[Buf(), Buf()], [Buf(), Buf()]]
            PmB = [Buf() for _ in range(4)]
            vsem = P.new_dsem()
            odsem = P.new_dsem()
            SC = 128.0 ** -0.5
            vd5 = vd_scr

            def kcols(t, off, kind, a, b_):
                if kind == 16:
                    p0 = 2048 * a + 128 * b_ - off
                    return t[:, p0:p0 + 128]
                if kind == 4:
                    r4, n = a, b_
                    s_, m = n // 4, n % 4
                    base = 2048 * s_ - off
                    return t[:, base:base + 2048].rearrange("p (a r u) -> p a r u", a=4, r=4, u=128)[:, :, r4, 32 * m:32 * m + 32]
                s_, m = a, b_
                base = 2048 * s_ - off
                return t[:, base:base + 2048].rearrange("p (r m u) -> p r m u", r=16, m=16, u=8)[:, :, m, :]

            def head_setup(h):
                wd, wdb, wds = wdring.next()
                P.dma("gpsimd", wd[:, :, :], wd_in[h].rearrange("(k p) n -> p k n", p=128), wds, (), [wdb])
                return wd, wdb

            def proj_gen(h, wd, wdb):
                dq, dk = dq2[h % 2], dk2[h % 2]
                DBq = {"q": DBQ[h % 2], "k": DBK[h % 2]}
                for blk in range(8):
                    xb, xbb = load_xp(blk)
                    cols = slice(blk * 512, (blk + 1) * 512)
                    todo = [(128, 288, dk[:, cols], "k")]
                    if blk >= 3:
                        todo.append((0, 256, dq[:, (blk - 3) * 512:(blk - 2) * 512], "q"))
                    for ti, (c0, cs, dst, hb) in enumerate(todo):
                        bk, bkb = nb(True)
                        with P.group("tensor"):
                            for kc in range(KC):
                                mm(bk[:, :], wd[:, kc, c0:c0 + 128], xb[:, kc, :], kc == 0, kc == KC - 1, [xbb, wdb], [bkb])
                        yield blk
                        bs_, bsb_ = nb(True)
                        with P.group("tensor"):
                            for kc in range(KC):
                                mm(bs_[0:32, :], wd[:, kc, cs:cs + 32], xb[:, kc, :], kc == 0, kc == KC - 1, [xbb, wdb], [bsb_])
                        cp("scalar", dst, bk[:, :], [bkb], [DBq[hb]])
                        tt("vector", rt1[ti][:, :], bk[0:32, :], cosT[:, cols], ALU.mult, [bkb, ROT], [RTB[ti][0]])
                        tt("vector", rt2[ti][:, :], bs_[0:32, :], sinT[:, cols], ALU.mult, [bsb_, ROT], [RTB[ti][1]])
                        P.op("vector", lambda e, dst=dst, ti=ti: e.tensor_tensor(out=dst[0:32], in0=rt1[ti][:, :], in1=rt2[ti][:, :], op=ALU.add),
                             [RTB[ti][0], RTB[ti][1]], [DBq[hb]])
                        rel(bkb)
                        rel(bsb_)
                        yield blk

            vdone = set()
            vsems = {16: P.new_dsem(), 4: P.new_dsem(), 1: P.new_dsem()}

            def vload(h, kind_):
                if (h, kind_) in vdone:
                    return
                vdone.add((h, kind_))
                hc = slice(h * 128, (h + 1) * 128)
                dst, src = {16: (V16, vd5), 4: (V4, vd4_scr), 1: (V1, vd1_scr)}[kind_]
                P.dma("gpsimd", dst[:, :, :], src[:, hc].rearrange("(t p) c -> p t c", p=128), vsems[kind_], [VDB], [DBV[kind_]])

            def post_proj(h):
                dq, dk = dq2[h % 2], dk2[h % 2]
                DBq = {"q": DBQ[h % 2], "k": DBK[h % 2]}
                for kind_ in (16, 4, 1):
                    vload(h, kind_)
                for s_ in range(2):
                    srck = dk[:, 2048 * s_:2048 * s_ + 2048]
                    for r4 in range(4):
                        P.op("vector" if r4 % 2 == 0 else "gpsimd", lambda e, s_=s_, r4=r4, srck=srck: e.tensor_copy(
                            out=dk4[:, (r4 * 8 + s_ * 4) * 128:(r4 * 8 + s_ * 4 + 4) * 128].rearrange("p (m a u) -> p m a u", m=4, a=4, u=32),
                            in_=srck.rearrange("p (a r m u) -> p r m a u", a=4, r=4, m=4, u=32)[:, r4]), [DBq["k"]], [DB["k4"]])
                    P.op("vector", lambda e, s_=s_, srck=srck: e.tensor_copy(
                        out=dk1[:, 2048 * s_:2048 * s_ + 2048].rearrange("p (m r u) -> p m r u", m=16, r=16, u=8),
                        in_=srck.rearrange("p (r m u) -> p m r u", r=16, m=16, u=8)), [DBq["k"]], [DB["k1"]])
                srcq = dq[:, 512:2560]
                for r4 in range(4):
                    P.op("scalar", lambda e, r4=r4: e.activation(
                        out=dq4[:, r4 * 512:(r4 + 1) * 512].rearrange("p (m a u) -> p m a u", m=4, a=4, u=32),
                        in_=srcq.rearrange("p (a r m u) -> p r m a u", a=4, r=4, m=4, u=32)[:, r4], func=AF.Copy), [DBq["q"]], [DB["q4"]])
                P.op("scalar", lambda e: e.activation(
                    out=dq1[:, :].rearrange("p (m r u) -> p m r u", m=16, r=16, u=8),
                    in_=srcq.rearrange("p (r m u) -> p m r u", r=16, m=16, u=8), func=AF.Copy), [DBq["q"]], [DB["q1"]])
                P.op("scalar", lambda e: e.activation(
                    out=hq1[:, :].rearrange("p (r u) -> p r u", r=2),
                    in_=dq[:, 256:512].rearrange("p (r u) -> p r u", r=2)[:, :, 120:128], func=AF.Copy), [DBq["q"]], [DB["q1"]])


            def att_gen(h):
                dq, dk = dq2[h % 2], dk2[h % 2]
                DBq = {"q": DBQ[h % 2], "k": DBK[h % 2]}

                def kap(kb):
                    kind, a_, b_ = kb
                    if kind == 16:
                        p0 = 2048 * a_ + 128 * b_
                        return dk[:, p0:p0 + 128], DBq["k"]
                    if kind == 4:
                        p0 = (a_ * 8 + b_) * 128
                        return dk4[:, p0:p0 + 128], DB["k4"]
                    p0 = (16 * a_ + b_) * 128
                    return dk1[:, p0:p0 + 128], DB["k1"]
                mset("gpsimd", acc[:, :], 0.0, [DB["acc"]])
                mset("gpsimd", dacc[:, :], 0.0, [DB["acc"]])
                blocks = []
                for r in range(16):
                    blocks.append((16, (kcols(dq, 1536, 16, 1, r), DBq["q"]), kcols(acc, 1536, 16, 1, r), kcols(dacc, 1536, 16, 1, r), 128, None,
                                   [((16, 0, r), V16[:, r, :], masksc[:, 0, :]), ((16, 1, r), V16[:, 16 + r, :], masks[:, 1, :])]))
                for r in (14, 15):
                    blocks.append((16, (kcols(dq, 1536, 16, 0, r), DBq["q"]), kcols(acc, 1536, 16, 0, r), kcols(dacc, 1536, 16, 0, r), 128, None,
                                   [((16, 0, r), V16[:, r, :], masksc[:, 1, :])]))
                for r4 in range(4):
                    for n in range(4, 8):
                        pm = masksc[:, 2, :] if n == 4 else masks[:, 2, :]
                        blocks.append((4, (dq4[:, (r4 * 4 + n - 4) * 128:(r4 * 4 + n - 3) * 128], DB["q4"]), kcols(acc, 1536, 4, r4, n), kcols(dacc, 1536, 4, r4, n), 128, [4, 32],
                                       [((4, r4, n - 1), V4[:, r4 * 8 + n - 1, :], pm), ((4, r4, n), V4[:, r4 * 8 + n, :], masks[:, 3, :])]))
                for r4 in (2, 3):
                    p0 = (12 + r4) * 128 + 96 - 1536
                    blocks.append((4, (dq[:, p0:p0 + 32], DBq["q"]), acc[:, p0:p0 + 32], dacc[:, p0:p0 + 32], 32, None,
                                   [((4, r4, 2), V4[:, r4 * 8 + 2, :], masksc[:, 2, 96:128]), ((4, r4, 3), V4[:, r4 * 8 + 3, :], masksc[:, 3, 96:128])]))
                for m in range(16):
                    pk = (1, 0, 15) if m == 0 else (1, 1, m - 1)
                    pm = masksc[:, 4, :] if m == 0 else masks[:, 4, :]
                    blocks.append((1, (dq1[:, m * 128:(m + 1) * 128], DB["q1"]), kcols(acc, 1536, 1, 1, m), kcols(dacc, 1536, 1, 1, m), 128, [16, 8],
                                   [(pk, V1[:, 16 * pk[1] + pk[2], :], pm), ((1, 1, m), V1[:, 16 + m, :], masks[:, 5, :])]))
                hq = lambda t: t[:, 14 * 128 - 1536:16 * 128 - 1536].rearrange("p (r u) -> p r u", r=2)[:, :, 120:128]
                blocks.append((1, (hq1[:, :], DB["q1"]), hq(acc), hq(dacc), 16, [2, 8],
                               [((1, 0, 14), V1[:, 14, :], masksc[:, 4, 112:128]), ((1, 0, 15), V1[:, 15, :], masksc[:, 5, 112:128])]))
                def a0(i, c):
                    kind, (qap, qbuf), accap, daccap, nq, qshape, keys = blocks[i]
                    c["bsc"], c["bscb"] = nb(True)
                    for ki, (kb, vt, mk) in enumerate(keys):
                        ka, kbuf = kap(kb)
                        mm(c["bsc"][:, ki * 128:ki * 128 + nq], ka, qap, True, True, [kbuf, qbuf], [c["bscb"]])

                def a1(i, c):
                    kind, (qap, qbuf), accap, daccap, nq, qshape, keys = blocks[i]
                    pi = i % 4
                    nk = len(keys)
                    act(Pm[pi][:, 0:nk, 0:nq], c["bsc"][:, 0:nk * 128].rearrange("p (a b) -> p a b", a=nk)[:, :, 0:nq], AF.Exp, [c["bscb"]], [PmB[pi]], scale=SC)
                    rel(c["bscb"])
                    for ki, (kb, vt, mk) in enumerate(keys):
                        tt("gpsimd", Pm[pi][:, ki, 0:nq], Pm[pi][:, ki, 0:nq], mk, ALU.mult, [CB], [PmB[pi]])

                def a2(i, c):
                    kind, (qap, qbuf), accap, daccap, nq, qshape, keys = blocks[i]
                    pi = i % 4
                    nk = len(keys)
                    c["bo"], c["bob"] = nb(True)
                    with P.group("tensor"):
                        for ki, (kb, vt, mk) in enumerate(keys):
                            mm(c["bo"][:, 0:nq], vt, Pm[pi][:, ki, 0:nq], ki == 0, ki == nk - 1, [DBV[kind], PmB[pi]], [c["bob"]])
                    with P.group("tensor"):
                        for ki, (kb, vt, mk) in enumerate(keys):
                            mm(c["bo"][:, 128:128 + nq], ones_bf[:, :], Pm[pi][:, ki, 0:nq], ki == 0, ki == nk - 1, [CB, PmB[pi]], [c["bob"]])

                def a3(i, c):
                    kind, (qap, qbuf), accap, daccap, nq, qshape, keys = blocks[i]
                    o_in = c["bo"][:, 0:nq]
                    d_in = c["bo"][:, 128:128 + nq]
                    if qshape is not None:
                        o_in = o_in.rearrange("p (a b) -> p a b", a=qshape[0])
                        d_in = d_in.rearrange("p (a b) -> p a b", a=qshape[0])
                    tt("vector", accap, accap, o_in, ALU.add, [c["bob"]], [DB["acc"]])
                    tt("vector", daccap, daccap, d_in, ALU.add, [c["bob"]], [DB["acc"]])
                    rel(c["bob"])
                for t_ in pipeline_gen(len(blocks), [a0, a1, a2, a3]):
                    yield t_
                ts("vector", dacc[:, 256:2560], dacc[:, 256:2560], 1e-30, None, ALU.max, None, [], [DB["acc"]])
                act(dacc[:, 256:2560], dacc[:, 256:2560], AF.Ln, [], [DB["acc"]])
                act(dacc[:, 256:2560], dacc[:, 256:2560], AF.Exp, [], [DB["acc"]], scale=-1.0)
                P.op("vector", lambda e: e.tensor_tensor(out=odb[:, 2:TQ].rearrange("p (u r) -> p r u", r=16),
                                                         in0=acc[:, 512:2560].rearrange("p (r u) -> p r u", r=16),
                                                         in1=dacc[:, 512:2560].rearrange("p (r u) -> p r u", r=16), op=ALU.mult), [DB["acc"]], [DB["od"]])
                tt("vector", odb[:, 0:1], acc[:, 383:384], dacc[:, 383:384], ALU.mult, [DB["acc"]], [DB["od"]])
                tt("vector", odb[:, 1:2], acc[:, 511:512], dacc[:, 511:512], ALU.mult, [DB["acc"]], [DB["od"]])
                P.dma("sync", mixT[(8 + h) * 128:(9 + h) * 128, :], odb[:, :], odsem, [DB["od"]], [MIXB])
                yield -1

            prev_att = None
            wd_next = head_setup(0)
            for h in range(ndil):
                wd, wdb = wd_next
                pg = proj_gen(h, wd, wdb)
                nst = 0
                for _blk in pg:
                    if prev_att is not None:
                        for _ in range(3):
                            if next(prev_att, None) is None:
                                break
                            nst += 1
                            if nst == 23:
                                vload(h, 16)
                            if nst == 41:
                                vload(h, 4)
                if h + 1 < ndil:
                    wd_next = head_setup(h + 1)
                if prev_att is not None:
                    for _ in prev_att:
                        pass
                post_proj(h)
                prev_att = att_gen(h)
            for _ in prev_att:
                pass
            P.flush()
        P.barrier()

        def gemm_ln_phase(tag, w_dram, a_dram, a_cast, resid_dram, lnidx, out_dram, out_col0, chunks, ABUF, RBUF, OBUF):
            with ExitStack() as es:
                sb = lambda name, shape, dt: es.enter_context(nc.sbuf_tensor(tag + name, list(shape), dt))
                W = sb("W", [128, KC, D], BF16)
                WB = [Buf() for _ in range(4)]
                for g in range(4):
                    P.dma("gpsimd", W[:, :, g * 512:(g + 1) * 512], w_dram[:, g * 512:(g + 1) * 512].rearrange("(k p) n -> p k n", p=128), P.new_dsem(), (), [WB[g]])
                aring = Ring(P, [sb("a%d" % i, [128, KC, 512], BF16) for i in range(2)])
                ln_chunks(tag, sb, chunks, KC,
                          lambda dc, kc: W[:, kc, dc * 128:(dc + 1) * 128], WB,
                          a_dram, a_cast, aring, ABUF, resid_dram, RBUF, lnidx, out_dram, out_col0, OBUF)
                P.flush()
            P.barrier()

        def ln_chunks(tag, sb, chunks, nk, wfn, wbufs, a_dram, a_cast, aring, ABUF, resid_dram, RBUF, lnidx, out_dram, out_col0, OBUF, wstream=None):
            ny = 2
            y = [sb("y%d" % i, [128, KC, 512], F32) for i in range(ny)]
            YB = [Buf() for _ in range(ny)]
            s1 = [sb("s1_%d" % i, [128, 512], F32) for i in range(2)]
            s2 = [sb("s2_%d" % i, [128, 512], F32) for i in range(2)]
            SB1, SB2 = [Buf(), Buf()], [Buf(), Buf()]
            ysq = [sb("ysq%d" % i, [128, 512], F32) for i in range(2)]
            YSQ = [Buf(), Buf()]
            rres = Ring(P, [sb("res%d" % i, [128, 512], F32) for i in range(3)])
            mean = sb("mean", [128, 512], F32)
            rstd = sb("rstd", [128, 512], F32)
            MB = Buf()
            tn = [sb("tn%d" % i, [128, 512], F32) for i in range(2)]
            TN = [Buf(), Buf()]
            oring = Ring(P, [sb("o%d" % i, [128, 512], F32) for i in range(3)])
            pend = iter(())
            for ci, (c0, c1) in enumerate(chunks):
                n = c1 - c0
                yi = ci % ny
                yc, ycb, s1c, s2c, S1B, S2B = y[yi], YB[yi], s1[ci % 2], s2[ci % 2], SB1[ci % 2], SB2[ci % 2]
                if ci == 0:
                    nxt = aring.next()
                    P.dma("gpsimd" if a_cast else "sync", nxt[0][:, 0:nk, 0:n], a_dram[:, c0:c1].rearrange("(k p) n -> p k n", p=128), nxt[2], [ABUF], [nxt[1]])
                a, ab, asem = nxt
                if ci + 1 < len(chunks):
                    d0, d1 = chunks[ci + 1]
                    nxt = aring.next()
                    P.dma("gpsimd" if a_cast else "sync", nxt[0][:, 0:nk, 0:d1 - d0], a_dram[:, d0:d1].rearrange("(k p) n -> p k n", p=128), nxt[2], [ABUF], [nxt[1]])

                def epi(dc, bk, bkb):
                    rt, rtb, rsem_ = rres.next()
                    P.dma("sync", rt[:, 0:n], resid_dram(dc, c0, c1), rsem_, [RBUF], [rtb])
                    stt("vector", yc[:, dc, 0:n], rt[:, 0:n], ALPHA, bk[:, 0:n], ALU.mult, ALU.add, [rtb, bkb], [ycb])
                    i2 = dc % 2
                    if dc == 0:
                        cp("vector", s1c[:, 0:n], yc[:, dc, 0:n], [ycb], [S1B])
                        act(s2c[:, 0:n], yc[:, dc, 0:n], AF.Square, [ycb], [S2B])
                    else:
                        tt("vector", s1c[:, 0:n], s1c[:, 0:n], yc[:, dc, 0:n], ALU.add, [ycb], [S1B])
                        act(ysq[i2][:, 0:n], yc[:, dc, 0:n], AF.Square, [ycb], [YSQ[i2]])
                        tt("gpsimd", s2c[:, 0:n], s2c[:, 0:n], ysq[i2][:, 0:n], ALU.add, [YSQ[i2]], [S2B])

                if wstream is None:
                    for dc in range(KC):
                        bk, bkb = nb()
                        with P.group("tensor"):
                            for kc in range(nk):
                                mm(bk[:, 0:n], wfn(dc, kc), a[:, kc, 0:n], kc == 0, kc == nk - 1, [ab, wbufs[dc // 4]], [bkb])
                        epi(dc, bk, bkb)
                        if dc >= 2:
                            next(pend, None)
                else:
                    for qd_ in range(4):
                        acc4 = [nb(True) for _ in range(4)]
                        for j in range(nk):
                            wt, wtb = wstream(j, qd_)
                            with P.group("tensor"):
                                for i_ in range(4):
                                    mm(acc4[i_][0][:, 0:n], wt[:, i_ * 128:(i_ + 1) * 128], a[:, j, 0:n], j == 0, j == nk - 1, [ab, wtb], [acc4[i_][1]])
                        for i_ in range(4):
                            epi(4 * qd_ + i_, acc4[i_][0], acc4[i_][1])
                            rel(acc4[i_][1])
                            if 4 * qd_ + i_ >= 2:
                                next(pend, None)
                def part2(n=n, c0=c0, c1=c1, yc=yc, ycb=ycb, s1c=s1c, s2c=s2c, S1B=S1B, S2B=S2B):
                    yield 0
                    bm, bmb = nb()
                    mm(bm[:, 0:n], ones2048[:, :], s1c[:, 0:n], True, True, [CB, S1B], [bmb])
                    bm2, bm2b = nb()
                    mm(bm2[:, 0:n], ones2048[:, :], s2c[:, 0:n], True, True, [CB, S2B], [bm2b])
                    cp("scalar", mean[:, 0:n], bm[:, 0:n], [bmb], [MB])
                    tt("vector", rstd[:, 0:n], mean[:, 0:n], mean[:, 0:n], ALU.mult, [MB], [MB])
                    tt("vector", rstd[:, 0:n], bm2[:, 0:n], rstd[:, 0:n], ALU.subtract, [bm2b], [MB])
                    act(rstd[:, 0:n], rstd[:, 0:n], AF.Ln, [], [MB], bias=LN_EPS)
                    act(rstd[:, 0:n], rstd[:, 0:n], AF.Exp, [], [MB], scale=-0.5)
                    for dc in range(KC):
                        i2 = dc % 2
                        tt("vector", tn[i2][:, 0:n], yc[:, dc, 0:n], mean[:, 0:n], ALU.subtract, [ycb, MB], [TN[i2]])
                        tt("vector" if dc % 2 == 0 else "gpsimd", tn[i2][:, 0:n], tn[i2][:, 0:n], rstd[:, 0:n], ALU.mult, [MB], [TN[i2]])
                        ot, otb, osem_ = oring.next()
                        act(ot[:, 0:n], tn[i2][:, 0:n], AF.Identity, [TN[i2], CB], [otb], scale=lnp[:, 2 * lnidx, dc:dc + 1], bias=lnp[:, 2 * lnidx + 1, dc:dc + 1])
                        P.dma("scalar", out_dram[dc * 128:(dc + 1) * 128, c0 - out_col0:c1 - out_col0], ot[:, 0:n], osem_, [otb], [OBUF])
                        yield dc + 1
                for _ in pend:
                    pass
                pend = part2()
            for _ in pend:
                pass

        if stop == "B":
            finish()
            return nc
        X1B, OCB, X2B, HTB, OUTB = Buf(), Buf(), Buf(), Buf(), Buf()
        W2B = Buf()
        w2sem = P.new_dsem()
        NOB = Buf()
        xres = lambda dc, c0, c1: xT_nat[dc * 128:(dc + 1) * 128, 2046 + c0:2046 + c1]
        gemm_ln_phase("C", wout_in, mixT, False, xres, 0, x1T, 0, TCH, MIXB, NOB, X1B)
        if stop == "C":
            finish()
            return nc

        with ExitStack() as es:
            sb = lambda name, shape, dt: es.enter_context(nc.sbuf_tensor("D1" + name, list(shape), dt))
            mkT = sb("mkT", [128, 16, 256], BF16)
            mv = sb("mv", [128, 2, D], BF16)
            MKB, MVB, MTB = Buf(), Buf(), Buf()
            Wq = sb("Wq", [128, KC, D], BF16)
            WQB = Buf()
            wqs = P.new_dsem()
            es2 = ExitStack()
            sb2 = lambda name, shape, dt: es2.enter_context(nc.sbuf_tensor("D0" + name, list(shape), dt))
            mT = sb2("memT", [128, KC, 256], BF16)
            P.dma("gpsimd", mT[:, :, :], memT.rearrange("(k p) n -> p k n", p=128), P.new_dsem(), (), [MTB])
            wkvr = Ring(P, [sb2("wkv%d" % i, [128, KC, 512], BF16) for i in range(2)])
            for g in range(8):
                wt, wtb, wts = wkvr.next()
                P.dma("gpsimd", wt[:, :, :], wkv_in[:, g * 512:(g + 1) * 512].rearrange("(k p) n -> p k n", p=128), wts, (), [wtb])
                if g < 4:
                    for j in range(4):
                        bk, bkb = nb()
                        with P.group("tensor"):
                            for kc in range(KC):
                                mm(bk[:, 0:256], wt[:, kc, j * 128:(j + 1) * 128], mT[:, kc, :], kc == 0, kc == KC - 1, [wtb, MTB], [bkb])
                        cp("scalar", mkT[:, 4 * g + j, :], bk[:, 0:256], [bkb], [MKB])
                else:
                    for mt in range(2):
                        bk, bkb = nb()
                        with P.group("tensor"):
                            for kc in range(KC):
                                mm(bk[:, :], mT[:, kc, mt * 128:(mt + 1) * 128], wt[:, kc, :], kc == 0, kc == KC - 1, [wtb, MTB], [bkb])
                        cp("vector", mv[:, mt, (g - 4) * 512:(g - 3) * 512], bk[:, :], [bkb], [MVB])
            for g in range(4):
                P.dma("gpsimd", Wq[:, :, g * 512:(g + 1) * 512], wq_in[:, g * 512:(g + 1) * 512].rearrange("(k p) n -> p k n", p=128), wqs, (), [WQB])
            P.flush()
            es2.close()
            P.barrier()
            aring = Ring(P, [sb("a%d" % i, [128, KC, 512], BF16) for i in range(2)])
            qc = sb("qc", [128, KC, 512], BF16)
            QCB = Buf()
            ocr = Ring(P, [sb("oc%d" % i, [128, KC, 512], BF16) for i in range(2)])
            Pc = [sb("Pc%d" % i, [128, 2, 512], BF16) for i in range(2)]
            PCB = [Buf(), Buf()]
            rden = [sb("rden%d" % i, [128, 512], F32) for i in range(2)]
            RDB = [Buf(), Buf()]
            SCC = 512.0 ** -0.5
            for (c0, c1) in TCH:
                n = c1 - c0
                a, ab, asem = aring.next()
                P.dma("gpsimd", a[:, :, 0:n], x1T[:, c0:c1].rearrange("(k p) n -> p k n", p=128), asem, [X1B], [ab])
                for dc in range(KC):
                    bk, bkb = nb()
                    with P.group("tensor"):
                        for kc in range(KC):
                            mm(bk[:, 0:n], Wq[:, kc, dc * 128:(dc + 1) * 128], a[:, kc, 0:n], kc == 0, kc == KC - 1, [ab, WQB], [bkb])
                    cp("scalar" if dc % 2 == 0 else "vector", qc[:, dc, 0:n], bk[:, 0:n], [bkb], [QCB])
                oc, ocb, ocs = ocr.next()
                for hh in range(4):
                    i2 = hh % 2
                    for mt in range(2):
                        bsx, bsxb = nb()
                        with P.group("tensor"):
                            for c in range(4):
                                mm(bsx[:, 0:n], mkT[:, 4 * hh + c, mt * 128:(mt + 1) * 128], qc[:, 4 * hh + c, 0:n], c == 0, c == 3, [MKB, QCB], [bsxb])
                        act(Pc[i2][:, mt, 0:n], bsx[:, 0:n], AF.Exp, [bsxb], [PCB[i2]], scale=SCC)
                    bd, bdb = nb()
                    with P.group("tensor"):
                        for mt in range(2):
                            mm(bd[:, 0:n], ones_bf[:, :], Pc[i2][:, mt, 0:n], mt == 0, mt == 1, [CB, PCB[i2]], [bdb])
                    recip("vector", rden[i2][:, 0:n], bd[:, 0:n], [bdb], [RDB[i2]])
                    for c in range(4):
                        bo, bob = nb()
                        with P.group("tensor"):
                            for mt in range(2):
                                mm(bo[:, 0:n], mv[:, mt, (4 * hh + c) * 128:(4 * hh + c + 1) * 128], Pc[i2][:, mt, 0:n], mt == 0, mt == 1, [MVB, PCB[i2]], [bob])
                        tt("vector", oc[:, 4 * hh + c, 0:n], bo[:, 0:n], rden[i2][:, 0:n], ALU.mult, [bob, RDB[i2]], [ocb])
                P.dma("sync", ocT[:, c0:c1].rearrange("(k p) n -> p k n", p=128), oc[:, :, 0:n], ocs, [ocb], [OCB])
            P.flush()
        P.barrier()

        if stop == "D1":
            finish()
            return nc
        x1res = lambda dc, c0, c1: x1T[dc * 128:(dc + 1) * 128, c0:c1]
        gemm_ln_phase("D2", wo_in, ocT, False, x1res, 1, x2T, 0, TCH, OCB, X1B, X2B)
        if stop == "D2":
            finish()
            return nc

        with ExitStack() as es:
            sb = lambda name, shape, dt: es.enter_context(nc.sbuf_tensor("E" + name, list(shape), dt))
            x2b = sb("x2b", [128, KC, TQ], BF16)
            X2S = [Buf() for _ in TCH]
            P.dma("gpsimd", x2b[:, :, 0:2], x2T[:, 0:2].rearrange("(k p) n -> p k n", p=128), P.new_dsem(), [X2B], [X2S[0]])
            ts("vector", x2b[:, :, 0:2], x2b[:, :, 0:2], flag[:, 0:1], None, ALU.mult, None, [CB], [X2S[0]])
            for ci_, (c0, c1) in enumerate(TCH):
                if ci_ > 0:
                    P.dma("gpsimd", x2b[:, :, c0:c1], x2T[:, c0:c1].rearrange("(k p) n -> p k n", p=128), P.new_dsem(), [X2B], [X2S[ci_]])
            wr = Ring(P, [sb("w%d" % i, [128, KC, 256], BF16) for i in range(3)])
            ug = [sb("ug%d" % i, [128, TQ], F32) for i in range(2)]
            uu = [sb("uu%d" % i, [128, TQ], F32) for i in range(2)]
            yg = [sb("yg%d" % i, [128, 2048], F32) for i in range(2)]
            yu = [sb("yu%d" % i, [128, 2048], F32) for i in range(2)]
            hr = Ring(P, [sb("h%d" % i, [128, 2048], BF16) for i in range(2)])
            UG, UU, YG, YU = [Buf(), Buf()], [Buf(), Buf()], [Buf(), Buf()], [Buf(), Buf()]
            for j in range(NJ):
                i2 = j % 2
                wt, wtb, wts = wr.next()
                P.dma("gpsimd", wt[:, :, 0:128], fwin_in[:, j * 128:(j + 1) * 128].rearrange("(k p) n -> p k n", p=128), wts, (), [wtb])
                P.dma("gpsimd", wt[:, :, 128:256], fwin_in[:, DFF + j * 128:DFF + (j + 1) * 128].rearrange("(k p) n -> p k n", p=128), wts, (), [wtb])
                P.dma("gpsimd", w2b[j * 128:(j + 1) * 128, :], fwout_in[j * 128:(j + 1) * 128, :], w2sem, (), [W2B])
                for part, (ubuf, UB) in enumerate([(ug[i2], UG[i2]), (uu[i2], UU[i2])]):
                    for ci, (c0, c1) in enumerate(TCH):
                        n = c1 - c0
                        bk, bkb = nb()
                        with P.group("tensor"):
                            for kc in range(KC):
                                mm(bk[:, 0:n], wt[:, kc, part * 128:(part + 1) * 128], x2b[:, kc, c0:c1], kc == 0, kc == KC - 1, [wtb, X2S[ci]], [bkb])
                        cp("scalar", ubuf[:, c0:c1], bk[:, 0:n], [bkb], [UB])
                for part, (eng, ubuf, UB, ybuf, YB_) in enumerate([("vector", ug[i2], UG[i2], yg[i2], YG[i2]), ("gpsimd", uu[i2], UU[i2], yu[i2], YU[i2])]):
                    cj = part * NJ + j
                    act(ybuf[:, :], ubuf[:, 2:TQ], AF.Identity, [UB, CB], [YB_], scale=cw[:, cj, 2:3], bias=cb[:, cj:cj + 1])
                    stt("vector", ybuf[:, :], ubuf[:, 1:TQ - 1], cw[:, cj, 1:2], ybuf[:, :], ALU.mult, ALU.add, [UB, CB], [YB_])
                    stt("vector", ybuf[:, :], ubuf[:, 0:TQ - 2], cw[:, cj, 0:1], ybuf[:, :], ALU.mult, ALU.add, [UB, CB], [YB_])
                act(yg[i2][:, :], yg[i2][:, :], AF.Silu, [], [YG[i2]])
                ht, htb, hts = hr.next()
                tt("vector", ht[:, :], yg[i2][:, :], yu[i2][:, :], ALU.mult, [YG[i2], YU[i2]], [htb])
                P.dma("sync", hT[j * 128:(j + 1) * 128, :], ht[:, :], hts, [htb], [HTB])
            P.flush()
        P.barrier()

        if stop == "E":
            finish()
            return nc
        with ExitStack() as es:
            sb = lambda name, shape, dt: es.enter_context(nc.sbuf_tensor("F" + name, list(shape), dt))
            aring = Ring(P, [sb("a%d" % i, [128, NJ, 512], BF16) for i in range(2)])
            w2r = Ring(P, [sb("w%d" % i, [128, 512], BF16) for i in range(8)])

            def wstream(j, qd_):
                wt, wtb, wts = w2r.next()
                P.dma("sync", wt[:, :], w2b[j * 128:(j + 1) * 128, qd_ * 512:(qd_ + 1) * 512], wts, [W2B], [wtb])
                return wt, wtb
            x2res = lambda dc, c0, c1: x2T[dc * 128:(dc + 1) * 128, 2 + c0:2 + c1]
            ln_chunks("F", sb, [(i * 512, (i + 1) * 512) for i in range(4)], NJ, None, None,
                      hT, False, aring, HTB, x2res, X2B, 2, outT, 0, OUTB, wstream=wstream)
            final = [(s[0], s[1], "dma") for s in P.dsems if s[1] > 0]
            P.flush(final)
    return nc


def _perm_idx():
    idx = np.empty(NL, np.int64)
    for s in range(2):
        for r in range(16):
            idx[s * 2048 + r * 128:s * 2048 + (r + 1) * 128] = s * 2048 + 16 * np.arange(128) + r
    return idx


def _masks():
    m = np.zeros((128, 6, 128), np.float32)
    j = np.arange(128)[:, None]
    i = np.arange(128)[None, :]
    for pi_, nat in enumerate([lambda x: x, lambda x: 4 * (x % 32) + x // 32, lambda x: 16 * (x % 8) + x // 8]):
        nj, ni = nat(j), nat(i)
        m[:, 2 * pi_, :] = (nj >= ni)
        m[:, 2 * pi_ + 1, :] = (nj <= ni)
    return m.reshape(128, 768)


def _fm(v, nchunk):
    return np.ascontiguousarray(v.reshape(nchunk, 128).T)


_CACHE = {}


def make_in_maps(x, mem, positions, w_in, gla_gate_w2, gla_gate_b, gla_norm_g, w_out, ln1_g, ln1_b,
                 ca_wq, ca_wkv, ca_wo, ln2_g, ln2_b, ffn_w_in, ffn_conv_w, ffn_conv_b, ffn_w_out, ln3_g, ln3_b):
    f32 = np.float32
    x = np.asarray(x, f32)
    mem = np.asarray(mem, f32)
    positions = np.asarray(positions, np.int32)
    w_in = np.asarray(w_in, f32)[0]
    pidx = _perm_idx()
    o = 0
    cols = {}
    for name, w in zip(["qg", "kg", "vg", "rg", "glr", "qd", "kd", "vd"], [512, 512, 1024, 1024, 16, 1024, 1024, 1024]):
        cols[name] = w_in[:, o:o + w]
        o += w
    wg = np.stack([np.concatenate([cols["qg"][:, h * 128:(h + 1) * 128], cols["kg"][:, h * 128:(h + 1) * 128],
                                   cols["rg"][:, h * 256:(h + 1) * 256], cols["vg"][:, h * 256:(h + 1) * 256]], axis=1) for h in range(4)])
    swap = np.concatenate([np.arange(16, 32), np.arange(0, 16)])
    wd = np.stack([np.concatenate([cols["qd"][:, h * 128:(h + 1) * 128], cols["kd"][:, h * 128:(h + 1) * 128],
                                   cols["qd"][:, h * 128 + swap], cols["kd"][:, h * 128 + swap]], axis=1) for h in range(8)])
    w2aug = np.concatenate([np.asarray(gla_gate_w2, f32)[0], np.asarray(gla_gate_b, f32)[0][None, :]], axis=0)
    lnp = np.stack([_fm(np.asarray(v, f32)[0], 16) for v in [ln1_g, ln1_b, ln2_g, ln2_b, ln3_g, ln3_b]], axis=1).reshape(128, 96)
    convw = np.ascontiguousarray(np.asarray(ffn_conv_w, f32)[0].T.reshape(86, 128, 3).transpose(1, 0, 2)).reshape(128, 258)
    convb = _fm(np.asarray(ffn_conv_b, f32)[0], 86)
    jj = np.arange(128)[:, None]
    ii = np.arange(128)[None, :]
    uneg = np.where(jj <= ii, f32(-1.0 / 16.0), f32(0.0)).astype(f32)
    invf = (500000.0 ** (-(np.arange(0, 32, 2, dtype=np.float32)) / 32.0)).astype(f32)
    rotc = np.stack([np.concatenate([invf, invf]), np.concatenate([-np.ones(16, f32), np.ones(16, f32)])], axis=1).astype(f32)
    shared = dict(masks=_masks(), uneg=uneg, rotc=rotc, wg=np.ascontiguousarray(wg), wglr=np.ascontiguousarray(cols["glr"]),
                  w2aug=np.ascontiguousarray(w2aug), glag=_fm(np.asarray(gla_norm_g, f32)[0], 8), wd=np.ascontiguousarray(wd),
                  wvd=np.ascontiguousarray(cols["vd"]), w_out=np.asarray(w_out, f32)[0], lnp=np.ascontiguousarray(lnp),
                  ca_wq=np.asarray(ca_wq, f32)[0], ca_wkv=np.asarray(ca_wkv, f32)[0], ca_wo=np.asarray(ca_wo, f32)[0],
                  ffn_w_in=np.asarray(ffn_w_in, f32)[0], convw=convw, convb=convb, ffn_w_out=np.asarray(ffn_w_out, f32)[0])
    in_maps = []
    for c in range(8):
        b, hf = c // 2, c % 2
        xl = np.zeros((NL, D), f32)
        pl = np.zeros((NL,), np.int32)
        if hf == 1:
            xl[:] = x[b]
            pl[:] = positions[b]
        else:
            xl[2048:] = x[b, :2048]
            pl[2048:] = positions[b, :2048]
        xTn = np.ascontiguousarray(xl.T)
        m = dict(shared)
        m.update(xT_nat=xTn, xT_perm=np.ascontiguousarray(xTn[:, pidx]), memT=np.ascontiguousarray(mem[b].T),
                 pos_perm=np.ascontiguousarray(pl[pidx][None, :]), flag=np.full((128, 1), float(hf), f32))
        in_maps.append(m)
    return in_maps


def kernel(**inputs):
    if "nc" not in _CACHE:
        _CACHE["nc"] = build(False)
    nc = _CACHE["nc"]
    in_maps = make_in_maps(**inputs)
    res = run_bass_kernel_spmd(nc, in_maps, core_ids=list(range(8)))
    out = np.empty((4, 4096, D), np.float32)
    for c in range(8):
        b, hf = c // 2, c % 2
        out[b, hf * 2048:(hf + 1) * 2048, :] = np.asarray(res.results[c]["outT"]).T
    return out
```

```python
import math
from contextlib import ExitStack, contextmanager
import numpy as np
import concourse.bass as bass
import concourse.mybir as mybir
from concourse.bass_utils import run_bass_kernel_spmd

F32 = mybir.dt.float32
BF16 = mybir.dt.bfloat16
I32 = mybir.dt.int32
AF = mybir.ActivationFunctionType
ALU = mybir.AluOpType

D = 2048
KC = 16
NL = 4096
TQ = 2050
QN = 2176
DFF = 5504
NJ = 43
ALPHA = 2.0 ** 0.25
LN_EPS = 1e-5
TCH = [(0, 2), (2, 514), (514, 1026), (1026, 1538), (1538, 2050)]
ENGS = ["tensor", "vector", "scalar", "gpsimd", "sync"]
PI = math.pi


class Buf:
    __slots__ = ("w", "r", "excl")

    def __init__(self, excl=False):
        self.w = None
        self.r = {}
        self.excl = excl


class Prog:
    def __init__(self, nc, es):
        self.nc = nc
        self.q = {e: [] for e in ENGS}
        self.cnt = {e: 0 for e in ENGS}
        self.sem = {e: es.enter_context(nc.semaphore("s_" + e)) for e in ENGS}
        self.grp = {e: None for e in ENGS}
        self.es = es
        self.dsems = []
        self.bar = {e: [] for e in ENGS}
        self.waited = {e: {} for e in ENGS}

    def _deps(self, reads, writes, extra):
        d = [x for x in extra if x is not None]
        if any(b.excl for b in reads):
            writes = list(writes) + [b for b in reads if b.excl]
            reads = [b for b in reads if not b.excl]
        for b in reads:
            if b.w is not None:
                d.append(b.w)
        for b in writes:
            if b.w is not None:
                d.append(b.w)
            d.extend(b.r.values())
        return d

    def _mark(self, tok, reads, writes):
        if any(b.excl for b in reads):
            writes = list(writes) + [b for b in reads if b.excl]
            reads = [b for b in reads if not b.excl]
        for b in reads:
            b.r[id(tok[0])] = tok
        for b in writes:
            b.w = tok
            b.r = {}

    def op(self, eng, fn, reads=(), writes=(), deps=()):
        d = self._deps(reads, writes, deps)
        if self.bar[eng]:
            d.extend(self.bar[eng])
            self.bar[eng] = []
        if eng == "tensor":
            d = [t for t in d if t[2] != "tensor"]
        if self.grp[eng] is not None:
            tok = self.grp[eng]
            d = [t for t in d if not (t[0] is tok[0] and t[1] == tok[1])]
            self.q[eng].append(["op", fn, d, False])
        else:
            self.cnt[eng] += 1
            tok = (self.sem[eng], self.cnt[eng], eng)
            self.q[eng].append(["op", fn, d, True])
        self._mark(tok, reads, writes)
        return tok

    @contextmanager
    def group(self, eng):
        tok = (self.sem[eng], self.cnt[eng] + 1, eng)
        self.grp[eng] = tok
        n0 = len(self.q[eng])
        yield tok
        self.grp[eng] = None
        assert len(self.q[eng]) > n0
        self.q[eng][-1][3] = True
        self.cnt[eng] += 1

    def _raw_dsem(self):
        s = [self.es.enter_context(self.nc.semaphore("d%d" % len(self.dsems))), 0]
        self.dsems.append(s)
        return s

    def new_dsem(self):
        return {}

    def dma(self, eng, out, in_, sem, reads=(), writes=(), deps=()):
        d = self._deps(reads, writes, deps)
        if self.bar[eng]:
            d.extend(self.bar[eng])
            self.bar[eng] = []
        if eng not in sem:
            sem[eng] = self._raw_dsem()
        sem = sem[eng]
        sem[1] += 16
        tok = (sem[0], sem[1], "dma")
        self.q[eng].append(["dma", (out, in_, sem[0]), d, True])
        self._mark(tok, reads, writes)
        return tok

    def barrier(self):
        toks = [(self.sem[e], self.cnt[e], "bar") for e in ENGS if self.cnt[e] > 0]
        toks += [(s[0], s[1], "dma") for s in self.dsems if s[1] > 0 and not (len(s) > 2 and s[2])]
        for e in ENGS:
            self.bar[e] = list(toks)

    def flush(self, final=()):
        with self.nc.Block() as block:
            def mk(ename):
                def body(eng):
                    waited = self.waited[ename]
                    for kind, payload, deps, sig in self.q[ename]:
                        for (s, v, src) in deps:
                            key = id(s)
                            if waited.get(key, 0) >= v:
                                continue
                            eng.wait_ge(s, v)
                            waited[key] = v
                        if kind == "op":
                            ins = payload(eng)
                            if sig:
                                ins.then_inc(self.sem[ename], 1)
                        else:
                            out, in_, s = payload
                            eng.dma_start(out=out, in_=in_).then_inc(s, 16)
                    if ename == "sync":
                        for (s, v, src) in final:
                            eng.wait_ge(s, v)
                    self.q[ename] = []
                return body
            block.tensor(mk("tensor"))
            block.vector(mk("vector"))
            block.scalar(mk("scalar"))
            block.gpsimd(mk("gpsimd"))
            block.sync(mk("sync"))


def pipeline(n, stages):
    S = len(stages)
    ctx = [dict() for _ in range(n)]
    for t in range(n + S - 1):
        for s_ in reversed(range(S)):
            i = t - s_
            if 0 <= i < n:
                stages[s_](i, ctx[i])


def pipeline_gen(n, stages):
    S = len(stages)
    ctx = [dict() for _ in range(n)]
    for t in range(n + S - 1):
        for s_ in reversed(range(S)):
            i = t - s_
            if 0 <= i < n:
                stages[s_](i, ctx[i])
        yield t


class Ring:
    def __init__(self, P, tensors):
        self.t = tensors
        self.b = [Buf() for _ in tensors]
        self.s = [P.new_dsem() for _ in tensors]
        self.i = -1

    def next(self):
        self.i = (self.i + 1) % len(self.t)
        return self.t[self.i], self.b[self.i], self.s[self.i]


def build(debug=False, stop=None, ngla=4, ndil=8):
    nc = bass.Bass("TRN2", target_bir_lowering=False)
    din = lambda name, shape, dt=F32: nc.dram_tensor(name, list(shape), dt, kind="ExternalInput").ap()
    okind = "ExternalOutput" if debug else "Internal"
    dscr = lambda name, shape, dt: nc.dram_tensor(name, list(shape), dt, kind=okind).ap()
    xT_nat = din("xT_nat", [D, NL])
    xT_perm = din("xT_perm", [D, NL])
    memT = din("memT", [D, 256])
    pos_in = din("pos_perm", [1, NL], I32)
    flag_in = din("flag", [128, 1])
    masks_in = din("masks", [128, 6 * 128])
    uneg_in = din("uneg", [128, 128])
    rotc_in = din("rotc", [32, 2])
    wg_in = din("wg", [4, D, 768])
    wglr_in = din("wglr", [D, 16])
    w2aug_in = din("w2aug", [17, 512])
    glag_in = din("glag", [128, 8])
    wd_in = din("wd", [8, D, 320])
    wvd_in = din("wvd", [D, 1024])
    wout_in = din("w_out", [D, D])
    lnp_in = din("lnp", [128, 6 * 16])
    wq_in = din("ca_wq", [D, D])
    wkv_in = din("ca_wkv", [D, 2 * D])
    wo_in = din("ca_wo", [D, D])
    fwin_in = din("ffn_w_in", [D, 2 * DFF])
    cw_in = din("convw", [128, 86 * 3])
    cb_in = din("convb", [128, 86])
    fwout_in = din("ffn_w_out", [DFF, D])
    outT = nc.dram_tensor("outT", [D, 2048], F32, kind="ExternalOutput").ap()

    xb_nat = dscr("xb_nat", [D, NL], BF16)
    xb_perm = dscr("xb_perm", [D, NL], BF16)
    vd_scr = dscr("vd_scr", [NL, 1024], BF16)
    vd4_scr = dscr("vd4_scr", [NL, 1024], BF16)
    vd1_scr = dscr("vd1_scr", [NL, 1024], BF16)
    mixT = dscr("mixT", [D, TQ], BF16)
    x1T = dscr("x1T", [D, TQ], F32)
    ocT = dscr("ocT", [D, TQ], BF16)
    x2T = dscr("x2T", [D, TQ], F32)
    hT = dscr("hT", [DFF, 2048], BF16)
    w2b = dscr("w2b", [DFF, D], BF16)

    with ExitStack() as ges:
        P = Prog(nc, ges)
        gsb = lambda name, shape, dt: ges.enter_context(nc.sbuf_tensor(name, list(shape), dt))
        banks = [ges.enter_context(nc.psum_tensor("pb%d" % i, [128, 512], F32)) for i in range(7)]
        bankb = [Buf(True) for _ in range(7)]
        ptb = ges.enter_context(nc.psum_tensor("ptb", [128, 1024], BF16))
        ptb_b = Buf(True)
        bi = [0]

        def finish():
            final = [(s_[0], s_[1], "dma") for s_ in P.dsems if s_[1] > 0]
            final += [(P.sem[e_], P.cnt[e_], "bar") for e_ in ENGS if P.cnt[e_] > 0 and e_ != "sync"]
            P.flush(final)

        busy = set()

        def nb(reserve=False):
            for _ in range(8):
                bi[0] = (bi[0] + 1) % 7
                if bi[0] not in busy:
                    break
            else:
                raise RuntimeError("no free PSUM bank")
            if reserve:
                busy.add(bi[0])
            return banks[bi[0]], bankb[bi[0]]

        def rel(bb):
            busy.discard(bankb.index(bb))

        def mm(out, lhsT, rhs, start, stop, reads, writes):
            return P.op("tensor", lambda e: e.matmul(out, lhsT=lhsT, rhs=rhs, start=start, stop=stop), reads, writes)

        def act(out, in_, func, reads, writes, bias=None, scale=None):
            kw = {}
            if bias is not None:
                kw["bias"] = bias
            if scale is not None:
                kw["scale"] = scale
            return P.op("scalar", lambda e: e.activation(out=out, in_=in_, func=func, **kw), reads, writes)

        def tt(eng, out, in0, in1, op, reads, writes):
            return P.op(eng, lambda e: e.tensor_tensor(out=out, in0=in0, in1=in1, op=op), reads, writes)

        def ts(eng, out, in0, s1, s2, op0, op1, reads, writes):
            if op1 is None:
                return P.op(eng, lambda e: e.tensor_scalar(out=out, in0=in0, scalar1=s1, scalar2=None, op0=op0), reads, writes)
            return P.op(eng, lambda e: e.tensor_scalar(out=out, in0=in0, scalar1=s1, scalar2=s2, op0=op0, op1=op1), reads, writes)

        def stt(eng, out, in0, scalar, in1, op0, op1, reads, writes):
            return P.op(eng, lambda e: e.scalar_tensor_tensor(out=out, in0=in0, scalar=scalar, in1=in1, op0=op0, op1=op1), reads, writes)

        def cp(eng, out, in_, reads, writes):
            if eng == "scalar":
                return P.op(eng, lambda e: e.activation(out=out, in_=in_, func=AF.Copy), reads, writes)
            return P.op(eng, lambda e: e.tensor_copy(out=out, in_=in_), reads, writes)

        def recip(eng, out, in_, reads, writes):
            return P.op(eng, lambda e: e.reciprocal(out=out, in_=in_), reads, writes)

        def mset(eng, ap, val, writes):
            return P.op(eng, lambda e: e.memset(ap, val), (), writes)

        ident = gsb("ident", [128, 128], BF16)
        ones_bf = gsb("ones_bf", [128, 128], BF16)
        ones256 = gsb("ones256", [128, 128], F32)
        ones2048 = gsb("ones2048", [128, 128], F32)
        masks = gsb("masks_sb", [128, 6, 128], BF16)
        masksc = gsb("masksc_sb", [128, 6, 128], BF16)
        uneg = gsb("uneg_sb", [128, 128], F32)
        flag = gsb("flag_sb", [128, 1], F32)
        lnp = gsb("lnp_sb", [128, 6, 16], F32)
        glag = gsb("glag_sb", [128, 8], F32)
        cw = gsb("cw_sb", [128, 86, 3], F32)
        cb = gsb("cb_sb", [128, 86], F32)
        mtmp = gsb("mtmp", [128, 6, 128], F32)
        CB = Buf()
        csem = P.new_dsem()
        P.dma("sync", mtmp[:, :, :], masks_in.rearrange("p (a b) -> p a b", a=6), csem, (), [CB])
        P.dma("sync", uneg[:, :], uneg_in, csem, (), [CB])
        P.dma("sync", flag[:, :], flag_in, csem, (), [CB])
        P.dma("sync", lnp[:, :, :], lnp_in.rearrange("p (a b) -> p a b", a=6), csem, (), [CB])
        P.dma("sync", glag[:, :], glag_in, csem, (), [CB])
        P.dma("sync", cw[:, :, :], cw_in.rearrange("p (a b) -> p a b", b=3), csem, (), [CB])
        P.dma("sync", cb[:, :], cb_in, csem, (), [CB])
        mset("gpsimd", ident[:, :], 0.0, [CB])
        P.op("gpsimd", lambda e: e.affine_select(out=ident[:, :], in_=ident[:, :], pattern=[[-1, 128]],
                                                 compare_op=ALU.not_equal, fill=1.0, base=0, channel_multiplier=1), [CB], [CB])
        mset("gpsimd", ones_bf[:, :], 1.0, [CB])
        mset("gpsimd", ones256[:, :], 1.0 / 256.0, [CB])
        mset("gpsimd", ones2048[:, :], 1.0 / 2048.0, [CB])
        cp("vector", masks[:, :, :], mtmp[:, :, :], [CB], [CB])
        ts("vector", masksc[:, :, :], mtmp[:, :, :], flag[:, 0:1], None, ALU.mult, None, [CB], [CB])

        XN = [Buf() for _ in range(8)]
        XP = [Buf() for _ in range(8)]
        P.flush()
        if stop == "pre":
            finish()
            return nc

        MIXB = Buf()
        VDB = Buf()

        with ExitStack() as es:
            sb = lambda name, shape, dt: es.enter_context(nc.sbuf_tensor(name, list(shape), dt))
            xring = Ring(P, [sb("xblk%d" % i, [128, KC, 512], BF16) for i in range(2)])

            def load_x(src, srcbufs, blk):
                t, b, s = xring.next()
                P.dma("sync", t[:, :, :], src[:, blk * 512:(blk + 1) * 512].rearrange("(k p) n -> p k n", p=128), s, [srcbufs[blk]], [b])
                return t, b

            glrT = sb("glrT", [32, NL], F32)
            GLR = Buf()
            wglr = sb("wglr_sb", [128, KC, 16], BF16)
            w2aug = sb("w2aug_sb", [17, 512], F32)
            WS = Buf()
            wsem = P.new_dsem()
            P.dma("gpsimd", wglr[:, :, :], wglr_in.rearrange("(k p) n -> p k n", p=128), wsem, (), [WS])
            P.dma("sync", w2aug[:, :], w2aug_in, wsem, (), [WS])
            def load_first(ring, src32, dst16, dbufs, blk):
                t, b, s_ = ring.next()
                cols = slice(blk * 512, (blk + 1) * 512)
                P.dma("gpsimd", t[:, :, :], src32[:, cols].rearrange("(k p) n -> p k n", p=128), s_, (), [b])
                P.dma("sync", dst16[:, cols].rearrange("(k p) n -> p k n", p=128), t[:, :, :], s_, [b], [dbufs[blk]])
                return t, b
            mset("vector", glrT[:, :], 1.0, [GLR])
            for blk in range(8):
                xb, xbb = load_first(xring, xT_nat, xb_nat, XN, blk)
                bk, bkb = nb()
                with P.group("tensor"):
                    for kc in range(KC):
                        mm(bk[0:16, :], wglr[:, kc, :], xb[:, kc, :], kc == 0, kc == KC - 1, [xbb, WS], [bkb])
                cp("scalar", glrT[0:16, blk * 512:(blk + 1) * 512], bk[0:16, :], [bkb], [GLR])

            if stop == "glr":
                finish()
                return nc
            wgring = Ring(P, [sb("wg%d" % i, [128, KC, 768], BF16) for i in range(1)])
            qT = sb("g_qT", [128, QN], BF16)
            kT = sb("g_kT", [128, NL], BF16)
            rT = sb("g_rT", [128, 2, QN], BF16)
            vsb = sb("g_v", [128, 32, 256], BF16)
            GB = [(sb("g_enb%d" % i, [128, NL], BF16), sb("g_kef%d" % i, [128, NL], BF16), sb("g_ebq%d" % i, [128, QN], BF16),
                   sb("g_decay%d" % i, [128, 32], F32), sb("g_blast%d" % i, [128, 32], F32)) for i in range(2)]
            HGB = [(Buf(), Buf()) for _ in range(2)]
            Sst = sb("g_S", [128, 256], F32)
            Sbf = [sb("g_Sbf%d" % i, [128, 256], BF16) for i in range(3)]
            og = sb("g_og", [128, 2, TQ], BF16)
            tA = [sb("g_tA%d" % i, [128, 128], F32) for i in range(4)]
            spb = [sb("g_sp%d" % i, [128, 128], F32) for i in range(4)]
            kin = [sb("g_kin%d" % i, [128, 128], BF16) for i in range(4)]
            kend = [sb("g_kend%d" % i, [128, 128], BF16) for i in range(4)]
            kendT = [sb("g_kendT%d" % i, [128, 128], BF16) for i in range(4)]
            qin = [sb("g_qin%d" % i, [128, 128], BF16) for i in range(4)]
            Am = [sb("g_Am%d" % i, [128, 128], BF16) for i in range(4)]
            osb = [sb("g_osb%d" % i, [128, 2, 128], F32) for i in range(4)]
            osq = [sb("g_osq%d" % i, [128, 2, 128], F32) for i in range(4)]
            mst = [sb("g_mst%d" % i, [128, 256], F32) for i in range(4)]
            t1 = [sb("g_t1%d" % i, [128, 128], F32) for i in range(4)]
            on2 = [[sb("g_on2_%d_%d" % (i, e_), [128, 128], F32) for e_ in range(2)] for i in range(4)]
            TB2 = {"on": [[Buf(), Buf()] for _ in range(4)]}
            HB = {k: Buf() for k in ["q", "k", "r", "v", "gate", "S", "og", "bl"]}
            Sbfb = [Buf(), Buf(), Buf()]
            TB = {k: [Buf() for _ in range(4)] for k in ["tA", "sp", "kin", "kend", "kendT", "qin", "Am", "osb", "osq", "mst", "t1", "t2", "sr", "on"]}
            ogsem = P.new_dsem()
            LNS = math.log(128.0 ** -0.5)

            wg, wgb, wgs = wgring.next()
            P.dma("gpsimd", wg[:, :, :], wg_in[0].rearrange("(k p) n -> p k n", p=128), wgs, (), [wgb])
            def make_gates(hh):
                enb, kef, ebq, decay, blast = GB[hh % 2]
                HBg, HBbl = HGB[hh % 2]

                def g0(n, c):
                    c["bz"], c["bzb"] = nb(True)
                    mm(c["bz"][:, 0:128], glrT[0:17, n * 128:(n + 1) * 128], w2aug[0:17, hh * 128:(hh + 1) * 128], True, True, [GLR, WS], [c["bzb"]])

                def g1(n, c):
                    i4 = n % 4
                    act(tA[i4][:, :], c["bz"][:, 0:128], AF.Exp, [c["bzb"]], [TB["tA"][i4]], scale=-1.0)
                    act(spb[i4][:, :], tA[i4][:, :], AF.Ln, [TB["tA"][i4]], [TB["sp"][i4]], bias=1.0)
                    rel(c["bzb"])

                def g2(n, c):
                    i4 = n % 4
                    c["bt"], c["btb"] = nb(True)
                    mm(c["bt"][:, 0:128], spb[i4][:, :], uneg[:, :], True, True, [TB["sp"][i4], CB], [c["btb"]])

                def g3(n, c):
                    bt, btb = c["bt"], c["btb"]
                    cp("vector", blast[:, n:n + 1], bt[:, 127:128], [btb], [HBbl])
                    act(enb[:, n * 128:(n + 1) * 128], bt[:, 0:128], AF.Exp, [btb], [HBg], scale=-1.0)
                    act(kef[:, n * 128:(n + 1) * 128], bt[:, 0:128], AF.Exp, [btb, HBbl], [HBg], scale=-1.0, bias=blast[:, n:n + 1])
                    if n >= 15:
                        act(ebq[:, (n - 15) * 128:(n - 14) * 128], bt[:, 0:128], AF.Exp, [btb], [HBg], bias=LNS)
                    act(decay[:, n:n + 1], blast[:, n:n + 1], AF.Exp, [HBbl], [HBg])
                    rel(btb)
                return pipeline_gen(32, [g0, g1, g2, g3])

            gsteps = make_gates(0)
            for h in range(ngla):
                enb, kef, ebq, decay, blast = GB[h % 2]
                HB["gate"], HB["bl"] = HGB[h % 2]
                for blk in range(8):
                    if h == 0:
                        for _ in range(5):
                            next(gsteps, None)
                    if False and ngla >= 3 and h in (1, 2) and blk % 2 == 0:
                        pb_ = (h - 1) * 4 + blk // 2
                        P.dma("gpsimd", xb_perm[:, pb_ * 512:(pb_ + 1) * 512], xT_perm[:, pb_ * 512:(pb_ + 1) * 512], P.new_dsem(), (), [XP[pb_]])
                    xb, xbb = load_x(xb_nat, XN, blk)
                    bk, bkb = nb()
                    with P.group("tensor"):
                        for kc in range(KC):
                            mm(bk[:, :], wg[:, kc, 128:256], xb[:, kc, :], kc == 0, kc == KC - 1, [xbb, wgb], [bkb])
                    cp("scalar", kT[:, blk * 512:(blk + 1) * 512], bk[:, :], [bkb], [HB["k"]])
                    if blk >= 3:
                        x0, nn_, q0 = (384, 128, 0) if blk == 3 else (0, 512, 128 + (blk - 4) * 512)
                        for (c0, dst, hb) in [(0, qT[:, q0:q0 + nn_], "q"), (256, rT[:, 0, q0:q0 + nn_], "r"), (384, rT[:, 1, q0:q0 + nn_], "r")]:
                            bq, bqb = nb()
                            with P.group("tensor"):
                                for kc in range(KC):
                                    mm(bq[:, 0:nn_], wg[:, kc, c0:c0 + 128], xb[:, kc, x0:x0 + nn_], kc == 0, kc == KC - 1, [xbb, wgb], [bqb])
                            cp("scalar" if hb == "q" else "vector", dst, bq[:, 0:nn_], [bqb], [HB[hb]])
                    for sub in range(4):
                        bv, bvb = nb()
                        with P.group("tensor"):
                            for kc in range(KC):
                                mm(bv[:, 0:256], xb[:, kc, sub * 128:(sub + 1) * 128], wg[:, kc, 512:768], kc == 0, kc == KC - 1, [xbb, wgb], [bvb])
                        cp("vector", vsb[:, blk * 4 + sub, :], bv[:, 0:256], [bvb], [HB["v"]])
                for _ in gsteps:
                    pass
                if h + 1 < ngla:
                    P.dma("gpsimd", wg[:, :, :], wg_in[h + 1].rearrange("(k p) n -> p k n", p=128), wgs, (), [wgb])
                act(rT[:, :, :], rT[:, :, :], AF.Silu, [], [HB["r"]])
                if stop == "proj":
                    finish()
                    return nc
                mset("vector", Sst[:, :], 0.0, [HB["S"]])
                mset("gpsimd", Sbf[2][:, :], 0.0, [Sbfb[2]])

                def c0_(n, c):
                    i4 = n % 4
                    tok = slice(n * 128, (n + 1) * 128)
                    tt("vector", kin[i4][:, :], kT[:, tok], enb[:, tok], ALU.mult, [HB["k"], HB["gate"]], [TB["kin"][i4]])
                    tt("gpsimd", kend[i4][:, :], kT[:, tok], kef[:, tok], ALU.mult, [HB["k"], HB["gate"]], [TB["kend"][i4]])
                    if n >= 15:
                        q0 = (n - 15) * 128
                        tt("vector", qin[i4][:, :], qT[:, q0:q0 + 128], ebq[:, q0:q0 + 128], ALU.mult, [HB["q"], HB["gate"]], [TB["qin"][i4]])

                def c1_(n, c):
                    i4 = n % 4
                    P.op("tensor", lambda e, i4=i4: e.transpose(out=ptb[:, i4 * 128:(i4 + 1) * 128], in_=kend[i4][:, :], identity=ident[:, :]),
                         [TB["kend"][i4], CB], [ptb_b])
                    if n >= 15:
                        c["ba"], c["bab"] = nb(True)
                        mm(c["ba"][:, 0:128], kin[i4][:, :], qin[i4][:, :], True, True, [TB["kin"][i4], TB["qin"][i4]], [c["bab"]])

                def c2_(n, c):
                    i4 = n % 4
                    cp("scalar", kendT[i4][:, :], ptb[:, i4 * 128:(i4 + 1) * 128], [ptb_b], [TB["kendT"][i4]])
                    if n >= 15:
                        q0 = (n - 15) * 128
                        tt("vector", Am[i4][:, :], c["ba"][:, 0:128], masks[:, 1, :], ALU.mult, [c["bab"], CB], [TB["Am"][i4]])
                        rel(c["bab"])

                def c3_(n, c):
                    i4 = n % 4
                    Sprev, Sprevb = Sbf[(n + 2) % 3], Sbfb[(n + 2) % 3]
                    if n >= 15:
                        c["bo"], c["bob"] = nb(True)
                        for e_ in range(2):
                            with P.group("tensor"):
                                mm(c["bo"][:, e_ * 128:(e_ + 1) * 128], vsb[:, n, e_ * 128:(e_ + 1) * 128], Am[i4][:, :], True, False, [HB["v"], TB["Am"][i4]], [c["bob"]])
                                mm(c["bo"][:, e_ * 128:(e_ + 1) * 128], Sprev[:, e_ * 128:(e_ + 1) * 128], qin[i4][:, :], False, True, [Sprevb, TB["qin"][i4]], [c["bob"]])
                    c["bs"], c["bsb"] = nb(True)
                    mm(c["bs"][:, 0:256], kendT[i4][:, :], vsb[:, n, :], True, True, [TB["kendT"][i4], HB["v"]], [c["bsb"]])

                def c4_(n, c):
                    i4 = n % 4
                    stt("vector", Sst[:, :], Sst[:, :], decay[:, n:n + 1], c["bs"][:, 0:256], ALU.mult, ALU.add, [c["bsb"], HB["gate"]], [HB["S"]])
                    cp("gpsimd", Sbf[n % 3][:, :], Sst[:, :], [HB["S"]], [Sbfb[n % 3]])
                    rel(c["bsb"])
                    if n >= 15:
                        cp("scalar", osb[i4][:, :, :], c["bo"][:, 0:256].rearrange("p (a b) -> p a b", a=2), [c["bob"]], [TB["osb"][i4]])
                        rel(c["bob"])

                def c5_(n, c):
                    i4 = n % 4
                    if n < 15:
                        return
                    tt("gpsimd", osq[i4][:, :, :], osb[i4][:, :, :], osb[i4][:, :, :], ALU.mult, [TB["osb"][i4]], [TB["osq"][i4]])
                    c["bm"], c["bmb"] = nb(True)
                    bm, bmb = c["bm"], c["bmb"]
                    with P.group("tensor"):
                        mm(bm[:, 0:128], ones256[:, :], osb[i4][:, 0, :], True, False, [CB, TB["osb"][i4]], [bmb])
                        mm(bm[:, 0:128], ones256[:, :], osb[i4][:, 1, :], False, True, [CB, TB["osb"][i4]], [bmb])
                    with P.group("tensor"):
                        mm(bm[:, 128:256], ones256[:, :], osq[i4][:, 0, :], True, False, [CB, TB["osq"][i4]], [bmb])
                        mm(bm[:, 128:256], ones256[:, :], osq[i4][:, 1, :], False, True, [CB, TB["osq"][i4]], [bmb])

                def c6_(n, c):
                    i4 = n % 4
                    if n < 15:
                        return
                    q0 = (n - 15) * 128
                    cp("scalar", mst[i4][:, :], c["bm"][:, 0:256], [c["bmb"]], [TB["mst"][i4]])
                    rel(c["bmb"])
                    tt("vector", t1[i4][:, :], mst[i4][:, 0:128], mst[i4][:, 0:128], ALU.mult, [TB["mst"][i4]], [TB["t1"][i4]])
                    tt("vector", t1[i4][:, :], mst[i4][:, 128:256], t1[i4][:, :], ALU.subtract, [TB["mst"][i4]], [TB["t1"][i4]])
                    act(t1[i4][:, :], t1[i4][:, :], AF.Ln, [], [TB["t1"][i4]], bias=LN_EPS)
                    act(t1[i4][:, :], t1[i4][:, :], AF.Exp, [], [TB["t1"][i4]], scale=-0.5)

                def c7_(n, c):
                    i4 = n % 4
                    if n < 15:
                        return
                    q0 = (n - 15) * 128
                    for e_ in range(2):
                        onb, ONB = on2[i4][e_], TB2["on"][i4][e_]
                        tt("vector", onb[:, :], osb[i4][:, e_, :], mst[i4][:, 0:128], ALU.subtract, [TB["osb"][i4], TB["mst"][i4]], [ONB])
                        tt("gpsimd", onb[:, :], onb[:, :], t1[i4][:, :], ALU.mult, [TB["t1"][i4]], [ONB])
                        if n == 15:
                            stt("vector", og[:, e_, 0:2], onb[:, 126:128], glag[:, 2 * h + e_:2 * h + e_ + 1], rT[:, e_, q0 + 126:q0 + 128],
                                ALU.mult, ALU.mult, [ONB, HB["r"], CB], [HB["og"]])
                        else:
                            o0 = 2 + (n - 16) * 128
                            stt("vector", og[:, e_, o0:o0 + 128], onb[:, :], glag[:, 2 * h + e_:2 * h + e_ + 1], rT[:, e_, q0:q0 + 128],
                                ALU.mult, ALU.mult, [ONB, HB["r"], CB], [HB["og"]])
                gsteps = make_gates(h + 1) if h + 1 < ngla else iter(())
                for _ in pipeline_gen(32, [c0_, c1_, c2_, c3_, c4_, c5_, c6_, c7_]):
                    next(gsteps, None)
                for _ in gsteps:
                    pass
                for e_ in range(2):
                    P.dma("sync", mixT[(2 * h + e_) * 128:(2 * h + e_ + 1) * 128, :], og[:, e_, :], ogsem, [HB["og"]], [MIXB])
            P.flush()

        P.barrier()
        if stop == "A":
            finish()
            return nc
        with ExitStack() as es:
            sb = lambda name, shape, dt: es.enter_context(nc.sbuf_tensor(name, list(shape), dt))
            xring = Ring(P, [sb("xblkb%d" % i, [128, KC, 512], BF16) for i in range(2)])

            def load_xp(blk):
                t, b, s = xring.next()
                P.dma("sync", t[:, :, :], xb_perm[:, blk * 512:(blk + 1) * 512].rearrange("(k p) n -> p k n", p=128), s, [XP[blk]], [b])
                return t, b

            cosT = sb("cosT", [32, NL], F32)
            sinT = sb("sinT", [32, NL], F32)
            ROT = Buf()
            with ExitStack() as es2:
                sb2 = lambda name, shape, dt: es2.enter_context(nc.sbuf_tensor(name, list(shape), dt))
                posi = sb2("posi", [32, NL], I32)
                ang = sb2("ang", [32, NL], F32)
                tf = sb2("tf", [32, NL], F32)
                rr = sb2("rr", [32, NL], F32)
                mk_ = sb2("mk_", [32, NL], F32)
                rotc = sb2("rotc_sb", [32, 2], F32)
                RB = Buf()
                rsem = P.new_dsem()
                P.dma("sync", posi[:, :], pos_in.partition_broadcast(32), rsem, (), [RB])
                P.dma("sync", rotc[:, :], rotc_in, rsem, (), [RB])
                defer = []

                def DF(fn, *a_, **k_):
                    defer.append(lambda: fn(*a_, **k_))
                DF(cp, "vector", ang[:, :], posi[:, :], [RB], [RB])
                DF(ts, "vector", ang[:, :], ang[:, :], rotc[:, 0:1], None, ALU.mult, None, [RB], [RB])
                DF(ts, "vector", tf[:, :], ang[:, :], 1.0 / (2 * PI), 0.5, ALU.mult, ALU.add, [RB], [RB])
                DF(cp, "vector", posi[:, :], tf[:, :], [RB], [RB])
                DF(cp, "vector", tf[:, :], posi[:, :], [RB], [RB])
                C1 = 6.28125
                C2 = 2 * PI - C1
                DF(stt, "vector", rr[:, :], tf[:, :], -C1, ang[:, :], ALU.mult, ALU.add, [RB], [RB])
                DF(stt, "vector", rr[:, :], tf[:, :], -C2, rr[:, :], ALU.mult, ALU.add, [RB], [RB])

                def wrap_clamp(r):
                    DF(ts, "vector", mk_[:, :], r[:, :], -PI, None, ALU.is_lt, None, [RB], [RB])
                    DF(stt, "vector", r[:, :], mk_[:, :], 2 * PI, r[:, :], ALU.mult, ALU.add, [RB], [RB])
                    DF(ts, "vector", mk_[:, :], r[:, :], PI, None, ALU.is_gt, None, [RB], [RB])
                    DF(stt, "vector", r[:, :], mk_[:, :], -2 * PI, r[:, :], ALU.mult, ALU.add, [RB], [RB])
                    DF(ts, "vector", r[:, :], r[:, :], -3.141592, 3.141592, ALU.max, ALU.min, [RB], [RB])
                wrap_clamp(rr)
                DF(act, sinT[:, :], rr[:, :], AF.Sin, [RB], [ROT], scale=rotc[:, 1:2])
                DF(ts, "vector", rr[:, :], rr[:, :], PI / 2, None, ALU.add, None, [RB], [RB])
                wrap_clamp(rr)
                DF(act, cosT[:, :], rr[:, :], AF.Sin, [RB], [ROT])

                wvd = sb2("wvd_sb", [128, KC, 1024], BF16)
                WV = Buf()
                wvs = P.new_dsem()
                for g in range(2):
                    P.dma("gpsimd", wvd[:, :, g * 512:(g + 1) * 512], wvd_in[:, g * 512:(g + 1) * 512].rearrange("(k p) n -> p k n", p=128), wvs, (), [WV])
                vst = Ring(P, [sb2("vst%d" % i, [128, 1024], BF16) for i in range(2)])
                for blk in range(8):
                    if False:
                        xb, xbb = load_xp(blk)
                    else:
                        t_, b_, s__ = xring.next()
                        cols_ = slice(blk * 512, (blk + 1) * 512)
                        P.dma("gpsimd", t_[:, :, :], xT_perm[:, cols_].rearrange("(k p) n -> p k n", p=128), s__, (), [b_])
                        P.dma("sync", xb_perm[:, cols_].rearrange("(k p) n -> p k n", p=128), t_[:, :, :], s__, [b_], [XP[blk]])
                        xb, xbb = t_, b_
                    for sub in range(4):
                        if defer:
                            defer.pop(0)()
                        vt, vtb, vts = vst.next()
                        for g in range(2):
                            bv, bvb = nb()
                            with P.group("tensor"):
                                for kc in range(KC):
                                    mm(bv[:, :], xb[:, kc, sub * 128:(sub + 1) * 128], wvd[:, kc, g * 512:(g + 1) * 512], kc == 0, kc == KC - 1, [xbb, WV], [bvb])
                            cp("scalar" if g == 0 else "vector", vt[:, g * 512:(g + 1) * 512], bv[:, :], [bvb], [vtb])
                        r0 = (blk * 4 + sub) * 128
                        P.dma("scalar", vd_scr[r0:r0 + 128, :], vt[:, :], vts, [vtb], [VDB])
                        T_ = blk * 4 + sub
                        s_, r16 = T_ // 16, T_ % 16
                        a_, r4 = r16 // 4, r16 % 4
                        d4 = vd4_scr.rearrange("(t i) c -> t i c", i=128)[r4 * 8 + 4 * s_:r4 * 8 + 4 * s_ + 4, 32 * a_:32 * a_ + 32, :]
                        P.dma("scalar", d4, vt[:, :], vts, [vtb], [VDB])
                        d1 = vd1_scr.rearrange("(t i) c -> t i c", i=128)[16 * s_:16 * s_ + 16, 8 * r16:8 * r16 + 8, :]
                        P.dma("scalar", d1, vt[:, :], vts, [vtb], [VDB])
                while defer:
                    defer.pop(0)()
                P.flush()
            P.barrier()
            if stop == "V":
                finish()
                return nc

            wdring = Ring(P, [sb("wd%d" % i, [128, KC, 320], BF16) for i in range(2)])
            dq2 = [sb("d_qq%d" % i, [128, 2560], BF16) for i in range(2)]
            dk2 = [sb("d_kk%d" % i, [128, NL], BF16) for i in range(2)]
            DBQ, DBK = [Buf(), Buf()], [Buf(), Buf()]
            dk4 = sb("d_k4", [128, NL], BF16)
            dk1 = sb("d_k1", [128, NL], BF16)
            dq4 = sb("d_q4", [128, 2048], BF16)
            dq1 = sb("d_q1", [128, 2048], BF16)
            hq1 = sb("d_hq1", [128, 16], BF16)
            V16 = sb("d_v16", [128, 32, 128], BF16)
            V4 = sb("d_v4", [128, 32, 128], BF16)
            V1 = sb("d_v1", [128, 32, 128], BF16)
            acc = sb("d_acc", [128, 2560], F32)
            dacc = sb("d_dacc", [128, 2560], F32)
            odb = sb("d_od", [128, TQ], BF16)
            rt1 = [sb("d_rt1%d" % i, [32, 512], F32) for i in range(2)]
            rt2 = [sb("d_rt2%d" % i, [32, 512], F32) for i in range(2)]
            Pm = [sb("d_P%d" % i, [128, 2, 128], BF16) for i in range(4)]
            DB = {k: Buf() for k in ["q", "k", "v", "acc", "od", "k4", "k1", "q4", "q1"]}
            DBV = {16: Buf(), 4: Buf(), 1: Buf()}
            RTB = [[Buf(), Buf()], [Buf(), Buf()]]
            PmB = [Buf() for _ in range(4)]
            vsem = P.new_dsem()
            odsem = P.new_dsem()
            SC = 128.0 ** -0.5
            vd5 = vd_scr

            def kcols(t, off, kind, a, b_):
                if kind == 16:
                    p0 = 2048 * a + 128 * b_ - off
                    return t[:, p0:p0 + 128]
                if kind == 4:
                    r4, n = a, b_
                    s_, m = n // 4, n % 4
                    base = 2048 * s_ - off
                    return t[:, base:base + 2048].rearrange("p (a r u) -> p a r u", a=4, r=4, u=128)[:, :, r4, 32 * m:32 * m + 32]
                s_, m = a, b_
                base = 2048 * s_ - off
                return t[:, base:base + 2048].rearrange("p (r m u) -> p r m u", r=16, m=16, u=8)[:, :, m, :]

            def head_setup(h):
                wd, wdb, wds = wdring.next()
                P.dma("gpsimd", wd[:, :, :], wd_in[h].rearrange("(k p) n -> p k n", p=128), wds, (), [wdb])
                return wd, wdb

            def proj_gen(h, wd, wdb):
                dq, dk = dq2[h % 2], dk2[h % 2]
                DBq = {"q": DBQ[h % 2], "k": DBK[h % 2]}
                for blk in range(8):
                    xb, xbb = load_xp(blk)
                    cols = slice(blk * 512, (blk + 1) * 512)
                    todo = [(128, 288, dk[:, cols], "k")]
                    if blk >= 3:
                        todo.append((0, 256, dq[:, (blk - 3) * 512:(blk - 2) * 512], "q"))
                    for ti, (c0, cs, dst, hb) in enumerate(todo):
                        bk, bkb = nb(True)
                        with P.group("tensor"):
                            for kc in range(KC):
                                mm(bk[:, :], wd[:, kc, c0:c0 + 128], xb[:, kc, :], kc == 0, kc == KC - 1, [xbb, wdb], [bkb])
                        yield blk
                        bs_, bsb_ = nb(True)
                        with P.group("tensor"):
                            for kc in range(KC):
                                mm(bs_[0:32, :], wd[:, kc, cs:cs + 32], xb[:, kc, :], kc == 0, kc == KC - 1, [xbb, wdb], [bsb_])
                        cp("scalar", dst, bk[:, :], [bkb], [DBq[hb]])
                        tt("vector", rt1[ti][:, :], bk[0:32, :], cosT[:, cols], ALU.mult, [bkb, ROT], [RTB[ti][0]])
                        tt("vector", rt2[ti][:, :], bs_[0:32, :], sinT[:, cols], ALU.mult, [bsb_, ROT], [RTB[ti][1]])
                        P.op("vector", lambda e, dst=dst, ti=ti: e.tensor_tensor(out=dst[0:32], in0=rt1[ti][:, :], in1=rt2[ti][:, :], op=ALU.add),
                             [RTB[ti][0], RTB[ti][1]], [DBq[hb]])
                        rel(bkb)
                        rel(bsb_)
                        yield blk

            vdone = set()
            vsems = {16: P.new_dsem(), 4: P.new_dsem(), 1: P.new_dsem()}

            def vload(h, kind_):
                if (h, kind_) in vdone:
                    return
                vdone.add((h, kind_))
                hc = slice(h * 128, (h + 1) * 128)
                dst, src = {16: (V16, vd5), 4: (V4, vd4_scr), 1: (V1, vd1_scr)}[kind_]
                P.dma("gpsimd", dst[:, :, :], src[:, hc].rearrange("(t p) c -> p t c", p=128), vsems[kind_], [VDB], [DBV[kind_]])

            def post_proj(h):
                dq, dk = dq2[h % 2], dk2[h % 2]
                DBq = {"q": DBQ[h % 2], "k": DBK[h % 2]}
                for kind_ in (16, 4, 1):
                    vload(h, kind_)
                for s_ in range(2):
                    srck = dk[:, 2048 * s_:2048 * s_ + 2048]
                    for r4 in range(4):
                        P.op("vector" if r4 % 2 == 0 else "gpsimd", lambda e, s_=s_, r4=r4, srck=srck: e.tensor_copy(
                            out=dk4[:, (r4 * 8 + s_ * 4) * 128:(r4 * 8 + s_ * 4 + 4) * 128].rearrange("p (m a u) -> p m a u", m=4, a=4, u=32),
                            in_=srck.rearrange("p (a r m u) -> p r m a u", a=4, r=4, m=4, u=32)[:, r4]), [DBq["k"]], [DB["k4"]])
                    P.op("vector", lambda e, s_=s_, srck=srck: e.tensor_copy(
                        out=dk1[:, 2048 * s_:2048 * s_ + 2048].rearrange("p (m r u) -> p m r u", m=16, r=16, u=8),
                        in_=srck.rearrange("p (r m u) -> p m r u", r=16, m=16, u=8)), [DBq["k"]], [DB["k1"]])
                srcq = dq[:, 512:2560]
                for r4 in range(4):
                    P.op("scalar", lambda e, r4=r4: e.activation(
                        out=dq4[:, r4 * 512:(r4 + 1) * 512].rearrange("p (m a u) -> p m a u", m=4, a=4, u=32),
                        in_=srcq.rearrange("p (a r m u) -> p r m a u", a=4, r=4, m=4, u=32)[:, r4], func=AF.Copy), [DBq["q"]], [DB["q4"]])
                P.op("scalar", lambda e: e.activation(
                    out=dq1[:, :].rearrange("p (m r u) -> p m r u", m=16, r=16, u=8),
                    in_=srcq.rearrange("p (r m u) -> p m r u", r=16, m=16, u=8), func=AF.Copy), [DBq["q"]], [DB["q1"]])
                P.op("scalar", lambda e: e.activation(
                    out=hq1[:, :].rearrange("p (r u) -> p r u", r=2),
                    in_=dq[:, 256:512].rearrange("p (r u) -> p r u", r=2)[:, :, 120:128], func=AF.Copy), [DBq["q"]], [DB["q1"]])


            def att_gen(h):
                dq, dk = dq2[h % 2], dk2[h % 2]
                DBq = {"q": DBQ[h % 2], "k": DBK[h % 2]}

                def kap(kb):
                    kind, a_, b_ = kb
                    if kind == 16:
                        p0 = 2048 * a_ + 128 * b_
                        return dk[:, p0:p0 + 128], DBq["k"]
                    if kind == 4:
                        p0 = (a_ * 8 + b_) * 128
                        return dk4[:, p0:p0 + 128], DB["k4"]
                    p0 = (16 * a_ + b_) * 128
                    return dk1[:, p0:p0 + 128], DB["k1"]
                mset("gpsimd", acc[:, :], 0.0, [DB["acc"]])
                mset("gpsimd", dacc[:, :], 0.0, [DB["acc"]])
                blocks = []
                for r in range(16):
                    blocks.append((16, (kcols(dq, 1536, 16, 1, r), DBq["q"]), kcols(acc, 1536, 16, 1, r), kcols(dacc, 1536, 16, 1, r), 128, None,
                                   [((16, 0, r), V16[:, r, :], masksc[:, 0, :]), ((16, 1, r), V16[:, 16 + r, :], masks[:, 1, :])]))
                for r in (14, 15):
                    blocks.append((16, (kcols(dq, 1536, 16, 0, r), DBq["q"]), kcols(acc, 1536, 16, 0, r), kcols(dacc, 1536, 16, 0, r), 128, None,
                                   [((16, 0, r), V16[:, r, :], masksc[:, 1, :])]))
                for r4 in range(4):
                    for n in range(4, 8):
                        pm = masksc[:, 2, :] if n == 4 else masks[:, 2, :]
                        blocks.append((4, (dq4[:, (r4 * 4 + n - 4) * 128:(r4 * 4 + n - 3) * 128], DB["q4"]), kcols(acc, 1536, 4, r4, n), kcols(dacc, 1536, 4, r4, n), 128, [4, 32],
                                       [((4, r4, n - 1), V4[:, r4 * 8 + n - 1, :], pm), ((4, r4, n), V4[:, r4 * 8 + n, :], masks[:, 3, :])]))
                for r4 in (2, 3):
                    p0 = (12 + r4) * 128 + 96 - 1536
                    blocks.append((4, (dq[:, p0:p0 + 32], DBq["q"]), acc[:, p0:p0 + 32], dacc[:, p0:p0 + 32], 32, None,
                                   [((4, r4, 2), V4[:, r4 * 8 + 2, :], masksc[:, 2, 96:128]), ((4, r4, 3), V4[:, r4 * 8 + 3, :], masksc[:, 3, 96:128])]))
                for m in range(16):
                    pk = (1, 0, 15) if m == 0 else (1, 1, m - 1)
                    pm = masksc[:, 4, :] if m == 0 else masks[:, 4, :]
                    blocks.append((1, (dq1[:, m * 128:(m + 1) * 128], DB["q1"]), kcols(acc, 1536, 1, 1, m), kcols(dacc, 1536, 1, 1, m), 128, [16, 8],
                                   [(pk, V1[:, 16 * pk[1] + pk[2], :], pm), ((1, 1, m), V1[:, 16 + m, :], masks[:, 5, :])]))
                hq = lambda t: t[:, 14 * 128 - 1536:16 * 128 - 1536].rearrange("p (r u) -> p r u", r=2)[:, :, 120:128]
                blocks.append((1, (hq1[:, :], DB["q1"]), hq(acc), hq(dacc), 16, [2, 8],
                               [((1, 0, 14), V1[:, 14, :], masksc[:, 4, 112:128]), ((1, 0, 15), V1[:, 15, :], masksc[:, 5, 112:128])]))
                def a0(i, c):
                    kind, (qap, qbuf), accap, daccap, nq, qshape, keys = blocks[i]
                    c["bsc"], c["bscb"] = nb(True)
                    for ki, (kb, vt, mk) in enumerate(keys):
                        ka, kbuf = kap(kb)
                        mm(c["bsc"][:, ki * 128:ki * 128 + nq], ka, qap, True, True, [kbuf, qbuf], [c["bscb"]])

                def a1(i, c):
                    kind, (qap, qbuf), accap, daccap, nq, qshape, keys = blocks[i]
                    pi = i % 4
                    nk = len(keys)
                    act(Pm[pi][:, 0:nk, 0:nq], c["bsc"][:, 0:nk * 128].rearrange("p (a b) -> p a b", a=nk)[:, :, 0:nq], AF.Exp, [c["bscb"]], [PmB[pi]], scale=SC)
                    rel(c["bscb"])
                    for ki, (kb, vt, mk) in enumerate(keys):
                        tt("gpsimd", Pm[pi][:, ki, 0:nq], Pm[pi][:, ki, 0:nq], mk, ALU.mult, [CB], [PmB[pi]])

                def a2(i, c):
                    kind, (qap, qbuf), accap, daccap, nq, qshape, keys = blocks[i]
                    pi = i % 4
                    nk = len(keys)
                    c["bo"], c["bob"] = nb(True)
                    with P.group("tensor"):
                        for ki, (kb, vt, mk) in enumerate(keys):
                            mm(c["bo"][:, 0:nq], vt, Pm[pi][:, ki, 0:nq], ki == 0, ki == nk - 1, [DBV[kind], PmB[pi]], [c["bob"]])
                    with P.group("tensor"):
                        for ki, (kb, vt, mk) in enumerate(keys):
                            mm(c["bo"][:, 128:128 + nq], ones_bf[:, :], Pm[pi][:, ki, 0:nq], ki == 0, ki == nk - 1, [CB, PmB[pi]], [c["bob"]])

                def a3(i, c):
                    kind, (qap, qbuf), accap, daccap, nq, qshape, keys = blocks[i]
                    o_in = c["bo"][:, 0:nq]
                    d_in = c["bo"][:, 128:128 + nq]
                    if qshape is not None:
                        o_in = o_in.rearrange("p (a b) -> p a b", a=qshape[0])
                        d_in = d_in.rearrange("p (a b) -> p a b", a=qshape[0])
                    tt("vector", accap, accap, o_in, ALU.add, [c["bob"]], [DB["acc"]])
                    tt("vector", daccap, daccap, d_in, ALU.add, [c["bob"]], [DB["acc"]])
                    rel(c["bob"])
                for t_ in pipeline_gen(len(blocks), [a0, a1, a2, a3]):
                    yield t_
                ts("vector", dacc[:, 256:2560], dacc[:, 256:2560], 1e-30, None, ALU.max, None, [], [DB["acc"]])
                act(dacc[:, 256:2560], dacc[:, 256:2560], AF.Ln, [], [DB["acc"]])
                act(dacc[:, 256:2560], dacc[:, 256:2560], AF.Exp, [], [DB["acc"]], scale=-1.0)
                P.op("vector", lambda e: e.tensor_tensor(out=odb[:, 2:TQ].rearrange("p (u r) -> p r u", r=16),
                                                         in0=acc[:, 512:2560].rearrange("p (r u) -> p r u", r=16),
                                                         in1=dacc[:, 512:2560].rearrange("p (r u) -> p r u", r=16), op=ALU.mult), [DB["acc"]], [DB["od"]])
                tt("vector", odb[:, 0:1], acc[:, 383:384], dacc[:, 383:384], ALU.mult, [DB["acc"]], [DB["od"]])
                tt("vector", odb[:, 1:2], acc[:, 511:512], dacc[:, 511:512], ALU.mult, [DB["acc"]], [DB["od"]])
                P.dma("sync", mixT[(8 + h) * 128:(9 + h) * 128, :], odb[:, :], odsem, [DB["od"]], [MIXB])
                yield -1

            prev_att = None
            wd_next = head_setup(0)
            for h in range(ndil):
                wd, wdb = wd_next
                pg = proj_gen(h, wd, wdb)
                nst = 0
                for _blk in pg:
                    if prev_att is not None:
                        for _ in range(3):
                            if next(prev_att, None) is None:
                                break
                            nst += 1
                            if nst == 23:
                                vload(h, 16)
                            if nst == 41:
                                vload(h, 4)
                if h + 1 < ndil:
                    wd_next = head_setup(h + 1)
                if prev_att is not None:
                    for _ in prev_att:
                        pass
                post_proj(h)
                prev_att = att_gen(h)
            for _ in prev_att:
                pass
            P.flush()
        P.barrier()

        def gemm_ln_phase(tag, w_dram, a_dram, a_cast, resid_dram, lnidx, out_dram, out_col0, chunks, ABUF, RBUF, OBUF):
            with ExitStack() as es:
                sb = lambda name, shape, dt: es.enter_context(nc.sbuf_tensor(tag + name, list(shape), dt))
                W = sb("W", [128, KC, D], BF16)
                WB = [Buf() for _ in range(4)]
                for g in range(4):
                    P.dma("gpsimd", W[:, :, g * 512:(g + 1) * 512], w_dram[:, g * 512:(g + 1) * 512].rearrange("(k p) n -> p k n", p=128), P.new_dsem(), (), [WB[g]])
                aring = Ring(P, [sb("a%d" % i, [128, KC, 512], BF16) for i in range(2)])
                ln_chunks(tag, sb, chunks, KC,
                          lambda dc, kc: W[:, kc, dc * 128:(dc + 1) * 128], WB,
                          a_dram, a_cast, aring, ABUF, resid_dram, RBUF, lnidx, out_dram, out_col0, OBUF)
                P.flush()
            P.barrier()

        def ln_chunks(tag, sb, chunks, nk, wfn, wbufs, a_dram, a_cast, aring, ABUF, resid_dram, RBUF, lnidx, out_dram, out_col0, OBUF, wstream=None):
            ny = 2
            y = [sb("y%d" % i, [128, KC, 512], F32) for i in range(ny)]
            YB = [Buf() for _ in range(ny)]
            s1 = [sb("s1_%d" % i, [128, 512], F32) for i in range(2)]
            s2 = [sb("s2_%d" % i, [128, 512], F32) for i in range(2)]
            SB1, SB2 = [Buf(), Buf()], [Buf(), Buf()]
            ysq = [sb("ysq%d" % i, [128, 512], F32) for i in range(2)]
            YSQ = [Buf(), Buf()]
            rres = Ring(P, [sb("res%d" % i, [128, 512], F32) for i in range(3)])
            mean = sb("mean", [128, 512], F32)
            rstd = sb("rstd", [128, 512], F32)
            MB = Buf()
            tn = [sb("tn%d" % i, [128, 512], F32) for i in range(2)]
            TN = [Buf(), Buf()]
            oring = Ring(P, [sb("o%d" % i, [128, 512], F32) for i in range(3)])
            pend = iter(())
            for ci, (c0, c1) in enumerate(chunks):
                n = c1 - c0
                yi = ci % ny
                yc, ycb, s1c, s2c, S1B, S2B = y[yi], YB[yi], s1[ci % 2], s2[ci % 2], SB1[ci % 2], SB2[ci % 2]
                if ci == 0:
                    nxt = aring.next()
                    P.dma("gpsimd" if a_cast else "sync", nxt[0][:, 0:nk, 0:n], a_dram[:, c0:c1].rearrange("(k p) n -> p k n", p=128), nxt[2], [ABUF], [nxt[1]])
                a, ab, asem = nxt
                if ci + 1 < len(chunks):
                    d0, d1 = chunks[ci + 1]
                    nxt = aring.next()
                    P.dma("gpsimd" if a_cast else "sync", nxt[0][:, 0:nk, 0:d1 - d0], a_dram[:, d0:d1].rearrange("(k p) n -> p k n", p=128), nxt[2], [ABUF], [nxt[1]])

                def epi(dc, bk, bkb):
                    rt, rtb, rsem_ = rres.next()
                    P.dma("sync", rt[:, 0:n], resid_dram(dc, c0, c1), rsem_, [RBUF], [rtb])
                    stt("vector", yc[:, dc, 0:n], rt[:, 0:n], ALPHA, bk[:, 0:n], ALU.mult, ALU.add, [rtb, bkb], [ycb])
                    i2 = dc % 2
                    if dc == 0:
                        cp("vector", s1c[:, 0:n], yc[:, dc, 0:n], [ycb], [S1B])
                        act(s2c[:, 0:n], yc[:, dc, 0:n], AF.Square, [ycb], [S2B])
                    else:
                        tt("vector", s1c[:, 0:n], s1c[:, 0:n], yc[:, dc, 0:n], ALU.add, [ycb], [S1B])
                        act(ysq[i2][:, 0:n], yc[:, dc, 0:n], AF.Square, [ycb], [YSQ[i2]])
                        tt("gpsimd", s2c[:, 0:n], s2c[:, 0:n], ysq[i2][:, 0:n], ALU.add, [YSQ[i2]], [S2B])

                if wstream is None:
                    for dc in range(KC):
                        bk, bkb = nb()
                        with P.group("tensor"):
                            for kc in range(nk):
                                mm(bk[:, 0:n], wfn(dc, kc), a[:, kc, 0:n], kc == 0, kc == nk - 1, [ab, wbufs[dc // 4]], [bkb])
                        epi(dc, bk, bkb)
                        if dc >= 2:
                            next(pend, None)
                else:
                    for qd_ in range(4):
                        acc4 = [nb(True) for _ in range(4)]
                        for j in range(nk):
                            wt, wtb = wstream(j, qd_)
                            with P.group("tensor"):
                                for i_ in range(4):
                                    mm(acc4[i_][0][:, 0:n], wt[:, i_ * 128:(i_ + 1) * 128], a[:, j, 0:n], j == 0, j == nk - 1, [ab, wtb], [acc4[i_][1]])
                        for i_ in range(4):
                            epi(4 * qd_ + i_, acc4[i_][0], acc4[i_][1])
                            rel(acc4[i_][1])
                            if 4 * qd_ + i_ >= 2:
                                next(pend, None)
                def part2(n=n, c0=c0, c1=c1, yc=yc, ycb=ycb, s1c=s1c, s2c=s2c, S1B=S1B, S2B=S2B):
                    yield 0
                    bm, bmb = nb()
                    mm(bm[:, 0:n], ones2048[:, :], s1c[:, 0:n], True, True, [CB, S1B], [bmb])
                    bm2, bm2b = nb()
                    mm(bm2[:, 0:n], ones2048[:, :], s2c[:, 0:n], True, True, [CB, S2B], [bm2b])
                    cp("scalar", mean[:, 0:n], bm[:, 0:n], [bmb], [MB])
                    tt("vector", rstd[:, 0:n], mean[:, 0:n], mean[:, 0:n], ALU.mult, [MB], [MB])
                    tt("vector", rstd[:, 0:n], bm2[:, 0:n], rstd[:, 0:n], ALU.subtract, [bm2b], [MB])
                    act(rstd[:, 0:n], rstd[:, 0:n], AF.Ln, [], [MB], bias=LN_EPS)
                    act(rstd[:, 0:n], rstd[:, 0:n], AF.Exp, [], [MB], scale=-0.5)
                    for dc in range(KC):
                        i2 = dc % 2
                        tt("vector", tn[i2][:, 0:n], yc[:, dc, 0:n], mean[:, 0:n], ALU.subtract, [ycb, MB], [TN[i2]])
                        tt("vector" if dc % 2 == 0 else "gpsimd", tn[i2][:, 0:n], tn[i2][:, 0:n], rstd[:, 0:n], ALU.mult, [MB], [TN[i2]])
                        ot, otb, osem_ = oring.next()
                        act(ot[:, 0:n], tn[i2][:, 0:n], AF.Identity, [TN[i2], CB], [otb], scale=lnp[:, 2 * lnidx, dc:dc + 1], bias=lnp[:, 2 * lnidx + 1, dc:dc + 1])
                        P.dma("scalar", out_dram[dc * 128:(dc + 1) * 128, c0 - out_col0:c1 - out_col0], ot[:, 0:n], osem_, [otb], [OBUF])
                        yield dc + 1
                for _ in pend:
                    pass
                pend = part2()
            for _ in pend:
                pass

        if stop == "B":
            finish()
            return nc
        X1B, OCB, X2B, HTB, OUTB = Buf(), Buf(), Buf(), Buf(), Buf()
        W2B = Buf()
        w2sem = P.new_dsem()
        NOB = Buf()
        xres = lambda dc, c0, c1: xT_nat[dc * 128:(dc + 1) * 128, 2046 + c0:2046 + c1]
        gemm_ln_phase("C", wout_in, mixT, False, xres, 0, x1T, 0, TCH, MIXB, NOB, X1B)
        if stop == "C":
            finish()
            return nc

        with ExitStack() as es:
            sb = lambda name, shape, dt: es.enter_context(nc.sbuf_tensor("D1" + name, list(shape), dt))
            mkT = sb("mkT", [128, 16, 256], BF16)
            mv = sb("mv", [128, 2, D], BF16)
            MKB, MVB, MTB = Buf(), Buf(), Buf()
            Wq = sb("Wq", [128, KC, D], BF16)
            WQB = Buf()
            wqs = P.new_dsem()
            es2 = ExitStack()
            sb2 = lambda name, shape, dt: es2.enter_context(nc.sbuf_tensor("D0" + name, list(shape), dt))
            mT = sb2("memT", [128, KC, 256], BF16)
            P.dma("gpsimd", mT[:, :, :], memT.rearrange("(k p) n -> p k n", p=128), P.new_dsem(), (), [MTB])
            wkvr = Ring(P, [sb2("wkv%d" % i, [128, KC, 512], BF16) for i in range(2)])
            for g in range(8):
                wt, wtb, wts = wkvr.next()
                P.dma("gpsimd", wt[:, :, :], wkv_in[:, g * 512:(g + 1) * 512].rearrange("(k p) n -> p k n", p=128), wts, (), [wtb])
                if g < 4:
                    for j in range(4):
                        bk, bkb = nb()
                        with P.group("tensor"):
                            for kc in range(KC):
                                mm(bk[:, 0:256], wt[:, kc, j * 128:(j + 1) * 128], mT[:, kc, :], kc == 0, kc == KC - 1, [wtb, MTB], [bkb])
                        cp("scalar", mkT[:, 4 * g + j, :], bk[:, 0:256], [bkb], [MKB])
                else:
                    for mt in range(2):
                        bk, bkb = nb()
                        with P.group("tensor"):
                            for kc in range(KC):
                                mm(bk[:, :], mT[:, kc, mt * 128:(mt + 1) * 128], wt[:, kc, :], kc == 0, kc == KC - 1, [wtb, MTB], [bkb])
                        cp("vector", mv[:, mt, (g - 4) * 512:(g - 3) * 512], bk[:, :], [bkb], [MVB])
            for g in range(4):
                P.dma("gpsimd", Wq[:, :, g * 512:(g + 1) * 512], wq_in[:, g * 512:(g + 1) * 512].rearrange("(k p) n -> p k n", p=128), wqs, (), [WQB])
            P.flush()
            es2.close()
            P.barrier()
            aring = Ring(P, [sb("a%d" % i, [128, KC, 512], BF16) for i in range(2)])
            qc = sb("qc", [128, KC, 512], BF16)
            QCB = Buf()
            ocr = Ring(P, [sb("oc%d" % i, [128, KC, 512], BF16) for i in range(2)])
            Pc = [sb("Pc%d" % i, [128, 2, 512], BF16) for i in range(2)]
            PCB = [Buf(), Buf()]
            rden = [sb("rden%d" % i, [128, 512], F32) for i in range(2)]
            RDB = [Buf(), Buf()]
            SCC = 512.0 ** -0.5
            for (c0, c1) in TCH:
                n = c1 - c0
                a, ab, asem = aring.next()
                P.dma("gpsimd", a[:, :, 0:n], x1T[:, c0:c1].rearrange("(k p) n -> p k n", p=128), asem, [X1B], [ab])
                for dc in range(KC):
                    bk, bkb = nb()
                    with P.group("tensor"):
                        for kc in range(KC):
                            mm(bk[:, 0:n], Wq[:, kc, dc * 128:(dc + 1) * 128], a[:, kc, 0:n], kc == 0, kc == KC - 1, [ab, WQB], [bkb])
                    cp("scalar" if dc % 2 == 0 else "vector", qc[:, dc, 0:n], bk[:, 0:n], [bkb], [QCB])
                oc, ocb, ocs = ocr.next()
                for hh in range(4):
                    i2 = hh % 2
                    for mt in range(2):
                        bsx, bsxb = nb()
                        with P.group("tensor"):
                            for c in range(4):
                                mm(bsx[:, 0:n], mkT[:, 4 * hh + c, mt * 128:(mt + 1) * 128], qc[:, 4 * hh + c, 0:n], c == 0, c == 3, [MKB, QCB], [bsxb])
                        act(Pc[i2][:, mt, 0:n], bsx[:, 0:n], AF.Exp, [bsxb], [PCB[i2]], scale=SCC)
                    bd, bdb = nb()
                    with P.group("tensor"):
                        for mt in range(2):
                            mm(bd[:, 0:n], ones_bf[:, :], Pc[i2][:, mt, 0:n], mt == 0, mt == 1, [CB, PCB[i2]], [bdb])
                    recip("vector", rden[i2][:, 0:n], bd[:, 0:n], [bdb], [RDB[i2]])
                    for c in range(4):
                        bo, bob = nb()
                        with P.group("tensor"):
                            for mt in range(2):
                                mm(bo[:, 0:n], mv[:, mt, (4 * hh + c) * 128:(4 * hh + c + 1) * 128], Pc[i2][:, mt, 0:n], mt == 0, mt == 1, [MVB, PCB[i2]], [bob])
                        tt("vector", oc[:, 4 * hh + c, 0:n], bo[:, 0:n], rden[i2][:, 0:n], ALU.mult, [bob, RDB[i2]], [ocb])
                P.dma("sync", ocT[:, c0:c1].rearrange("(k p) n -> p k n", p=128), oc[:, :, 0:n], ocs, [ocb], [OCB])
            P.flush()
        P.barrier()

        if stop == "D1":
            finish()
            return nc
        x1res = lambda dc, c0, c1: x1T[dc * 128:(dc + 1) * 128, c0:c1]
        gemm_ln_phase("D2", wo_in, ocT, False, x1res, 1, x2T, 0, TCH, OCB, X1B, X2B)
        if stop == "D2":
            finish()
            return nc

        with ExitStack() as es:
            sb = lambda name, shape, dt: es.enter_context(nc.sbuf_tensor("E" + name, list(shape), dt))
            x2b = sb("x2b", [128, KC, TQ], BF16)
            X2S = [Buf() for _ in TCH]
            P.dma("gpsimd", x2b[:, :, 0:2], x2T[:, 0:2].rearrange("(k p) n -> p k n", p=128), P.new_dsem(), [X2B], [X2S[0]])
            ts("vector", x2b[:, :, 0:2], x2b[:, :, 0:2], flag[:, 0:1], None, ALU.mult, None, [CB], [X2S[0]])
            for ci_, (c0, c1) in enumerate(TCH):
                if ci_ > 0:
                    P.dma("gpsimd", x2b[:, :, c0:c1], x2T[:, c0:c1].rearrange("(k p) n -> p k n", p=128), P.new_dsem(), [X2B], [X2S[ci_]])
            wr = Ring(P, [sb("w%d" % i, [128, KC, 256], BF16) for i in range(3)])
            ug = [sb("ug%d" % i, [128, TQ], F32) for i in range(2)]
            uu = [sb("uu%d" % i, [128, TQ], F32) for i in range(2)]
            yg = [sb("yg%d" % i, [128, 2048], F32) for i in range(2)]
            yu = [sb("yu%d" % i, [128, 2048], F32) for i in range(2)]
            hr = Ring(P, [sb("h%d" % i, [128, 2048], BF16) for i in range(2)])
            UG, UU, YG, YU = [Buf(), Buf()], [Buf(), Buf()], [Buf(), Buf()], [Buf(), Buf()]
            for j in range(NJ):
                i2 = j % 2
                wt, wtb, wts = wr.next()
                P.dma("gpsimd", wt[:, :, 0:128], fwin_in[:, j * 128:(j + 1) * 128].rearrange("(k p) n -> p k n", p=128), wts, (), [wtb])
                P.dma("gpsimd", wt[:, :, 128:256], fwin_in[:, DFF + j * 128:DFF + (j + 1) * 128].rearrange("(k p) n -> p k n", p=128), wts, (), [wtb])
                P.dma("gpsimd", w2b[j * 128:(j + 1) * 128, :], fwout_in[j * 128:(j + 1) * 128, :], w2sem, (), [W2B])
                for part, (ubuf, UB) in enumerate([(ug[i2], UG[i2]), (uu[i2], UU[i2])]):
                    for ci, (c0, c1) in enumerate(TCH):
                        n = c1 - c0
                        bk, bkb = nb()
                        with P.group("tensor"):
                            for kc in range(KC):
                                mm(bk[:, 0:n], wt[:, kc, part * 128:(part + 1) * 128], x2b[:, kc, c0:c1], kc == 0, kc == KC - 1, [wtb, X2S[ci]], [bkb])
                        cp("scalar", ubuf[:, c0:c1], bk[:, 0:n], [bkb], [UB])
                for part, (eng, ubuf, UB, ybuf, YB_) in enumerate([("vector", ug[i2], UG[i2], yg[i2], YG[i2]), ("gpsimd", uu[i2], UU[i2], yu[i2], YU[i2])]):
                    cj = part * NJ + j
                    act(ybuf[:, :], ubuf[:, 2:TQ], AF.Identity, [UB, CB], [YB_], scale=cw[:, cj, 2:3], bias=cb[:, cj:cj + 1])
                    stt("vector", ybuf[:, :], ubuf[:, 1:TQ - 1], cw[:, cj, 1:2], ybuf[:, :], ALU.mult, ALU.add, [UB, CB], [YB_])
                    stt("vector", ybuf[:, :], ubuf[:, 0:TQ - 2], cw[:, cj, 0:1], ybuf[:, :], ALU.mult, ALU.add, [UB, CB], [YB_])
                act(yg[i2][:, :], yg[i2][:, :], AF.Silu, [], [YG[i2]])
                ht, htb, hts = hr.next()
                tt("vector", ht[:, :], yg[i2][:, :], yu[i2][:, :], ALU.mult, [YG[i2], YU[i2]], [htb])
                P.dma("sync", hT[j * 128:(j + 1) * 128, :], ht[:, :], hts, [htb], [HTB])
            P.flush()
        P.barrier()

        if stop == "E":
            finish()
            return nc
        with ExitStack() as es:
            sb = lambda name, shape, dt: es.enter_context(nc.sbuf_tensor("F" + name, list(shape), dt))
            aring = Ring(P, [sb("a%d" % i, [128, NJ, 512], BF16) for i in range(2)])
            w2r = Ring(P, [sb("w%d" % i, [128, 512], BF16) for i in range(8)])

            def wstream(j, qd_):
                wt, wtb, wts = w2r.next()
                P.dma("sync", wt[:, :], w2b[j * 128:(j + 1) * 128, qd_ * 512:(qd_ + 1) * 512], wts, [W2B], [wtb])
                return wt, wtb
            x2res = lambda dc, c0, c1: x2T[dc * 128:(dc + 1) * 128, 2 + c0:2 + c1]
            ln_chunks("F", sb, [(i * 512, (i + 1) * 512) for i in range(4)], NJ, None, None,
                      hT, False, aring, HTB, x2res, X2B, 2, outT, 0, OUTB, wstream=wstream)
            final = [(s[0], s[1], "dma") for s in P.dsems if s[1] > 0]
            P.flush(final)
    return nc


def _perm_idx():
    idx = np.empty(NL, np.int64)
    for s in range(2):
        for r in range(16):
            idx[s * 2048 + r * 128:s * 2048 + (r + 1) * 128] = s * 2048 + 16 * np.arange(128) + r
    return idx


def _masks():
    m = np.zeros((128, 6, 128), np.float32)
    j = np.arange(128)[:, None]
    i = np.arange(128)[None, :]
    for pi_, nat in enumerate([lambda x: x, lambda x: 4 * (x % 32) + x // 32, lambda x: 16 * (x % 8) + x // 8]):
        nj, ni = nat(j), nat(i)
        m[:, 2 * pi_, :] = (nj >= ni)
        m[:, 2 * pi_ + 1, :] = (nj <= ni)
    return m.reshape(128, 768)


def _fm(v, nchunk):
    return np.ascontiguousarray(v.reshape(nchunk, 128).T)


_CACHE = {}


def make_in_maps(x, mem, positions, w_in, gla_gate_w2, gla_gate_b, gla_norm_g, w_out, ln1_g, ln1_b,
                 ca_wq, ca_wkv, ca_wo, ln2_g, ln2_b, ffn_w_in, ffn_conv_w, ffn_conv_b, ffn_w_out, ln3_g, ln3_b):
    f32 = np.float32
    x = np.asarray(x, f32)
    mem = np.asarray(mem, f32)
    positions = np.asarray(positions, np.int32)
    w_in = np.asarray(w_in, f32)[0]
    pidx = _perm_idx()
    o = 0
    cols = {}
    for name, w in zip(["qg", "kg", "vg", "rg", "glr", "qd", "kd", "vd"], [512, 512, 1024, 1024, 16, 1024, 1024, 1024]):
        cols[name] = w_in[:, o:o + w]
        o += w
    wg = np.stack([np.concatenate([cols["qg"][:, h * 128:(h + 1) * 128], cols["kg"][:, h * 128:(h + 1) * 128],
                                   cols["rg"][:, h * 256:(h + 1) * 256], cols["vg"][:, h * 256:(h + 1) * 256]], axis=1) for h in range(4)])
    swap = np.concatenate([np.arange(16, 32), np.arange(0, 16)])
    wd = np.stack([np.concatenate([cols["qd"][:, h * 128:(h + 1) * 128], cols["kd"][:, h * 128:(h + 1) * 128],
                                   cols["qd"][:, h * 128 + swap], cols["kd"][:, h * 128 + swap]], axis=1) for h in range(8)])
    w2aug = np.concatenate([np.asarray(gla_gate_w2, f32)[0], np.asarray(gla_gate_b, f32)[0][None, :]], axis=0)
    lnp = np.stack([_fm(np.asarray(v, f32)[0], 16) for v in [ln1_g, ln1_b, ln2_g, ln2_b, ln3_g, ln3_b]], axis=1).reshape(128, 96)
    convw = np.ascontiguousarray(np.asarray(ffn_conv_w, f32)[0].T.reshape(86, 128, 3).transpose(1, 0, 2)).reshape(128, 258)
    convb = _fm(np.asarray(ffn_conv_b, f32)[0], 86)
    jj = np.arange(128)[:, None]
    ii = np.arange(128)[None, :]
    uneg = np.where(jj <= ii, f32(-1.0 / 16.0), f32(0.0)).astype(f32)
    invf = (500000.0 ** (-(np.arange(0, 32, 2, dtype=np.float32)) / 32.0)).astype(f32)
    rotc = np.stack([np.concatenate([invf, invf]), np.concatenate([-np.ones(16, f32), np.ones(16, f32)])], axis=1).astype(f32)
    shared = dict(masks=_masks(), uneg=uneg, rotc=rotc, wg=np.ascontiguousarray(wg), wglr=np.ascontiguousarray(cols["glr"]),
                  w2aug=np.ascontiguousarray(w2aug), glag=_fm(np.asarray(gla_norm_g, f32)[0], 8), wd=np.ascontiguousarray(wd),
                  wvd=np.ascontiguousarray(cols["vd"]), w_out=np.asarray(w_out, f32)[0], lnp=np.ascontiguousarray(lnp),
                  ca_wq=np.asarray(ca_wq, f32)[0], ca_wkv=np.asarray(ca_wkv, f32)[0], ca_wo=np.asarray(ca_wo, f32)[0],
                  ffn_w_in=np.asarray(ffn_w_in, f32)[0], convw=convw, convb=convb, ffn_w_out=np.asarray(ffn_w_out, f32)[0])
    in_maps = []
    for c in range(8):
        b, hf = c // 2, c % 2
        xl = np.zeros((NL, D), f32)
        pl = np.zeros((NL,), np.int32)
        if hf == 1:
            xl[:] = x[b]
            pl[:] = positions[b]
        else:
            xl[2048:] = x[b, :2048]
            pl[2048:] = positions[b, :2048]
        xTn = np.ascontiguousarray(xl.T)
        m = dict(shared)
        m.update(xT_nat=xTn, xT_perm=np.ascontiguousarray(xTn[:, pidx]), memT=np.ascontiguousarray(mem[b].T),
                 pos_perm=np.ascontiguousarray(pl[pidx][None, :]), flag=np.full((128, 1), float(hf), f32))
        in_maps.append(m)
    return in_maps


def kernel(**inputs):
    if "nc" not in _CACHE:
        _CACHE["nc"] = build(False)
    nc = _CACHE["nc"]
    in_maps = make_in_maps(**inputs)
    res = run_bass_kernel_spmd(nc, in_maps, core_ids=list(range(8)))
    out = np.empty((4, 4096, D), np.float32)
    for c in range(8):
        b, hf = c // 2, c % 2
        out[b, hf * 2048:(hf + 1) * 2048, :] = np.asarray(res.results[c]["outT"]).T
    return out
```
